# Optimizing a Trainium2 kernel written in Bass

```python
import math
import jax
import jax.numpy as jnp
from jax import lax
import numpy as np

D_MODEL = 1024
BATCH = 4
SEQ = 8192
DEPTH = 2

HEAD_DIM = 64
A_HEADS = 6
A_CONFIGS = ((128, 1), (512, 4), (2048, 16))
B_HEADS = 4
B_KV_HEADS = 2
B_RADIUS = 128
C_HEADS = 6
C_NOPE = 64
C_ROPE = 32
C_VDIM = 64
C_QK = C_NOPE + C_ROPE
C_Q_RANK = 256
C_KV_RANK = 128
ROPE_THETA = 10000.0
Q_BLOCK = 128
N_BUCKETS = 32
MAX_DISTANCE = 1024
BIAS_HEADS = A_HEADS + B_HEADS
D_FF = 4 * D_MODEL
EPS = 1e-6
NEG = -1e30

A_W = A_HEADS * HEAD_DIM
B_QW = B_HEADS * HEAD_DIM
B_KVW = B_KV_HEADS * HEAD_DIM
C_OW = C_HEADS * C_VDIM
D_MIX = A_W + B_QW + C_OW
IN_SPLITS = (A_W, A_W, A_W, B_QW, B_KVW, B_KVW, C_Q_RANK, C_KV_RANK, C_ROPE)
IN_WIDTH = 3 * A_W + B_QW + 2 * B_KVW + C_Q_RANK + C_KV_RANK + C_ROPE

kernel_name = 'hybrid_parallel_heads_encoder'


def rmsnorm(x, g):
    xf = x.astype(jnp.float32)
    y = xf * lax.rsqrt(jnp.mean(xf * xf, axis=-1, keepdims=True) + EPS)
    return (y * g.astype(jnp.float32)).astype(x.dtype)


def split_heads(t, n):
    return t.reshape(t.shape[0], t.shape[1], n, -1)


def t5_bucket(rel):
    half = N_BUCKETS // 2
    exact = half // 2
    n = jnp.abs(rel)
    far = exact + (jnp.log(jnp.maximum(n, 1).astype(jnp.float32) / exact)
                   / math.log(MAX_DISTANCE / exact) * (half - exact)).astype(jnp.int32)
    far = jnp.minimum(far, half - 1)
    return jnp.where(rel > 0, half, 0) + jnp.where(n < exact, n, far)


def band_rel(blk):
    return jnp.arange(3 * blk)[None, :] - blk - jnp.arange(blk)[:, None]


def rel_bias(table, dist):
    return jnp.transpose(table[t5_bucket(dist)], (2, 0, 1))


def banded_attention(q, k, v, radius, bias, sink=None):
    b, L, hq, d = q.shape
    hkv = k.shape[2]
    g = hq // hkv
    blk = radius
    nb = -(-L // blk)
    pad = nb * blk - L
    qb = jnp.pad(q, ((0, 0), (0, pad), (0, 0), (0, 0))).reshape(b, nb, blk, hkv, g, d)

    def windows(t):
        tp = jnp.pad(t, ((0, 0), (blk, pad + blk), (0, 0), (0, 0))).reshape(b, nb + 2, blk, hkv, d)
        return jnp.concatenate([tp[:, :-2], tp[:, 1:-1], tp[:, 2:]], axis=2)

    kw, vw = windows(k), windows(v)
    s = jnp.einsum('bnqhgd,bnkhd->bnhgqk', qb, kw, preferred_element_type=jnp.float32) * (d ** -0.5)
    s = s + bias.astype(jnp.float32).reshape(hkv, g, blk, 3 * blk)
    rel = band_rel(blk)
    kpos = jnp.arange(nb)[:, None, None] * blk + jnp.arange(3 * blk)[None, None, :] - blk
    mask = (jnp.abs(rel)[None] <= radius) & (kpos >= 0) & (kpos < L)
    s = jnp.where(mask[None, :, None, None], s, NEG)
    m = jnp.max(s, axis=-1)
    if sink is not None:
        sk = sink.astype(jnp.float32).reshape(hkv, g)[None, None, :, :, None]
        m = jnp.maximum(m, sk)
    p = jnp.exp(s - m[..., None])
    l = jnp.sum(p, axis=-1)
    if sink is not None:
        l = l + jnp.exp(sk - m)
    o = jnp.einsum('bnhgqk,bnkhd->bnqhgd', p.astype(v.dtype), vw, preferred_element_type=jnp.float32)
    l_t = jnp.transpose(l, (0, 1, 4, 2, 3))
    m_t = jnp.transpose(m, (0, 1, 4, 2, 3))
    o = (o / l_t[..., None]).astype(v.dtype).reshape(b, nb * blk, hq, d)[:, :L]
    return o, m_t.reshape(b, nb * blk, hq)[:, :L], l_t.reshape(b, nb * blk, hq)[:, :L]


def dilated_attention(q, k, v, table):
    b, L, h, d = q.shape
    outs, ms, ls = [], [], []
    for window, r in A_CONFIGS:
        radius = window // (2 * r)
        n = L // r

        def by_residue(t):
            return t.reshape(b, n, r, h, d).transpose(0, 2, 1, 3, 4).reshape(b * r, n, h, d)

        bias = rel_bias(table, band_rel(radius) * r)
        o, m, l = banded_attention(by_residue(q), by_residue(k), by_residue(v), radius, bias)
        outs.append(o.reshape(b, r, n, h, d).transpose(0, 2, 1, 3, 4).reshape(b, L, h, d).astype(jnp.float32))
        ms.append(m.reshape(b, r, n, h).transpose(0, 2, 1, 3).reshape(b, L, h))
        ls.append(l.reshape(b, r, n, h).transpose(0, 2, 1, 3).reshape(b, L, h))
    m_all = jnp.stack(ms)
    w = jnp.stack(ls) * jnp.exp(m_all - jnp.max(m_all, axis=0))
    o = jnp.einsum('cblh,cblhd->blhd', w, jnp.stack(outs)) / jnp.sum(w, axis=0)[..., None]
    return o.astype(q.dtype)


def rope(t, positions):
    half = t.shape[-1] // 2
    inv = ROPE_THETA ** (-jnp.arange(half, dtype=jnp.float32) / half)
    ang = positions.astype(jnp.float32)[:, :, None, None] * inv
    cos, sin = jnp.cos(ang), jnp.sin(ang)
    t1 = t[..., :half].astype(jnp.float32)
    t2 = t[..., half:].astype(jnp.float32)
    return jnp.concatenate([t1 * cos - t2 * sin, t2 * cos + t1 * sin], axis=-1).astype(t.dtype)


def latent_attention(q, k, v):
    b, L, h, dqk = q.shape
    nq = L // Q_BLOCK
    qb = q.reshape(b, nq, Q_BLOCK, h, dqk).transpose(1, 0, 2, 3, 4)

    def block(qblk):
        s = jnp.einsum('bqhd,bkhd->bhqk', qblk, k, preferred_element_type=jnp.float32) * (dqk ** -0.5)
        p = jax.nn.softmax(s, axis=-1)
        return jnp.einsum('bhqk,bkhd->bqhd', p.astype(v.dtype), v)

    o = lax.map(block, qb)
    return o.transpose(1, 0, 2, 3, 4).reshape(b, L, h, v.shape[-1])


def setup_inputs(seed: int = 0) -> dict:
    key = jax.random.key(seed)
    ks = jax.random.split(key, 20)

    def nrm(k, shape, scale):
        return jax.random.normal(k, shape, jnp.float32) * scale

    def gain(k, shape):
        return 1.0 + 0.02 * jax.random.normal(k, shape, jnp.float32)

    offset = jax.random.randint(ks[1], (BATCH, 1), 0, 4096, dtype=jnp.int32)
    positions = (jnp.arange(SEQ, dtype=jnp.int32)[None, :] + offset).astype(jnp.int32)
    return {
        'x': nrm(ks[0], (BATCH, SEQ, D_MODEL), 1.0),
        'positions': positions,
        'rel_bias_table': nrm(ks[2], (N_BUCKETS, BIAS_HEADS), 0.2),
        'norm_mix': gain(ks[3], (DEPTH, D_MODEL)),
        'w_in': nrm(ks[4], (DEPTH, D_MODEL, IN_WIDTH), D_MODEL ** -0.5),
        'qk_gain_a': gain(ks[5], (DEPTH, 2, HEAD_DIM)),
        'qk_gain_b': gain(ks[6], (DEPTH, 2, HEAD_DIM)),
        'sink_b': nrm(ks[7], (DEPTH, B_HEADS), 0.5),
        'q_lat_gain': gain(ks[8], (DEPTH, C_Q_RANK)),
        'kv_lat_gain': gain(ks[9], (DEPTH, C_KV_RANK)),
        'w_uq': nrm(ks[10], (DEPTH, C_Q_RANK, C_HEADS * C_QK), C_Q_RANK ** -0.5),
        'w_ukv': nrm(ks[11], (DEPTH, C_KV_RANK, C_HEADS * (C_NOPE + C_VDIM)), C_KV_RANK ** -0.5),
        'qk_gain_c': gain(ks[12], (DEPTH, 2, C_QK)),
        'out_norm': gain(ks[13], (DEPTH, D_MIX)),
        'w_out': nrm(ks[14], (DEPTH, D_MIX, D_MODEL), D_MIX ** -0.5),
        'norm_mlp': gain(ks[15], (DEPTH, D_MODEL)),
        'w_up': nrm(ks[16], (DEPTH, D_MODEL, D_FF), D_MODEL ** -0.5),
        'w_down': nrm(ks[17], (DEPTH, D_FF, D_MODEL), D_FF ** -0.5),
    }


def reference(x, positions, rel_bias_table, norm_mix, w_in, qk_gain_a, qk_gain_b, sink_b,
              q_lat_gain, kv_lat_gain, w_uq, w_ukv, qk_gain_c, out_norm, w_out,
              norm_mlp, w_up, w_down):
    b, L, _ = x.shape
    split_at = np.cumsum(IN_SPLITS)[:-1].tolist()
    table_a = rel_bias_table[:, :A_HEADS]
    bias_b = rel_bias(rel_bias_table[:, A_HEADS:], band_rel(B_RADIUS))
    for i in range(DEPTH):
        h = rmsnorm(x, norm_mix[i])
        qa, ka, va, qb, kb, vb, cq, ckv, kr = jnp.split(h @ w_in[i], split_at, axis=-1)

        qa = rmsnorm(split_heads(qa, A_HEADS), qk_gain_a[i, 0])
        ka = rmsnorm(split_heads(ka, A_HEADS), qk_gain_a[i, 1])
        o_a = dilated_attention(qa, ka, split_heads(va, A_HEADS), table_a)

        qb = rmsnorm(split_heads(qb, B_HEADS), qk_gain_b[i, 0])
        kb = rmsnorm(split_heads(kb, B_KV_HEADS), qk_gain_b[i, 1])
        o_b, _, _ = banded_attention(qb, kb, split_heads(vb, B_KV_HEADS), B_RADIUS, bias_b, sink_b[i])

        qc = split_heads(rmsnorm(cq, q_lat_gain[i]) @ w_uq[i], C_HEADS)
        kvc = split_heads(rmsnorm(ckv, kv_lat_gain[i]) @ w_ukv[i], C_HEADS)
        vc = kvc[..., C_NOPE:]
        kc = jnp.concatenate(
            [kvc[..., :C_NOPE], jnp.broadcast_to(kr[:, :, None, :], (b, L, C_HEADS, C_ROPE))], axis=-1)
        qc = rmsnorm(qc, qk_gain_c[i, 0])
        kc = rmsnorm(kc, qk_gain_c[i, 1])
        qc = jnp.concatenate([qc[..., :C_NOPE], rope(qc[..., C_NOPE:], positions)], axis=-1)
        kc = jnp.concatenate([kc[..., :C_NOPE], rope(kc[..., C_NOPE:], positions)], axis=-1)
        o_c = latent_attention(qc, kc, vc)

        g = out_norm[i]
        mixed = jnp.concatenate([
            rmsnorm(o_a.reshape(b, L, A_W), g[:A_W]),
            rmsnorm(o_b.reshape(b, L, B_QW), g[A_W:A_W + B_QW]),
            rmsnorm(o_c.reshape(b, L, C_OW), g[A_W + B_QW:]),
        ], axis=-1)
        x = x + mixed @ w_out[i]

        h = rmsnorm(x, norm_mlp[i])
        x = x + jnp.square(jax.nn.relu(h @ w_up[i])) @ w_down[i]
    return x
```

```python
import math
import ml_dtypes
from concourse.bass_utils import run_bass_kernel_spmd
import numpy as np
import concourse.bass as bass
import concourse.mybir as mybir
from contextlib import ExitStack

F32 = mybir.dt.float32
BF16 = mybir.dt.bfloat16
I32 = mybir.dt.int32
ALU = mybir.AluOpType
AF = mybir.ActivationFunctionType
AX = mybir.AxisListType

ENGS = ['sync', 'scalar', 'vector', 'gpsimd', 'tensor']
SEM_ROT = 24000


class Buf:
    __slots__ = ('name', 'writer', 'readers', 'dreaders', 'multi', 'mw')

    def __init__(self, name, multi=False):
        self.name = name
        self.writer = None
        self.readers = {}
        self.dreaders = []
        self.multi = multi
        self.mw = []


class Op:
    __slots__ = ('eng', 'fn', 'deps', 'idx', 'signal', 'is_dma', 'lane', 'ev', 'raw', 'barrier')


class Prog:
    def __init__(self, nc, es):
        self.nc = nc
        self.es = es
        self.ops = {e: [] for e in ENGS}
        self.order = []
        self.lanes = {}
        self.nsem = 0
        self.fence = []
        self.fence_pending = set()

    def phase_barrier(self):
        fence = []
        for e in ENGS:
            for o in reversed(self.ops[e]):
                if not o.is_dma and not o.barrier:
                    fence.append(o)
                    break
        last = {}
        for o in self.order:
            if o.is_dma:
                last[o.lane] = o
        fence += list(last.values())
        self.fence = fence
        self.fence_pending = set(ENGS)

    def new_sem(self, name):
        self.nsem += 1
        return self.es.enter_context(self.nc.semaphore(name))

    def sb(self, name, shape, dt):
        return self.es.enter_context(self.nc.sbuf_tensor(name, shape, dt))

    def ps(self, name, shape, dt):
        return self.es.enter_context(self.nc.psum_tensor(name, shape, dt))

    def barrier(self, eng, reads=(), writes=()):
        o = self.op(eng, lambda e: None, reads, writes)
        o.barrier = True
        return o

    def op(self, eng, fn, reads=(), writes=(), lane=None):
        o = Op()
        o.eng = eng
        o.fn = fn
        o.barrier = False
        o.is_dma = lane is not None
        o.lane = lane
        o.signal = False
        o.ev = None
        deps = {}
        raw = set()
        for b in reads:
            if b.writer is not None:
                deps[id(b.writer)] = b.writer
                raw.add(id(b.writer))
            for w in b.mw:
                deps[id(w)] = w
                raw.add(id(w))
        for b in writes:
            if b.writer is not None and not b.multi:
                deps[id(b.writer)] = b.writer
                raw.add(id(b.writer))
            for r in b.readers.values():
                deps[id(r)] = r
            for r in b.dreaders:
                deps[id(r)] = r
        if eng in self.fence_pending:
            self.fence_pending.discard(eng)
            for w in self.fence:
                deps[id(w)] = w
        deps.pop(id(o), None)
        o.deps = list(deps.values())
        o.raw = raw
        for b in writes:
            if b.multi:
                b.mw.append(o)
            else:
                b.writer = o
            b.readers = {}
            b.dreaders = []
        for b in reads:
            if b.multi:
                continue
            if o.is_dma:
                b.dreaders.append(o)
            else:
                b.readers[eng] = o
        o.idx = len(self.ops[eng])
        self.ops[eng].append(o)
        self.order.append(o)
        return o

    def dma(self, q, out, in_, reads=(), writes=(), lane=None, **kw):
        assert lane is not None
        return self.op(q, lambda e: e.dma_start(out=out, in_=in_, **kw), reads, writes, lane=lane)

    def emit(self):
        nc = self.nc
        for o in self.order:
            for d in o.deps:
                if d.is_dma:
                    continue
                if d.barrier:
                    assert d.eng == o.eng, 'barrier dep across engines'
                    continue
                if d.eng == o.eng and not o.is_dma:
                    if o.eng == 'tensor':
                        continue
                    if id(d) not in o.raw:
                        continue
                d.signal = True
        esems = {}
        for e in ENGS:
            cnt = 0
            cur = None
            for o in self.ops[e]:
                if o.is_dma:
                    ln = self.lanes.get(o.lane)
                    if ln is None:
                        ln = [self.new_sem('l_%s' % o.lane), 0]
                        self.lanes[o.lane] = ln
                    ln[1] += 16
                    o.ev = (ln[0], ln[1])
                elif o.signal:
                    if cur is None or cnt >= SEM_ROT:
                        cur = self.new_sem('e_%s_%d' % (e, len(esems)))
                        esems[(e, len(esems))] = cur
                        cnt = 0
                    cnt += 1
                    o.ev = (cur, cnt)
        blk = self.es.enter_context(nc.Block())
        prog = self

        def run(e, eng):
            waited = {}
            for o in prog.ops[e]:
                need = {}
                for d in o.deps:
                    if not d.is_dma:
                        if d.barrier:
                            continue
                        if d.eng == o.eng and not o.is_dma:
                            if o.eng == 'tensor' or id(d) not in o.raw:
                                continue
                    sem, val = d.ev
                    k = id(sem)
                    if k not in need or need[k][1] < val:
                        need[k] = (sem, val)
                for k, (sem, val) in need.items():
                    if waited.get(k, 0) >= val:
                        continue
                    eng.wait_ge(sem, val)
                    waited[k] = val
                ins = o.fn(eng)
                if ins is None:
                    continue
                if o.is_dma:
                    ins.then_inc(o.ev[0], 16)
                elif o.signal:
                    ins.then_inc(o.ev[0], 1)

        @blk.sync
        def _(eng):
            run('sync', eng)

        @blk.scalar
        def _(eng):
            run('scalar', eng)

        @blk.vector
        def _(eng):
            run('vector', eng)

        @blk.gpsimd
        def _(eng):
            run('gpsimd', eng)

        @blk.tensor
        def _(eng):
            run('tensor', eng)

import numpy as np
import math

EPS = 1e-6
NEGB = -30000.0
TWO_PI_S = 6.2831845


def bc(ap, shape):
    return ap.to_broadcast(list(shape))


def psum_view(ph, nc, name, shape, dt):
    esz = 4 if dt == F32 else 2
    n = 1
    for d in shape[1:]:
        n *= d
    per_bank = 2048 // esz
    tot = ((n + per_bank - 1) // per_bank) * per_bank
    t = ph.enter_context(nc.psum_tensor(name, [128, tot], dt))
    v = t[0:shape[0], 0:n]
    if len(shape) == 3:
        v = v.rearrange("p (a b) -> p a b", a=shape[1])
    elif len(shape) == 4:
        v = v.rearrange("p (a b c) -> p a b c", a=shape[1], b=shape[2])
    return v


class LayerBuilder:
    def __init__(self, nc, P, D, debug=False):
        self.nc = nc
        self.P = P
        self.D = D
        self.debug = debug
        self.uid = 0

    def name(self, s):
        self.uid += 1
        return "%s_%d" % (s, self.uid)

    def V(self, fn, reads, writes, **kw):
        return self.P.op('vector', lambda e: getattr(e, fn)(**kw), reads, writes)

    def G(self, fn, reads, writes, **kw):
        return self.P.op('gpsimd', lambda e: getattr(e, fn)(**kw), reads, writes)

    def A(self, fn, reads, writes, **kw):
        return self.P.op('scalar', lambda e: getattr(e, fn)(**kw), reads, writes)

    def T(self, fn, reads, writes, **kw):
        return self.P.op('tensor', lambda e: getattr(e, fn)(**kw), reads, writes)

    def E(self, eng, fn, reads, writes, **kw):
        return self.P.op(eng, lambda e: getattr(e, fn)(**kw), reads, writes)

    def dma(self, q, out, in_, reads, writes, lane, **kw):
        return self.P.dma(q, out, in_, reads=reads, writes=writes, lane=lane, **kw)

    def phase1(self, L, x_own, x_oth, x_halo, first=True, pos_key='pos', valid_key='valid', halo_rows=None,
               skip_oth=False, skip_ck=False):
        nc, P, D = self.nc, self.P, self.D
        V, G, A, T, E = self.V, self.G, self.A, self.T, self.E
        with ExitStack() as ph:
            def sb(nm, shape, dt, n=1):
                r = []
                for i in range(n):
                    t = ph.enter_context(nc.sbuf_tensor(self.name(nm), shape, dt))
                    r.append((t, Buf(nm + str(i))))
                return r if n > 1 else r[0]

            def ps(nm, shape, dt):
                t = psum_view(ph, nc, self.name(nm), shape, dt)
                return (t, Buf(nm))

            idb, b_idb = sb("idb", [128, 128], BF16)
            wib, b_wib = sb("wib", [128, 8, 2080], BF16)
            wuq, b_wuq = sb("wuq", [128, 2, 576], BF16)
            wukv, b_wukv = sb("wukv", [128, 768], BF16)
            g8, b_g8 = sb("g8", [128, 8], F32)
            gq2, b_gq2 = sb("gq2", [128, 2], F32)
            gkv1, b_gkv1 = sb("gkv1", [128, 1], F32)
            ga, b_ga = sb("ga", [128, 2, 64], F32)
            gb, b_gb = sb("gb", [128, 2, 64], F32)
            gc, b_gc = sb("gc", [128, 2, 96], F32)
            GAq, b_GAq = sb("GAq", [128, 64], F32)
            GBq, b_GBq = sb("GBq", [128, 64], F32)
            GCq, b_GCq = sb("GCq", [128, 96], F32)
            invf, b_invf = sb("invf", [128, 16], F32)
            posi, b_posi = sb("posi", [128, 64], I32)
            posf, b_posf = sb("posf", [128, 64], F32)
            ang, b_ang = sb("ang", [128, 64, 16], F32)
            angk, b_angk = sb("angk", [128, 64, 16], I32)
            angf, b_angf = sb("angf", [128, 64, 16], F32)
            sin_t, b_sin = sb("sin_t", [128, 64, 16], F32)
            cos_t, b_cos = sb("cos_t", [128, 64, 16], F32)
            invd, b_invd = sb("invd", [128, 24], F32)
            nh24, b_nh24 = sb("nh24", [128, 24], F32)
            valid, b_valid = sb("valid", [128, 16], F32)
            stage = sb("stage", [128, 2080], F32, 2)
            stq, b_stq = sb("stq", [128, 2, 576], F32)
            stkv, b_stkv = sb("stkv", [128, 768], F32)

            self.dma('sync', idb[:], D['idb'].ap(), [], [b_idb], 'idb')
            self.dma('sync', g8[:], D['norm_mix'].ap()[L].rearrange("(c p) -> p c", p=128), [], [b_g8], 'g8',
                     allow_slow_non_contiguous=True)
            self.dma('sync', gq2[:], D['q_lat_gain'].ap()[L].rearrange("(c p) -> p c", p=128), [], [b_gq2], 'gq2',
                     allow_slow_non_contiguous=True)
            self.dma('sync', gkv1[:], D['kv_lat_gain'].ap()[L].rearrange("(c p) -> p c", p=128), [], [b_gkv1],
                     'gkv1', allow_slow_non_contiguous=True)
            self.dma('sync', ga[:], D['qk_gain_a'].ap()[L].rearrange("a d -> (a d)").partition_broadcast(128),
                     [], [b_ga], 'ga')
            self.dma('sync', gb[:], D['qk_gain_b'].ap()[L].rearrange("a d -> (a d)").partition_broadcast(128),
                     [], [b_gb], 'gb')
            self.dma('sync', gc[:], D['qk_gain_c'].ap()[L].rearrange("a d -> (a d)").partition_broadcast(128),
                     [], [b_gc], 'gc')
            self.dma('sync', invf[:], D['invf'].ap().rearrange("a d -> (a d)").partition_broadcast(128),
                     [], [b_invf], 'invf')
            self.dma('sync', posi[:], D[pos_key].ap(), [], [b_posi], 'posi')
            self.dma('sync', valid[:], D[valid_key].ap(), [], [b_valid], 'valid')
            V('scalar_tensor_tensor', [b_ga], [b_GAq], out=GAq[:], in0=ga[:, 0, :], scalar=0.125, in1=ga[:, 1, :],
              op0=ALU.mult, op1=ALU.mult)
            V('scalar_tensor_tensor', [b_gb], [b_GBq], out=GBq[:], in0=gb[:, 0, :], scalar=0.125, in1=gb[:, 1, :],
              op0=ALU.mult, op1=ALU.mult)
            V('tensor_scalar', [b_gc], [b_GCq], out=GCq[:], in0=gc[:, 0, :], scalar1=96.0 ** -0.5, scalar2=None,
              op0=ALU.mult)
            GCk = gc[:, 1, :]
            b_GCk = b_gc
            self.P.op('gpsimd', lambda e: e.memset(invd[:], 1.0 / 64), [], [b_invd])
            self.P.op('gpsimd', lambda e: e.memset(invd[:, 18:19], 1.0 / 256), [], [b_invd])
            self.P.op('gpsimd', lambda e: e.memset(invd[:, 19:20], 1.0 / 128), [], [b_invd])
            self.P.op('gpsimd', lambda e: e.memset(invd[:, 20:24], 1.0), [], [b_invd])
            self.P.op('gpsimd', lambda e: e.memset(nh24[:], -0.5), [], [b_nh24])

            V('tensor_copy', [b_posi], [b_posf], out=posf[:], in_=posi[:])
            V('tensor_tensor', [b_posf, b_invf], [b_ang], out=ang[:],
              in0=bc(posf[:].unsqueeze(2), [128, 64, 16]), in1=bc(invf[:].unsqueeze(1), [128, 64, 16]), op=ALU.mult)
            for (tab, b_tab, off) in ((sin_t, b_sin, 0.0), (cos_t, b_cos, 0.25)):
                V('tensor_scalar', [b_ang], [b_angf], out=angf[:], in0=ang[:], scalar1=1.0 / (2 * math.pi),
                  scalar2=off, op0=ALU.mult, op1=ALU.add)
                V('tensor_copy', [b_angf], [b_angk], out=angk[:], in_=angf[:])
                V('tensor_copy', [b_angk], [b_tab], out=tab[:], in_=angk[:])
                V('tensor_tensor', [b_angf, b_tab], [b_angf], out=angf[:], in0=angf[:], in1=tab[:], op=ALU.subtract)
                A('activation', [b_angf], [b_tab], out=tab[:], in_=angf[:], func=AF.Sin, scale=TWO_PI_S)

            blocks = [(0, 384, 0), (384, 768, 512), (768, 1152, 1024), (1152, 1408, 1536), (1408, 1536, 896),
                      (1536, 1664, 1408), (1664, 1920, 1792), (1920, 2048, 384), (2048, 2080, 2048)]
            k = 0
            for c in range(8):
                st_t, st_b = stage[c % 2]
                self.dma('sync', st_t[:], D['w_in'].ap()[L, c * 128:(c + 1) * 128, :], [], [st_b], 'stage%d' % (c % 2))
                for (o0, o1, n0) in blocks:
                    eng = 'vector' if k % 2 == 0 else 'gpsimd'
                    k += 1
                    E(eng, 'tensor_scalar', [st_b, b_g8], [b_wib], out=wib[:, c, n0:n0 + (o1 - o0)],
                      in0=st_t[:, o0:o1], scalar1=g8[:, c:c + 1], scalar2=None, op0=ALU.mult)
            self.dma('sync', stq[:], D['w_uq'].ap()[L].rearrange("(c p) n -> p c n", p=128), [], [b_stq], 'stq')
            self.dma('sync', stkv[:], D['w_ukv'].ap()[L], [], [b_stkv], 'stkv')
            for c in range(2):
                V('tensor_scalar', [b_stq, b_gq2], [b_wuq], out=wuq[:, c, :], in0=stq[:, c, :],
                  scalar1=gq2[:, c:c + 1], scalar2=None, op0=ALU.mult)
            V('tensor_scalar', [b_stkv, b_gkv1], [b_wukv], out=wukv[:], in0=stkv[:], scalar1=gkv1[:, 0:1],
              scalar2=None, op0=ALU.mult)

            xt = sb("xt", [128, 1024], F32, 3)
            junk, b_junk = sb("junk", [128, 1024], BF16)
            ssx = sb("ssx", [128, 1], F32, 2)
            rsx = sb("rsx", [128, 1], F32, 2)
            hb = sb("hb", [128, 1024], BF16, 2)
            hT = sb("hT", [128, 8, 128], BF16, 2)
            pj = sb("pj", [128, 2080], F32, 3)
            sq, b_sq = sb("sq", [128, 2080], F32)
            st = sb("st", [128, 24], F32, 3)
            rstd = sb("rstd", [128, 24], F32, 3)
            QAo = sb("QAo", [128, 384], BF16, 2)
            QAt = sb("QAt", [128, 384], F32, 1)
            KABo = sb("KABo", [128, 512], BF16, 2)
            QBo = sb("QBo", [128, 256], BF16, 2)
            QBt = sb("QBt", [128, 256], F32, 1)
            VABo = sb("VABo", [128, 8, 65], BF16, 2)
            LAT = sb("LAT", [128, 384], BF16, 2)
            latT = sb("latT", [128, 3, 128], BF16, 2)
            qcs = sb("qcs", [128, 576], F32, 2)
            kvcs = sb("kvcs", [128, 768], F32, 2)
            st2 = sb("st2", [128, 12], F32, 2)
            rstd2 = sb("rstd2", [128, 12], F32, 2)
            tmp1, b_tmp1 = sb("tmp1", [128, 6, 96], F32)
            trq, b_trq = sb("trq", [128, 6, 32], F32)
            tmpk, b_tmpk = sb("tmpk", [128, 6, 64], F32)
            krg, b_krg = sb("krg", [128, 1, 32], F32)
            krr, b_krr = sb("krr", [128, 1, 32], F32)
            rm = [sb("rm%d" % i, [128, 6, 16], F32) for i in range(4)]
            QCo = sb("QCo", [128, 6, 96], BF16, 2)
            KCo = sb("KCo", [128, 6, 96], BF16, 2)
            VCo = sb("VCo", [128, 6, 65], BF16, 2)
            QTs = sb("QTs", [96, 6, 128], BF16, 2)
            KTs = sb("KTs", [96, 6, 128], BF16, 2)
            TR, b_TR = ps("TR", [128, 8, 128], BF16)
            PJ = [ps("PJ%d" % i, [128, 512], F32) for i in range(5)]
            S = [ps("S%d" % i, [128, 512], F32) for i in range(2)]

            for (t_, b_) in VABo:
                self.P.op('gpsimd', lambda e, t_=t_: e.memset(t_[:], 1.0), [], [b_])
            for (t_, b_) in VCo:
                self.P.op('gpsimd', lambda e, t_=t_: e.memset(t_[:], 1.0), [], [b_])

            def rope(src, b_src, dst, b_dst, H, ti):
                cb = bc(cos_t[:, ti, :].unsqueeze(1), [128, H, 16])
                sbb = bc(sin_t[:, ti, :].unsqueeze(1), [128, H, 16])
                (m1, b1), (m2, b2), (m3, b3), (m4, b4) = rm
                G('tensor_tensor', [b_src, b_cos], [b1], out=m1[:, 0:H, :], in0=src[:, :, 0:16], in1=cb, op=ALU.mult)
                G('tensor_tensor', [b_src, b_sin], [b2], out=m2[:, 0:H, :], in0=src[:, :, 16:32], in1=sbb, op=ALU.mult)
                G('tensor_tensor', [b1, b2], [b_dst], out=dst[:, :, 0:16], in0=m1[:, 0:H, :], in1=m2[:, 0:H, :],
                  op=ALU.subtract)
                G('tensor_tensor', [b_src, b_cos], [b3], out=m3[:, 0:H, :], in0=src[:, :, 16:32], in1=cb, op=ALU.mult)
                G('tensor_tensor', [b_src, b_sin], [b4], out=m4[:, 0:H, :], in0=src[:, :, 0:16], in1=sbb, op=ALU.mult)
                G('tensor_tensor', [b3, b4], [b_dst], out=dst[:, :, 16:32], in0=m3[:, 0:H, :], in1=m4[:, 0:H, :],
                  op=ALU.add)

            SC = self.SC
            it = 0
            jobs = [('own', t) for t in range(32)] + [('oth', t) for t in range(32)] + [('halo', t) for t in range(16)]
            if skip_oth:
                jobs = [j for j in jobs if j[0] != 'oth']
            if self.debug and self.debug.get('p1_tiles'):
                jobs = self.debug['p1_tiles']
            def tile_gen(it, kind, t):
                s2 = it % 2
                s3 = it % 3
                src = {'own': x_own, 'oth': x_oth, 'halo': x_halo}[kind]
                x_t, b_x = xt[s3]
                ss_t, b_ss = ssx[s2]
                rs_t, b_rs = rsx[s2]
                hb_t, b_hb = hb[s2]
                hT_t, b_hT = hT[s2]
                pj_t, b_pj = pj[s3]
                st_t, b_st = st[s3]
                rstd_t, b_rstd = rstd[s3]
                if kind == 'halo':
                    G('tensor_scalar', [b_x, b_valid], [b_x], out=x_t[:], in0=x_t[:], scalar1=valid[:, t:t + 1],
                      scalar2=None, op0=ALU.mult)
                A('activation', [b_x], [b_junk, b_ss], out=junk[:], in_=x_t[:], func=AF.Square, accum_out=ss_t[:])
                G('tensor_scalar', [b_ss], [b_ss], out=ss_t[:], in0=ss_t[:], scalar1=1.0 / 1024, scalar2=EPS,
                  op0=ALU.mult, op1=ALU.add)
                G('tensor_tensor', [b_ss, b_nh24], [b_rs], out=rs_t[:], in0=ss_t[:], in1=nh24[:, 0:1], op=ALU.pow)
                A('activation', [b_x, b_rs], [b_hb], out=hb_t[:], in_=x_t[:], func=AF.Copy, scale=rs_t[:, 0:1])
                for c in range(8):
                    T('transpose', [b_hb, b_idb], [b_TR], out=TR[:, c, :], in_=hb_t[:, c * 128:(c + 1) * 128],
                      identity=idb[:])
                V('tensor_copy', [b_TR], [b_hT], out=hT_t[:], in_=TR[:])
                if kind == 'own':
                    groups = [(0, 0, 512, 0), (1, 512, 1024, 0), (2, 1024, 1536, 0), (3, 1536, 2048, 0),
                              (4, 2048, 2080, 0)]
                elif kind == 'oth':
                    groups = [(0, 384, 512, 384), (4, 2048, 2080, 0)]
                else:
                    groups = [(1, 512, 1024, 0), (2, 1024, 1536, 0)]
                for (bk, c0, c1, po) in groups:
                    pt, pb = PJ[bk]
                    for c in range(8):
                        T('matmul', [b_hT, b_wib], [pb], out=pt[:, po:po + (c1 - c0)], lhsT=hT_t[:, c, :],
                          rhs=wib[:, c, c0:c1], start=(c == 0), stop=(c == 7))
                    A('activation', [pb], [b_pj], out=pj_t[:, c0:c1], in_=pt[:, po:po + (c1 - c0)], func=AF.Copy)
                yield
                if kind == 'own':
                    V('tensor_tensor', [b_pj], [b_sq], out=sq[:, 0:1024], in0=pj_t[:, 0:1024], in1=pj_t[:, 0:1024],
                      op=ALU.mult)
                    V('tensor_tensor', [b_pj], [b_sq], out=sq[:, 1536:2080], in0=pj_t[:, 1536:2080],
                      in1=pj_t[:, 1536:2080], op=ALU.mult)
                    red = [(0, 6, 0, 384, 64), (6, 14, 512, 1024, 64), (14, 18, 1536, 1792, 64),
                           (18, 19, 1792, 2048, 256), (19, 20, 384, 512, 128), (20, 21, 2048, 2080, 32)]
                elif kind == 'oth':
                    V('tensor_tensor', [b_pj], [b_sq], out=sq[:, 384:512], in0=pj_t[:, 384:512], in1=pj_t[:, 384:512],
                      op=ALU.mult)
                    V('tensor_tensor', [b_pj], [b_sq], out=sq[:, 2048:2080], in0=pj_t[:, 2048:2080],
                      in1=pj_t[:, 2048:2080], op=ALU.mult)
                    red = [(19, 20, 384, 512, 128), (20, 21, 2048, 2080, 32)]
                else:
                    V('tensor_tensor', [b_pj], [b_sq], out=sq[:, 512:1024], in0=pj_t[:, 512:1024],
                      in1=pj_t[:, 512:1024], op=ALU.mult)
                    red = [(6, 14, 512, 1024, 64)]
                for (a0, a1, c0, c1, dd) in red:
                    V('tensor_reduce', [b_sq], [b_st], out=st_t[:, a0:a1],
                      in_=sq[:, c0:c1].rearrange("p (h d) -> p h d", d=dd), axis=AX.X, op=ALU.add)
                G('tensor_tensor', [b_st, b_invd], [b_rstd], out=rstd_t[:, 0:20], in0=st_t[:, 0:20], in1=invd[:, 0:20],
                  op=ALU.mult)
                G('tensor_scalar', [b_rstd], [b_rstd], out=rstd_t[:, 0:20], in0=rstd_t[:, 0:20], scalar1=EPS,
                  scalar2=None, op0=ALU.add)
                G('tensor_tensor', [b_rstd, b_nh24], [b_rstd], out=rstd_t[:, 0:20], in0=rstd_t[:, 0:20],
                  in1=nh24[:, 0:20], op=ALU.pow)
                if kind in ('own', 'halo'):
                    et = (8 + t) if kind == 'own' else (t if t < 8 else 40 + (t - 8))
                    kab_t, b_kab = KABo[s2]
                    vab_t, b_vab = VABo[s2]
                    V('tensor_tensor', [b_pj, b_rstd], [b_kab], out=kab_t[:].rearrange("p (h d) -> p h d", d=64),
                      in0=pj_t[:, 512:1024].rearrange("p (h d) -> p h d", d=64),
                      in1=bc(rstd_t[:, 6:14].unsqueeze(2), [128, 8, 64]), op=ALU.mult)
                    V('tensor_copy', [b_pj], [b_vab], out=vab_t[:, :, 0:64],
                      in_=pj_t[:, 1024:1536].rearrange("p (h d) -> p h d", d=64))
                    if kind == 'halo':
                        V('tensor_copy', [b_valid], [b_vab], out=vab_t[:, :, 64:65],
                          in_=bc(valid[:, t:t + 1].unsqueeze(1), [128, 8, 1]))
                    else:
                        self.P.op('vector', lambda e, vab_t=vab_t: e.memset(vab_t[:, :, 64:65], 1.0), [], [b_vab])
                    rows = slice(et * 128, (et + 1) * 128)
                    self.dma('sync', SC['KAx'].ap()[rows, :], kab_t[:, 0:384], [b_kab], [SC['b']], 'kabo%d' % s2)
                    self.dma('sync', SC['KBx'].ap()[rows, :], kab_t[:, 384:512], [b_kab], [SC['b']], 'kabo%d' % s2)
                    self.dma('sync', SC['VAx'].ap()[rows, :].rearrange("p (h d) -> p h d", d=65), vab_t[:, 0:6, :],
                             [b_vab], [SC['b']], 'vabo%d' % s2)
                    self.dma('sync', SC['VBx'].ap()[rows, :].rearrange("p (h d) -> p h d", d=65), vab_t[:, 6:8, :],
                             [b_vab], [SC['b']], 'vabo%d' % s2)
                if kind == 'own':
                    rows = slice(t * 128, (t + 1) * 128)
                    qa_t, b_qa = QAo[s2]
                    qat, b_qat = QAt
                    V('tensor_tensor', [b_pj, b_rstd], [b_qat], out=qat[:].rearrange("p (h d) -> p h d", d=64),
                      in0=pj_t[:, 0:384].rearrange("p (h d) -> p h d", d=64),
                      in1=bc(rstd_t[:, 0:6].unsqueeze(2), [128, 6, 64]), op=ALU.mult)
                    V('tensor_tensor', [b_qat, b_GAq], [b_qa], out=qa_t[:].rearrange("p (h d) -> p h d", d=64),
                      in0=qat[:].rearrange("p (h d) -> p h d", d=64),
                      in1=bc(GAq[:].unsqueeze(1), [128, 6, 64]), op=ALU.mult)
                    self.dma('sync', SC['QA'].ap()[rows, :], qa_t[:], [b_qa], [SC['b']], 'qao%d' % s2)
                    qb_t, b_qb = QBo[s2]
                    qbt, b_qbt = QBt
                    V('tensor_tensor', [b_pj, b_rstd], [b_qbt], out=qbt[:].rearrange("p (h d) -> p h d", d=64),
                      in0=pj_t[:, 1536:1792].rearrange("p (h d) -> p h d", d=64),
                      in1=bc(rstd_t[:, 14:18].unsqueeze(2), [128, 4, 64]), op=ALU.mult)
                    V('tensor_tensor', [b_qbt, b_GBq], [b_qb],
                      out=qb_t[:].rearrange("p (b a d) -> p a b d", b=2, a=2, d=64),
                      in0=qbt[:].rearrange("p (a b d) -> p a b d", a=2, b=2, d=64),
                      in1=bc(GBq[:].unsqueeze(1).unsqueeze(1), [128, 2, 2, 64]), op=ALU.mult)
                    self.dma('sync', SC['QB'].ap()[rows, :], qb_t[:], [b_qb], [SC['b']], 'qbo%d' % s2)
                if kind in ('own', 'oth'):
                    ti = t if kind == 'own' else 32 + t
                    lat_t, b_lat = LAT[s2]
                    latT_t, b_latT = latT[s2]
                    qcs_t, b_qcs = qcs[s2]
                    kvcs_t, b_kvcs = kvcs[s2]
                    st2_t, b_st2 = st2[s2]
                    rstd2_t, b_rstd2 = rstd2[s2]
                    if kind == 'own':
                        V('tensor_scalar', [b_pj, b_rstd], [b_lat], out=lat_t[:, 0:256], in0=pj_t[:, 1792:2048],
                          scalar1=rstd_t[:, 18:19], scalar2=None, op0=ALU.mult)
                    if not skip_ck:
                        V('tensor_scalar', [b_pj, b_rstd], [b_lat], out=lat_t[:, 256:384], in0=pj_t[:, 384:512],
                          scalar1=rstd_t[:, 19:20], scalar2=None, op0=ALU.mult)
                    jl = ([0, 1] if skip_ck else [0, 1, 2]) if kind == 'own' else [2]
                    for j in jl:
                        T('transpose', [b_lat, b_idb], [b_TR], out=TR[:, j, :], in_=lat_t[:, j * 128:(j + 1) * 128],
                          identity=idb[:])
                    V('tensor_copy', [b_TR], [b_latT], out=latT_t[:, jl[0]:jl[-1] + 1, :], in_=TR[:, jl[0]:jl[-1] + 1, :])
                    if kind == 'own':
                        for hf in range(2):
                            for c in range(2):
                                T('matmul', [b_latT, b_wuq], [S[hf][1]], out=S[hf][0][:, 0:288], lhsT=latT_t[:, c, :],
                                  rhs=wuq[:, c, hf * 288:(hf + 1) * 288], start=(c == 0), stop=(c == 1))
                            A('activation', [S[hf][1]], [b_qcs], out=qcs_t[:, hf * 288:(hf + 1) * 288],
                              in_=S[hf][0][:, 0:288], func=AF.Copy)
                    for hf in (range(2) if not skip_ck else []):
                        T('matmul', [b_latT, b_wukv], [S[hf][1]], out=S[hf][0][:, 0:384], lhsT=latT_t[:, 2, :],
                          rhs=wukv[:, hf * 384:(hf + 1) * 384], start=True, stop=True)
                        A('activation', [S[hf][1]], [b_kvcs], out=kvcs_t[:, hf * 384:(hf + 1) * 384],
                          in_=S[hf][0][:, 0:384], func=AF.Copy)
                    yield
                    kv3 = kvcs_t[:].rearrange("p (h d) -> p h d", d=128)
                    if kind == 'own':
                        V('tensor_tensor', [b_qcs], [b_sq], out=sq[:, 0:576], in0=qcs_t[:], in1=qcs_t[:], op=ALU.mult)
                        V('tensor_reduce', [b_sq], [b_st2], out=st2_t[:, 0:6],
                          in_=sq[:, 0:576].rearrange("p (h d) -> p h d", d=96), axis=AX.X, op=ALU.add)
                    if not skip_ck:
                        V('tensor_tensor', [b_kvcs], [b_sq], out=sq[:, 1024:1408].rearrange("p (h d) -> p h d", d=64),
                          in0=kv3[:, :, 0:64], in1=kv3[:, :, 0:64], op=ALU.mult)
                        V('tensor_reduce', [b_sq], [b_st2], out=st2_t[:, 6:12],
                          in_=sq[:, 1024:1408].rearrange("p (h d) -> p h d", d=64), axis=AX.X, op=ALU.add)
                        V('tensor_scalar', [b_st2, b_st], [b_st2], out=st2_t[:, 6:12], in0=st2_t[:, 6:12],
                          scalar1=st_t[:, 20:21], scalar2=None, op0=ALU.add)
                    lo = 0 if kind == 'own' else 6
                    hi_ = 6 if skip_ck else 12
                    G('tensor_scalar', [b_st2], [b_rstd2], out=rstd2_t[:, lo:hi_], in0=st2_t[:, lo:hi_],
                      scalar1=1.0 / 96, scalar2=EPS, op0=ALU.mult, op1=ALU.add)
                    G('tensor_tensor', [b_rstd2, b_nh24], [b_rstd2], out=rstd2_t[:, lo:hi_], in0=rstd2_t[:, lo:hi_],
                      in1=nh24[:, lo:hi_], op=ALU.pow)
                    kc_t, b_kc = KCo[s2]
                    vc_t, b_vc = VCo[s2]
                    if kind == 'own':
                        qc_t, b_qc = QCo[s2]
                        V('tensor_tensor', [b_qcs, b_rstd2], [b_tmp1], out=tmp1[:],
                          in0=qcs_t[:].rearrange("p (h d) -> p h d", d=96),
                          in1=bc(rstd2_t[:, 0:6].unsqueeze(2), [128, 6, 96]), op=ALU.mult)
                        V('tensor_tensor', [b_tmp1, b_GCq], [b_qc], out=qc_t[:, :, 0:64], in0=tmp1[:, :, 0:64],
                          in1=bc(GCq[:, 0:64].unsqueeze(1), [128, 6, 64]), op=ALU.mult)
                        V('tensor_tensor', [b_tmp1, b_GCq], [b_trq], out=trq[:], in0=tmp1[:, :, 64:96],
                          in1=bc(GCq[:, 64:96].unsqueeze(1), [128, 6, 32]), op=ALU.mult)
                        rope(trq, b_trq, qc_t[:, :, 64:96], b_qc, 6, ti)
                    if kind == 'own':
                        qT_t, b_qT = QTs[s2]
                        for h in range(6):
                            T('transpose', [b_qc, b_idb], [b_TR], out=TR[0:96, h, :], in_=qc_t[:, h, :],
                              identity=idb[:])
                        V('tensor_copy', [b_TR], [b_qT], out=qT_t[:], in_=TR[0:96, 0:6, :])
                        self.dma('sync', SC['QCT'].ap()[:, :, t * 128:(t + 1) * 128].rearrange("h d n -> d h n"),
                                 qT_t[:], [b_qT], [SC['b']], 'qto%d' % s2)
                    if skip_ck:
                        return
                    V('tensor_tensor', [b_kvcs, b_rstd2], [b_tmpk], out=tmpk[:], in0=kv3[:, :, 0:64],
                      in1=bc(rstd2_t[:, 6:12].unsqueeze(2), [128, 6, 64]), op=ALU.mult)
                    V('tensor_tensor', [b_tmpk, b_GCk], [b_kc], out=kc_t[:, :, 0:64], in0=tmpk[:],
                      in1=bc(GCk[:, 0:64].unsqueeze(1), [128, 6, 64]), op=ALU.mult)
                    V('tensor_tensor', [b_pj, b_GCk], [b_krg], out=krg[:, 0, :], in0=pj_t[:, 2048:2080],
                      in1=GCk[:, 64:96], op=ALU.mult)
                    rope(krg, b_krg, krr[:], b_krr, 1, ti)
                    V('tensor_tensor', [b_krr, b_rstd2], [b_kc], out=kc_t[:, :, 64:96],
                      in0=bc(krr[:], [128, 6, 32]), in1=bc(rstd2_t[:, 6:12].unsqueeze(2), [128, 6, 32]), op=ALU.mult)
                    V('tensor_copy', [b_kvcs], [b_vc], out=vc_t[:, :, 0:64], in_=kv3[:, :, 64:128])
                    kT_t, b_kT = KTs[s2]
                    for h in range(6):
                        T('transpose', [b_kc, b_idb], [b_TR], out=TR[0:96, h, :], in_=kc_t[:, h, :], identity=idb[:])
                    V('tensor_copy', [b_TR], [b_kT], out=kT_t[:], in_=TR[0:96, 0:6, :])
                    self.dma('sync', SC['KCT'].ap()[:, :, ti * 128:(ti + 1) * 128].rearrange("h d n -> d h n"),
                             kT_t[:], [b_kT], [SC['b']], 'kto%d' % s2)
                    self.dma('sync', SC['VC'].ap()[ti * 128:(ti + 1) * 128, :].rearrange("p (h d) -> p h d", d=65),
                             vc_t[:], [b_vc], [SC['b']], 'vco%d' % s2)

            def issue_load(j):
                kind, t = jobs[j]
                x_t, b_x = xt[j % 3]
                if kind == 'halo' and halo_rows is not None:
                    self.dma('sync', x_t[:], halo_rows(t), [], [b_x], 'xt%d' % (j % 3))
                else:
                    src = {'own': x_own, 'oth': x_oth, 'halo': x_halo}[kind]
                    self.dma('sync', x_t[:], src[t * 128:(t + 1) * 128, :], [], [b_x], 'xt%d' % (j % 3))

            gens = []
            for j in range(min(2, len(jobs))):
                issue_load(j)
            for idx, (kind, t) in enumerate(jobs):
                if idx + 2 < len(jobs):
                    issue_load(idx + 2)
                gens.append(tile_gen(idx, kind, t))
                for g in list(gens):
                    try:
                        next(g)
                    except StopIteration:
                        gens.remove(g)
            while gens:
                for g in list(gens):
                    try:
                        next(g)
                    except StopIteration:
                        gens.remove(g)
        self.P.phase_barrier()

    def phaseC(self, heads=range(6), nqt=8):
        nc, P, D, SC = self.nc, self.P, self.D, self.SC
        V, G, A, T, E = self.V, self.G, self.A, self.T, self.E
        with ExitStack() as ph:
            def sb(nm, shape, dt, n=1):
                r = []
                for i in range(n):
                    t = ph.enter_context(nc.sbuf_tensor(self.name(nm), shape, dt))
                    r.append((t, Buf(nm + str(i))))
                return r if n > 1 else r[0]

            def ps(nm, shape, dt, n=1):
                r = []
                for i in range(n):
                    t = psum_view(ph, nc, self.name(nm), shape, dt)
                    r.append((t, Buf(nm + str(i))))
                return r if n > 1 else r[0]

            P.barrier('sync', reads=[SC['b']])
            idf, b_idf = sb("idf", [128, 128], F32)
            self.dma('sync', idf[:], D['idf'].ap(), [], [b_idf], 'idf')
            KT = sb("cKT", [96, 8192], BF16, 2)
            VV = sb("cV", [128, 64, 65], BF16, 2)
            QT = sb("cQT", [96, 4096], BF16, 2)
            PT = sb("cPT", [128, 512], BF16, 4)
            OTs = sb("cOTs", [65, 512], F32, 2)
            rc = sb("crc", [128, 4], F32, 2)
            oc = sb("coc", [128, 4, 64], BF16, 2)
            ST = ps("cST", [128, 512], F32, 3)
            OT = ps("cOT", [65, 512], F32, 2)
            TO, b_TO = ps("cTO", [128, 4, 65], F32)

            heads = list(heads)

            def load_head(hi):
                h = heads[hi]
                s = hi % 2
                self.dma('sync', KT[s][0][:], SC['KCT'].ap()[h], [SC['b']], [KT[s][1]], 'cKT%d' % s)
                self.dma('sync', QT[s][0][:], SC['QCT'].ap()[h], [SC['b']], [QT[s][1]], 'cQT%d' % s)
                self.dma('sync', VV[s][0][:],
                         SC['VC'].ap().rearrange("(t p) c -> p t c", p=128)[:, :, h * 65:(h + 1) * 65],
                         [SC['b']], [VV[s][1]], 'cV%d' % s)

            steps = [(hi, qt, kc) for hi in range(len(heads)) for qt in range(nqt) for kc in range(64)]
            n = len(steps)
            LA = 2
            deferred = []
            load_head(0)
            nq = 0
            for i in range(n + LA):
                if i < n:
                    hi, qt, kc = steps[i]
                    s = hi % 2
                    st_t, st_b = ST[i % 3]
                    pt_t, pt_b = PT[i % 4]
                    T('matmul', [KT[s][1], QT[s][1]], [st_b], out=st_t[:], lhsT=KT[s][0][:, kc * 128:(kc + 1) * 128],
                      rhs=QT[s][0][:, qt * 512:(qt + 1) * 512], start=True, stop=True)
                    A('activation', [st_b], [pt_b], out=pt_t[:], in_=st_t[:], func=AF.Exp)
                j = i - LA
                if j >= 0:
                    hi, qt, kc = steps[j]
                    if qt == 0 and kc == 0 and hi + 1 < len(heads):
                        load_head(hi + 1)
                    s = hi % 2
                    qi = (hi * nqt + qt)
                    ot_t, ot_b = OT[qi % 2]
                    pt_t, pt_b = PT[j % 4]
                    T('matmul', [VV[s][1], pt_b], [ot_b], out=ot_t[:], lhsT=VV[s][0][:, kc, :], rhs=pt_t[:],
                      start=(kc == 0), stop=(kc == 63))
                    if kc == 63:
                        h = heads[hi]
                        os_t, os_b = OTs[qi % 2]
                        V('tensor_copy', [ot_b], [os_b], out=os_t[:], in_=ot_t[:])

                        def fin(os_t=os_t, os_b=os_b, qi=qi, qt=qt, h=h):
                            for jj in range(4):
                                T('transpose', [os_b, b_idf], [b_TO], out=TO[:, jj, :],
                                  in_=os_t[:, jj * 128:(jj + 1) * 128], identity=idf[0:65, 0:65])
                            rc_t, rc_b = rc[qi % 2]
                            oc_t, oc_b = oc[qi % 2]
                            V('reciprocal', [b_TO], [rc_b], out=rc_t[:].unsqueeze(2), in_=TO[:, :, 64:65])
                            V('tensor_tensor', [b_TO, rc_b], [oc_b], out=oc_t[:], in0=TO[:, :, 0:64],
                              in1=bc(rc_t[:].unsqueeze(2), [128, 4, 64]), op=ALU.mult)
                            self.dma('sync',
                                     SC['MIX'].ap()[qt * 512:(qt + 1) * 512, 640 + h * 64:640 + (h + 1) * 64]
                                     .rearrange("(t p) d -> p t d", p=128),
                                     oc_t[:], [oc_b], [SC['bmix']], 'coc%d' % (qi % 2))
                        deferred.append((i + 6, fin))
                while deferred and deferred[0][0] <= i:
                    deferred.pop(0)[1]()
            for (_, fn) in deferred:
                fn()
        self.P.phase_barrier()

    def bias_setup(self):
        nc, P, D, SC = self.nc, self.P, self.D, self.SC
        V, G, A, T, E = self.V, self.G, self.A, self.T, self.E
        with ExitStack() as ph:
            tabN = ph.enter_context(nc.sbuf_tensor(self.name("tabN"), [33, 10], F32)); b_tab = Buf("tabN")
            oh = ph.enter_context(nc.sbuf_tensor(self.name("oh"), [33, 1660], F32)); b_oh = Buf("oh")
            fv = ph.enter_context(nc.sbuf_tensor(self.name("fv"), [10, 1660], F32)); b_fv = Buf("fv")
            pf = [(psum_view(ph, nc, self.name("pf"), [10, 512], F32), Buf("pf%d" % i)) for i in range(4)]
            P.op('vector', lambda e: e.memset(tabN[:], NEGB), [], [b_tab])
            self.dma('sync', tabN[0:32, :], D['rel_bias_table'].ap(), [], [b_tab], 'tabN')
            self.dma('sync', oh[:], D['oh'].ap(), [], [b_oh], 'oh')
            segs = [(0, 511), (511, 894), (894, 1277), (1277, 1660)]
            for i, (a, b) in enumerate(segs):
                T('matmul', [b_tab, b_oh], [pf[i][1]], out=pf[i][0][:, 0:b - a], lhsT=tabN[:], rhs=oh[:, a:b],
                  start=True, stop=True)
                V('tensor_copy', [pf[i][1]], [b_fv], out=fv[:, a:b], in_=pf[i][0][:, 0:b - a])
            self.dma('sync', SC['FV'].ap(), fv[:], [b_fv], [SC['bfv']], 'fvo')
        self.P.phase_barrier()

    def phaseB(self, L):
        nc, P, D, SC = self.nc, self.P, self.D, self.SC
        V, G, A, T, E = self.V, self.G, self.A, self.T, self.E
        with ExitStack() as ph:
            def sb(nm, shape, dt, n=1):
                r = []
                for i in range(n):
                    t = ph.enter_context(nc.sbuf_tensor(self.name(nm), shape, dt))
                    r.append((t, Buf(nm + str(i))))
                return r if n > 1 else r[0]

            def ps(nm, shape, dt, n=1):
                r = []
                for i in range(n):
                    t = psum_view(ph, nc, self.name(nm), shape, dt)
                    r.append((t, Buf(nm + str(i))))
                return r if n > 1 else r[0]

            P.barrier('sync', reads=[SC['b'], SC['bfv']])
            idf, b_idf = sb("idf", [128, 128], F32)
            idb, b_idb = sb("idb", [128, 128], BF16)
            self.dma('sync', idf[:], D['idf'].ap(), [], [b_idf], 'idf')
            self.dma('sync', idb[:], D['idb'].ap(), [], [b_idb], 'idb')
            biasB = sb("biasB", [128, 4, 3, 128], F32)
            hk = sb("hkB", [128, 128], F32, 4)
            for h in range(4):
                for o in range(3):
                    g_t, g_b = hk[(h * 3 + o) % 4]
                    self.dma('sync', g_t[:], bass.AP(SC['FV'], (6 + h) * 1660 + 128 * o, [[1, 128], [1, 128]]),
                             [SC['bfv']], [g_b], 'hkB%d' % ((h * 3 + o) % 4))
                    V('tensor_copy', [g_b], [biasB[1]], out=biasB[0][:, h, o, :],
                      in_=bass.AP(g_t, 127, [[128, 128], [-1, 128]]))
            sk, b_sk = sb("sink", [128, 4], F32)
            esk, b_esk = sb("esink", [128, 4], F32)
            self.dma('sync', sk[:], D['sink_b'].ap()[L].partition_broadcast(128), [], [b_sk], 'sink')
            A('activation', [b_sk], [b_esk], out=esk[:], in_=sk[:], func=AF.Exp)
            Qc = sb("bQc", [128, 4, 256], BF16, 2)
            Kc = sb("bKc", [128, 6, 128], BF16, 2)
            Vc = sb("bVc", [128, 6, 130], BF16, 2)
            QTb = sb("bQT", [128, 2, 4, 128], BF16, 2)
            KTb = sb("bKT", [128, 6, 128], BF16, 2)
            Sb = sb("bS", [128, 3, 128], F32, 2)
            PTb = sb("bPT", [128, 3, 128], BF16, 2)
            OTs = sb("bOTs", [65, 4, 128], F32, 2)
            den = sb("bden", [128, 4], F32, 2)
            ob = sb("bo", [128, 4, 64], BF16, 2)
            TRq = ps("bTRq", [128, 2, 4, 128], BF16)
            TRk = ps("bTRk", [128, 6, 128], BF16)
            SP = ps("bSP", [128, 3, 128], F32, 2)
            OTp = ps("bOT", [65, 4, 128], F32, 2)
            TOp = ps("bTO", [128, 4, 65], F32)

            def stage1(J):
                s = J % 2
                self.dma('sync', Qc[s][0][:], SC['QB'].ap()[J * 512:(J + 1) * 512, :].rearrange("(t p) c -> p t c", p=128),
                         [SC['b']], [Qc[s][1]], 'bQc%d' % s)
                r0 = (7 + 4 * J) * 128
                self.dma('sync', Kc[s][0][:], SC['KBx'].ap()[r0:r0 + 768, :].rearrange("(t p) c -> p t c", p=128),
                         [SC['b']], [Kc[s][1]], 'bKc%d' % s)
                self.dma('sync', Vc[s][0][:], SC['VBx'].ap()[r0:r0 + 768, :].rearrange("(t p) c -> p t c", p=128),
                         [SC['b']], [Vc[s][1]], 'bVc%d' % s)
                for t in range(4):
                    for pi in range(2):
                        T('transpose', [Qc[s][1], b_idb], [TRq[1]], out=TRq[0][:, pi, t, :],
                          in_=Qc[s][0][:, t, pi * 128:(pi + 1) * 128], identity=idb[:])
                V('tensor_copy', [TRq[1]], [QTb[s][1]], out=QTb[s][0][:], in_=TRq[0][:])
                for kt in range(6):
                    T('transpose', [Kc[s][1], b_idb], [TRk[1]], out=TRk[0][:, kt, :], in_=Kc[s][0][:, kt, :],
                      identity=idb[:])
                V('tensor_copy', [TRk[1]], [KTb[s][1]], out=KTb[s][0][:], in_=TRk[0][:])

            cnt = [0]

            def stage2(J):
                s = J % 2
                for t in range(4):
                    qi = J * 4 + t
                    ot_t, ot_b = OTp[qi % 2]
                    for h in range(4):
                        base = 64 * (h // 2)
                        pi = h % 2
                        kvh = h // 2
                        c = cnt[0]
                        cnt[0] += 1
                        sp_t, sp_b = SP[c % 2]
                        for o in range(3):
                            T('matmul', [KTb[s][1], QTb[s][1]], [sp_b], out=sp_t[:, o, :],
                              lhsT=KTb[s][0][base:base + 64, t + o, :], rhs=QTb[s][0][base:base + 64, pi, t, :],
                              start=True, stop=True)
                        s_t, s_b = Sb[c % 2]
                        p_t, p_b = PTb[c % 2]
                        V('tensor_tensor', [sp_b, biasB[1]], [s_b], out=s_t[:], in0=sp_t[:], in1=biasB[0][:, h, :, :],
                          op=ALU.add)
                        A('activation', [s_b], [p_b], out=p_t[:], in_=s_t[:], func=AF.Exp)
                        for o in range(3):
                            T('matmul', [Vc[s][1], p_b], [ot_b], out=ot_t[:, h, :],
                              lhsT=Vc[s][0][:, t + o, kvh * 65:(kvh + 1) * 65], rhs=p_t[:, o, :],
                              start=(o == 0), stop=(o == 2))
                    os_t, os_b = OTs[qi % 2]
                    V('tensor_copy', [ot_b], [os_b], out=os_t[:], in_=ot_t[:])
                    for h in range(4):
                        T('transpose', [os_b, b_idf], [TOp[1]], out=TOp[0][:, h, :], in_=os_t[:, h, :],
                          identity=idf[0:65, 0:65])
                    d_t, d_b = den[qi % 2]
                    o_t, o_b = ob[qi % 2]
                    V('tensor_tensor', [TOp[1], b_esk], [d_b], out=d_t[:].unsqueeze(2), in0=TOp[0][:, :, 64:65],
                      in1=esk[:].unsqueeze(2), op=ALU.add)
                    V('reciprocal', [d_b], [d_b], out=d_t[:], in_=d_t[:])
                    V('tensor_tensor', [TOp[1], d_b], [o_b], out=o_t[:], in0=TOp[0][:, :, 0:64],
                      in1=bc(d_t[:].unsqueeze(2), [128, 4, 64]), op=ALU.mult)
                    self.dma('sync', SC['MIX'].ap()[qi * 128:(qi + 1) * 128, 384:640], o_t[:], [o_b], [SC['bmix']],
                             'bo%d' % (qi % 2))

            stage1(0)
            for J in range(8):
                if J + 1 < 8:
                    stage1(J + 1)
                stage2(J)
        self.P.phase_barrier()

    def phaseA(self, cfgs=(0, 1, 2)):
        nc, P, D, SC = self.nc, self.P, self.D, self.SC
        V, G, A, T, E = self.V, self.G, self.A, self.T, self.E
        RS = (1, 4, 16)
        with ExitStack() as ph:
            def sb(nm, shape, dt, n=1):
                r = []
                for i in range(n):
                    t = ph.enter_context(nc.sbuf_tensor(self.name(nm), shape, dt))
                    r.append((t, Buf(nm + str(i))))
                return r if n > 1 else r[0]

            def ps(nm, shape, dt, n=1):
                r = []
                for i in range(n):
                    t = psum_view(ph, nc, self.name(nm), shape, dt)
                    r.append((t, Buf(nm + str(i))))
                return r if n > 1 else r[0]

            P.barrier('sync', reads=[SC['b'], SC['bfv']])
            idf, b_idf = sb("idf", [128, 128], F32)
            idb, b_idb = sb("idb", [128, 128], BF16)
            self.dma('sync', idf[:], D['idf'].ap(), [], [b_idf], 'idf')
            self.dma('sync', idb[:], D['idb'].ap(), [], [b_idb], 'idb')
            biasA = sb("biasA", [128, 3, 6, 2, 128], F32)
            hk = sb("hkA", [128, 128], F32, 4)
            for ci in range(3):
                for h in range(6):
                    for c in range(2):
                        kk = (ci * 6 + h) * 2 + c
                        g_t, g_b = hk[kk % 4]
                        self.dma('sync', g_t[:],
                                 bass.AP(SC['FV'], h * 1660 + 511 + 383 * ci + 128 * c, [[1, 128], [1, 128]]),
                                 [SC['bfv']], [g_b], 'hkA%d' % (kk % 4))
                        V('tensor_copy', [g_b], [biasA[1]], out=biasA[0][:, ci, h, c, :],
                          in_=bass.AP(g_t, 127, [[128, 128], [-1, 128]]))
            Qa = sb("aQ", [128, 384], BF16, 3)
            Ka = sb("aK", [128, 2, 384], BF16, 3)
            Va = sb("aV", [128, 2, 390], BF16, 3)
            QTa = sb("aQT", [128, 2, 3, 128], BF16, 2)
            for (t_, b_) in QTa:
                P.op('vector', lambda e, t_=t_: e.memset(t_[:], 0.0), [], [b_])
            KTa = sb("aKT", [128, 3, 2, 128], BF16, 2)
            Sa = sb("aS", [128, 6, 2, 128], F32, 2)
            PTa = sb("aPT", [128, 6, 2, 128], BF16, 2)
            OTs = sb("aOTs", [65, 6, 128], F32, 2)
            Oo = sb("aOo", [128, 6, 65], F32, 2)
            TRq = ps("aTRq", [128, 3, 128], BF16)
            TRk = ps("aTRk", [128, 3, 2, 128], BF16)
            SP = ps("aSP", [128, 2, 2, 128], F32, 3)
            OTp = ps("aOT", [65, 3, 128], F32, 2)
            TOp = ps("aTO", [128, 6, 65], F32)

            jobs = []
            for ci in cfgs:
                r = RS[ci]
                for rho in range(r):
                    for j in range(32 // r):
                        jobs.append((ci, r, rho, j))

            if self.debug and self.debug.get('a_jobs'):
                jobs = self.debug['a_jobs']

            def loadA(i):
                ci, r, rho, j = jobs[i]
                s3 = i % 3
                self.dma('sync', Qa[s3][0][:], bass.AP(SC['QA'], (r * 128 * j + rho) * 384, [[r * 384, 128], [1, 384]]),
                         [SC['b']], [Qa[s3][1]], 'aQ%d' % s3)
                e0 = r * (128 * j - 64) + rho + 1024
                self.dma('sync', Ka[s3][0][:],
                         bass.AP(SC['KAx'], e0 * 384, [[r * 384, 128], [r * 384 * 128, 2], [1, 384]]),
                         [SC['b']], [Ka[s3][1]], 'aK%d' % s3)
                self.dma('sync', Va[s3][0][:],
                         bass.AP(SC['VAx'], e0 * 390, [[r * 390, 128], [r * 390 * 128, 2], [1, 390]]),
                         [SC['b']], [Va[s3][1]], 'aV%d' % s3)

            def stage1(i):
                ci, r, rho, j = jobs[i]
                s3 = i % 3
                s2 = i % 2
                for pi in range(3):
                    T('transpose', [Qa[s3][1], b_idb], [TRq[1]], out=TRq[0][:, pi, :],
                      in_=Qa[s3][0][:, pi * 128:(pi + 1) * 128], identity=idb[:])
                V('tensor_copy', [TRq[1]], [QTa[s2][1]], out=QTa[s2][0][0:64, 0, :, :], in_=TRq[0][0:64, :, :])
                V('tensor_copy', [TRq[1]], [QTa[s2][1]], out=QTa[s2][0][64:128, 1, :, :], in_=TRq[0][64:128, :, :])
                for pi in range(3):
                    for c in range(2):
                        T('transpose', [Ka[s3][1], b_idb], [TRk[1]], out=TRk[0][:, pi, c, :],
                          in_=Ka[s3][0][:, c, pi * 128:(pi + 1) * 128], identity=idb[:])
                V('tensor_copy', [TRk[1]], [KTa[s2][1]], out=KTa[s2][0][:], in_=TRk[0][:])

            astop = self.debug.get('a_stop', 99) if self.debug else 99

            def stage2(i):
                ci, r, rho, j = jobs[i]
                s3 = i % 3
                s2 = i % 2
                s_t, s_b = Sa[s2]
                p_t, p_b = PTa[s2]
                if astop < 2:
                    return
                for pi in range(3):
                    sp_t, sp_b = SP[pi]
                    for hh in range(2):
                        for c in range(2):
                            T('matmul', [KTa[s2][1], QTa[s2][1]], [sp_b], out=sp_t[:, hh, c, :],
                              lhsT=KTa[s2][0][:, pi, c, :],
                              rhs=QTa[s2][0][:, hh, pi, :], start=True, stop=True)
                    if self.debug and self.debug.get('a_nobias'):
                        continue
                    V('tensor_tensor', [sp_b, biasA[1]], [s_b], out=s_t[:, 2 * pi:2 * pi + 2, :, :], in0=sp_t[:],
                      in1=biasA[0][:, ci, 2 * pi:2 * pi + 2, :, :], op=ALU.add)
                if astop < 3:
                    return
                A('activation', [s_b], [p_b], out=p_t[:], in_=s_t[:], func=AF.Exp)
                os_t, os_b = OTs[s2]
                if astop < 4:
                    return
                for g3 in range(2):
                    ot_t, ot_b = OTp[g3]
                    for hh in range(3):
                        h = g3 * 3 + hh
                        for c in range(2):
                            T('matmul', [Va[s3][1], p_b], [ot_b], out=ot_t[:, hh, :],
                              lhsT=Va[s3][0][:, c, h * 65:(h + 1) * 65], rhs=p_t[:, h, c, :],
                              start=(c == 0), stop=(c == 1))
                    V('tensor_copy', [ot_b], [os_b], out=os_t[:, g3 * 3:g3 * 3 + 3, :], in_=ot_t[:])
                if astop < 5:
                    return
                for h in range(6):
                    T('transpose', [os_b, b_idf], [TOp[1]], out=TOp[0][:, h, :], in_=os_t[:, h, :],
                      identity=idf[0:65, 0:65])
                o_t, o_b = Oo[s2]
                V('tensor_copy', [TOp[1]], [o_b], out=o_t[:], in_=TOp[0][:])
                self.dma('sync', bass.AP(SC['OA'], ci * 4096 * 390 + (r * 128 * j + rho) * 390, [[r * 390, 128], [1, 390]]),
                         o_t[:].rearrange("p h d -> p (h d)"), [o_b], [SC['boa']], 'aOo%d' % s2)

            n = len(jobs)
            for i in range(min(2, n)):
                loadA(i)
            stage1(0)
            for i in range(n):
                if i + 2 < n:
                    loadA(i + 2)
                if i + 1 < n:
                    stage1(i + 1)
                stage2(i)
        self.P.phase_barrier()

    def phaseA2(self):
        nc, P, D, SC = self.nc, self.P, self.D, self.SC
        V, G, A, T, E = self.V, self.G, self.A, self.T, self.E
        with ExitStack() as ph:
            def sb(nm, shape, dt, n=1):
                r = []
                for i in range(n):
                    t = ph.enter_context(nc.sbuf_tensor(self.name(nm), shape, dt))
                    r.append((t, Buf(nm + str(i))))
                return r if n > 1 else r[0]
            P.barrier('sync', reads=[SC['boa']])
            O3 = sb("a2O", [128, 3, 6, 65], F32, 3)
            acc = sb("a2acc", [128, 6, 65], F32, 2)
            rc = sb("a2rc", [128, 6], F32, 2)
            oo = sb("a2o", [128, 6, 64], BF16, 2)
            for t in range(32):
                s3, s2 = t % 3, t % 2
                self.dma('sync', O3[s3][0][:].rearrange("p c h d -> p c (h d)"),
                         SC['OA'].ap()[:, t * 128:(t + 1) * 128, :].rearrange("c p d -> p c d"),
                         [SC['boa']], [O3[s3][1]], 'a2O%d' % s3)
                o3 = O3[s3][0]
                G('tensor_tensor', [O3[s3][1]], [acc[s2][1]], out=acc[s2][0][:], in0=o3[:, 0], in1=o3[:, 1], op=ALU.add)
                G('tensor_tensor', [O3[s3][1], acc[s2][1]], [acc[s2][1]], out=acc[s2][0][:], in0=acc[s2][0][:],
                  in1=o3[:, 2], op=ALU.add)
                V('reciprocal', [acc[s2][1]], [rc[s2][1]], out=rc[s2][0][:].unsqueeze(2), in_=acc[s2][0][:, :, 64:65])
                V('tensor_tensor', [acc[s2][1], rc[s2][1]], [oo[s2][1]], out=oo[s2][0][:], in0=acc[s2][0][:, :, 0:64],
                  in1=bc(rc[s2][0][:].unsqueeze(2), [128, 6, 64]), op=ALU.mult)
                self.dma('sync', SC['MIX'].ap()[t * 128:(t + 1) * 128, 0:384], oo[s2][0][:].rearrange("p h d -> p (h d)"),
                         [oo[s2][1]], [SC['bmix']], 'a2o%d' % s2)
        self.P.phase_barrier()

    def phase4a(self, L, x_own, x1_d):
        nc, P, D, SC = self.nc, self.P, self.D, self.SC
        V, G, A, T, E = self.V, self.G, self.A, self.T, self.E
        with ExitStack() as ph:
            def sb(nm, shape, dt, n=1):
                r = []
                for i in range(n):
                    t = ph.enter_context(nc.sbuf_tensor(self.name(nm), shape, dt))
                    r.append((t, Buf(nm + str(i))))
                return r if n > 1 else r[0]

            def ps(nm, shape, dt, n=1):
                r = []
                for i in range(n):
                    t = psum_view(ph, nc, self.name(nm), shape, dt)
                    r.append((t, Buf(nm + str(i))))
                return r if n > 1 else r[0]

            P.barrier('sync', reads=[SC['bmix']])
            idb, b_idb = sb("idb", [128, 128], BF16)
            self.dma('sync', idb[:], D['idb'].ap(), [], [b_idb], 'idb')
            wo, b_wo = sb("wo", [128, 8, 1024], BF16)
            go, b_go = sb("go", [128, 8], F32)
            self.dma('sync', go[:], D['out_norm'].ap()[L].rearrange("(c p) -> p c", p=128), [], [b_go], 'go',
                     allow_slow_non_contiguous=True)
            stg = sb("stg4a", [128, 1024], F32, 2)
            for c in range(8):
                self.dma('sync', stg[c % 2][0][:], D['w_out'].ap()[L, c * 128:(c + 1) * 128, :], [], [stg[c % 2][1]],
                         'stg4a%d' % (c % 2))
                E('vector' if c % 2 == 0 else 'gpsimd', 'tensor_scalar', [stg[c % 2][1], b_go], [b_wo], out=wo[:, c, :],
                  in0=stg[c % 2][0][:], scalar1=go[:, c:c + 1], scalar2=None, op0=ALU.mult)
            invd3, b_invd3 = sb("invd3", [128, 3], F32)
            nh3, b_nh3 = sb("nh3", [128, 3], F32)
            P.op('gpsimd', lambda e: e.memset(invd3[:], 1.0 / 384), [], [b_invd3])
            P.op('gpsimd', lambda e: e.memset(invd3[:, 1:2], 1.0 / 256), [], [b_invd3])
            P.op('gpsimd', lambda e: e.memset(nh3[:], -0.5), [], [b_nh3])
            mx = sb("mx", [128, 1024], BF16, 3)
            xt = sb("x4a", [128, 1024], F32, 3)
            sq, b_sq = sb("sq4a", [128, 1024], F32)
            ss = sb("ss4a", [128, 3], F32, 2)
            rs = sb("rs4a", [128, 3], F32, 2)
            mn = sb("mn", [128, 1024], BF16, 2)
            mT = sb("mT", [128, 8, 128], BF16, 2)
            TR = ps("TR4a", [128, 8, 128], BF16, 2)
            Y = ps("Y4a", [128, 1024], F32, 2)
            grp = [(0, 384), (384, 640), (640, 1024)]
            def load4a(t):
                s3 = t % 3
                rows = slice(t * 128, (t + 1) * 128)
                self.dma('sync', mx[s3][0][:], SC['MIX'].ap()[rows, :], [SC['bmix']], [mx[s3][1]], 'mx%d' % s3)
                self.dma('sync', xt[s3][0][:], x_own[rows, :], [], [xt[s3][1]], 'x4a%d' % s3)
            load4a(0)
            load4a(1)
            for t in range(32):
                s3, s2 = t % 3, t % 2
                rows = slice(t * 128, (t + 1) * 128)
                if t + 2 < 32:
                    load4a(t + 2)
                m_t, m_b = mx[s3]
                V('tensor_tensor', [m_b], [b_sq], out=sq[:], in0=m_t[:], in1=m_t[:], op=ALU.mult)
                for gi, (a, b) in enumerate(grp):
                    V('tensor_reduce', [b_sq], [ss[s2][1]], out=ss[s2][0][:, gi:gi + 1], in_=sq[:, a:b], axis=AX.X,
                      op=ALU.add)
                G('tensor_tensor', [ss[s2][1], b_invd3], [rs[s2][1]], out=rs[s2][0][:], in0=ss[s2][0][:], in1=invd3[:],
                  op=ALU.mult)
                G('tensor_scalar', [rs[s2][1]], [rs[s2][1]], out=rs[s2][0][:], in0=rs[s2][0][:], scalar1=EPS, scalar2=None,
                  op0=ALU.add)
                G('tensor_tensor', [rs[s2][1], b_nh3], [rs[s2][1]], out=rs[s2][0][:], in0=rs[s2][0][:], in1=nh3[:],
                  op=ALU.pow)
                for gi, (a, b) in enumerate(grp):
                    E('vector' if gi != 1 else 'gpsimd', 'tensor_scalar', [m_b, rs[s2][1]], [mn[s2][1]],
                      out=mn[s2][0][:, a:b], in0=m_t[:, a:b], scalar1=rs[s2][0][:, gi:gi + 1], scalar2=None, op0=ALU.mult)
                for c in range(8):
                    T('transpose', [mn[s2][1], b_idb], [TR[s2][1]], out=TR[s2][0][:, c, :],
                      in_=mn[s2][0][:, c * 128:(c + 1) * 128], identity=idb[:])
                A('activation', [TR[s2][1]], [mT[s2][1]], out=mT[s2][0][:], in_=TR[s2][0][:], func=AF.Copy)
                for hf in range(2):
                    for c in range(8):
                        T('matmul', [mT[s2][1], b_wo], [Y[s2][1]], out=Y[s2][0][:, hf * 512:(hf + 1) * 512],
                          lhsT=mT[s2][0][:, c, :], rhs=wo[:, c, hf * 512:(hf + 1) * 512], start=(c == 0), stop=(c == 7))
                V('tensor_tensor', [Y[s2][1], xt[s3][1]], [xt[s3][1]], out=xt[s3][0][:], in0=Y[s2][0][:],
                  in1=xt[s3][0][:], op=ALU.add)
                self.dma('sync', x1_d[rows, :], xt[s3][0][:], [xt[s3][1]], [SC['bx1']], 'x4ao%d' % s3)
        self.P.phase_barrier()

    def phase4b(self, L, x1_d, out_d, b_out):
        nc, P, D, SC = self.nc, self.P, self.D, self.SC
        V, G, A, T, E = self.V, self.G, self.A, self.T, self.E
        with ExitStack() as ph:
            def sb(nm, shape, dt, n=1):
                r = []
                for i in range(n):
                    t = ph.enter_context(nc.sbuf_tensor(self.name(nm), shape, dt))
                    r.append((t, Buf(nm + str(i))))
                return r if n > 1 else r[0]

            def ps(nm, shape, dt, n=1):
                r = []
                for i in range(n):
                    t = psum_view(ph, nc, self.name(nm), shape, dt)
                    r.append((t, Buf(nm + str(i))))
                return r if n > 1 else r[0]

            P.barrier('sync', reads=[SC['bx1']])
            idb, b_idb = sb("idb", [128, 128], BF16)
            self.dma('sync', idb[:], D['idb'].ap(), [], [b_idb], 'idb')
            wu, b_wu = sb("wu", [128, 8, 4096], BF16)
            wd, b_wd = sb("wd", [128, 32, 1024], BF16)
            gm, b_gm = sb("gm", [128, 8], F32)
            self.dma('sync', gm[:], D['norm_mlp'].ap()[L].rearrange("(c p) -> p c", p=128), [], [b_gm], 'gm',
                     allow_slow_non_contiguous=True)
            stg = sb("stg4b", [128, 1024], F32, 2)
            k = 0
            for c in range(8):
                for hf in range(4):
                    s = k % 2
                    self.dma('sync', stg[s][0][:], D['w_up'].ap()[L, c * 128:(c + 1) * 128, hf * 1024:(hf + 1) * 1024], [],
                             [stg[s][1]], 'stg4b%d' % s)
                    E('vector' if k % 2 == 0 else 'gpsimd', 'tensor_scalar', [stg[s][1], b_gm], [b_wu],
                      out=wu[:, c, hf * 1024:(hf + 1) * 1024], in0=stg[s][0][:], scalar1=gm[:, c:c + 1], scalar2=None,
                      op0=ALU.mult)
                    k += 1
            for c2 in range(32):
                s = k % 2
                self.dma('sync', stg[s][0][:], D['w_down'].ap()[L, c2 * 128:(c2 + 1) * 128, :], [],
                         [stg[s][1]], 'stg4b%d' % s)
                E('vector' if k % 2 == 0 else 'gpsimd', 'tensor_copy', [stg[s][1]], [b_wd],
                  out=wd[:, c2, :], in_=stg[s][0][:])
                k += 1
            nh1, b_nh1 = sb("nh1", [128, 1], F32)
            P.op('gpsimd', lambda e: e.memset(nh1[:], -0.5), [], [b_nh1])
            xc = sb("x4b", [128, 2, 1024], F32, 2)
            junk, b_junk = sb("junk4b", [128, 1024], BF16)
            ss = sb("ss4b", [128, 2], F32, 2)
            h2 = [sb("h2", [128, 2, 1024], BF16)] * 2
            h2T = [sb("h2T", [128, 8, 256], BF16)] * 2
            uT = [sb("uT", [128, 32, 256], BF16)] * 2
            rr = sb("rr", [128, 2, 256], F32, 3)
            TR = ps("TR4b", [128, 8, 128], BF16)
            U = ps("U4b", [128, 2, 256], F32, 2)
            Z = ps("Z4b", [128, 1024], F32, 2)
            def load4b(ch):
                s2 = ch % 2
                rows = slice(ch * 256, (ch + 1) * 256)
                self.dma('sync', xc[s2][0][:], x1_d[rows, :].rearrange("(t p) d -> p t d", p=128), [SC['bx1']],
                         [xc[s2][1]], 'x4b%d' % s2)
            load4b(0)
            for ch in range(16):
                s2 = ch % 2
                rows = slice(ch * 256, (ch + 1) * 256)
                x_t, x_b = xc[s2]
                if ch + 1 < 16:
                    load4b(ch + 1)
                for t in range(2):
                    A('activation', [x_b], [b_junk, ss[s2][1]], out=junk[:], in_=x_t[:, t, :], func=AF.Square,
                      accum_out=ss[s2][0][:, t:t + 1])
                G('tensor_scalar', [ss[s2][1]], [ss[s2][1]], out=ss[s2][0][:], in0=ss[s2][0][:], scalar1=1.0 / 1024,
                  scalar2=EPS, op0=ALU.mult, op1=ALU.add)
                G('tensor_tensor', [ss[s2][1], b_nh1], [ss[s2][1]], out=ss[s2][0][:], in0=ss[s2][0][:],
                  in1=bc(nh1[:], [128, 2]), op=ALU.pow)
                for t in range(2):
                    A('activation', [x_b, ss[s2][1]], [h2[s2][1]], out=h2[s2][0][:, t, :], in_=x_t[:, t, :], func=AF.Copy,
                      scale=ss[s2][0][:, t:t + 1])
                    for c in range(8):
                        T('transpose', [h2[s2][1], b_idb], [TR[1]], out=TR[0][:, c, :],
                          in_=h2[s2][0][:, t, c * 128:(c + 1) * 128], identity=idb[:])
                    V('tensor_copy', [TR[1]], [h2T[s2][1]], out=h2T[s2][0][:, :, t * 128:(t + 1) * 128], in_=TR[0][:])
                for f2 in range(16):
                    u_t, u_b = U[f2 % 2]
                    for ff in range(2):
                        fc = f2 * 2 + ff
                        for c in range(8):
                            T('matmul', [h2T[s2][1], b_wu], [u_b], out=u_t[:, ff, :], lhsT=wu[:, c, fc * 128:(fc + 1) * 128],
                              rhs=h2T[s2][0][:, c, :], start=(c == 0), stop=(c == 7))
                    r_t, r_b = rr[f2 % 3]
                    A('activation', [u_b], [r_b], out=r_t[:], in_=u_t[:], func=AF.Relu)
                    E('vector' if f2 % 2 == 0 else 'gpsimd', 'tensor_tensor', [r_b], [uT[s2][1]],
                      out=uT[s2][0][:, 2 * f2:2 * f2 + 2, :], in0=r_t[:], in1=r_t[:], op=ALU.mult)
                for t in range(2):
                    z_t, z_b = Z[t]
                    for hf in range(2):
                        for fc in range(32):
                            T('matmul', [uT[s2][1], b_wd], [z_b], out=z_t[:, hf * 512:(hf + 1) * 512],
                              lhsT=uT[s2][0][:, fc, t * 128:(t + 1) * 128], rhs=wd[:, fc, hf * 512:(hf + 1) * 512],
                              start=(fc == 0), stop=(fc == 31))
                    V('tensor_tensor', [z_b, x_b], [x_b], out=x_t[:, t, :], in0=z_t[:], in1=x_t[:, t, :], op=ALU.add)
                self.dma('sync', out_d[rows, :].rearrange("(t p) d -> p t d", p=128), x_t[:], [x_b], [b_out],
                         'x4bo%d' % s2)
        self.P.phase_barrier()

    def layer(self, L, x_own, x_oth, x_halo, x1_d, out_d, b_out, with_bias_setup=True, phases=None, pos_key='pos',
              valid_key='valid', halo_rows=None, reuse_ckv=False):
        ph = phases or ['bias', 'p1', 'C', 'B', 'A', 'A2', '4a', '4b']
        if with_bias_setup and 'bias' in ph:
            self.bias_setup()
        if 'p1' in ph:
            self.phase1(L, x_own, x_oth, x_halo, pos_key=pos_key, valid_key=valid_key, halo_rows=halo_rows,
                        skip_oth=reuse_ckv, skip_ck=reuse_ckv)
        if 'C' in ph:
            self.phaseC()
        if 'B' in ph:
            self.phaseB(L)
        if 'A' in ph:
            self.phaseA(cfgs=self.debug.get('cfgs', (0, 1, 2)) if self.debug else (0, 1, 2))
        if 'A2' in ph:
            self.phaseA2()
        if '4a' in ph:
            self.phase4a(L, x_own, x1_d)
        if '4b' in ph:
            self.phase4b(L, x1_d, out_d, b_out)


NLAYER = 2


def t5_bucket_np(rel):
    half, exact = 16, 8
    n = np.abs(rel)
    far = exact + (np.log(np.maximum(n, 1).astype(np.float32) / np.float32(exact))
                   / np.float32(math.log(1024 / exact)) * np.float32(half - exact)).astype(np.int32)
    far = np.minimum(far, half - 1)
    return np.where(rel > 0, half, 0) + np.where(n < exact, n, far)


def make_oh():
    oh = np.zeros((33, 1660), np.float32)
    d = np.arange(-255, 256)
    bk = t5_bucket_np(d)
    for i, dd in enumerate(d):
        if abs(dd) <= 128:
            oh[bk[i], i] = 1
        else:
            oh[32, i] = 1
    for ci, r in enumerate((1, 4, 16)):
        d = np.arange(-191, 192)
        bk = t5_bucket_np(d * r)
        for i, dd in enumerate(d):
            if abs(dd) <= 64:
                oh[bk[i], 511 + 383 * ci + i] = 1
            else:
                oh[32, 511 + 383 * ci + i] = 1
    return oh


WNAMES = ['norm_mix', 'w_in', 'qk_gain_a', 'qk_gain_b', 'sink_b', 'q_lat_gain', 'kv_lat_gain', 'w_uq', 'w_ukv',
          'qk_gain_c', 'out_norm', 'w_out', 'norm_mlp', 'w_up', 'w_down']


def declare(nc, NL):
    D = {}

    def inp(n, shape, dt=F32):
        D[n] = nc.dram_tensor(n, shape, dt, kind="ExternalInput")
    inp('x_own', [4096, 1024]); inp('x_oth', [4096, 1024]); inp('x_halo', [2048, 1024]); inp('x_halo2', [2048, 1024])
    inp('valid', [128, 16]); inp('valid2', [128, 16]); inp('pos', [128, 64], I32); inp('pos2', [128, 64], I32)
    inp('invf', [1, 16]); inp('idb', [128, 128], BF16); inp('idf', [128, 128])
    inp('norm_mix', [NL, 1024]); inp('w_in', [NL, 1024, 2080]); inp('qk_gain_a', [NL, 2, 64])
    inp('qk_gain_b', [NL, 2, 64]); inp('sink_b', [NL, 4]); inp('q_lat_gain', [NL, 256]); inp('kv_lat_gain', [NL, 128])
    inp('w_uq', [NL, 256, 576]); inp('w_ukv', [NL, 128, 768]); inp('qk_gain_c', [NL, 2, 96]); inp('out_norm', [NL, 1024])
    inp('w_out', [NL, 1024, 1024]); inp('norm_mlp', [NL, 1024]); inp('w_up', [NL, 1024, 4096]); inp('w_down', [NL, 4096, 1024])
    inp('rel_bias_table', [32, 10]); inp('oh', [33, 1660])
    SC = {}

    def scr(n, shape, dt=BF16):
        SC[n] = nc.dram_tensor(n, shape, dt, kind="Internal")
    scr('QA', [4096, 384]); scr('KAx', [6144, 384]); scr('VAx', [6144, 390]); scr('QB', [4096, 256])
    scr('KBx', [6144, 128]); scr('VBx', [6144, 130]); scr('QCT', [6, 96, 4096]); scr('KCT', [6, 96, 8192])
    scr('VC', [8192, 390]); scr('MIX', [4096, 1024]); scr('OA', [3, 4096, 390], F32); scr('FV', [10, 1660], F32)
    scr('X1', [4096, 1024], F32); scr('XA', [4096, 1024], F32); scr('XB', [4096, 1024], F32)
    SC['b'] = Buf('scratch', multi=True)
    SC['bmix'] = Buf('mix', multi=True)
    SC['boa'] = Buf('oa', multi=True)
    SC['bfv'] = Buf('fv', multi=True)
    SC['bx1'] = Buf('x1', multi=True)
    return D, SC


def make_inputs(inputs, core):
    b, half = core // 2, core % 2
    xs = np.asarray(inputs['x'], dtype=np.float32)[b]
    own = xs[half * 4096:(half + 1) * 4096]
    oth = xs[(1 - half) * 4096:(2 - half) * 4096]
    halo = np.zeros((2048, 1024), np.float32)
    valid = np.zeros((2048,), np.float32)
    halo2 = np.zeros((2048, 1024), np.float32)
    valid2 = np.zeros((2048,), np.float32)
    if half == 1:
        halo[0:1024] = oth[3072:4096]
        valid[0:1024] = 1
        halo2[1024:2048] = own[0:1024]
        valid2[1024:2048] = 1
    else:
        halo[1024:2048] = oth[0:1024]
        valid[1024:2048] = 1
        halo2[0:1024] = own[3072:4096]
        valid2[0:1024] = 1
    pos = np.asarray(inputs['positions'][b])
    p_own = pos[half * 4096:(half + 1) * 4096]
    p_oth = pos[(1 - half) * 4096:(2 - half) * 4096]
    pos_l = np.concatenate([p_own, p_oth])
    pos_l2 = np.concatenate([p_oth, p_own])
    m = {
        'x_own': np.ascontiguousarray(own), 'x_oth': np.ascontiguousarray(oth), 'x_halo': halo, 'x_halo2': halo2,
        'valid': np.ascontiguousarray(valid.reshape(16, 128).T),
        'valid2': np.ascontiguousarray(valid2.reshape(16, 128).T),
        'pos': np.ascontiguousarray(pos_l.reshape(64, 128).T.astype(np.int32)),
        'pos2': np.ascontiguousarray(pos_l2.reshape(64, 128).T.astype(np.int32)),
        'invf': (10000.0 ** (-np.arange(16, dtype=np.float32) / 16)).astype(np.float32).reshape(1, 16),
        'idb': np.eye(128).astype(ml_dtypes.bfloat16), 'idf': np.eye(128).astype(np.float32),
        'rel_bias_table': np.ascontiguousarray(inputs['rel_bias_table'], dtype=np.float32), 'oh': make_oh(),
    }
    for k in WNAMES:
        m[k] = np.ascontiguousarray(np.asarray(inputs[k], dtype=np.float32))
    return m


def build_program():
    nc = bass.Bass("TRN2", target_bir_lowering=False)
    D, SC = declare(nc, NLAYER)
    out_d = nc.dram_tensor('out', [4096, 1024], F32, kind="ExternalOutput")
    b_xa = Buf('xa', multi=True)
    b_xb = Buf('xb', multi=True)
    b_out = Buf('out', multi=True)
    with ExitStack() as es:
        P = Prog(nc, es)
        LB = LayerBuilder(nc, P, D, debug={})
        LB.SC = SC
        XA, XB = SC['XA'].ap(), SC['XB'].ap()
        LB.layer(0, D['x_own'].ap(), D['x_oth'].ap(), D['x_halo'].ap(), SC['X1'].ap(), XA, b_xa)
        LB.layer(0, D['x_oth'].ap(), D['x_own'].ap(), D['x_halo2'].ap(), SC['X1'].ap(), XB, b_xb,
                 with_bias_setup=False, pos_key='pos2', valid_key='valid2', reuse_ckv=True)

        def halo_rows(t):
            r0 = 3072 + t * 128 if t < 8 else (t - 8) * 128
            return XB[r0:r0 + 128, :]
        P.barrier('sync', reads=[b_xa, b_xb])
        LB.layer(1, XA, XB, None, SC['X1'].ap(), out_d.ap(), b_out, with_bias_setup=False, halo_rows=halo_rows)
        P.barrier('sync', reads=[b_out])
        P.emit()
    return nc


def kernel(**inputs):
    nc = build_program()
    in_maps = [make_inputs(inputs, c) for c in range(8)]
    res = run_bass_kernel_spmd(nc, in_maps, core_ids=list(range(8)))
    x = np.asarray(inputs['x'])
    out = np.empty(x.shape, np.float32)
    for c in range(8):
        b, half = c // 2, c % 2
        out[b, half * 4096:(half + 1) * 4096] = np.asarray(res.results[c]['out'], dtype=np.float32)
    return out
```

```python
import math
import ml_dtypes
from concourse.bass_utils import run_bass_kernel_spmd
import numpy as np
import concourse.bass as bass
import concourse.mybir as mybir
from contextlib import ExitStack

F32 = mybir.dt.float32
BF16 = mybir.dt.bfloat16
I32 = mybir.dt.int32
ALU = mybir.AluOpType
AF = mybir.ActivationFunctionType
AX = mybir.AxisListType

ENGS = ['sync', 'scalar', 'vector', 'gpsimd', 'tensor']
SEM_ROT = 24000


class Buf:
    __slots__ = ('name', 'writer', 'readers', 'dreaders', 'multi', 'mw')

    def __init__(self, name, multi=False):
        self.name = name
        self.writer = None
        self.readers = {}
        self.dreaders = []
        self.multi = multi
        self.mw = []


class Op:
    __slots__ = ('eng', 'fn', 'deps', 'idx', 'signal', 'is_dma', 'lane', 'ev', 'raw', 'barrier')


class Prog:
    def __init__(self, nc, es):
        self.nc = nc
        self.es = es
        self.ops = {e: [] for e in ENGS}
        self.order = []
        self.lanes = {}
        self.nsem = 0
        self.fence = []
        self.fence_pending = set()
        self.phase_lanes = {}

    def phase_barrier(self):
        fence = []
        for e in ENGS:
            for o in reversed(self.ops[e]):
                if not o.is_dma and not o.barrier:
                    fence.append(o)
                    break
        last = {}
        for o in self.order:
            if o.is_dma:
                last[o.lane] = o
        fence += list(last.values())
        self.fence = fence
        self.fence_pending = set(ENGS)
        self.phase_lanes = {}

    def new_sem(self, name):
        self.nsem += 1
        return self.es.enter_context(self.nc.semaphore(name))

    def sb(self, name, shape, dt):
        return self.es.enter_context(self.nc.sbuf_tensor(name, shape, dt))

    def ps(self, name, shape, dt):
        return self.es.enter_context(self.nc.psum_tensor(name, shape, dt))

    def barrier(self, eng, reads=(), writes=()):
        o = self.op(eng, lambda e: None, reads, writes)
        o.barrier = True
        return o

    def op(self, eng, fn, reads=(), writes=(), lane=None):
        o = Op()
        o.eng = eng
        o.fn = fn
        o.barrier = False
        o.is_dma = lane is not None
        if lane is not None:
            if lane not in self.phase_lanes:
                self.phase_lanes[lane] = 'L%d' % len(self.phase_lanes)
            lane = self.phase_lanes[lane]
        o.lane = lane
        o.signal = False
        o.ev = None
        deps = {}
        raw = set()
        for b in reads:
            if b.writer is not None:
                deps[id(b.writer)] = b.writer
                raw.add(id(b.writer))
            for w in b.mw:
                deps[id(w)] = w
                raw.add(id(w))
        for b in writes:
            if b.writer is not None and not b.multi:
                deps[id(b.writer)] = b.writer
                raw.add(id(b.writer))
            for r in b.readers.values():
                deps[id(r)] = r
            for r in b.dreaders:
                deps[id(r)] = r
        if eng in self.fence_pending:
            self.fence_pending.discard(eng)
            for w in self.fence:
                deps[id(w)] = w
        deps.pop(id(o), None)
        o.deps = list(deps.values())
        o.raw = raw
        for b in writes:
            if b.multi:
                b.mw.append(o)
            else:
                b.writer = o
            b.readers = {}
            b.dreaders = []
        for b in reads:
            if b.multi:
                continue
            if o.is_dma:
                b.dreaders.append(o)
            else:
                b.readers[eng] = o
        o.idx = len(self.ops[eng])
        self.ops[eng].append(o)
        self.order.append(o)
        return o

    def dma(self, q, out, in_, reads=(), writes=(), lane=None, **kw):
        assert lane is not None
        return self.op(q, lambda e: e.dma_start(out=out, in_=in_, **kw), reads, writes, lane=lane)

    def emit(self):
        nc = self.nc
        for o in self.order:
            for d in o.deps:
                if d.is_dma:
                    continue
                if d.barrier:
                    assert d.eng == o.eng, 'barrier dep across engines'
                    continue
                if d.eng == o.eng and not o.is_dma:
                    if o.eng == 'tensor':
                        continue
                    if id(d) not in o.raw:
                        continue
                d.signal = True
        esems = {}
        for e in ENGS:
            cnt = 0
            cur = None
            for o in self.ops[e]:
                if o.is_dma:
                    ln = self.lanes.get(o.lane)
                    if ln is None:
                        ln = [self.new_sem('l_%s' % o.lane), 0]
                        self.lanes[o.lane] = ln
                    ln[1] += 16
                    o.ev = (ln[0], ln[1])
                elif o.signal:
                    if cur is None or cnt >= SEM_ROT:
                        cur = self.new_sem('e_%s_%d' % (e, len(esems)))
                        esems[(e, len(esems))] = cur
                        cnt = 0
                    cnt += 1
                    o.ev = (cur, cnt)
        blk = self.es.enter_context(nc.Block())
        prog = self

        def run(e, eng):
            waited = {}
            for o in prog.ops[e]:
                need = {}
                for d in o.deps:
                    if not d.is_dma:
                        if d.barrier:
                            continue
                        if d.eng == o.eng and not o.is_dma:
                            if o.eng == 'tensor' or id(d) not in o.raw:
                                continue
                    sem, val = d.ev
                    k = id(sem)
                    if k not in need or need[k][1] < val:
                        need[k] = (sem, val)
                for k, (sem, val) in need.items():
                    if waited.get(k, 0) >= val:
                        continue
                    eng.wait_ge(sem, val)
                    waited[k] = val
                ins = o.fn(eng)
                if ins is None:
                    continue
                if o.is_dma:
                    ins.then_inc(o.ev[0], 16)
                elif o.signal:
                    ins.then_inc(o.ev[0], 1)

        @blk.sync
        def _(eng):
            run('sync', eng)

        @blk.scalar
        def _(eng):
            run('scalar', eng)

        @blk.vector
        def _(eng):
            run('vector', eng)

        @blk.gpsimd
        def _(eng):
            run('gpsimd', eng)

        @blk.tensor
        def _(eng):
            run('tensor', eng)

import numpy as np
import math

EPS = 1e-6
NEGB = -30000.0
TWO_PI_S = 6.2831845


def bc(ap, shape):
    return ap.to_broadcast(list(shape))


def psum_view(ph, nc, name, shape, dt):
    esz = 4 if dt == F32 else 2
    n = 1
    for d in shape[1:]:
        n *= d
    per_bank = 2048 // esz
    tot = ((n + per_bank - 1) // per_bank) * per_bank
    t = ph.enter_context(nc.psum_tensor(name, [128, tot], dt))
    v = t[0:shape[0], 0:n]
    if len(shape) == 3:
        v = v.rearrange("p (a b) -> p a b", a=shape[1])
    elif len(shape) == 4:
        v = v.rearrange("p (a b c) -> p a b c", a=shape[1], b=shape[2])
    return v


class LayerBuilder:
    def __init__(self, nc, P, D, debug=False):
        self.nc = nc
        self.P = P
        self.D = D
        self.debug = debug
        self.uid = 0

    def name(self, s):
        self.uid += 1
        return "%s_%d" % (s, self.uid)

    def V(self, fn, reads, writes, **kw):
        return self.P.op('vector', lambda e: getattr(e, fn)(**kw), reads, writes)

    def G(self, fn, reads, writes, **kw):
        return self.P.op('gpsimd', lambda e: getattr(e, fn)(**kw), reads, writes)

    def A(self, fn, reads, writes, **kw):
        return self.P.op('scalar', lambda e: getattr(e, fn)(**kw), reads, writes)

    def T(self, fn, reads, writes, **kw):
        return self.P.op('tensor', lambda e: getattr(e, fn)(**kw), reads, writes)

    def E(self, eng, fn, reads, writes, **kw):
        return self.P.op(eng, lambda e: getattr(e, fn)(**kw), reads, writes)

    def dma(self, q, out, in_, reads, writes, lane, **kw):
        return self.P.dma(q, out, in_, reads=reads, writes=writes, lane=lane, **kw)

    def phase1(self, L, x_own, x_oth, x_halo, first=True, pos_key='pos', valid_key='valid', halo_rows=None,
               skip_oth=False, skip_ck=False):
        nc, P, D = self.nc, self.P, self.D
        V, G, A, T, E = self.V, self.G, self.A, self.T, self.E
        with ExitStack() as ph:
            def sb(nm, shape, dt, n=1):
                r = []
                for i in range(n):
                    t = ph.enter_context(nc.sbuf_tensor(self.name(nm), shape, dt))
                    r.append((t, Buf(nm + str(i))))
                return r if n > 1 else r[0]

            def ps(nm, shape, dt):
                t = psum_view(ph, nc, self.name(nm), shape, dt)
                return (t, Buf(nm))

            idb, b_idb = sb("idb", [128, 128], BF16)
            wib, b_wib = sb("wib", [128, 8, 2080], BF16)
            wuq, b_wuq = sb("wuq", [128, 2, 576], BF16)
            wukv, b_wukv = sb("wukv", [128, 768], BF16)
            g8, b_g8 = sb("g8", [128, 8], F32)
            gq2, b_gq2 = sb("gq2", [128, 2], F32)
            gkv1, b_gkv1 = sb("gkv1", [128, 1], F32)
            ga, b_ga = sb("ga", [128, 2, 64], F32)
            gb, b_gb = sb("gb", [128, 2, 64], F32)
            gc, b_gc = sb("gc", [128, 2, 96], F32)
            GAq, b_GAq = sb("GAq", [128, 64], F32)
            GBq, b_GBq = sb("GBq", [128, 64], F32)
            GCq, b_GCq = sb("GCq", [128, 96], F32)
            invf, b_invf = sb("invf", [128, 16], F32)
            posi, b_posi = sb("posi", [128, 64], I32)
            posf, b_posf = sb("posf", [128, 64], F32)
            ang, b_ang = sb("ang", [128, 64, 16], F32)
            angk, b_angk = sb("angk", [128, 64, 16], I32)
            angf, b_angf = sb("angf", [128, 64, 16], F32)
            sin_t, b_sin = sb("sin_t", [128, 64, 16], F32)
            cos_t, b_cos = sb("cos_t", [128, 64, 16], F32)
            invd, b_invd = sb("invd", [128, 24], F32)
            nh24, b_nh24 = sb("nh24", [128, 24], F32)
            valid, b_valid = sb("valid", [128, 16], F32)
            stage = sb("stage", [128, 2080], F32, 2)
            stq, b_stq = sb("stq", [128, 2, 576], F32)
            stkv, b_stkv = sb("stkv", [128, 768], F32)

            self.dma('sync', idb[:], D['idb'].ap(), [], [b_idb], 'idb')
            self.dma('sync', g8[:], D['norm_mix'].ap()[L].rearrange("(c p) -> p c", p=128), [], [b_g8], 'g8',
                     allow_slow_non_contiguous=True)
            self.dma('sync', gq2[:], D['q_lat_gain'].ap()[L].rearrange("(c p) -> p c", p=128), [], [b_gq2], 'gq2',
                     allow_slow_non_contiguous=True)
            self.dma('sync', gkv1[:], D['kv_lat_gain'].ap()[L].rearrange("(c p) -> p c", p=128), [], [b_gkv1],
                     'gkv1', allow_slow_non_contiguous=True)
            self.dma('sync', ga[:], D['qk_gain_a'].ap()[L].rearrange("a d -> (a d)").partition_broadcast(128),
                     [], [b_ga], 'ga')
            self.dma('sync', gb[:], D['qk_gain_b'].ap()[L].rearrange("a d -> (a d)").partition_broadcast(128),
                     [], [b_gb], 'gb')
            self.dma('sync', gc[:], D['qk_gain_c'].ap()[L].rearrange("a d -> (a d)").partition_broadcast(128),
                     [], [b_gc], 'gc')
            self.dma('sync', invf[:], D['invf'].ap().rearrange("a d -> (a d)").partition_broadcast(128),
                     [], [b_invf], 'invf')
            self.dma('sync', posi[:], D[pos_key].ap(), [], [b_posi], 'posi')
            self.dma('sync', valid[:], D[valid_key].ap(), [], [b_valid], 'valid')
            V('scalar_tensor_tensor', [b_ga], [b_GAq], out=GAq[:], in0=ga[:, 0, :], scalar=0.125, in1=ga[:, 1, :],
              op0=ALU.mult, op1=ALU.mult)
            V('scalar_tensor_tensor', [b_gb], [b_GBq], out=GBq[:], in0=gb[:, 0, :], scalar=0.125, in1=gb[:, 1, :],
              op0=ALU.mult, op1=ALU.mult)
            V('tensor_scalar', [b_gc], [b_GCq], out=GCq[:], in0=gc[:, 0, :], scalar1=96.0 ** -0.5, scalar2=None,
              op0=ALU.mult)
            GCk = gc[:, 1, :]
            b_GCk = b_gc
            self.P.op('gpsimd', lambda e: e.memset(invd[:], 1.0 / 64), [], [b_invd])
            self.P.op('gpsimd', lambda e: e.memset(invd[:, 18:19], 1.0 / 256), [], [b_invd])
            self.P.op('gpsimd', lambda e: e.memset(invd[:, 19:20], 1.0 / 128), [], [b_invd])
            self.P.op('gpsimd', lambda e: e.memset(invd[:, 20:24], 1.0), [], [b_invd])
            self.P.op('gpsimd', lambda e: e.memset(nh24[:], -0.5), [], [b_nh24])

            V('tensor_copy', [b_posi], [b_posf], out=posf[:], in_=posi[:])
            V('tensor_tensor', [b_posf, b_invf], [b_ang], out=ang[:],
              in0=bc(posf[:].unsqueeze(2), [128, 64, 16]), in1=bc(invf[:].unsqueeze(1), [128, 64, 16]), op=ALU.mult)
            for (tab, b_tab, off) in ((sin_t, b_sin, 0.0), (cos_t, b_cos, 0.25)):
                V('tensor_scalar', [b_ang], [b_angf], out=angf[:], in0=ang[:], scalar1=1.0 / (2 * math.pi),
                  scalar2=off, op0=ALU.mult, op1=ALU.add)
                V('tensor_copy', [b_angf], [b_angk], out=angk[:], in_=angf[:])
                V('tensor_copy', [b_angk], [b_tab], out=tab[:], in_=angk[:])
                V('tensor_tensor', [b_angf, b_tab], [b_angf], out=angf[:], in0=angf[:], in1=tab[:], op=ALU.subtract)
                A('activation', [b_angf], [b_tab], out=tab[:], in_=angf[:], func=AF.Sin, scale=TWO_PI_S)

            blocks = [(0, 384, 0), (384, 768, 512), (768, 1152, 1024), (1152, 1408, 1536), (1408, 1536, 896),
                      (1536, 1664, 1408), (1664, 1920, 1792), (1920, 2048, 384), (2048, 2080, 2048)]
            k = 0
            for c in range(8):
                st_t, st_b = stage[c % 2]
                self.dma('sync', st_t[:], D['w_in'].ap()[L, c * 128:(c + 1) * 128, :], [], [st_b], 'stage%d' % (c % 2))
                for (o0, o1, n0) in blocks:
                    eng = 'vector' if k % 2 == 0 else 'gpsimd'
                    k += 1
                    E(eng, 'tensor_scalar', [st_b, b_g8], [b_wib], out=wib[:, c, n0:n0 + (o1 - o0)],
                      in0=st_t[:, o0:o1], scalar1=g8[:, c:c + 1], scalar2=None, op0=ALU.mult)
            self.dma('sync', stq[:], D['w_uq'].ap()[L].rearrange("(c p) n -> p c n", p=128), [], [b_stq], 'stq')
            self.dma('sync', stkv[:], D['w_ukv'].ap()[L], [], [b_stkv], 'stkv')
            for c in range(2):
                V('tensor_scalar', [b_stq, b_gq2], [b_wuq], out=wuq[:, c, :], in0=stq[:, c, :],
                  scalar1=gq2[:, c:c + 1], scalar2=None, op0=ALU.mult)
            V('tensor_scalar', [b_stkv, b_gkv1], [b_wukv], out=wukv[:], in0=stkv[:], scalar1=gkv1[:, 0:1],
              scalar2=None, op0=ALU.mult)

            xt = sb("xt", [128, 1024], F32, 3)
            junk, b_junk = sb("junk", [128, 1024], BF16)
            ssx = sb("ssx", [128, 1], F32, 3)
            rsx = sb("rsx", [128, 1], F32, 3)
            hb = sb("hb", [128, 1024], BF16, 3)
            hT = sb("hT", [128, 8, 128], BF16, 3)
            pj = sb("pj", [128, 2080], F32, 3)
            sq, b_sq = sb("sq", [128, 2080], F32)
            st = sb("st", [128, 24], F32, 3)
            rstd = sb("rstd", [128, 24], F32, 3)
            QAo = sb("QAo", [128, 384], BF16, 3)
            QAt = sb("QAt", [128, 384], F32, 1)
            KABo = sb("KABo", [128, 512], BF16, 3)
            QBo = sb("QBo", [128, 256], BF16, 3)
            QBt = sb("QBt", [128, 256], F32, 1)
            VABo = sb("VABo", [128, 8, 65], BF16, 3)
            LAT = sb("LAT", [128, 384], BF16, 3)
            latT = sb("latT", [128, 3, 128], BF16, 3)
            qcs = sb("qcs", [128, 576], F32, 3)
            kvcs = sb("kvcs", [128, 768], F32, 3)
            st2 = sb("st2", [128, 12], F32, 3)
            rstd2 = sb("rstd2", [128, 12], F32, 3)
            tmp1, b_tmp1 = sb("tmp1", [128, 6, 96], F32)
            trq, b_trq = sb("trq", [128, 6, 32], F32)
            tmpk, b_tmpk = sb("tmpk", [128, 6, 64], F32)
            krg, b_krg = sb("krg", [128, 1, 32], F32)
            krr, b_krr = sb("krr", [128, 1, 32], F32)
            rm = [sb("rm%d" % i, [128, 6, 16], F32) for i in range(4)]
            QCo = sb("QCo", [128, 6, 96], BF16, 3)
            KCo = sb("KCo", [128, 6, 96], BF16, 3)
            VCo = sb("VCo", [128, 6, 65], BF16, 3)
            QTs = sb("QTs", [96, 6, 128], BF16, 3)
            KTs = sb("KTs", [96, 6, 128], BF16, 3)
            TR, b_TR = ps("TR", [128, 8, 128], BF16)
            PJ = [ps("PJ%d" % i, [128, 512], F32) for i in range(5)]
            S = [ps("S%d" % i, [128, 512], F32) for i in range(2)]

            for (t_, b_) in VABo:
                self.P.op('gpsimd', lambda e, t_=t_: e.memset(t_[:], 1.0), [], [b_])
            for (t_, b_) in VCo:
                self.P.op('gpsimd', lambda e, t_=t_: e.memset(t_[:], 1.0), [], [b_])

            def rope(src, b_src, dst, b_dst, H, ti):
                cb = bc(cos_t[:, ti, :].unsqueeze(1), [128, H, 16])
                sbb = bc(sin_t[:, ti, :].unsqueeze(1), [128, H, 16])
                (m1, b1), (m2, b2), (m3, b3), (m4, b4) = rm
                G('tensor_tensor', [b_src, b_cos], [b1], out=m1[:, 0:H, :], in0=src[:, :, 0:16], in1=cb, op=ALU.mult)
                G('tensor_tensor', [b_src, b_sin], [b2], out=m2[:, 0:H, :], in0=src[:, :, 16:32], in1=sbb, op=ALU.mult)
                G('tensor_tensor', [b1, b2], [b_dst], out=dst[:, :, 0:16], in0=m1[:, 0:H, :], in1=m2[:, 0:H, :],
                  op=ALU.subtract)
                G('tensor_tensor', [b_src, b_cos], [b3], out=m3[:, 0:H, :], in0=src[:, :, 16:32], in1=cb, op=ALU.mult)
                G('tensor_tensor', [b_src, b_sin], [b4], out=m4[:, 0:H, :], in0=src[:, :, 0:16], in1=sbb, op=ALU.mult)
                G('tensor_tensor', [b3, b4], [b_dst], out=dst[:, :, 16:32], in0=m3[:, 0:H, :], in1=m4[:, 0:H, :],
                  op=ALU.add)

            SC = self.SC
            it = 0
            jobs = [('own', t) for t in range(32)] + [('oth', t) for t in range(32)] + [('halo', t) for t in range(16)]
            if skip_oth:
                jobs = [j for j in jobs if j[0] != 'oth']
            if self.debug and self.debug.get('p1_tiles'):
                jobs = self.debug['p1_tiles']
            def tile_gen(it, kind, t):
                s2 = it % 3
                s3 = it % 3
                src = {'own': x_own, 'oth': x_oth, 'halo': x_halo}[kind]
                x_t, b_x = xt[s3]
                ss_t, b_ss = ssx[s2]
                rs_t, b_rs = rsx[s2]
                hb_t, b_hb = hb[s2]
                hT_t, b_hT = hT[s2]
                pj_t, b_pj = pj[s3]
                st_t, b_st = st[s3]
                rstd_t, b_rstd = rstd[s3]
                if kind == 'halo':
                    G('tensor_scalar', [b_x, b_valid], [b_x], out=x_t[:], in0=x_t[:], scalar1=valid[:, t:t + 1],
                      scalar2=None, op0=ALU.mult)
                    yield
                A('activation', [b_x], [b_junk, b_ss], out=junk[:], in_=x_t[:], func=AF.Square, accum_out=ss_t[:])
                yield
                G('tensor_scalar', [b_ss], [b_ss], out=ss_t[:], in0=ss_t[:], scalar1=1.0 / 1024, scalar2=EPS,
                  op0=ALU.mult, op1=ALU.add)
                yield
                G('tensor_tensor', [b_ss, b_nh24], [b_rs], out=rs_t[:], in0=ss_t[:], in1=nh24[:, 0:1], op=ALU.pow)
                yield
                A('activation', [b_x, b_rs], [b_hb], out=hb_t[:], in_=x_t[:], func=AF.Copy, scale=rs_t[:, 0:1])
                yield
                for c in range(8):
                    T('transpose', [b_hb, b_idb], [b_TR], out=TR[:, c, :], in_=hb_t[:, c * 128:(c + 1) * 128],
                      identity=idb[:])
                V('tensor_copy', [b_TR], [b_hT], out=hT_t[:], in_=TR[:])
                yield
                if kind == 'own':
                    groups = [(0, 0, 512, 0), (1, 512, 1024, 0), (2, 1024, 1536, 0), (3, 1536, 2048, 0),
                              (4, 2048, 2080, 0)]
                elif kind == 'oth':
                    groups = [(0, 384, 512, 384), (4, 2048, 2080, 0)]
                else:
                    groups = [(1, 512, 1024, 0), (2, 1024, 1536, 0)]
                for (bk, c0, c1, po) in groups:
                    pt, pb = PJ[bk]
                    for c in range(8):
                        T('matmul', [b_hT, b_wib], [pb], out=pt[:, po:po + (c1 - c0)], lhsT=hT_t[:, c, :],
                          rhs=wib[:, c, c0:c1], start=(c == 0), stop=(c == 7))
                    A('activation', [pb], [b_pj], out=pj_t[:, c0:c1], in_=pt[:, po:po + (c1 - c0)], func=AF.Copy)
                    yield
                if kind == 'own':
                    V('tensor_tensor', [b_pj], [b_sq], out=sq[:, 0:1024], in0=pj_t[:, 0:1024], in1=pj_t[:, 0:1024],
                      op=ALU.mult)
                    V('tensor_tensor', [b_pj], [b_sq], out=sq[:, 1536:2080], in0=pj_t[:, 1536:2080],
                      in1=pj_t[:, 1536:2080], op=ALU.mult)
                    red = [(0, 6, 0, 384, 64), (6, 14, 512, 1024, 64), (14, 18, 1536, 1792, 64),
                           (18, 19, 1792, 2048, 256), (19, 20, 384, 512, 128), (20, 21, 2048, 2080, 32)]
                elif kind == 'oth':
                    V('tensor_tensor', [b_pj], [b_sq], out=sq[:, 384:512], in0=pj_t[:, 384:512], in1=pj_t[:, 384:512],
                      op=ALU.mult)
                    V('tensor_tensor', [b_pj], [b_sq], out=sq[:, 2048:2080], in0=pj_t[:, 2048:2080],
                      in1=pj_t[:, 2048:2080], op=ALU.mult)
                    red = [(19, 20, 384, 512, 128), (20, 21, 2048, 2080, 32)]
                else:
                    V('tensor_tensor', [b_pj], [b_sq], out=sq[:, 512:1024], in0=pj_t[:, 512:1024],
                      in1=pj_t[:, 512:1024], op=ALU.mult)
                    red = [(6, 14, 512, 1024, 64)]
                for (a0, a1, c0, c1, dd) in red:
                    V('tensor_reduce', [b_sq], [b_st], out=st_t[:, a0:a1],
                      in_=sq[:, c0:c1].rearrange("p (h d) -> p h d", d=dd), axis=AX.X, op=ALU.add)
                G('tensor_tensor', [b_st, b_invd], [b_rstd], out=rstd_t[:, 0:20], in0=st_t[:, 0:20], in1=invd[:, 0:20],
                  op=ALU.mult)
                yield
                G('tensor_scalar', [b_rstd], [b_rstd], out=rstd_t[:, 0:20], in0=rstd_t[:, 0:20], scalar1=EPS,
                  scalar2=None, op0=ALU.add)
                yield
                G('tensor_tensor', [b_rstd, b_nh24], [b_rstd], out=rstd_t[:, 0:20], in0=rstd_t[:, 0:20],
                  in1=nh24[:, 0:20], op=ALU.pow)
                yield
                if kind in ('own', 'halo'):
                    et = (8 + t) if kind == 'own' else (t if t < 8 else 40 + (t - 8))
                    kab_t, b_kab = KABo[s2]
                    vab_t, b_vab = VABo[s2]
                    V('tensor_tensor', [b_pj, b_rstd], [b_kab], out=kab_t[:].rearrange("p (h d) -> p h d", d=64),
                      in0=pj_t[:, 512:1024].rearrange("p (h d) -> p h d", d=64),
                      in1=bc(rstd_t[:, 6:14].unsqueeze(2), [128, 8, 64]), op=ALU.mult)
                    yield
                    V('tensor_copy', [b_pj], [b_vab], out=vab_t[:, :, 0:64],
                      in_=pj_t[:, 1024:1536].rearrange("p (h d) -> p h d", d=64))
                    yield
                    if kind == 'halo':
                        V('tensor_copy', [b_valid], [b_vab], out=vab_t[:, :, 64:65],
                          in_=bc(valid[:, t:t + 1].unsqueeze(1), [128, 8, 1]))
                        yield
                    else:
                        self.P.op('vector', lambda e, vab_t=vab_t: e.memset(vab_t[:, :, 64:65], 1.0), [], [b_vab])
                        yield
                    rows = slice(et * 128, (et + 1) * 128)
                    self.dma('sync', SC['KAx'].ap()[rows, :], kab_t[:, 0:384], [b_kab], [SC['b']], 'kabo%d' % s2)
                    yield
                    self.dma('sync', SC['KBx'].ap()[rows, :], kab_t[:, 384:512], [b_kab], [SC['b']], 'kabo%d' % s2)
                    yield
                    self.dma('sync', SC['VAx'].ap()[rows, :].rearrange("p (h d) -> p h d", d=65), vab_t[:, 0:6, :],
                             [b_vab], [SC['b']], 'vabo%d' % s2)
                    yield
                    self.dma('sync', SC['VBx'].ap()[rows, :].rearrange("p (h d) -> p h d", d=65), vab_t[:, 6:8, :],
                             [b_vab], [SC['b']], 'vabo%d' % s2)
                    yield
                if kind == 'own':
                    rows = slice(t * 128, (t + 1) * 128)
                    qa_t, b_qa = QAo[s2]
                    qat, b_qat = QAt
                    V('tensor_tensor', [b_pj, b_rstd], [b_qat], out=qat[:].rearrange("p (h d) -> p h d", d=64),
                      in0=pj_t[:, 0:384].rearrange("p (h d) -> p h d", d=64),
                      in1=bc(rstd_t[:, 0:6].unsqueeze(2), [128, 6, 64]), op=ALU.mult)
                    V('tensor_tensor', [b_qat, b_GAq], [b_qa], out=qa_t[:].rearrange("p (h d) -> p h d", d=64),
                      in0=qat[:].rearrange("p (h d) -> p h d", d=64),
                      in1=bc(GAq[:].unsqueeze(1), [128, 6, 64]), op=ALU.mult)
                    self.dma('sync', SC['QA'].ap()[rows, :], qa_t[:], [b_qa], [SC['b']], 'qao%d' % s2)
                    yield
                    qb_t, b_qb = QBo[s2]
                    qbt, b_qbt = QBt
                    V('tensor_tensor', [b_pj, b_rstd], [b_qbt], out=qbt[:].rearrange("p (h d) -> p h d", d=64),
                      in0=pj_t[:, 1536:1792].rearrange("p (h d) -> p h d", d=64),
                      in1=bc(rstd_t[:, 14:18].unsqueeze(2), [128, 4, 64]), op=ALU.mult)
                    V('tensor_tensor', [b_qbt, b_GBq], [b_qb],
                      out=qb_t[:].rearrange("p (b a d) -> p a b d", b=2, a=2, d=64),
                      in0=qbt[:].rearrange("p (a b d) -> p a b d", a=2, b=2, d=64),
                      in1=bc(GBq[:].unsqueeze(1).unsqueeze(1), [128, 2, 2, 64]), op=ALU.mult)
                    self.dma('sync', SC['QB'].ap()[rows, :], qb_t[:], [b_qb], [SC['b']], 'qbo%d' % s2)
                    yield
                if kind in ('own', 'oth'):
                    ti = t if kind == 'own' else 32 + t
                    lat_t, b_lat = LAT[s2]
                    latT_t, b_latT = latT[s2]
                    qcs_t, b_qcs = qcs[s2]
                    kvcs_t, b_kvcs = kvcs[s2]
                    st2_t, b_st2 = st2[s2]
                    rstd2_t, b_rstd2 = rstd2[s2]
                    if kind == 'own':
                        V('tensor_scalar', [b_pj, b_rstd], [b_lat], out=lat_t[:, 0:256], in0=pj_t[:, 1792:2048],
                          scalar1=rstd_t[:, 18:19], scalar2=None, op0=ALU.mult)
                        yield
                    if not skip_ck:
                        V('tensor_scalar', [b_pj, b_rstd], [b_lat], out=lat_t[:, 256:384], in0=pj_t[:, 384:512],
                          scalar1=rstd_t[:, 19:20], scalar2=None, op0=ALU.mult)
                        yield
                    jl = ([0, 1] if skip_ck else [0, 1, 2]) if kind == 'own' else [2]
                    for j in jl:
                        T('transpose', [b_lat, b_idb], [b_TR], out=TR[:, j, :], in_=lat_t[:, j * 128:(j + 1) * 128],
                          identity=idb[:])
                    V('tensor_copy', [b_TR], [b_latT], out=latT_t[:, jl[0]:jl[-1] + 1, :], in_=TR[:, jl[0]:jl[-1] + 1, :])
                    if kind == 'own':
                        for hf in range(2):
                            for c in range(2):
                                T('matmul', [b_latT, b_wuq], [S[hf][1]], out=S[hf][0][:, 0:288], lhsT=latT_t[:, c, :],
                                  rhs=wuq[:, c, hf * 288:(hf + 1) * 288], start=(c == 0), stop=(c == 1))
                            A('activation', [S[hf][1]], [b_qcs], out=qcs_t[:, hf * 288:(hf + 1) * 288],
                              in_=S[hf][0][:, 0:288], func=AF.Copy)
                    for hf in (range(2) if not skip_ck else []):
                        T('matmul', [b_latT, b_wukv], [S[hf][1]], out=S[hf][0][:, 0:384], lhsT=latT_t[:, 2, :],
                          rhs=wukv[:, hf * 384:(hf + 1) * 384], start=True, stop=True)
                        A('activation', [S[hf][1]], [b_kvcs], out=kvcs_t[:, hf * 384:(hf + 1) * 384],
                          in_=S[hf][0][:, 0:384], func=AF.Copy)
                    kv3 = kvcs_t[:].rearrange("p (h d) -> p h d", d=128)
                    if kind == 'own':
                        V('tensor_tensor', [b_qcs], [b_sq], out=sq[:, 0:576], in0=qcs_t[:], in1=qcs_t[:], op=ALU.mult)
                        V('tensor_reduce', [b_sq], [b_st2], out=st2_t[:, 0:6],
                          in_=sq[:, 0:576].rearrange("p (h d) -> p h d", d=96), axis=AX.X, op=ALU.add)
                    if not skip_ck:
                        V('tensor_tensor', [b_kvcs], [b_sq], out=sq[:, 1024:1408].rearrange("p (h d) -> p h d", d=64),
                          in0=kv3[:, :, 0:64], in1=kv3[:, :, 0:64], op=ALU.mult)
                        V('tensor_reduce', [b_sq], [b_st2], out=st2_t[:, 6:12],
                          in_=sq[:, 1024:1408].rearrange("p (h d) -> p h d", d=64), axis=AX.X, op=ALU.add)
                        V('tensor_scalar', [b_st2, b_st], [b_st2], out=st2_t[:, 6:12], in0=st2_t[:, 6:12],
                          scalar1=st_t[:, 20:21], scalar2=None, op0=ALU.add)
                    lo = 0 if kind == 'own' else 6
                    hi_ = 6 if skip_ck else 12
                    G('tensor_scalar', [b_st2], [b_rstd2], out=rstd2_t[:, lo:hi_], in0=st2_t[:, lo:hi_],
                      scalar1=1.0 / 96, scalar2=EPS, op0=ALU.mult, op1=ALU.add)
                    yield
                    G('tensor_tensor', [b_rstd2, b_nh24], [b_rstd2], out=rstd2_t[:, lo:hi_], in0=rstd2_t[:, lo:hi_],
                      in1=nh24[:, lo:hi_], op=ALU.pow)
                    yield
                    kc_t, b_kc = KCo[s2]
                    vc_t, b_vc = VCo[s2]
                    if kind == 'own':
                        qc_t, b_qc = QCo[s2]
                        V('tensor_tensor', [b_qcs, b_rstd2], [b_tmp1], out=tmp1[:],
                          in0=qcs_t[:].rearrange("p (h d) -> p h d", d=96),
                          in1=bc(rstd2_t[:, 0:6].unsqueeze(2), [128, 6, 96]), op=ALU.mult)
                        V('tensor_tensor', [b_tmp1, b_GCq], [b_qc], out=qc_t[:, :, 0:64], in0=tmp1[:, :, 0:64],
                          in1=bc(GCq[:, 0:64].unsqueeze(1), [128, 6, 64]), op=ALU.mult)
                        V('tensor_tensor', [b_tmp1, b_GCq], [b_trq], out=trq[:], in0=tmp1[:, :, 64:96],
                          in1=bc(GCq[:, 64:96].unsqueeze(1), [128, 6, 32]), op=ALU.mult)
                        rope(trq, b_trq, qc_t[:, :, 64:96], b_qc, 6, ti)
                    if kind == 'own':
                        qT_t, b_qT = QTs[s2]
                        for h in range(6):
                            T('transpose', [b_qc, b_idb], [b_TR], out=TR[0:96, h, :], in_=qc_t[:, h, :],
                              identity=idb[:])
                        V('tensor_copy', [b_TR], [b_qT], out=qT_t[:], in_=TR[0:96, 0:6, :])
                        self.dma('sync', SC['QCT'].ap()[:, :, t * 128:(t + 1) * 128].rearrange("h d n -> d h n"),
                                 qT_t[:], [b_qT], [SC['b']], 'qto%d' % s2)
                        yield
                    if skip_ck:
                        return
                    V('tensor_tensor', [b_kvcs, b_rstd2], [b_tmpk], out=tmpk[:], in0=kv3[:, :, 0:64],
                      in1=bc(rstd2_t[:, 6:12].unsqueeze(2), [128, 6, 64]), op=ALU.mult)
                    V('tensor_tensor', [b_tmpk, b_GCk], [b_kc], out=kc_t[:, :, 0:64], in0=tmpk[:],
                      in1=bc(GCk[:, 0:64].unsqueeze(1), [128, 6, 64]), op=ALU.mult)
                    V('tensor_tensor', [b_pj, b_GCk], [b_krg], out=krg[:, 0, :], in0=pj_t[:, 2048:2080],
                      in1=GCk[:, 64:96], op=ALU.mult)
                    rope(krg, b_krg, krr[:], b_krr, 1, ti)
                    V('tensor_tensor', [b_krr, b_rstd2], [b_kc], out=kc_t[:, :, 64:96],
                      in0=bc(krr[:], [128, 6, 32]), in1=bc(rstd2_t[:, 6:12].unsqueeze(2), [128, 6, 32]), op=ALU.mult)
                    V('tensor_copy', [b_kvcs], [b_vc], out=vc_t[:, :, 0:64], in_=kv3[:, :, 64:128])
                    yield
                    kT_t, b_kT = KTs[s2]
                    for h in range(6):
                        T('transpose', [b_kc, b_idb], [b_TR], out=TR[0:96, h, :], in_=kc_t[:, h, :], identity=idb[:])
                    V('tensor_copy', [b_TR], [b_kT], out=kT_t[:], in_=TR[0:96, 0:6, :])
                    self.dma('sync', SC['KCT'].ap()[:, :, ti * 128:(ti + 1) * 128].rearrange("h d n -> d h n"),
                             kT_t[:], [b_kT], [SC['b']], 'kto%d' % s2)
                    yield
                    self.dma('sync', SC['VC'].ap()[ti * 128:(ti + 1) * 128, :].rearrange("p (h d) -> p h d", d=65),
                             vc_t[:], [b_vc], [SC['b']], 'vco%d' % s2)
                    yield

            def issue_load(j):
                kind, t = jobs[j]
                x_t, b_x = xt[j % 3]
                if kind == 'halo' and halo_rows is not None:
                    self.dma('sync', x_t[:], halo_rows(t), [], [b_x], 'xt%d' % (j % 3))
                else:
                    src = {'own': x_own, 'oth': x_oth, 'halo': x_halo}[kind]
                    self.dma('sync', x_t[:], src[t * 128:(t + 1) * 128, :], [], [b_x], 'xt%d' % (j % 3))

            for j in range(min(2, len(jobs))):
                issue_load(j)
            active = []
            nxt = 0
            since = 10 ** 9
            STAG = 9
            while active or nxt < len(jobs):
                if nxt < len(jobs) and len(active) < 3 and (since >= STAG or not active):
                    if nxt + 2 < len(jobs):
                        issue_load(nxt + 2)
                    active.append(tile_gen(nxt, jobs[nxt][0], jobs[nxt][1]))
                    nxt += 1
                    since = 0
                for g in list(active):
                    try:
                        next(g)
                    except StopIteration:
                        active.remove(g)
                since += 1
        self.P.phase_barrier()

    def phaseC(self, heads=range(6), nqt=8):
        nc, P, D, SC = self.nc, self.P, self.D, self.SC
        V, G, A, T, E = self.V, self.G, self.A, self.T, self.E
        with ExitStack() as ph:
            def sb(nm, shape, dt, n=1):
                r = []
                for i in range(n):
                    t = ph.enter_context(nc.sbuf_tensor(self.name(nm), shape, dt))
                    r.append((t, Buf(nm + str(i))))
                return r if n > 1 else r[0]

            def ps(nm, shape, dt, n=1):
                r = []
                for i in range(n):
                    t = psum_view(ph, nc, self.name(nm), shape, dt)
                    r.append((t, Buf(nm + str(i))))
                return r if n > 1 else r[0]

            P.barrier('sync', reads=[SC['b']])
            idf, b_idf = sb("idf", [128, 128], F32)
            self.dma('sync', idf[:], D['idf'].ap(), [], [b_idf], 'idf')
            KT = sb("cKT", [96, 8192], BF16, 2)
            VV = sb("cV", [128, 64, 65], BF16, 2)
            QT = sb("cQT", [96, 4096], BF16, 2)
            PT = sb("cPT", [128, 512], BF16, 4)
            OTs = sb("cOTs", [65, 512], F32, 2)
            rc = sb("crc", [128, 4], F32, 2)
            oc = sb("coc", [128, 4, 64], BF16, 2)
            ST = ps("cST", [128, 512], F32, 3)
            OT = ps("cOT", [65, 512], F32, 2)
            TO, b_TO = ps("cTO", [128, 4, 65], F32)

            heads = list(heads)

            def load_head(hi):
                h = heads[hi]
                s = hi % 2
                self.dma('sync', KT[s][0][:], SC['KCT'].ap()[h], [SC['b']], [KT[s][1]], 'cKT%d' % s)
                self.dma('sync', QT[s][0][:], SC['QCT'].ap()[h], [SC['b']], [QT[s][1]], 'cQT%d' % s)
                self.dma('sync', VV[s][0][:],
                         SC['VC'].ap().rearrange("(t p) c -> p t c", p=128)[:, :, h * 65:(h + 1) * 65],
                         [SC['b']], [VV[s][1]], 'cV%d' % s)

            steps = [(hi, qt, kc) for hi in range(len(heads)) for qt in range(nqt) for kc in range(64)]
            n = len(steps)
            LA = 2
            deferred = []
            load_head(0)
            nq = 0
            for i in range(n + LA):
                if i < n:
                    hi, qt, kc = steps[i]
                    s = hi % 2
                    st_t, st_b = ST[i % 3]
                    pt_t, pt_b = PT[i % 4]
                    T('matmul', [KT[s][1], QT[s][1]], [st_b], out=st_t[:], lhsT=KT[s][0][:, kc * 128:(kc + 1) * 128],
                      rhs=QT[s][0][:, qt * 512:(qt + 1) * 512], start=True, stop=True)
                    A('activation', [st_b], [pt_b], out=pt_t[:], in_=st_t[:], func=AF.Exp)
                j = i - LA
                if j >= 0:
                    hi, qt, kc = steps[j]
                    if qt == 0 and kc == 0 and hi + 1 < len(heads):
                        load_head(hi + 1)
                    s = hi % 2
                    qi = (hi * nqt + qt)
                    ot_t, ot_b = OT[qi % 2]
                    pt_t, pt_b = PT[j % 4]
                    T('matmul', [VV[s][1], pt_b], [ot_b], out=ot_t[:], lhsT=VV[s][0][:, kc, :], rhs=pt_t[:],
                      start=(kc == 0), stop=(kc == 63))
                    if kc == 63:
                        h = heads[hi]
                        os_t, os_b = OTs[qi % 2]
                        V('tensor_copy', [ot_b], [os_b], out=os_t[:], in_=ot_t[:])

                        def fin(os_t=os_t, os_b=os_b, qi=qi, qt=qt, h=h):
                            for jj in range(4):
                                T('transpose', [os_b, b_idf], [b_TO], out=TO[:, jj, :],
                                  in_=os_t[:, jj * 128:(jj + 1) * 128], identity=idf[0:65, 0:65])
                            rc_t, rc_b = rc[qi % 2]
                            oc_t, oc_b = oc[qi % 2]
                            V('reciprocal', [b_TO], [rc_b], out=rc_t[:].unsqueeze(2), in_=TO[:, :, 64:65])
                            V('tensor_tensor', [b_TO, rc_b], [oc_b], out=oc_t[:], in0=TO[:, :, 0:64],
                              in1=bc(rc_t[:].unsqueeze(2), [128, 4, 64]), op=ALU.mult)
                            self.dma('sync',
                                     SC['MIX'].ap()[qt * 512:(qt + 1) * 512, 640 + h * 64:640 + (h + 1) * 64]
                                     .rearrange("(t p) d -> p t d", p=128),
                                     oc_t[:], [oc_b], [SC['bmix']], 'coc%d' % (qi % 2))
                        deferred.append((i + 6, fin))
                while deferred and deferred[0][0] <= i:
                    deferred.pop(0)[1]()
            for (_, fn) in deferred:
                fn()
        self.P.phase_barrier()

    def bias_setup(self):
        nc, P, D, SC = self.nc, self.P, self.D, self.SC
        V, G, A, T, E = self.V, self.G, self.A, self.T, self.E
        with ExitStack() as ph:
            tabN = ph.enter_context(nc.sbuf_tensor(self.name("tabN"), [33, 10], F32)); b_tab = Buf("tabN")
            oh = ph.enter_context(nc.sbuf_tensor(self.name("oh"), [33, 1660], F32)); b_oh = Buf("oh")
            fv = ph.enter_context(nc.sbuf_tensor(self.name("fv"), [10, 1660], F32)); b_fv = Buf("fv")
            pf = [(psum_view(ph, nc, self.name("pf"), [10, 512], F32), Buf("pf%d" % i)) for i in range(4)]
            P.op('vector', lambda e: e.memset(tabN[:], NEGB), [], [b_tab])
            self.dma('sync', tabN[0:32, :], D['rel_bias_table'].ap(), [], [b_tab], 'tabN')
            self.dma('sync', oh[:], D['oh'].ap(), [], [b_oh], 'oh')
            segs = [(0, 511), (511, 894), (894, 1277), (1277, 1660)]
            for i, (a, b) in enumerate(segs):
                T('matmul', [b_tab, b_oh], [pf[i][1]], out=pf[i][0][:, 0:b - a], lhsT=tabN[:], rhs=oh[:, a:b],
                  start=True, stop=True)
                V('tensor_copy', [pf[i][1]], [b_fv], out=fv[:, a:b], in_=pf[i][0][:, 0:b - a])
            self.dma('sync', SC['FV'].ap(), fv[:], [b_fv], [SC['bfv']], 'fvo')
        self.P.phase_barrier()

    def phaseB(self, L):
        nc, P, D, SC = self.nc, self.P, self.D, self.SC
        V, G, A, T, E = self.V, self.G, self.A, self.T, self.E
        with ExitStack() as ph:
            def sb(nm, shape, dt, n=1):
                r = []
                for i in range(n):
                    t = ph.enter_context(nc.sbuf_tensor(self.name(nm), shape, dt))
                    r.append((t, Buf(nm + str(i))))
                return r if n > 1 else r[0]

            def ps(nm, shape, dt, n=1):
                r = []
                for i in range(n):
                    t = psum_view(ph, nc, self.name(nm), shape, dt)
                    r.append((t, Buf(nm + str(i))))
                return r if n > 1 else r[0]

            P.barrier('sync', reads=[SC['b'], SC['bfv']])
            idf, b_idf = sb("idf", [128, 128], F32)
            idb, b_idb = sb("idb", [128, 128], BF16)
            self.dma('sync', idf[:], D['idf'].ap(), [], [b_idf], 'idf')
            self.dma('sync', idb[:], D['idb'].ap(), [], [b_idb], 'idb')
            biasB = sb("biasB", [128, 4, 3, 128], F32)
            hk = sb("hkB", [128, 128], F32, 4)
            for h in range(4):
                for o in range(3):
                    g_t, g_b = hk[(h * 3 + o) % 4]
                    self.dma('sync', g_t[:], bass.AP(SC['FV'], (6 + h) * 1660 + 128 * o, [[1, 128], [1, 128]]),
                             [SC['bfv']], [g_b], 'hkB%d' % ((h * 3 + o) % 4))
                    V('tensor_copy', [g_b], [biasB[1]], out=biasB[0][:, h, o, :],
                      in_=bass.AP(g_t, 127, [[128, 128], [-1, 128]]))
            sk, b_sk = sb("sink", [128, 4], F32)
            esk, b_esk = sb("esink", [128, 4], F32)
            self.dma('sync', sk[:], D['sink_b'].ap()[L].partition_broadcast(128), [], [b_sk], 'sink')
            A('activation', [b_sk], [b_esk], out=esk[:], in_=sk[:], func=AF.Exp)
            Qc = sb("bQc", [128, 4, 256], BF16, 2)
            Kc = sb("bKc", [128, 6, 128], BF16, 2)
            Vc = sb("bVc", [128, 6, 130], BF16, 2)
            QTb = sb("bQT", [128, 2, 4, 128], BF16, 2)
            KTb = sb("bKT", [128, 6, 128], BF16, 2)
            Sb = sb("bS", [128, 3, 128], F32, 2)
            PTb = sb("bPT", [128, 3, 128], BF16, 2)
            OTs = sb("bOTs", [65, 4, 128], F32, 2)
            den = sb("bden", [128, 4], F32, 2)
            ob = sb("bo", [128, 4, 64], BF16, 2)
            TRq = ps("bTRq", [128, 2, 4, 128], BF16)
            TRk = ps("bTRk", [128, 6, 128], BF16)
            SP = ps("bSP", [128, 3, 128], F32, 2)
            OTp = ps("bOT", [65, 4, 128], F32, 2)
            TOp = ps("bTO", [128, 4, 65], F32)

            def stage1(J):
                s = J % 2
                self.dma('sync', Qc[s][0][:], SC['QB'].ap()[J * 512:(J + 1) * 512, :].rearrange("(t p) c -> p t c", p=128),
                         [SC['b']], [Qc[s][1]], 'bQc%d' % s)
                r0 = (7 + 4 * J) * 128
                self.dma('sync', Kc[s][0][:], SC['KBx'].ap()[r0:r0 + 768, :].rearrange("(t p) c -> p t c", p=128),
                         [SC['b']], [Kc[s][1]], 'bKc%d' % s)
                self.dma('sync', Vc[s][0][:], SC['VBx'].ap()[r0:r0 + 768, :].rearrange("(t p) c -> p t c", p=128),
                         [SC['b']], [Vc[s][1]], 'bVc%d' % s)
                for t in range(4):
                    for pi in range(2):
                        T('transpose', [Qc[s][1], b_idb], [TRq[1]], out=TRq[0][:, pi, t, :],
                          in_=Qc[s][0][:, t, pi * 128:(pi + 1) * 128], identity=idb[:])
                V('tensor_copy', [TRq[1]], [QTb[s][1]], out=QTb[s][0][:], in_=TRq[0][:])
                for kt in range(6):
                    T('transpose', [Kc[s][1], b_idb], [TRk[1]], out=TRk[0][:, kt, :], in_=Kc[s][0][:, kt, :],
                      identity=idb[:])
                V('tensor_copy', [TRk[1]], [KTb[s][1]], out=KTb[s][0][:], in_=TRk[0][:])

            cnt = [0]

            def stage2(J):
                s = J % 2
                for t in range(4):
                    qi = J * 4 + t
                    ot_t, ot_b = OTp[qi % 2]
                    for h in range(4):
                        base = 64 * (h // 2)
                        pi = h % 2
                        kvh = h // 2
                        c = cnt[0]
                        cnt[0] += 1
                        sp_t, sp_b = SP[c % 2]
                        for o in range(3):
                            T('matmul', [KTb[s][1], QTb[s][1]], [sp_b], out=sp_t[:, o, :],
                              lhsT=KTb[s][0][base:base + 64, t + o, :], rhs=QTb[s][0][base:base + 64, pi, t, :],
                              start=True, stop=True)
                        s_t, s_b = Sb[c % 2]
                        p_t, p_b = PTb[c % 2]
                        V('tensor_tensor', [sp_b, biasB[1]], [s_b], out=s_t[:], in0=sp_t[:], in1=biasB[0][:, h, :, :],
                          op=ALU.add)
                        A('activation', [s_b], [p_b], out=p_t[:], in_=s_t[:], func=AF.Exp)
                        for o in range(3):
                            T('matmul', [Vc[s][1], p_b], [ot_b], out=ot_t[:, h, :],
                              lhsT=Vc[s][0][:, t + o, kvh * 65:(kvh + 1) * 65], rhs=p_t[:, o, :],
                              start=(o == 0), stop=(o == 2))
                    os_t, os_b = OTs[qi % 2]
                    V('tensor_copy', [ot_b], [os_b], out=os_t[:], in_=ot_t[:])
                    for h in range(4):
                        T('transpose', [os_b, b_idf], [TOp[1]], out=TOp[0][:, h, :], in_=os_t[:, h, :],
                          identity=idf[0:65, 0:65])
                    d_t, d_b = den[qi % 2]
                    o_t, o_b = ob[qi % 2]
                    V('tensor_tensor', [TOp[1], b_esk], [d_b], out=d_t[:].unsqueeze(2), in0=TOp[0][:, :, 64:65],
                      in1=esk[:].unsqueeze(2), op=ALU.add)
                    V('reciprocal', [d_b], [d_b], out=d_t[:], in_=d_t[:])
                    V('tensor_tensor', [TOp[1], d_b], [o_b], out=o_t[:], in0=TOp[0][:, :, 0:64],
                      in1=bc(d_t[:].unsqueeze(2), [128, 4, 64]), op=ALU.mult)
                    self.dma('sync', SC['MIX'].ap()[qi * 128:(qi + 1) * 128, 384:640], o_t[:], [o_b], [SC['bmix']],
                             'bo%d' % (qi % 2))

            stage1(0)
            for J in range(8):
                if J + 1 < 8:
                    stage1(J + 1)
                stage2(J)
        self.P.phase_barrier()

    def phaseA(self, cfgs=(0, 1, 2)):
        nc, P, D, SC = self.nc, self.P, self.D, self.SC
        V, G, A, T, E = self.V, self.G, self.A, self.T, self.E
        RS = (1, 4, 16)
        with ExitStack() as ph:
            def sb(nm, shape, dt, n=1):
                r = []
                for i in range(n):
                    t = ph.enter_context(nc.sbuf_tensor(self.name(nm), shape, dt))
                    r.append((t, Buf(nm + str(i))))
                return r if n > 1 else r[0]

            def ps(nm, shape, dt, n=1):
                r = []
                for i in range(n):
                    t = psum_view(ph, nc, self.name(nm), shape, dt)
                    r.append((t, Buf(nm + str(i))))
                return r if n > 1 else r[0]

            P.barrier('sync', reads=[SC['b'], SC['bfv']])
            idf, b_idf = sb("idf", [128, 128], F32)
            idb, b_idb = sb("idb", [128, 128], BF16)
            self.dma('sync', idf[:], D['idf'].ap(), [], [b_idf], 'idf')
            self.dma('sync', idb[:], D['idb'].ap(), [], [b_idb], 'idb')
            biasA = sb("biasA", [128, 3, 6, 2, 128], F32)
            hk = sb("hkA", [128, 128], F32, 4)
            for ci in range(3):
                for h in range(6):
                    for c in range(2):
                        kk = (ci * 6 + h) * 2 + c
                        g_t, g_b = hk[kk % 4]
                        self.dma('sync', g_t[:],
                                 bass.AP(SC['FV'], h * 1660 + 511 + 383 * ci + 128 * c, [[1, 128], [1, 128]]),
                                 [SC['bfv']], [g_b], 'hkA%d' % (kk % 4))
                        V('tensor_copy', [g_b], [biasA[1]], out=biasA[0][:, ci, h, c, :],
                          in_=bass.AP(g_t, 127, [[128, 128], [-1, 128]]))
            Qa = sb("aQ", [128, 384], BF16, 3)
            Ka = sb("aK", [128, 2, 384], BF16, 3)
            Va = sb("aV", [128, 2, 390], BF16, 3)
            QTa = sb("aQT", [128, 2, 3, 128], BF16, 2)
            for (t_, b_) in QTa:
                P.op('vector', lambda e, t_=t_: e.memset(t_[:], 0.0), [], [b_])
            KTa = sb("aKT", [128, 3, 2, 128], BF16, 2)
            Sa = sb("aS", [128, 6, 2, 128], F32, 2)
            PTa = sb("aPT", [128, 6, 2, 128], BF16, 2)
            OTs = sb("aOTs", [65, 6, 128], F32, 2)
            Oo = sb("aOo", [128, 6, 65], F32, 2)
            TRq = ps("aTRq", [128, 3, 128], BF16)
            TRk = ps("aTRk", [128, 3, 2, 128], BF16)
            SP = ps("aSP", [128, 2, 2, 128], F32, 3)
            OTp = ps("aOT", [65, 3, 128], F32, 2)
            TOp = ps("aTO", [128, 6, 65], F32)

            jobs = []
            for ci in cfgs:
                r = RS[ci]
                for rho in range(r):
                    for j in range(32 // r):
                        jobs.append((ci, r, rho, j))

            if self.debug and self.debug.get('a_jobs'):
                jobs = self.debug['a_jobs']

            def loadA(i):
                ci, r, rho, j = jobs[i]
                s3 = i % 3
                self.dma('sync', Qa[s3][0][:], bass.AP(SC['QA'], (r * 128 * j + rho) * 384, [[r * 384, 128], [1, 384]]),
                         [SC['b']], [Qa[s3][1]], 'aQ%d' % s3)
                e0 = r * (128 * j - 64) + rho + 1024
                self.dma('sync', Ka[s3][0][:],
                         bass.AP(SC['KAx'], e0 * 384, [[r * 384, 128], [r * 384 * 128, 2], [1, 384]]),
                         [SC['b']], [Ka[s3][1]], 'aK%d' % s3)
                self.dma('sync', Va[s3][0][:],
                         bass.AP(SC['VAx'], e0 * 390, [[r * 390, 128], [r * 390 * 128, 2], [1, 390]]),
                         [SC['b']], [Va[s3][1]], 'aV%d' % s3)

            def stage1(i):
                ci, r, rho, j = jobs[i]
                s3 = i % 3
                s2 = i % 2
                for pi in range(3):
                    T('transpose', [Qa[s3][1], b_idb], [TRq[1]], out=TRq[0][:, pi, :],
                      in_=Qa[s3][0][:, pi * 128:(pi + 1) * 128], identity=idb[:])
                V('tensor_copy', [TRq[1]], [QTa[s2][1]], out=QTa[s2][0][0:64, 0, :, :], in_=TRq[0][0:64, :, :])
                V('tensor_copy', [TRq[1]], [QTa[s2][1]], out=QTa[s2][0][64:128, 1, :, :], in_=TRq[0][64:128, :, :])
                for pi in range(3):
                    for c in range(2):
                        T('transpose', [Ka[s3][1], b_idb], [TRk[1]], out=TRk[0][:, pi, c, :],
                          in_=Ka[s3][0][:, c, pi * 128:(pi + 1) * 128], identity=idb[:])
                V('tensor_copy', [TRk[1]], [KTa[s2][1]], out=KTa[s2][0][:], in_=TRk[0][:])

            astop = self.debug.get('a_stop', 99) if self.debug else 99

            def stage2(i):
                ci, r, rho, j = jobs[i]
                s3 = i % 3
                s2 = i % 2
                s_t, s_b = Sa[s2]
                p_t, p_b = PTa[s2]
                if astop < 2:
                    return
                for pi in range(3):
                    sp_t, sp_b = SP[pi]
                    for hh in range(2):
                        for c in range(2):
                            T('matmul', [KTa[s2][1], QTa[s2][1]], [sp_b], out=sp_t[:, hh, c, :],
                              lhsT=KTa[s2][0][:, pi, c, :],
                              rhs=QTa[s2][0][:, hh, pi, :], start=True, stop=True)
                    if self.debug and self.debug.get('a_nobias'):
                        continue
                    V('tensor_tensor', [sp_b, biasA[1]], [s_b], out=s_t[:, 2 * pi:2 * pi + 2, :, :], in0=sp_t[:],
                      in1=biasA[0][:, ci, 2 * pi:2 * pi + 2, :, :], op=ALU.add)
                if astop < 3:
                    return
                A('activation', [s_b], [p_b], out=p_t[:], in_=s_t[:], func=AF.Exp)
                os_t, os_b = OTs[s2]
                if astop < 4:
                    return
                for g3 in range(2):
                    ot_t, ot_b = OTp[g3]
                    for hh in range(3):
                        h = g3 * 3 + hh
                        for c in range(2):
                            T('matmul', [Va[s3][1], p_b], [ot_b], out=ot_t[:, hh, :],
                              lhsT=Va[s3][0][:, c, h * 65:(h + 1) * 65], rhs=p_t[:, h, c, :],
                              start=(c == 0), stop=(c == 1))
                    V('tensor_copy', [ot_b], [os_b], out=os_t[:, g3 * 3:g3 * 3 + 3, :], in_=ot_t[:])
                if astop < 5:
                    return
                for h in range(6):
                    T('transpose', [os_b, b_idf], [TOp[1]], out=TOp[0][:, h, :], in_=os_t[:, h, :],
                      identity=idf[0:65, 0:65])
                o_t, o_b = Oo[s2]
                V('tensor_copy', [TOp[1]], [o_b], out=o_t[:], in_=TOp[0][:])
                self.dma('sync', bass.AP(SC['OA'], ci * 4096 * 390 + (r * 128 * j + rho) * 390, [[r * 390, 128], [1, 390]]),
                         o_t[:].rearrange("p h d -> p (h d)"), [o_b], [SC['boa']], 'aOo%d' % s2)

            n = len(jobs)
            for i in range(min(2, n)):
                loadA(i)
            stage1(0)
            for i in range(n):
                if i + 2 < n:
                    loadA(i + 2)
                if i + 1 < n:
                    stage1(i + 1)
                stage2(i)
        self.P.phase_barrier()

    def phaseA2(self):
        nc, P, D, SC = self.nc, self.P, self.D, self.SC
        V, G, A, T, E = self.V, self.G, self.A, self.T, self.E
        with ExitStack() as ph:
            def sb(nm, shape, dt, n=1):
                r = []
                for i in range(n):
                    t = ph.enter_context(nc.sbuf_tensor(self.name(nm), shape, dt))
                    r.append((t, Buf(nm + str(i))))
                return r if n > 1 else r[0]
            P.barrier('sync', reads=[SC['boa']])
            O3 = sb("a2O", [128, 3, 6, 65], F32, 3)
            acc = sb("a2acc", [128, 6, 65], F32, 2)
            rc = sb("a2rc", [128, 6], F32, 2)
            oo = sb("a2o", [128, 6, 64], BF16, 2)
            for t in range(32):
                s3, s2 = t % 3, t % 2
                self.dma('sync', O3[s3][0][:].rearrange("p c h d -> p c (h d)"),
                         SC['OA'].ap()[:, t * 128:(t + 1) * 128, :].rearrange("c p d -> p c d"),
                         [SC['boa']], [O3[s3][1]], 'a2O%d' % s3)
                o3 = O3[s3][0]
                G('tensor_tensor', [O3[s3][1]], [acc[s2][1]], out=acc[s2][0][:], in0=o3[:, 0], in1=o3[:, 1], op=ALU.add)
                G('tensor_tensor', [O3[s3][1], acc[s2][1]], [acc[s2][1]], out=acc[s2][0][:], in0=acc[s2][0][:],
                  in1=o3[:, 2], op=ALU.add)
                V('reciprocal', [acc[s2][1]], [rc[s2][1]], out=rc[s2][0][:].unsqueeze(2), in_=acc[s2][0][:, :, 64:65])
                V('tensor_tensor', [acc[s2][1], rc[s2][1]], [oo[s2][1]], out=oo[s2][0][:], in0=acc[s2][0][:, :, 0:64],
                  in1=bc(rc[s2][0][:].unsqueeze(2), [128, 6, 64]), op=ALU.mult)
                self.dma('sync', SC['MIX'].ap()[t * 128:(t + 1) * 128, 0:384], oo[s2][0][:].rearrange("p h d -> p (h d)"),
                         [oo[s2][1]], [SC['bmix']], 'a2o%d' % s2)
        self.P.phase_barrier()

    def phase4a(self, L, x_own, x1_d):
        nc, P, D, SC = self.nc, self.P, self.D, self.SC
        V, G, A, T, E = self.V, self.G, self.A, self.T, self.E
        with ExitStack() as ph:
            def sb(nm, shape, dt, n=1):
                r = []
                for i in range(n):
                    t = ph.enter_context(nc.sbuf_tensor(self.name(nm), shape, dt))
                    r.append((t, Buf(nm + str(i))))
                return r if n > 1 else r[0]

            def ps(nm, shape, dt, n=1):
                r = []
                for i in range(n):
                    t = psum_view(ph, nc, self.name(nm), shape, dt)
                    r.append((t, Buf(nm + str(i))))
                return r if n > 1 else r[0]

            P.barrier('sync', reads=[SC['bmix']])
            idb, b_idb = sb("idb", [128, 128], BF16)
            self.dma('sync', idb[:], D['idb'].ap(), [], [b_idb], 'idb')
            wo, b_wo = sb("wo", [128, 8, 1024], BF16)
            go, b_go = sb("go", [128, 8], F32)
            self.dma('sync', go[:], D['out_norm'].ap()[L].rearrange("(c p) -> p c", p=128), [], [b_go], 'go',
                     allow_slow_non_contiguous=True)
            stg = sb("stg4a", [128, 1024], F32, 2)
            for c in range(8):
                self.dma('sync', stg[c % 2][0][:], D['w_out'].ap()[L, c * 128:(c + 1) * 128, :], [], [stg[c % 2][1]],
                         'stg4a%d' % (c % 2))
                E('vector' if c % 2 == 0 else 'gpsimd', 'tensor_scalar', [stg[c % 2][1], b_go], [b_wo], out=wo[:, c, :],
                  in0=stg[c % 2][0][:], scalar1=go[:, c:c + 1], scalar2=None, op0=ALU.mult)
            invd3, b_invd3 = sb("invd3", [128, 3], F32)
            nh3, b_nh3 = sb("nh3", [128, 3], F32)
            P.op('gpsimd', lambda e: e.memset(invd3[:], 1.0 / 384), [], [b_invd3])
            P.op('gpsimd', lambda e: e.memset(invd3[:, 1:2], 1.0 / 256), [], [b_invd3])
            P.op('gpsimd', lambda e: e.memset(nh3[:], -0.5), [], [b_nh3])
            mx = sb("mx", [128, 1024], BF16, 3)
            xt = sb("x4a", [128, 1024], F32, 3)
            sq, b_sq = sb("sq4a", [128, 1024], F32)
            ss = sb("ss4a", [128, 3], F32, 2)
            rs = sb("rs4a", [128, 3], F32, 2)
            mn = sb("mn", [128, 1024], BF16, 2)
            mT = sb("mT", [128, 8, 128], BF16, 2)
            TR = ps("TR4a", [128, 8, 128], BF16, 2)
            Y = ps("Y4a", [128, 1024], F32, 2)
            grp = [(0, 384), (384, 640), (640, 1024)]
            def load4a(t):
                s3 = t % 3
                rows = slice(t * 128, (t + 1) * 128)
                self.dma('sync', mx[s3][0][:], SC['MIX'].ap()[rows, :], [SC['bmix']], [mx[s3][1]], 'mx%d' % s3)
                self.dma('sync', xt[s3][0][:], x_own[rows, :], [], [xt[s3][1]], 'x4a%d' % s3)
            load4a(0)
            load4a(1)
            for t in range(32):
                s3, s2 = t % 3, t % 2
                rows = slice(t * 128, (t + 1) * 128)
                if t + 2 < 32:
                    load4a(t + 2)
                m_t, m_b = mx[s3]
                V('tensor_tensor', [m_b], [b_sq], out=sq[:], in0=m_t[:], in1=m_t[:], op=ALU.mult)
                for gi, (a, b) in enumerate(grp):
                    V('tensor_reduce', [b_sq], [ss[s2][1]], out=ss[s2][0][:, gi:gi + 1], in_=sq[:, a:b], axis=AX.X,
                      op=ALU.add)
                G('tensor_tensor', [ss[s2][1], b_invd3], [rs[s2][1]], out=rs[s2][0][:], in0=ss[s2][0][:], in1=invd3[:],
                  op=ALU.mult)
                G('tensor_scalar', [rs[s2][1]], [rs[s2][1]], out=rs[s2][0][:], in0=rs[s2][0][:], scalar1=EPS, scalar2=None,
                  op0=ALU.add)
                G('tensor_tensor', [rs[s2][1], b_nh3], [rs[s2][1]], out=rs[s2][0][:], in0=rs[s2][0][:], in1=nh3[:],
                  op=ALU.pow)
                for gi, (a, b) in enumerate(grp):
                    E('vector' if gi != 1 else 'gpsimd', 'tensor_scalar', [m_b, rs[s2][1]], [mn[s2][1]],
                      out=mn[s2][0][:, a:b], in0=m_t[:, a:b], scalar1=rs[s2][0][:, gi:gi + 1], scalar2=None, op0=ALU.mult)
                for c in range(8):
                    T('transpose', [mn[s2][1], b_idb], [TR[s2][1]], out=TR[s2][0][:, c, :],
                      in_=mn[s2][0][:, c * 128:(c + 1) * 128], identity=idb[:])
                A('activation', [TR[s2][1]], [mT[s2][1]], out=mT[s2][0][:], in_=TR[s2][0][:], func=AF.Copy)
                for hf in range(2):
                    for c in range(8):
                        T('matmul', [mT[s2][1], b_wo], [Y[s2][1]], out=Y[s2][0][:, hf * 512:(hf + 1) * 512],
                          lhsT=mT[s2][0][:, c, :], rhs=wo[:, c, hf * 512:(hf + 1) * 512], start=(c == 0), stop=(c == 7))
                V('tensor_tensor', [Y[s2][1], xt[s3][1]], [xt[s3][1]], out=xt[s3][0][:], in0=Y[s2][0][:],
                  in1=xt[s3][0][:], op=ALU.add)
                self.dma('sync', x1_d[rows, :], xt[s3][0][:], [xt[s3][1]], [SC['bx1']], 'x4ao%d' % s3)
        self.P.phase_barrier()

    def phase4b(self, L, x1_d, out_d, b_out):
        nc, P, D, SC = self.nc, self.P, self.D, self.SC
        V, G, A, T, E = self.V, self.G, self.A, self.T, self.E
        with ExitStack() as ph:
            def sb(nm, shape, dt, n=1):
                r = []
                for i in range(n):
                    t = ph.enter_context(nc.sbuf_tensor(self.name(nm), shape, dt))
                    r.append((t, Buf(nm + str(i))))
                return r if n > 1 else r[0]

            def ps(nm, shape, dt, n=1):
                r = []
                for i in range(n):
                    t = psum_view(ph, nc, self.name(nm), shape, dt)
                    r.append((t, Buf(nm + str(i))))
                return r if n > 1 else r[0]

            P.barrier('sync', reads=[SC['bx1']])
            idb, b_idb = sb("idb", [128, 128], BF16)
            self.dma('sync', idb[:], D['idb'].ap(), [], [b_idb], 'idb')
            wu, b_wu = sb("wu", [128, 8, 4096], BF16)
            wd, b_wd = sb("wd", [128, 32, 1024], BF16)
            gm, b_gm = sb("gm", [128, 8], F32)
            self.dma('sync', gm[:], D['norm_mlp'].ap()[L].rearrange("(c p) -> p c", p=128), [], [b_gm], 'gm',
                     allow_slow_non_contiguous=True)
            stg = sb("stg4b", [128, 1024], F32, 2)
            k = 0
            for c in range(8):
                for hf in range(4):
                    s = k % 2
                    self.dma('sync', stg[s][0][:], D['w_up'].ap()[L, c * 128:(c + 1) * 128, hf * 1024:(hf + 1) * 1024], [],
                             [stg[s][1]], 'stg4b%d' % s)
                    E('vector' if k % 2 == 0 else 'gpsimd', 'tensor_scalar', [stg[s][1], b_gm], [b_wu],
                      out=wu[:, c, hf * 1024:(hf + 1) * 1024], in0=stg[s][0][:], scalar1=gm[:, c:c + 1], scalar2=None,
                      op0=ALU.mult)
                    k += 1
            for c2 in range(32):
                s = k % 2
                self.dma('sync', stg[s][0][:], D['w_down'].ap()[L, c2 * 128:(c2 + 1) * 128, :], [],
                         [stg[s][1]], 'stg4b%d' % s)
                E('vector' if k % 2 == 0 else 'gpsimd', 'tensor_copy', [stg[s][1]], [b_wd],
                  out=wd[:, c2, :], in_=stg[s][0][:])
                k += 1
            nh1, b_nh1 = sb("nh1", [128, 1], F32)
            P.op('gpsimd', lambda e: e.memset(nh1[:], -0.5), [], [b_nh1])
            xc = sb("x4b", [128, 2, 1024], F32, 2)
            junk, b_junk = sb("junk4b", [128, 1024], BF16)
            ss = sb("ss4b", [128, 2], F32, 2)
            h2 = [sb("h2", [128, 2, 1024], BF16)] * 2
            h2T = [sb("h2T", [128, 8, 256], BF16)] * 2
            uT = [sb("uT", [128, 32, 256], BF16)] * 2
            rr = sb("rr", [128, 2, 256], F32, 3)
            TR = ps("TR4b", [128, 8, 128], BF16)
            U = ps("U4b", [128, 2, 256], F32, 2)
            Z = ps("Z4b", [128, 1024], F32, 2)
            def load4b(ch):
                s2 = ch % 2
                rows = slice(ch * 256, (ch + 1) * 256)
                self.dma('sync', xc[s2][0][:], x1_d[rows, :].rearrange("(t p) d -> p t d", p=128), [SC['bx1']],
                         [xc[s2][1]], 'x4b%d' % s2)
            load4b(0)
            for ch in range(16):
                s2 = ch % 2
                rows = slice(ch * 256, (ch + 1) * 256)
                x_t, x_b = xc[s2]
                if ch + 1 < 16:
                    load4b(ch + 1)
                for t in range(2):
                    A('activation', [x_b], [b_junk, ss[s2][1]], out=junk[:], in_=x_t[:, t, :], func=AF.Square,
                      accum_out=ss[s2][0][:, t:t + 1])
                G('tensor_scalar', [ss[s2][1]], [ss[s2][1]], out=ss[s2][0][:], in0=ss[s2][0][:], scalar1=1.0 / 1024,
                  scalar2=EPS, op0=ALU.mult, op1=ALU.add)
                G('tensor_tensor', [ss[s2][1], b_nh1], [ss[s2][1]], out=ss[s2][0][:], in0=ss[s2][0][:],
                  in1=bc(nh1[:], [128, 2]), op=ALU.pow)
                for t in range(2):
                    A('activation', [x_b, ss[s2][1]], [h2[s2][1]], out=h2[s2][0][:, t, :], in_=x_t[:, t, :], func=AF.Copy,
                      scale=ss[s2][0][:, t:t + 1])
                    for c in range(8):
                        T('transpose', [h2[s2][1], b_idb], [TR[1]], out=TR[0][:, c, :],
                          in_=h2[s2][0][:, t, c * 128:(c + 1) * 128], identity=idb[:])
                    V('tensor_copy', [TR[1]], [h2T[s2][1]], out=h2T[s2][0][:, :, t * 128:(t + 1) * 128], in_=TR[0][:])
                for f2 in range(16):
                    u_t, u_b = U[f2 % 2]
                    for ff in range(2):
                        fc = f2 * 2 + ff
                        for c in range(8):
                            T('matmul', [h2T[s2][1], b_wu], [u_b], out=u_t[:, ff, :], lhsT=wu[:, c, fc * 128:(fc + 1) * 128],
                              rhs=h2T[s2][0][:, c, :], start=(c == 0), stop=(c == 7))
                    r_t, r_b = rr[f2 % 3]
                    A('activation', [u_b], [r_b], out=r_t[:], in_=u_t[:], func=AF.Relu)
                    E('vector' if f2 % 2 == 0 else 'gpsimd', 'tensor_tensor', [r_b], [uT[s2][1]],
                      out=uT[s2][0][:, 2 * f2:2 * f2 + 2, :], in0=r_t[:], in1=r_t[:], op=ALU.mult)
                for t in range(2):
                    z_t, z_b = Z[t]
                    for hf in range(2):
                        for fc in range(32):
                            T('matmul', [uT[s2][1], b_wd], [z_b], out=z_t[:, hf * 512:(hf + 1) * 512],
                              lhsT=uT[s2][0][:, fc, t * 128:(t + 1) * 128], rhs=wd[:, fc, hf * 512:(hf + 1) * 512],
                              start=(fc == 0), stop=(fc == 31))
                    V('tensor_tensor', [z_b, x_b], [x_b], out=x_t[:, t, :], in0=z_t[:], in1=x_t[:, t, :], op=ALU.add)
                self.dma('sync', out_d[rows, :].rearrange("(t p) d -> p t d", p=128), x_t[:], [x_b], [b_out],
                         'x4bo%d' % s2)
        self.P.phase_barrier()

    def layer(self, L, x_own, x_oth, x_halo, x1_d, out_d, b_out, with_bias_setup=True, phases=None, pos_key='pos',
              valid_key='valid', halo_rows=None, reuse_ckv=False):
        ph = phases or ['bias', 'p1', 'C', 'B', 'A', 'A2', '4a', '4b']
        if with_bias_setup and 'bias' in ph:
            self.bias_setup()
        if 'p1' in ph:
            self.phase1(L, x_own, x_oth, x_halo, pos_key=pos_key, valid_key=valid_key, halo_rows=halo_rows,
                        skip_oth=reuse_ckv, skip_ck=reuse_ckv)
        if 'C' in ph:
            self.phaseC()
        if 'B' in ph:
            self.phaseB(L)
        if 'A' in ph:
            self.phaseA(cfgs=self.debug.get('cfgs', (0, 1, 2)) if self.debug else (0, 1, 2))
        if 'A2' in ph:
            self.phaseA2()
        if '4a' in ph:
            self.phase4a(L, x_own, x1_d)
        if '4b' in ph:
            self.phase4b(L, x1_d, out_d, b_out)


NLAYER = 2


def t5_bucket_np(rel):
    half, exact = 16, 8
    n = np.abs(rel)
    far = exact + (np.log(np.maximum(n, 1).astype(np.float32) / np.float32(exact))
                   / np.float32(math.log(1024 / exact)) * np.float32(half - exact)).astype(np.int32)
    far = np.minimum(far, half - 1)
    return np.where(rel > 0, half, 0) + np.where(n < exact, n, far)


def make_oh():
    oh = np.zeros((33, 1660), np.float32)
    d = np.arange(-255, 256)
    bk = t5_bucket_np(d)
    for i, dd in enumerate(d):
        if abs(dd) <= 128:
            oh[bk[i], i] = 1
        else:
            oh[32, i] = 1
    for ci, r in enumerate((1, 4, 16)):
        d = np.arange(-191, 192)
        bk = t5_bucket_np(d * r)
        for i, dd in enumerate(d):
            if abs(dd) <= 64:
                oh[bk[i], 511 + 383 * ci + i] = 1
            else:
                oh[32, 511 + 383 * ci + i] = 1
    return oh


WNAMES = ['norm_mix', 'w_in', 'qk_gain_a', 'qk_gain_b', 'sink_b', 'q_lat_gain', 'kv_lat_gain', 'w_uq', 'w_ukv',
          'qk_gain_c', 'out_norm', 'w_out', 'norm_mlp', 'w_up', 'w_down']


def declare(nc, NL):
    D = {}

    def inp(n, shape, dt=F32):
        D[n] = nc.dram_tensor(n, shape, dt, kind="ExternalInput")
    inp('x_own', [4096, 1024]); inp('x_oth', [4096, 1024]); inp('x_halo', [2048, 1024]); inp('x_halo2', [2048, 1024])
    inp('valid', [128, 16]); inp('valid2', [128, 16]); inp('pos', [128, 64], I32); inp('pos2', [128, 64], I32)
    inp('invf', [1, 16]); inp('idb', [128, 128], BF16); inp('idf', [128, 128])
    inp('norm_mix', [NL, 1024]); inp('w_in', [NL, 1024, 2080]); inp('qk_gain_a', [NL, 2, 64])
    inp('qk_gain_b', [NL, 2, 64]); inp('sink_b', [NL, 4]); inp('q_lat_gain', [NL, 256]); inp('kv_lat_gain', [NL, 128])
    inp('w_uq', [NL, 256, 576]); inp('w_ukv', [NL, 128, 768]); inp('qk_gain_c', [NL, 2, 96]); inp('out_norm', [NL, 1024])
    inp('w_out', [NL, 1024, 1024]); inp('norm_mlp', [NL, 1024]); inp('w_up', [NL, 1024, 4096]); inp('w_down', [NL, 4096, 1024])
    inp('rel_bias_table', [32, 10]); inp('oh', [33, 1660])
    SC = {}

    def scr(n, shape, dt=BF16):
        SC[n] = nc.dram_tensor(n, shape, dt, kind="Internal")
    scr('QA', [4096, 384]); scr('KAx', [6144, 384]); scr('VAx', [6144, 390]); scr('QB', [4096, 256])
    scr('KBx', [6144, 128]); scr('VBx', [6144, 130]); scr('QCT', [6, 96, 4096]); scr('KCT', [6, 96, 8192])
    scr('VC', [8192, 390]); scr('MIX', [4096, 1024]); scr('OA', [3, 4096, 390], F32); scr('FV', [10, 1660], F32)
    scr('X1', [4096, 1024], F32); scr('XA', [4096, 1024], F32); scr('XB', [4096, 1024], F32)
    SC['b'] = Buf('scratch', multi=True)
    SC['bmix'] = Buf('mix', multi=True)
    SC['boa'] = Buf('oa', multi=True)
    SC['bfv'] = Buf('fv', multi=True)
    SC['bx1'] = Buf('x1', multi=True)
    return D, SC


def make_inputs(inputs, core):
    b, half = core // 2, core % 2
    xs = np.asarray(inputs['x'], dtype=np.float32)[b]
    own = xs[half * 4096:(half + 1) * 4096]
    oth = xs[(1 - half) * 4096:(2 - half) * 4096]
    halo = np.zeros((2048, 1024), np.float32)
    valid = np.zeros((2048,), np.float32)
    halo2 = np.zeros((2048, 1024), np.float32)
    valid2 = np.zeros((2048,), np.float32)
    if half == 1:
        halo[0:1024] = oth[3072:4096]
        valid[0:1024] = 1
        halo2[1024:2048] = own[0:1024]
        valid2[1024:2048] = 1
    else:
        halo[1024:2048] = oth[0:1024]
        valid[1024:2048] = 1
        halo2[0:1024] = own[3072:4096]
        valid2[0:1024] = 1
    pos = np.asarray(inputs['positions'][b])
    p_own = pos[half * 4096:(half + 1) * 4096]
    p_oth = pos[(1 - half) * 4096:(2 - half) * 4096]
    pos_l = np.concatenate([p_own, p_oth])
    pos_l2 = np.concatenate([p_oth, p_own])
    m = {
        'x_own': np.ascontiguousarray(own), 'x_oth': np.ascontiguousarray(oth), 'x_halo': halo, 'x_halo2': halo2,
        'valid': np.ascontiguousarray(valid.reshape(16, 128).T),
        'valid2': np.ascontiguousarray(valid2.reshape(16, 128).T),
        'pos': np.ascontiguousarray(pos_l.reshape(64, 128).T.astype(np.int32)),
        'pos2': np.ascontiguousarray(pos_l2.reshape(64, 128).T.astype(np.int32)),
        'invf': (10000.0 ** (-np.arange(16, dtype=np.float32) / 16)).astype(np.float32).reshape(1, 16),
        'idb': np.eye(128).astype(ml_dtypes.bfloat16), 'idf': np.eye(128).astype(np.float32),
        'rel_bias_table': np.ascontiguousarray(inputs['rel_bias_table'], dtype=np.float32), 'oh': make_oh(),
    }
    for k in WNAMES:
        m[k] = np.ascontiguousarray(np.asarray(inputs[k], dtype=np.float32))
    return m


def build_program():
    nc = bass.Bass("TRN2", target_bir_lowering=False)
    D, SC = declare(nc, NLAYER)
    out_d = nc.dram_tensor('out', [4096, 1024], F32, kind="ExternalOutput")
    b_xa = Buf('xa', multi=True)
    b_xb = Buf('xb', multi=True)
    b_out = Buf('out', multi=True)
    with ExitStack() as es:
        P = Prog(nc, es)
        LB = LayerBuilder(nc, P, D, debug={})
        LB.SC = SC
        XA, XB = SC['XA'].ap(), SC['XB'].ap()
        LB.layer(0, D['x_own'].ap(), D['x_oth'].ap(), D['x_halo'].ap(), SC['X1'].ap(), XA, b_xa)
        LB.layer(0, D['x_oth'].ap(), D['x_own'].ap(), D['x_halo2'].ap(), SC['X1'].ap(), XB, b_xb,
                 with_bias_setup=False, pos_key='pos2', valid_key='valid2', reuse_ckv=True)

        def halo_rows(t):
            r0 = 3072 + t * 128 if t < 8 else (t - 8) * 128
            return XB[r0:r0 + 128, :]
        P.barrier('sync', reads=[b_xa, b_xb])
        LB.layer(1, XA, XB, None, SC['X1'].ap(), out_d.ap(), b_out, with_bias_setup=False, halo_rows=halo_rows)
        P.barrier('sync', reads=[b_out])
        P.emit()
    return nc


def kernel(**inputs):
    nc = build_program()
    in_maps = [make_inputs(inputs, c) for c in range(8)]
    res = run_bass_kernel_spmd(nc, in_maps, core_ids=list(range(8)))
    x = np.asarray(inputs['x'])
    out = np.empty(x.shape, np.float32)
    for c in range(8):
        b, half = c // 2, c % 2
        out[b, half * 4096:(half + 1) * 4096] = np.asarray(res.results[c]['out'], dtype=np.float32)
    return out
```

```python
import math
import ml_dtypes
from concourse.bass_utils import run_bass_kernel_spmd
import numpy as np
import concourse.bass as bass
import concourse.mybir as mybir
from contextlib import ExitStack

F32 = mybir.dt.float32
BF16 = mybir.dt.bfloat16
I32 = mybir.dt.int32
ALU = mybir.AluOpType
AF = mybir.ActivationFunctionType
AX = mybir.AxisListType

ENGS = ['sync', 'scalar', 'vector', 'gpsimd', 'tensor']
SEM_ROT = 24000


class Buf:
    __slots__ = ('name', 'writer', 'readers', 'dreaders', 'multi', 'mw')

    def __init__(self, name, multi=False):
        self.name = name
        self.writer = None
        self.readers = {}
        self.dreaders = []
        self.multi = multi
        self.mw = []


class Op:
    __slots__ = ('eng', 'fn', 'deps', 'idx', 'signal', 'is_dma', 'lane', 'ev', 'raw', 'barrier')


class Prog:
    def __init__(self, nc, es):
        self.nc = nc
        self.es = es
        self.ops = {e: [] for e in ENGS}
        self.order = []
        self.lanes = {}
        self.nsem = 0
        self.fence = []
        self.fence_pending = set()
        self.phase_lanes = {}

    def phase_barrier(self):
        fence = []
        for e in ENGS:
            for o in reversed(self.ops[e]):
                if not o.is_dma and not o.barrier:
                    fence.append(o)
                    break
        last = {}
        for o in self.order:
            if o.is_dma:
                last[o.lane] = o
        fence += list(last.values())
        self.fence = fence
        self.fence_pending = set(ENGS)
        self.phase_lanes = {}

    def new_sem(self, name):
        self.nsem += 1
        return self.es.enter_context(self.nc.semaphore(name))

    def sb(self, name, shape, dt):
        return self.es.enter_context(self.nc.sbuf_tensor(name, shape, dt))

    def ps(self, name, shape, dt):
        return self.es.enter_context(self.nc.psum_tensor(name, shape, dt))

    def barrier(self, eng, reads=(), writes=()):
        o = self.op(eng, lambda e: None, reads, writes)
        o.barrier = True
        return o

    def op(self, eng, fn, reads=(), writes=(), lane=None):
        o = Op()
        o.eng = eng
        o.fn = fn
        o.barrier = False
        o.is_dma = lane is not None
        if lane is not None:
            if lane not in self.phase_lanes:
                self.phase_lanes[lane] = 'L%d' % len(self.phase_lanes)
            lane = self.phase_lanes[lane]
        o.lane = lane
        o.signal = False
        o.ev = None
        deps = {}
        raw = set()
        for b in reads:
            if b.writer is not None:
                deps[id(b.writer)] = b.writer
                raw.add(id(b.writer))
            for w in b.mw:
                deps[id(w)] = w
                raw.add(id(w))
        for b in writes:
            if b.writer is not None and not b.multi:
                deps[id(b.writer)] = b.writer
                raw.add(id(b.writer))
            for r in b.readers.values():
                deps[id(r)] = r
            for r in b.dreaders:
                deps[id(r)] = r
        if eng in self.fence_pending:
            self.fence_pending.discard(eng)
            for w in self.fence:
                deps[id(w)] = w
        deps.pop(id(o), None)
        o.deps = list(deps.values())
        o.raw = raw
        for b in writes:
            if b.multi:
                b.mw.append(o)
            else:
                b.writer = o
            b.readers = {}
            b.dreaders = []
        for b in reads:
            if b.multi:
                continue
            if o.is_dma:
                b.dreaders.append(o)
            else:
                b.readers[eng] = o
        o.idx = len(self.ops[eng])
        self.ops[eng].append(o)
        self.order.append(o)
        return o

    def dma(self, q, out, in_, reads=(), writes=(), lane=None, **kw):
        assert lane is not None
        return self.op(q, lambda e: e.dma_start(out=out, in_=in_, **kw), reads, writes, lane=lane)

    def emit(self):
        nc = self.nc
        for o in self.order:
            for d in o.deps:
                if d.is_dma:
                    continue
                if d.barrier:
                    assert d.eng == o.eng, 'barrier dep across engines'
                    continue
                if d.eng == o.eng and not o.is_dma:
                    if o.eng == 'tensor':
                        continue
                    if id(d) not in o.raw:
                        continue
                d.signal = True
        esems = {}
        for e in ENGS:
            cnt = 0
            cur = None
            for o in self.ops[e]:
                if o.is_dma:
                    ln = self.lanes.get(o.lane)
                    if ln is None:
                        ln = [self.new_sem('l_%s' % o.lane), 0]
                        self.lanes[o.lane] = ln
                    ln[1] += 16
                    o.ev = (ln[0], ln[1])
                elif o.signal:
                    if cur is None or cnt >= SEM_ROT:
                        cur = self.new_sem('e_%s_%d' % (e, len(esems)))
                        esems[(e, len(esems))] = cur
                        cnt = 0
                    cnt += 1
                    o.ev = (cur, cnt)
        blk = self.es.enter_context(nc.Block())
        prog = self

        def run(e, eng):
            waited = {}
            for o in prog.ops[e]:
                need = {}
                for d in o.deps:
                    if not d.is_dma:
                        if d.barrier:
                            continue
                        if d.eng == o.eng and not o.is_dma:
                            if o.eng == 'tensor' or id(d) not in o.raw:
                                continue
                    sem, val = d.ev
                    k = id(sem)
                    if k not in need or need[k][1] < val:
                        need[k] = (sem, val)
                for k, (sem, val) in need.items():
                    if waited.get(k, 0) >= val:
                        continue
                    eng.wait_ge(sem, val)
                    waited[k] = val
                ins = o.fn(eng)
                if ins is None:
                    continue
                if o.is_dma:
                    ins.then_inc(o.ev[0], 16)
                elif o.signal:
                    ins.then_inc(o.ev[0], 1)

        @blk.sync
        def _(eng):
            run('sync', eng)

        @blk.scalar
        def _(eng):
            run('scalar', eng)

        @blk.vector
        def _(eng):
            run('vector', eng)

        @blk.gpsimd
        def _(eng):
            run('gpsimd', eng)

        @blk.tensor
        def _(eng):
            run('tensor', eng)

import numpy as np
import math

EPS = 1e-6
NEGB = -30000.0
TWO_PI_S = 6.2831845


def bc(ap, shape):
    return ap.to_broadcast(list(shape))


def psum_view(ph, nc, name, shape, dt):
    esz = 4 if dt == F32 else 2
    n = 1
    for d in shape[1:]:
        n *= d
    per_bank = 2048 // esz
    tot = ((n + per_bank - 1) // per_bank) * per_bank
    t = ph.enter_context(nc.psum_tensor(name, [128, tot], dt))
    v = t[0:shape[0], 0:n]
    if len(shape) == 3:
        v = v.rearrange("p (a b) -> p a b", a=shape[1])
    elif len(shape) == 4:
        v = v.rearrange("p (a b c) -> p a b c", a=shape[1], b=shape[2])
    return v


class LayerBuilder:
    def __init__(self, nc, P, D, debug=False):
        self.nc = nc
        self.P = P
        self.D = D
        self.debug = debug
        self.uid = 0

    def name(self, s):
        self.uid += 1
        return "%s_%d" % (s, self.uid)

    def V(self, fn, reads, writes, **kw):
        return self.P.op('vector', lambda e: getattr(e, fn)(**kw), reads, writes)

    def G(self, fn, reads, writes, **kw):
        return self.P.op('gpsimd', lambda e: getattr(e, fn)(**kw), reads, writes)

    def A(self, fn, reads, writes, **kw):
        return self.P.op('scalar', lambda e: getattr(e, fn)(**kw), reads, writes)

    def T(self, fn, reads, writes, **kw):
        return self.P.op('tensor', lambda e: getattr(e, fn)(**kw), reads, writes)

    def E(self, eng, fn, reads, writes, **kw):
        return self.P.op(eng, lambda e: getattr(e, fn)(**kw), reads, writes)

    def dma(self, q, out, in_, reads, writes, lane, **kw):
        return self.P.dma(q, out, in_, reads=reads, writes=writes, lane=lane, **kw)

    def phase1(self, L, x_own, x_oth, x_halo, first=True, pos_key='pos', valid_key='valid', halo_rows=None,
               skip_oth=False, skip_ck=False):
        nc, P, D = self.nc, self.P, self.D
        V, G, A, T, E = self.V, self.G, self.A, self.T, self.E
        NS = 4
        with ExitStack() as ph:
            def sb(nm, shape, dt, n=1):
                r = []
                for i in range(n):
                    t = ph.enter_context(nc.sbuf_tensor(self.name(nm), shape, dt))
                    r.append((t, Buf(nm + str(i))))
                return r if n > 1 else r[0]

            def ps(nm, shape, dt):
                t = psum_view(ph, nc, self.name(nm), shape, dt)
                return (t, Buf(nm))

            setup = ExitStack()

            def sbs(nm, shape, dt):
                t = setup.enter_context(nc.sbuf_tensor(self.name(nm), shape, dt))
                return (t, Buf(nm))

            idb, b_idb = sb("idb", [128, 128], BF16)
            wib, b_wib = sb("wib", [128, 8, 2080], BF16)
            wuq, b_wuq = sb("wuq", [128, 2, 576], BF16)
            wukv, b_wukv = sb("wukv", [128, 768], BF16)
            g8, b_g8 = sb("g8", [128, 8], F32)
            gq2, b_gq2 = sb("gq2", [128, 2], F32)
            gkv1, b_gkv1 = sb("gkv1", [128, 1], F32)
            ga, b_ga = sb("ga", [128, 2, 64], F32)
            gb, b_gb = sb("gb", [128, 2, 64], F32)
            gc, b_gc = sb("gc", [128, 2, 96], F32)
            GAq, b_GAq = sb("GAq", [128, 64], F32)
            GBq, b_GBq = sb("GBq", [128, 64], F32)
            GCq, b_GCq = sb("GCq", [128, 96], F32)
            invf, b_invf = sb("invf", [128, 16], F32)
            sin_t, b_sin = sb("sin_t", [128, 64, 16], F32)
            cos_t, b_cos = sb("cos_t", [128, 64, 16], F32)
            invd, b_invd = sb("invd", [128, 24], F32)
            nh24, b_nh24 = sb("nh24", [128, 24], F32)
            valid, b_valid = sb("valid", [128, 16], F32)
            posi, b_posi = sbs("posi", [128, 64], I32)
            posf, b_posf = sbs("posf", [128, 64], F32)
            ang, b_ang = sbs("ang", [128, 64, 16], F32)
            angk, b_angk = sbs("angk", [128, 64, 16], I32)
            angf, b_angf = sbs("angf", [128, 64, 16], F32)
            stage = [sbs("stage", [128, 2080], F32)] * 2
            stq, b_stq = sbs("stq", [128, 2, 576], F32)
            stkv, b_stkv = sbs("stkv", [128, 768], F32)

            self.dma('sync', idb[:], D['idb'].ap(), [], [b_idb], 'idb')
            self.dma('sync', g8[:], D['norm_mix'].ap()[L].rearrange("(c p) -> p c", p=128), [], [b_g8], 'g8',
                     allow_slow_non_contiguous=True)
            self.dma('sync', gq2[:], D['q_lat_gain'].ap()[L].rearrange("(c p) -> p c", p=128), [], [b_gq2], 'gq2',
                     allow_slow_non_contiguous=True)
            self.dma('sync', gkv1[:], D['kv_lat_gain'].ap()[L].rearrange("(c p) -> p c", p=128), [], [b_gkv1],
                     'gkv1', allow_slow_non_contiguous=True)
            self.dma('sync', ga[:], D['qk_gain_a'].ap()[L].rearrange("a d -> (a d)").partition_broadcast(128),
                     [], [b_ga], 'ga')
            self.dma('sync', gb[:], D['qk_gain_b'].ap()[L].rearrange("a d -> (a d)").partition_broadcast(128),
                     [], [b_gb], 'gb')
            self.dma('sync', gc[:], D['qk_gain_c'].ap()[L].rearrange("a d -> (a d)").partition_broadcast(128),
                     [], [b_gc], 'gc')
            self.dma('sync', invf[:], D['invf'].ap().rearrange("a d -> (a d)").partition_broadcast(128),
                     [], [b_invf], 'invf')
            self.dma('sync', posi[:], D[pos_key].ap(), [], [b_posi], 'posi')
            self.dma('sync', valid[:], D[valid_key].ap(), [], [b_valid], 'valid')
            V('scalar_tensor_tensor', [b_ga], [b_GAq], out=GAq[:], in0=ga[:, 0, :], scalar=0.125, in1=ga[:, 1, :],
              op0=ALU.mult, op1=ALU.mult)
            V('scalar_tensor_tensor', [b_gb], [b_GBq], out=GBq[:], in0=gb[:, 0, :], scalar=0.125, in1=gb[:, 1, :],
              op0=ALU.mult, op1=ALU.mult)
            V('tensor_scalar', [b_gc], [b_GCq], out=GCq[:], in0=gc[:, 0, :], scalar1=96.0 ** -0.5, scalar2=None,
              op0=ALU.mult)
            GCk = gc[:, 1, :]
            b_GCk = b_gc
            self.P.op('gpsimd', lambda e: e.memset(invd[:], 1.0 / 64), [], [b_invd])
            self.P.op('gpsimd', lambda e: e.memset(invd[:, 18:19], 1.0 / 256), [], [b_invd])
            self.P.op('gpsimd', lambda e: e.memset(invd[:, 19:20], 1.0 / 128), [], [b_invd])
            self.P.op('gpsimd', lambda e: e.memset(invd[:, 20:24], 1.0), [], [b_invd])
            self.P.op('gpsimd', lambda e: e.memset(nh24[:], -0.5), [], [b_nh24])

            V('tensor_copy', [b_posi], [b_posf], out=posf[:], in_=posi[:])
            V('tensor_tensor', [b_posf, b_invf], [b_ang], out=ang[:],
              in0=bc(posf[:].unsqueeze(2), [128, 64, 16]), in1=bc(invf[:].unsqueeze(1), [128, 64, 16]), op=ALU.mult)
            for (tab, b_tab, off) in ((sin_t, b_sin, 0.0), (cos_t, b_cos, 0.25)):
                V('tensor_scalar', [b_ang], [b_angf], out=angf[:], in0=ang[:], scalar1=1.0 / (2 * math.pi),
                  scalar2=off, op0=ALU.mult, op1=ALU.add)
                V('tensor_copy', [b_angf], [b_angk], out=angk[:], in_=angf[:])
                V('tensor_copy', [b_angk], [b_tab], out=tab[:], in_=angk[:])
                V('tensor_tensor', [b_angf, b_tab], [b_angf], out=angf[:], in0=angf[:], in1=tab[:], op=ALU.subtract)
                A('activation', [b_angf], [b_tab], out=tab[:], in_=angf[:], func=AF.Sin, scale=TWO_PI_S)

            blocks = [(0, 384, 0), (384, 768, 512), (768, 1152, 1024), (1152, 1408, 1536), (1408, 1536, 896),
                      (1536, 1664, 1408), (1664, 1920, 1792), (1920, 2048, 384), (2048, 2080, 2048)]
            k = 0
            for c in range(8):
                st_t, st_b = stage[c % 2]
                self.dma('sync', st_t[:], D['w_in'].ap()[L, c * 128:(c + 1) * 128, :], [], [st_b], 'stage0')
                for (o0, o1, n0) in blocks:
                    eng = 'vector' if k % 2 == 0 else 'gpsimd'
                    k += 1
                    E(eng, 'tensor_scalar', [st_b, b_g8], [b_wib], out=wib[:, c, n0:n0 + (o1 - o0)],
                      in0=st_t[:, o0:o1], scalar1=g8[:, c:c + 1], scalar2=None, op0=ALU.mult)
            self.dma('sync', stq[:], D['w_uq'].ap()[L].rearrange("(c p) n -> p c n", p=128), [], [b_stq], 'stq')
            self.dma('sync', stkv[:], D['w_ukv'].ap()[L], [], [b_stkv], 'stkv')
            for c in range(2):
                V('tensor_scalar', [b_stq, b_gq2], [b_wuq], out=wuq[:, c, :], in0=stq[:, c, :],
                  scalar1=gq2[:, c:c + 1], scalar2=None, op0=ALU.mult)
            V('tensor_scalar', [b_stkv, b_gkv1], [b_wukv], out=wukv[:], in0=stkv[:], scalar1=gkv1[:, 0:1],
              scalar2=None, op0=ALU.mult)

            self.P.phase_barrier()
            setup.close()
            xt = sb("xt", [128, 1024], F32, NS)
            junk, b_junk = sb("junk", [128, 1024], BF16)
            ssx = sb("ssx", [128, 1], F32, NS)
            rsx = sb("rsx", [128, 1], F32, NS)
            hb = sb("hb", [128, 1024], BF16, NS)
            hT = sb("hT", [128, 8, 128], BF16, NS)
            pj = sb("pj", [128, 2080], F32, NS)
            sq, b_sq = sb("sq", [128, 2080], F32)
            st = sb("st", [128, 24], F32, NS)
            rstd = sb("rstd", [128, 24], F32, NS)
            QAo = sb("QAo", [128, 384], BF16, NS)
            QAt = sb("QAt", [128, 384], F32, 1)
            KABo = sb("KABo", [128, 512], BF16, NS)
            QBo = sb("QBo", [128, 256], BF16, NS)
            QBt = sb("QBt", [128, 256], F32, 1)
            VABo = sb("VABo", [128, 8, 65], BF16, NS)
            LAT = sb("LAT", [128, 384], BF16, NS)
            latT = sb("latT", [128, 3, 128], BF16, NS)
            qcs = sb("qcs", [128, 576], F32, NS)
            kvcs = sb("kvcs", [128, 768], F32, NS)
            st2 = sb("st2", [128, 12], F32, NS)
            rstd2 = sb("rstd2", [128, 12], F32, NS)
            tmp1, b_tmp1 = sb("tmp1", [128, 6, 96], F32)
            trq, b_trq = sb("trq", [128, 6, 32], F32)
            tmpk, b_tmpk = sb("tmpk", [128, 6, 64], F32)
            krg, b_krg = sb("krg", [128, 1, 32], F32)
            krr, b_krr = sb("krr", [128, 1, 32], F32)
            rm = [sb("rm%d" % i, [128, 6, 16], F32) for i in range(4)]
            QCo = sb("QCo", [128, 6, 96], BF16, NS)
            KCo = sb("KCo", [128, 6, 96], BF16, NS)
            VCo = sb("VCo", [128, 6, 65], BF16, NS)
            QTs = sb("QTs", [96, 6, 128], BF16, NS)
            KTs = sb("KTs", [96, 6, 128], BF16, NS)
            TR, b_TR = ps("TR", [128, 8, 128], BF16)
            PJ = [ps("PJ%d" % i, [128, 512], F32) for i in range(5)]
            S = [ps("S%d" % i, [128, 512], F32) for i in range(2)]

            for (t_, b_) in VABo:
                self.P.op('gpsimd', lambda e, t_=t_: e.memset(t_[:], 1.0), [], [b_])
            for (t_, b_) in VCo:
                self.P.op('gpsimd', lambda e, t_=t_: e.memset(t_[:], 1.0), [], [b_])

            def rope(src, b_src, dst, b_dst, H, ti):
                cb = bc(cos_t[:, ti, :].unsqueeze(1), [128, H, 16])
                sbb = bc(sin_t[:, ti, :].unsqueeze(1), [128, H, 16])
                (m1, b1), (m2, b2), (m3, b3), (m4, b4) = rm
                G('tensor_tensor', [b_src, b_cos], [b1], out=m1[:, 0:H, :], in0=src[:, :, 0:16], in1=cb, op=ALU.mult)
                G('tensor_tensor', [b_src, b_sin], [b2], out=m2[:, 0:H, :], in0=src[:, :, 16:32], in1=sbb, op=ALU.mult)
                G('tensor_tensor', [b1, b2], [b_dst], out=dst[:, :, 0:16], in0=m1[:, 0:H, :], in1=m2[:, 0:H, :],
                  op=ALU.subtract)
                G('tensor_tensor', [b_src, b_cos], [b3], out=m3[:, 0:H, :], in0=src[:, :, 16:32], in1=cb, op=ALU.mult)
                G('tensor_tensor', [b_src, b_sin], [b4], out=m4[:, 0:H, :], in0=src[:, :, 0:16], in1=sbb, op=ALU.mult)
                G('tensor_tensor', [b3, b4], [b_dst], out=dst[:, :, 16:32], in0=m3[:, 0:H, :], in1=m4[:, 0:H, :],
                  op=ALU.add)

            SC = self.SC
            it = 0
            jobs = [('own', t) for t in range(32)] + [('oth', t) for t in range(32)] + [('halo', t) for t in range(16)]
            if skip_oth:
                jobs = [j for j in jobs if j[0] != 'oth']
            if self.debug and self.debug.get('p1_tiles'):
                jobs = self.debug['p1_tiles']
            def tile_gen(it, kind, t):
                s2 = it % NS
                s3 = it % NS
                src = {'own': x_own, 'oth': x_oth, 'halo': x_halo}[kind]
                x_t, b_x = xt[s3]
                ss_t, b_ss = ssx[s2]
                rs_t, b_rs = rsx[s2]
                hb_t, b_hb = hb[s2]
                hT_t, b_hT = hT[s2]
                pj_t, b_pj = pj[s3]
                st_t, b_st = st[s3]
                rstd_t, b_rstd = rstd[s3]
                if kind == 'halo':
                    G('tensor_scalar', [b_x, b_valid], [b_x], out=x_t[:], in0=x_t[:], scalar1=valid[:, t:t + 1],
                      scalar2=None, op0=ALU.mult)
                    yield
                A('activation', [b_x], [b_junk, b_ss], out=junk[:], in_=x_t[:], func=AF.Square, accum_out=ss_t[:])
                yield
                G('tensor_scalar', [b_ss], [b_ss], out=ss_t[:], in0=ss_t[:], scalar1=1.0 / 1024, scalar2=EPS,
                  op0=ALU.mult, op1=ALU.add)
                yield
                G('tensor_tensor', [b_ss, b_nh24], [b_rs], out=rs_t[:], in0=ss_t[:], in1=nh24[:, 0:1], op=ALU.pow)
                yield
                A('activation', [b_x, b_rs], [b_hb], out=hb_t[:], in_=x_t[:], func=AF.Copy, scale=rs_t[:, 0:1])
                yield
                for c in range(8):
                    T('transpose', [b_hb, b_idb], [b_TR], out=TR[:, c, :], in_=hb_t[:, c * 128:(c + 1) * 128],
                      identity=idb[:])
                V('tensor_copy', [b_TR], [b_hT], out=hT_t[:], in_=TR[:])
                yield
                if kind == 'own':
                    groups = [(0, 0, 512, 0), (1, 512, 1024, 0), (2, 1024, 1536, 0), (3, 1536, 2048, 0),
                              (4, 2048, 2080, 0)]
                elif kind == 'oth':
                    groups = [(0, 384, 512, 384), (4, 2048, 2080, 0)]
                else:
                    groups = [(1, 512, 1024, 0), (2, 1024, 1536, 0)]
                for (bk, c0, c1, po) in groups:
                    pt, pb = PJ[bk]
                    for c in range(8):
                        T('matmul', [b_hT, b_wib], [pb], out=pt[:, po:po + (c1 - c0)], lhsT=hT_t[:, c, :],
                          rhs=wib[:, c, c0:c1], start=(c == 0), stop=(c == 7))
                    A('activation', [pb], [b_pj], out=pj_t[:, c0:c1], in_=pt[:, po:po + (c1 - c0)], func=AF.Copy)
                    yield
                if kind == 'own':
                    V('tensor_tensor', [b_pj], [b_sq], out=sq[:, 0:1024], in0=pj_t[:, 0:1024], in1=pj_t[:, 0:1024],
                      op=ALU.mult)
                    V('tensor_tensor', [b_pj], [b_sq], out=sq[:, 1536:2080], in0=pj_t[:, 1536:2080],
                      in1=pj_t[:, 1536:2080], op=ALU.mult)
                    red = [(0, 6, 0, 384, 64), (6, 14, 512, 1024, 64), (14, 18, 1536, 1792, 64),
                           (18, 19, 1792, 2048, 256), (19, 20, 384, 512, 128), (20, 21, 2048, 2080, 32)]
                elif kind == 'oth':
                    V('tensor_tensor', [b_pj], [b_sq], out=sq[:, 384:512], in0=pj_t[:, 384:512], in1=pj_t[:, 384:512],
                      op=ALU.mult)
                    V('tensor_tensor', [b_pj], [b_sq], out=sq[:, 2048:2080], in0=pj_t[:, 2048:2080],
                      in1=pj_t[:, 2048:2080], op=ALU.mult)
                    red = [(19, 20, 384, 512, 128), (20, 21, 2048, 2080, 32)]
                else:
                    V('tensor_tensor', [b_pj], [b_sq], out=sq[:, 512:1024], in0=pj_t[:, 512:1024],
                      in1=pj_t[:, 512:1024], op=ALU.mult)
                    red = [(6, 14, 512, 1024, 64)]
                for (a0, a1, c0, c1, dd) in red:
                    V('tensor_reduce', [b_sq], [b_st], out=st_t[:, a0:a1],
                      in_=sq[:, c0:c1].rearrange("p (h d) -> p h d", d=dd), axis=AX.X, op=ALU.add)
                G('tensor_tensor', [b_st, b_invd], [b_rstd], out=rstd_t[:, 0:20], in0=st_t[:, 0:20], in1=invd[:, 0:20],
                  op=ALU.mult)
                yield
                G('tensor_scalar', [b_rstd], [b_rstd], out=rstd_t[:, 0:20], in0=rstd_t[:, 0:20], scalar1=EPS,
                  scalar2=None, op0=ALU.add)
                yield
                G('tensor_tensor', [b_rstd, b_nh24], [b_rstd], out=rstd_t[:, 0:20], in0=rstd_t[:, 0:20],
                  in1=nh24[:, 0:20], op=ALU.pow)
                yield
                if kind in ('own', 'halo'):
                    et = (8 + t) if kind == 'own' else (t if t < 8 else 40 + (t - 8))
                    kab_t, b_kab = KABo[s2]
                    vab_t, b_vab = VABo[s2]
                    V('tensor_tensor', [b_pj, b_rstd], [b_kab], out=kab_t[:].rearrange("p (h d) -> p h d", d=64),
                      in0=pj_t[:, 512:1024].rearrange("p (h d) -> p h d", d=64),
                      in1=bc(rstd_t[:, 6:14].unsqueeze(2), [128, 8, 64]), op=ALU.mult)
                    yield
                    V('tensor_copy', [b_pj], [b_vab], out=vab_t[:, :, 0:64],
                      in_=pj_t[:, 1024:1536].rearrange("p (h d) -> p h d", d=64))
                    yield
                    if kind == 'halo':
                        V('tensor_copy', [b_valid], [b_vab], out=vab_t[:, :, 64:65],
                          in_=bc(valid[:, t:t + 1].unsqueeze(1), [128, 8, 1]))
                        yield
                    else:
                        self.P.op('vector', lambda e, vab_t=vab_t: e.memset(vab_t[:, :, 64:65], 1.0), [], [b_vab])
                        yield
                    rows = slice(et * 128, (et + 1) * 128)
                    self.dma('sync', SC['KAx'].ap()[rows, :], kab_t[:, 0:384], [b_kab], [SC['b']], 'kabo%d' % s2)
                    yield
                    self.dma('sync', SC['KBx'].ap()[rows, :], kab_t[:, 384:512], [b_kab], [SC['b']], 'kabo%d' % s2)
                    yield
                    self.dma('sync', SC['VAx'].ap()[rows, :].rearrange("p (h d) -> p h d", d=65), vab_t[:, 0:6, :],
                             [b_vab], [SC['b']], 'vabo%d' % s2)
                    yield
                    self.dma('sync', SC['VBx'].ap()[rows, :].rearrange("p (h d) -> p h d", d=65), vab_t[:, 6:8, :],
                             [b_vab], [SC['b']], 'vabo%d' % s2)
                    yield
                if kind == 'own':
                    rows = slice(t * 128, (t + 1) * 128)
                    qa_t, b_qa = QAo[s2]
                    qat, b_qat = QAt
                    V('tensor_tensor', [b_pj, b_rstd], [b_qat], out=qat[:].rearrange("p (h d) -> p h d", d=64),
                      in0=pj_t[:, 0:384].rearrange("p (h d) -> p h d", d=64),
                      in1=bc(rstd_t[:, 0:6].unsqueeze(2), [128, 6, 64]), op=ALU.mult)
                    V('tensor_tensor', [b_qat, b_GAq], [b_qa], out=qa_t[:].rearrange("p (h d) -> p h d", d=64),
                      in0=qat[:].rearrange("p (h d) -> p h d", d=64),
                      in1=bc(GAq[:].unsqueeze(1), [128, 6, 64]), op=ALU.mult)
                    self.dma('sync', SC['QA'].ap()[rows, :], qa_t[:], [b_qa], [SC['b']], 'qao%d' % s2)
                    yield
                    qb_t, b_qb = QBo[s2]
                    qbt, b_qbt = QBt
                    V('tensor_tensor', [b_pj, b_rstd], [b_qbt], out=qbt[:].rearrange("p (h d) -> p h d", d=64),
                      in0=pj_t[:, 1536:1792].rearrange("p (h d) -> p h d", d=64),
                      in1=bc(rstd_t[:, 14:18].unsqueeze(2), [128, 4, 64]), op=ALU.mult)
                    V('tensor_tensor', [b_qbt, b_GBq], [b_qb],
                      out=qb_t[:].rearrange("p (b a d) -> p a b d", b=2, a=2, d=64),
                      in0=qbt[:].rearrange("p (a b d) -> p a b d", a=2, b=2, d=64),
                      in1=bc(GBq[:].unsqueeze(1).unsqueeze(1), [128, 2, 2, 64]), op=ALU.mult)
                    self.dma('sync', SC['QB'].ap()[rows, :], qb_t[:], [b_qb], [SC['b']], 'qbo%d' % s2)
                    yield
                if kind in ('own', 'oth'):
                    ti = t if kind == 'own' else 32 + t
                    lat_t, b_lat = LAT[s2]
                    latT_t, b_latT = latT[s2]
                    qcs_t, b_qcs = qcs[s2]
                    kvcs_t, b_kvcs = kvcs[s2]
                    st2_t, b_st2 = st2[s2]
                    rstd2_t, b_rstd2 = rstd2[s2]
                    if kind == 'own':
                        V('tensor_scalar', [b_pj, b_rstd], [b_lat], out=lat_t[:, 0:256], in0=pj_t[:, 1792:2048],
                          scalar1=rstd_t[:, 18:19], scalar2=None, op0=ALU.mult)
                        yield
                    if not skip_ck:
                        V('tensor_scalar', [b_pj, b_rstd], [b_lat], out=lat_t[:, 256:384], in0=pj_t[:, 384:512],
                          scalar1=rstd_t[:, 19:20], scalar2=None, op0=ALU.mult)
                        yield
                    jl = ([0, 1] if skip_ck else [0, 1, 2]) if kind == 'own' else [2]
                    for j in jl:
                        T('transpose', [b_lat, b_idb], [b_TR], out=TR[:, j, :], in_=lat_t[:, j * 128:(j + 1) * 128],
                          identity=idb[:])
                    V('tensor_copy', [b_TR], [b_latT], out=latT_t[:, jl[0]:jl[-1] + 1, :], in_=TR[:, jl[0]:jl[-1] + 1, :])
                    if kind == 'own':
                        for hf in range(2):
                            for c in range(2):
                                T('matmul', [b_latT, b_wuq], [S[hf][1]], out=S[hf][0][:, 0:288], lhsT=latT_t[:, c, :],
                                  rhs=wuq[:, c, hf * 288:(hf + 1) * 288], start=(c == 0), stop=(c == 1))
                            A('activation', [S[hf][1]], [b_qcs], out=qcs_t[:, hf * 288:(hf + 1) * 288],
                              in_=S[hf][0][:, 0:288], func=AF.Copy)
                    for hf in (range(2) if not skip_ck else []):
                        T('matmul', [b_latT, b_wukv], [S[hf][1]], out=S[hf][0][:, 0:384], lhsT=latT_t[:, 2, :],
                          rhs=wukv[:, hf * 384:(hf + 1) * 384], start=True, stop=True)
                        A('activation', [S[hf][1]], [b_kvcs], out=kvcs_t[:, hf * 384:(hf + 1) * 384],
                          in_=S[hf][0][:, 0:384], func=AF.Copy)
                    kv3 = kvcs_t[:].rearrange("p (h d) -> p h d", d=128)
                    if kind == 'own':
                        V('tensor_tensor', [b_qcs], [b_sq], out=sq[:, 0:576], in0=qcs_t[:], in1=qcs_t[:], op=ALU.mult)
                        V('tensor_reduce', [b_sq], [b_st2], out=st2_t[:, 0:6],
                          in_=sq[:, 0:576].rearrange("p (h d) -> p h d", d=96), axis=AX.X, op=ALU.add)
                    if not skip_ck:
                        V('tensor_tensor', [b_kvcs], [b_sq], out=sq[:, 1024:1408].rearrange("p (h d) -> p h d", d=64),
                          in0=kv3[:, :, 0:64], in1=kv3[:, :, 0:64], op=ALU.mult)
                        V('tensor_reduce', [b_sq], [b_st2], out=st2_t[:, 6:12],
                          in_=sq[:, 1024:1408].rearrange("p (h d) -> p h d", d=64), axis=AX.X, op=ALU.add)
                        V('tensor_scalar', [b_st2, b_st], [b_st2], out=st2_t[:, 6:12], in0=st2_t[:, 6:12],
                          scalar1=st_t[:, 20:21], scalar2=None, op0=ALU.add)
                    lo = 0 if kind == 'own' else 6
                    hi_ = 6 if skip_ck else 12
                    G('tensor_scalar', [b_st2], [b_rstd2], out=rstd2_t[:, lo:hi_], in0=st2_t[:, lo:hi_],
                      scalar1=1.0 / 96, scalar2=EPS, op0=ALU.mult, op1=ALU.add)
                    yield
                    G('tensor_tensor', [b_rstd2, b_nh24], [b_rstd2], out=rstd2_t[:, lo:hi_], in0=rstd2_t[:, lo:hi_],
                      in1=nh24[:, lo:hi_], op=ALU.pow)
                    yield
                    kc_t, b_kc = KCo[s2]
                    vc_t, b_vc = VCo[s2]
                    if kind == 'own':
                        qc_t, b_qc = QCo[s2]
                        V('tensor_tensor', [b_qcs, b_rstd2], [b_tmp1], out=tmp1[:],
                          in0=qcs_t[:].rearrange("p (h d) -> p h d", d=96),
                          in1=bc(rstd2_t[:, 0:6].unsqueeze(2), [128, 6, 96]), op=ALU.mult)
                        V('tensor_tensor', [b_tmp1, b_GCq], [b_qc], out=qc_t[:, :, 0:64], in0=tmp1[:, :, 0:64],
                          in1=bc(GCq[:, 0:64].unsqueeze(1), [128, 6, 64]), op=ALU.mult)
                        V('tensor_tensor', [b_tmp1, b_GCq], [b_trq], out=trq[:], in0=tmp1[:, :, 64:96],
                          in1=bc(GCq[:, 64:96].unsqueeze(1), [128, 6, 32]), op=ALU.mult)
                        rope(trq, b_trq, qc_t[:, :, 64:96], b_qc, 6, ti)
                    if kind == 'own':
                        qT_t, b_qT = QTs[s2]
                        for h in range(6):
                            T('transpose', [b_qc, b_idb], [b_TR], out=TR[0:96, h, :], in_=qc_t[:, h, :],
                              identity=idb[:])
                        V('tensor_copy', [b_TR], [b_qT], out=qT_t[:], in_=TR[0:96, 0:6, :])
                        self.dma('sync', SC['QCT'].ap()[:, :, t * 128:(t + 1) * 128].rearrange("h d n -> d h n"),
                                 qT_t[:], [b_qT], [SC['b']], 'qto%d' % s2)
                        yield
                    if skip_ck:
                        return
                    V('tensor_tensor', [b_kvcs, b_rstd2], [b_tmpk], out=tmpk[:], in0=kv3[:, :, 0:64],
                      in1=bc(rstd2_t[:, 6:12].unsqueeze(2), [128, 6, 64]), op=ALU.mult)
                    V('tensor_tensor', [b_tmpk, b_GCk], [b_kc], out=kc_t[:, :, 0:64], in0=tmpk[:],
                      in1=bc(GCk[:, 0:64].unsqueeze(1), [128, 6, 64]), op=ALU.mult)
                    V('tensor_tensor', [b_pj, b_GCk], [b_krg], out=krg[:, 0, :], in0=pj_t[:, 2048:2080],
                      in1=GCk[:, 64:96], op=ALU.mult)
                    rope(krg, b_krg, krr[:], b_krr, 1, ti)
                    V('tensor_tensor', [b_krr, b_rstd2], [b_kc], out=kc_t[:, :, 64:96],
                      in0=bc(krr[:], [128, 6, 32]), in1=bc(rstd2_t[:, 6:12].unsqueeze(2), [128, 6, 32]), op=ALU.mult)
                    V('tensor_copy', [b_kvcs], [b_vc], out=vc_t[:, :, 0:64], in_=kv3[:, :, 64:128])
                    yield
                    kT_t, b_kT = KTs[s2]
                    for h in range(6):
                        T('transpose', [b_kc, b_idb], [b_TR], out=TR[0:96, h, :], in_=kc_t[:, h, :], identity=idb[:])
                    V('tensor_copy', [b_TR], [b_kT], out=kT_t[:], in_=TR[0:96, 0:6, :])
                    self.dma('sync', SC['KCT'].ap()[:, :, ti * 128:(ti + 1) * 128].rearrange("h d n -> d h n"),
                             kT_t[:], [b_kT], [SC['b']], 'kto%d' % s2)
                    yield
                    self.dma('sync', SC['VC'].ap()[ti * 128:(ti + 1) * 128, :].rearrange("p (h d) -> p h d", d=65),
                             vc_t[:], [b_vc], [SC['b']], 'vco%d' % s2)
                    yield

            def issue_load(j):
                kind, t = jobs[j]
                x_t, b_x = xt[j % NS]
                if kind == 'halo' and halo_rows is not None:
                    self.dma('sync', x_t[:], halo_rows(t), [], [b_x], 'xt%d' % (j % NS))
                else:
                    src = {'own': x_own, 'oth': x_oth, 'halo': x_halo}[kind]
                    self.dma('sync', x_t[:], src[t * 128:(t + 1) * 128, :], [], [b_x], 'xt%d' % (j % NS))

            for j in range(min(2, len(jobs))):
                issue_load(j)
            active = []
            nxt = 0
            since = 10 ** 9
            STAG = 9
            while active or nxt < len(jobs):
                if nxt < len(jobs) and len(active) < NS and (since >= STAG or not active):
                    if nxt + 2 < len(jobs):
                        issue_load(nxt + 2)
                    active.append(tile_gen(nxt, jobs[nxt][0], jobs[nxt][1]))
                    nxt += 1
                    since = 0
                for g in list(active):
                    try:
                        next(g)
                    except StopIteration:
                        active.remove(g)
                since += 1
        self.P.phase_barrier()

    def phaseC(self, heads=range(6), nqt=8):
        nc, P, D, SC = self.nc, self.P, self.D, self.SC
        V, G, A, T, E = self.V, self.G, self.A, self.T, self.E
        with ExitStack() as ph:
            def sb(nm, shape, dt, n=1):
                r = []
                for i in range(n):
                    t = ph.enter_context(nc.sbuf_tensor(self.name(nm), shape, dt))
                    r.append((t, Buf(nm + str(i))))
                return r if n > 1 else r[0]

            def ps(nm, shape, dt, n=1):
                r = []
                for i in range(n):
                    t = psum_view(ph, nc, self.name(nm), shape, dt)
                    r.append((t, Buf(nm + str(i))))
                return r if n > 1 else r[0]

            P.barrier('sync', reads=[SC['b']])
            idf, b_idf = sb("idf", [128, 128], F32)
            self.dma('sync', idf[:], D['idf'].ap(), [], [b_idf], 'idf')
            KT = sb("cKT", [96, 8192], BF16, 2)
            VV = sb("cV", [128, 64, 65], BF16, 2)
            QT = sb("cQT", [96, 4096], BF16, 2)
            PT = sb("cPT", [128, 512], BF16, 4)
            OTs = sb("cOTs", [65, 512], F32, 2)
            rc = sb("crc", [128, 4], F32, 2)
            oc = sb("coc", [128, 4, 64], BF16, 2)
            ST = ps("cST", [128, 512], F32, 3)
            OT = ps("cOT", [65, 512], F32, 2)
            TO, b_TO = ps("cTO", [128, 4, 65], F32)

            heads = list(heads)

            def load_head(hi):
                h = heads[hi]
                s = hi % 2
                self.dma('sync', KT[s][0][:], SC['KCT'].ap()[h], [SC['b']], [KT[s][1]], 'cKT%d' % s)
                self.dma('sync', QT[s][0][:], SC['QCT'].ap()[h], [SC['b']], [QT[s][1]], 'cQT%d' % s)
                self.dma('sync', VV[s][0][:],
                         SC['VC'].ap().rearrange("(t p) c -> p t c", p=128)[:, :, h * 65:(h + 1) * 65],
                         [SC['b']], [VV[s][1]], 'cV%d' % s)

            steps = [(hi, qt, kc) for hi in range(len(heads)) for qt in range(nqt) for kc in range(64)]
            n = len(steps)
            LA = 2
            deferred = []
            load_head(0)
            nq = 0
            for i in range(n + LA):
                if i < n:
                    hi, qt, kc = steps[i]
                    s = hi % 2
                    st_t, st_b = ST[i % 3]
                    pt_t, pt_b = PT[i % 4]
                    T('matmul', [KT[s][1], QT[s][1]], [st_b], out=st_t[:], lhsT=KT[s][0][:, kc * 128:(kc + 1) * 128],
                      rhs=QT[s][0][:, qt * 512:(qt + 1) * 512], start=True, stop=True)
                    A('activation', [st_b], [pt_b], out=pt_t[:], in_=st_t[:], func=AF.Exp)
                j = i - LA
                if j >= 0:
                    hi, qt, kc = steps[j]
                    if qt == 0 and kc == 0 and hi + 1 < len(heads):
                        load_head(hi + 1)
                    s = hi % 2
                    qi = (hi * nqt + qt)
                    ot_t, ot_b = OT[qi % 2]
                    pt_t, pt_b = PT[j % 4]
                    T('matmul', [VV[s][1], pt_b], [ot_b], out=ot_t[:], lhsT=VV[s][0][:, kc, :], rhs=pt_t[:],
                      start=(kc == 0), stop=(kc == 63))
                    if kc == 63:
                        h = heads[hi]
                        os_t, os_b = OTs[qi % 2]
                        V('tensor_copy', [ot_b], [os_b], out=os_t[:], in_=ot_t[:])

                        def fin(os_t=os_t, os_b=os_b, qi=qi, qt=qt, h=h):
                            for jj in range(4):
                                T('transpose', [os_b, b_idf], [b_TO], out=TO[:, jj, :],
                                  in_=os_t[:, jj * 128:(jj + 1) * 128], identity=idf[0:65, 0:65])
                            rc_t, rc_b = rc[qi % 2]
                            oc_t, oc_b = oc[qi % 2]
                            V('reciprocal', [b_TO], [rc_b], out=rc_t[:].unsqueeze(2), in_=TO[:, :, 64:65])
                            V('tensor_tensor', [b_TO, rc_b], [oc_b], out=oc_t[:], in0=TO[:, :, 0:64],
                              in1=bc(rc_t[:].unsqueeze(2), [128, 4, 64]), op=ALU.mult)
                            self.dma('sync',
                                     SC['MIX'].ap()[qt * 512:(qt + 1) * 512, 640 + h * 64:640 + (h + 1) * 64]
                                     .rearrange("(t p) d -> p t d", p=128),
                                     oc_t[:], [oc_b], [SC['bmix']], 'coc%d' % (qi % 2))
                        deferred.append((i + 6, fin))
                while deferred and deferred[0][0] <= i:
                    deferred.pop(0)[1]()
            for (_, fn) in deferred:
                fn()
        self.P.phase_barrier()

    def bias_setup(self):
        nc, P, D, SC = self.nc, self.P, self.D, self.SC
        V, G, A, T, E = self.V, self.G, self.A, self.T, self.E
        with ExitStack() as ph:
            tabN = ph.enter_context(nc.sbuf_tensor(self.name("tabN"), [33, 10], F32)); b_tab = Buf("tabN")
            oh = ph.enter_context(nc.sbuf_tensor(self.name("oh"), [33, 1660], F32)); b_oh = Buf("oh")
            fv = ph.enter_context(nc.sbuf_tensor(self.name("fv"), [10, 1660], F32)); b_fv = Buf("fv")
            pf = [(psum_view(ph, nc, self.name("pf"), [10, 512], F32), Buf("pf%d" % i)) for i in range(4)]
            P.op('vector', lambda e: e.memset(tabN[:], NEGB), [], [b_tab])
            self.dma('sync', tabN[0:32, :], D['rel_bias_table'].ap(), [], [b_tab], 'tabN')
            self.dma('sync', oh[:], D['oh'].ap(), [], [b_oh], 'oh')
            segs = [(0, 511), (511, 894), (894, 1277), (1277, 1660)]
            for i, (a, b) in enumerate(segs):
                T('matmul', [b_tab, b_oh], [pf[i][1]], out=pf[i][0][:, 0:b - a], lhsT=tabN[:], rhs=oh[:, a:b],
                  start=True, stop=True)
                V('tensor_copy', [pf[i][1]], [b_fv], out=fv[:, a:b], in_=pf[i][0][:, 0:b - a])
            self.dma('sync', SC['FV'].ap(), fv[:], [b_fv], [SC['bfv']], 'fvo')
        self.P.phase_barrier()

    def phaseB(self, L):
        nc, P, D, SC = self.nc, self.P, self.D, self.SC
        V, G, A, T, E = self.V, self.G, self.A, self.T, self.E
        with ExitStack() as ph:
            def sb(nm, shape, dt, n=1):
                r = []
                for i in range(n):
                    t = ph.enter_context(nc.sbuf_tensor(self.name(nm), shape, dt))
                    r.append((t, Buf(nm + str(i))))
                return r if n > 1 else r[0]

            def ps(nm, shape, dt, n=1):
                r = []
                for i in range(n):
                    t = psum_view(ph, nc, self.name(nm), shape, dt)
                    r.append((t, Buf(nm + str(i))))
                return r if n > 1 else r[0]

            P.barrier('sync', reads=[SC['b'], SC['bfv']])
            idf, b_idf = sb("idf", [128, 128], F32)
            idb, b_idb = sb("idb", [128, 128], BF16)
            self.dma('sync', idf[:], D['idf'].ap(), [], [b_idf], 'idf')
            self.dma('sync', idb[:], D['idb'].ap(), [], [b_idb], 'idb')
            biasB = sb("biasB", [128, 4, 3, 128], F32)
            hk = sb("hkB", [128, 128], F32, 4)
            for h in range(4):
                for o in range(3):
                    g_t, g_b = hk[(h * 3 + o) % 4]
                    self.dma('sync', g_t[:], bass.AP(SC['FV'], (6 + h) * 1660 + 128 * o, [[1, 128], [1, 128]]),
                             [SC['bfv']], [g_b], 'hkB%d' % ((h * 3 + o) % 4))
                    V('tensor_copy', [g_b], [biasB[1]], out=biasB[0][:, h, o, :],
                      in_=bass.AP(g_t, 127, [[128, 128], [-1, 128]]))
            sk, b_sk = sb("sink", [128, 4], F32)
            esk, b_esk = sb("esink", [128, 4], F32)
            self.dma('sync', sk[:], D['sink_b'].ap()[L].partition_broadcast(128), [], [b_sk], 'sink')
            A('activation', [b_sk], [b_esk], out=esk[:], in_=sk[:], func=AF.Exp)
            Qc = sb("bQc", [128, 4, 256], BF16, 2)
            Kc = sb("bKc", [128, 6, 128], BF16, 2)
            Vc = sb("bVc", [128, 6, 130], BF16, 2)
            QTb = sb("bQT", [128, 2, 4, 128], BF16, 2)
            KTb = sb("bKT", [128, 6, 128], BF16, 2)
            Sb = sb("bS", [128, 3, 128], F32, 2)
            PTb = sb("bPT", [128, 3, 128], BF16, 2)
            OTs = sb("bOTs", [65, 4, 128], F32, 2)
            den = sb("bden", [128, 4], F32, 2)
            ob = sb("bo", [128, 4, 64], BF16, 2)
            TRq = ps("bTRq", [128, 2, 4, 128], BF16)
            TRk = ps("bTRk", [128, 6, 128], BF16)
            SP = ps("bSP", [128, 3, 128], F32, 2)
            OTp = ps("bOT", [65, 4, 128], F32, 2)
            TOp = ps("bTO", [128, 4, 65], F32)

            def stage1(J):
                s = J % 2
                self.dma('sync', Qc[s][0][:], SC['QB'].ap()[J * 512:(J + 1) * 512, :].rearrange("(t p) c -> p t c", p=128),
                         [SC['b']], [Qc[s][1]], 'bQc%d' % s)
                r0 = (7 + 4 * J) * 128
                self.dma('sync', Kc[s][0][:], SC['KBx'].ap()[r0:r0 + 768, :].rearrange("(t p) c -> p t c", p=128),
                         [SC['b']], [Kc[s][1]], 'bKc%d' % s)
                self.dma('sync', Vc[s][0][:], SC['VBx'].ap()[r0:r0 + 768, :].rearrange("(t p) c -> p t c", p=128),
                         [SC['b']], [Vc[s][1]], 'bVc%d' % s)
                for t in range(4):
                    for pi in range(2):
                        T('transpose', [Qc[s][1], b_idb], [TRq[1]], out=TRq[0][:, pi, t, :],
                          in_=Qc[s][0][:, t, pi * 128:(pi + 1) * 128], identity=idb[:])
                V('tensor_copy', [TRq[1]], [QTb[s][1]], out=QTb[s][0][:], in_=TRq[0][:])
                for kt in range(6):
                    T('transpose', [Kc[s][1], b_idb], [TRk[1]], out=TRk[0][:, kt, :], in_=Kc[s][0][:, kt, :],
                      identity=idb[:])
                V('tensor_copy', [TRk[1]], [KTb[s][1]], out=KTb[s][0][:], in_=TRk[0][:])

            cnt = [0]

            def stage2(J):
                s = J % 2
                for t in range(4):
                    qi = J * 4 + t
                    ot_t, ot_b = OTp[qi % 2]
                    for h in range(4):
                        base = 64 * (h // 2)
                        pi = h % 2
                        kvh = h // 2
                        c = cnt[0]
                        cnt[0] += 1
                        sp_t, sp_b = SP[c % 2]
                        for o in range(3):
                            T('matmul', [KTb[s][1], QTb[s][1]], [sp_b], out=sp_t[:, o, :],
                              lhsT=KTb[s][0][base:base + 64, t + o, :], rhs=QTb[s][0][base:base + 64, pi, t, :],
                              start=True, stop=True)
                        s_t, s_b = Sb[c % 2]
                        p_t, p_b = PTb[c % 2]
                        V('tensor_tensor', [sp_b, biasB[1]], [s_b], out=s_t[:], in0=sp_t[:], in1=biasB[0][:, h, :, :],
                          op=ALU.add)
                        A('activation', [s_b], [p_b], out=p_t[:], in_=s_t[:], func=AF.Exp)
                        for o in range(3):
                            T('matmul', [Vc[s][1], p_b], [ot_b], out=ot_t[:, h, :],
                              lhsT=Vc[s][0][:, t + o, kvh * 65:(kvh + 1) * 65], rhs=p_t[:, o, :],
                              start=(o == 0), stop=(o == 2))
                    os_t, os_b = OTs[qi % 2]
                    V('tensor_copy', [ot_b], [os_b], out=os_t[:], in_=ot_t[:])
                    for h in range(4):
                        T('transpose', [os_b, b_idf], [TOp[1]], out=TOp[0][:, h, :], in_=os_t[:, h, :],
                          identity=idf[0:65, 0:65])
                    d_t, d_b = den[qi % 2]
                    o_t, o_b = ob[qi % 2]
                    V('tensor_tensor', [TOp[1], b_esk], [d_b], out=d_t[:].unsqueeze(2), in0=TOp[0][:, :, 64:65],
                      in1=esk[:].unsqueeze(2), op=ALU.add)
                    V('reciprocal', [d_b], [d_b], out=d_t[:], in_=d_t[:])
                    V('tensor_tensor', [TOp[1], d_b], [o_b], out=o_t[:], in0=TOp[0][:, :, 0:64],
                      in1=bc(d_t[:].unsqueeze(2), [128, 4, 64]), op=ALU.mult)
                    self.dma('sync', SC['MIX'].ap()[qi * 128:(qi + 1) * 128, 384:640], o_t[:], [o_b], [SC['bmix']],
                             'bo%d' % (qi % 2))

            stage1(0)
            for J in range(8):
                if J + 1 < 8:
                    stage1(J + 1)
                stage2(J)
        self.P.phase_barrier()

    def phaseA(self, cfgs=(0, 1, 2)):
        nc, P, D, SC = self.nc, self.P, self.D, self.SC
        V, G, A, T, E = self.V, self.G, self.A, self.T, self.E
        RS = (1, 4, 16)
        with ExitStack() as ph:
            def sb(nm, shape, dt, n=1):
                r = []
                for i in range(n):
                    t = ph.enter_context(nc.sbuf_tensor(self.name(nm), shape, dt))
                    r.append((t, Buf(nm + str(i))))
                return r if n > 1 else r[0]

            def ps(nm, shape, dt, n=1):
                r = []
                for i in range(n):
                    t = psum_view(ph, nc, self.name(nm), shape, dt)
                    r.append((t, Buf(nm + str(i))))
                return r if n > 1 else r[0]

            P.barrier('sync', reads=[SC['b'], SC['bfv']])
            idf, b_idf = sb("idf", [128, 128], F32)
            idb, b_idb = sb("idb", [128, 128], BF16)
            self.dma('sync', idf[:], D['idf'].ap(), [], [b_idf], 'idf')
            self.dma('sync', idb[:], D['idb'].ap(), [], [b_idb], 'idb')
            biasA = sb("biasA", [128, 3, 6, 2, 128], F32)
            hk = sb("hkA", [128, 128], F32, 4)
            for ci in range(3):
                for h in range(6):
                    for c in range(2):
                        kk = (ci * 6 + h) * 2 + c
                        g_t, g_b = hk[kk % 4]
                        self.dma('sync', g_t[:],
                                 bass.AP(SC['FV'], h * 1660 + 511 + 383 * ci + 128 * c, [[1, 128], [1, 128]]),
                                 [SC['bfv']], [g_b], 'hkA%d' % (kk % 4))
                        V('tensor_copy', [g_b], [biasA[1]], out=biasA[0][:, ci, h, c, :],
                          in_=bass.AP(g_t, 127, [[128, 128], [-1, 128]]))
            Qa = sb("aQ", [128, 384], BF16, 3)
            Ka = sb("aK", [128, 2, 384], BF16, 3)
            Va = sb("aV", [128, 2, 390], BF16, 3)
            QTa = sb("aQT", [128, 2, 3, 128], BF16, 2)
            for (t_, b_) in QTa:
                P.op('vector', lambda e, t_=t_: e.memset(t_[:], 0.0), [], [b_])
            KTa = sb("aKT", [128, 3, 2, 128], BF16, 2)
            Sa = sb("aS", [128, 6, 2, 128], F32, 2)
            PTa = sb("aPT", [128, 6, 2, 128], BF16, 2)
            OTs = sb("aOTs", [65, 6, 128], F32, 2)
            Oo = sb("aOo", [128, 6, 65], F32, 2)
            TRq = ps("aTRq", [128, 3, 128], BF16)
            TRk = ps("aTRk", [128, 3, 2, 128], BF16)
            SP = ps("aSP", [128, 2, 2, 128], F32, 3)
            OTp = ps("aOT", [65, 3, 128], F32, 2)
            TOp = ps("aTO", [128, 6, 65], F32)

            jobs = []
            for ci in cfgs:
                r = RS[ci]
                for rho in range(r):
                    for j in range(32 // r):
                        jobs.append((ci, r, rho, j))

            if self.debug and self.debug.get('a_jobs'):
                jobs = self.debug['a_jobs']

            def loadA(i):
                ci, r, rho, j = jobs[i]
                s3 = i % 3
                self.dma('sync', Qa[s3][0][:], bass.AP(SC['QA'], (r * 128 * j + rho) * 384, [[r * 384, 128], [1, 384]]),
                         [SC['b']], [Qa[s3][1]], 'aQ%d' % s3)
                e0 = r * (128 * j - 64) + rho + 1024
                self.dma('sync', Ka[s3][0][:],
                         bass.AP(SC['KAx'], e0 * 384, [[r * 384, 128], [r * 384 * 128, 2], [1, 384]]),
                         [SC['b']], [Ka[s3][1]], 'aK%d' % s3)
                self.dma('sync', Va[s3][0][:],
                         bass.AP(SC['VAx'], e0 * 390, [[r * 390, 128], [r * 390 * 128, 2], [1, 390]]),
                         [SC['b']], [Va[s3][1]], 'aV%d' % s3)

            def stage1(i):
                ci, r, rho, j = jobs[i]
                s3 = i % 3
                s2 = i % 2
                for pi in range(3):
                    T('transpose', [Qa[s3][1], b_idb], [TRq[1]], out=TRq[0][:, pi, :],
                      in_=Qa[s3][0][:, pi * 128:(pi + 1) * 128], identity=idb[:])
                V('tensor_copy', [TRq[1]], [QTa[s2][1]], out=QTa[s2][0][0:64, 0, :, :], in_=TRq[0][0:64, :, :])
                V('tensor_copy', [TRq[1]], [QTa[s2][1]], out=QTa[s2][0][64:128, 1, :, :], in_=TRq[0][64:128, :, :])
                for pi in range(3):
                    for c in range(2):
                        T('transpose', [Ka[s3][1], b_idb], [TRk[1]], out=TRk[0][:, pi, c, :],
                          in_=Ka[s3][0][:, c, pi * 128:(pi + 1) * 128], identity=idb[:])
                V('tensor_copy', [TRk[1]], [KTa[s2][1]], out=KTa[s2][0][:], in_=TRk[0][:])

            astop = self.debug.get('a_stop', 99) if self.debug else 99

            def stage2(i):
                ci, r, rho, j = jobs[i]
                s3 = i % 3
                s2 = i % 2
                s_t, s_b = Sa[s2]
                p_t, p_b = PTa[s2]
                if astop < 2:
                    return
                for pi in range(3):
                    sp_t, sp_b = SP[pi]
                    for hh in range(2):
                        for c in range(2):
                            T('matmul', [KTa[s2][1], QTa[s2][1]], [sp_b], out=sp_t[:, hh, c, :],
                              lhsT=KTa[s2][0][:, pi, c, :],
                              rhs=QTa[s2][0][:, hh, pi, :], start=True, stop=True)
                    if self.debug and self.debug.get('a_nobias'):
                        continue
                    V('tensor_tensor', [sp_b, biasA[1]], [s_b], out=s_t[:, 2 * pi:2 * pi + 2, :, :], in0=sp_t[:],
                      in1=biasA[0][:, ci, 2 * pi:2 * pi + 2, :, :], op=ALU.add)
                if astop < 3:
                    return
                A('activation', [s_b], [p_b], out=p_t[:], in_=s_t[:], func=AF.Exp)
                os_t, os_b = OTs[s2]
                if astop < 4:
                    return
                for g3 in range(2):
                    ot_t, ot_b = OTp[g3]
                    for hh in range(3):
                        h = g3 * 3 + hh
                        for c in range(2):
                            T('matmul', [Va[s3][1], p_b], [ot_b], out=ot_t[:, hh, :],
                              lhsT=Va[s3][0][:, c, h * 65:(h + 1) * 65], rhs=p_t[:, h, c, :],
                              start=(c == 0), stop=(c == 1))
                    V('tensor_copy', [ot_b], [os_b], out=os_t[:, g3 * 3:g3 * 3 + 3, :], in_=ot_t[:])
                if astop < 5:
                    return
                for h in range(6):
                    T('transpose', [os_b, b_idf], [TOp[1]], out=TOp[0][:, h, :], in_=os_t[:, h, :],
                      identity=idf[0:65, 0:65])
                o_t, o_b = Oo[s2]
                V('tensor_copy', [TOp[1]], [o_b], out=o_t[:], in_=TOp[0][:])
                self.dma('sync', bass.AP(SC['OA'], ci * 4096 * 390 + (r * 128 * j + rho) * 390, [[r * 390, 128], [1, 390]]),
                         o_t[:].rearrange("p h d -> p (h d)"), [o_b], [SC['boa']], 'aOo%d' % s2)

            n = len(jobs)
            for i in range(min(2, n)):
                loadA(i)
            stage1(0)
            for i in range(n):
                if i + 2 < n:
                    loadA(i + 2)
                if i + 1 < n:
                    stage1(i + 1)
                stage2(i)
        self.P.phase_barrier()

    def phaseA2(self):
        nc, P, D, SC = self.nc, self.P, self.D, self.SC
        V, G, A, T, E = self.V, self.G, self.A, self.T, self.E
        with ExitStack() as ph:
            def sb(nm, shape, dt, n=1):
                r = []
                for i in range(n):
                    t = ph.enter_context(nc.sbuf_tensor(self.name(nm), shape, dt))
                    r.append((t, Buf(nm + str(i))))
                return r if n > 1 else r[0]
            P.barrier('sync', reads=[SC['boa']])
            O3 = sb("a2O", [128, 3, 6, 65], F32, 3)
            acc = sb("a2acc", [128, 6, 65], F32, 2)
            rc = sb("a2rc", [128, 6], F32, 2)
            oo = sb("a2o", [128, 6, 64], BF16, 2)
            for t in range(32):
                s3, s2 = t % 3, t % 2
                self.dma('sync', O3[s3][0][:].rearrange("p c h d -> p c (h d)"),
                         SC['OA'].ap()[:, t * 128:(t + 1) * 128, :].rearrange("c p d -> p c d"),
                         [SC['boa']], [O3[s3][1]], 'a2O%d' % s3)
                o3 = O3[s3][0]
                G('tensor_tensor', [O3[s3][1]], [acc[s2][1]], out=acc[s2][0][:], in0=o3[:, 0], in1=o3[:, 1], op=ALU.add)
                G('tensor_tensor', [O3[s3][1], acc[s2][1]], [acc[s2][1]], out=acc[s2][0][:], in0=acc[s2][0][:],
                  in1=o3[:, 2], op=ALU.add)
                V('reciprocal', [acc[s2][1]], [rc[s2][1]], out=rc[s2][0][:].unsqueeze(2), in_=acc[s2][0][:, :, 64:65])
                V('tensor_tensor', [acc[s2][1], rc[s2][1]], [oo[s2][1]], out=oo[s2][0][:], in0=acc[s2][0][:, :, 0:64],
                  in1=bc(rc[s2][0][:].unsqueeze(2), [128, 6, 64]), op=ALU.mult)
                self.dma('sync', SC['MIX'].ap()[t * 128:(t + 1) * 128, 0:384], oo[s2][0][:].rearrange("p h d -> p (h d)"),
                         [oo[s2][1]], [SC['bmix']], 'a2o%d' % s2)
        self.P.phase_barrier()

    def phase4a(self, L, x_own, x1_d):
        nc, P, D, SC = self.nc, self.P, self.D, self.SC
        V, G, A, T, E = self.V, self.G, self.A, self.T, self.E
        with ExitStack() as ph:
            def sb(nm, shape, dt, n=1):
                r = []
                for i in range(n):
                    t = ph.enter_context(nc.sbuf_tensor(self.name(nm), shape, dt))
                    r.append((t, Buf(nm + str(i))))
                return r if n > 1 else r[0]

            def ps(nm, shape, dt, n=1):
                r = []
                for i in range(n):
                    t = psum_view(ph, nc, self.name(nm), shape, dt)
                    r.append((t, Buf(nm + str(i))))
                return r if n > 1 else r[0]

            P.barrier('sync', reads=[SC['bmix']])
            idb, b_idb = sb("idb", [128, 128], BF16)
            self.dma('sync', idb[:], D['idb'].ap(), [], [b_idb], 'idb')
            wo, b_wo = sb("wo", [128, 8, 1024], BF16)
            go, b_go = sb("go", [128, 8], F32)
            self.dma('sync', go[:], D['out_norm'].ap()[L].rearrange("(c p) -> p c", p=128), [], [b_go], 'go',
                     allow_slow_non_contiguous=True)
            stg = sb("stg4a", [128, 1024], F32, 2)
            for c in range(8):
                self.dma('sync', stg[c % 2][0][:], D['w_out'].ap()[L, c * 128:(c + 1) * 128, :], [], [stg[c % 2][1]],
                         'stg4a%d' % (c % 2))
                E('vector' if c % 2 == 0 else 'gpsimd', 'tensor_scalar', [stg[c % 2][1], b_go], [b_wo], out=wo[:, c, :],
                  in0=stg[c % 2][0][:], scalar1=go[:, c:c + 1], scalar2=None, op0=ALU.mult)
            invd3, b_invd3 = sb("invd3", [128, 3], F32)
            nh3, b_nh3 = sb("nh3", [128, 3], F32)
            P.op('gpsimd', lambda e: e.memset(invd3[:], 1.0 / 384), [], [b_invd3])
            P.op('gpsimd', lambda e: e.memset(invd3[:, 1:2], 1.0 / 256), [], [b_invd3])
            P.op('gpsimd', lambda e: e.memset(nh3[:], -0.5), [], [b_nh3])
            mx = sb("mx", [128, 1024], BF16, 3)
            xt = sb("x4a", [128, 1024], F32, 3)
            sq, b_sq = sb("sq4a", [128, 1024], F32)
            ss = sb("ss4a", [128, 3], F32, 2)
            rs = sb("rs4a", [128, 3], F32, 2)
            mn = sb("mn", [128, 1024], BF16, 2)
            mT = sb("mT", [128, 8, 128], BF16, 2)
            TR = ps("TR4a", [128, 8, 128], BF16, 2)
            Y = ps("Y4a", [128, 1024], F32, 2)
            grp = [(0, 384), (384, 640), (640, 1024)]
            def load4a(t):
                s3 = t % 3
                rows = slice(t * 128, (t + 1) * 128)
                self.dma('sync', mx[s3][0][:], SC['MIX'].ap()[rows, :], [SC['bmix']], [mx[s3][1]], 'mx%d' % s3)
                self.dma('sync', xt[s3][0][:], x_own[rows, :], [], [xt[s3][1]], 'x4a%d' % s3)
            load4a(0)

            def s1_4a(t):
                s3, s2 = t % 3, t % 2
                rows = slice(t * 128, (t + 1) * 128)
                if t + 1 < 32:
                    load4a(t + 1)
                m_t, m_b = mx[s3]
                V('tensor_tensor', [m_b], [b_sq], out=sq[:], in0=m_t[:], in1=m_t[:], op=ALU.mult)
                for gi, (a, b) in enumerate(grp):
                    V('tensor_reduce', [b_sq], [ss[s2][1]], out=ss[s2][0][:, gi:gi + 1], in_=sq[:, a:b], axis=AX.X,
                      op=ALU.add)
                G('tensor_tensor', [ss[s2][1], b_invd3], [rs[s2][1]], out=rs[s2][0][:], in0=ss[s2][0][:], in1=invd3[:],
                  op=ALU.mult)
                G('tensor_scalar', [rs[s2][1]], [rs[s2][1]], out=rs[s2][0][:], in0=rs[s2][0][:], scalar1=EPS, scalar2=None,
                  op0=ALU.add)
                G('tensor_tensor', [rs[s2][1], b_nh3], [rs[s2][1]], out=rs[s2][0][:], in0=rs[s2][0][:], in1=nh3[:],
                  op=ALU.pow)
                for gi, (a, b) in enumerate(grp):
                    E('vector' if gi != 1 else 'gpsimd', 'tensor_scalar', [m_b, rs[s2][1]], [mn[s2][1]],
                      out=mn[s2][0][:, a:b], in0=m_t[:, a:b], scalar1=rs[s2][0][:, gi:gi + 1], scalar2=None, op0=ALU.mult)
                for c in range(8):
                    T('transpose', [mn[s2][1], b_idb], [TR[s2][1]], out=TR[s2][0][:, c, :],
                      in_=mn[s2][0][:, c * 128:(c + 1) * 128], identity=idb[:])
                A('activation', [TR[s2][1]], [mT[s2][1]], out=mT[s2][0][:], in_=TR[s2][0][:], func=AF.Copy)

            def s2_4a(t):
                s3, s2 = t % 3, t % 2
                rows = slice(t * 128, (t + 1) * 128)
                for hf in range(2):
                    for c in range(8):
                        T('matmul', [mT[s2][1], b_wo], [Y[s2][1]], out=Y[s2][0][:, hf * 512:(hf + 1) * 512],
                          lhsT=mT[s2][0][:, c, :], rhs=wo[:, c, hf * 512:(hf + 1) * 512], start=(c == 0), stop=(c == 7))
                V('tensor_tensor', [Y[s2][1], xt[s3][1]], [xt[s3][1]], out=xt[s3][0][:], in0=Y[s2][0][:],
                  in1=xt[s3][0][:], op=ALU.add)
                self.dma('sync', x1_d[rows, :], xt[s3][0][:], [xt[s3][1]], [SC['bx1']], 'x4ao%d' % s3)

            s1_4a(0)
            for t in range(32):
                if t + 1 < 32:
                    s1_4a(t + 1)
                s2_4a(t)
        self.P.phase_barrier()

    def phase4b(self, L, x1_d, out_d, b_out):
        nc, P, D, SC = self.nc, self.P, self.D, self.SC
        V, G, A, T, E = self.V, self.G, self.A, self.T, self.E
        with ExitStack() as ph:
            def sb(nm, shape, dt, n=1):
                r = []
                for i in range(n):
                    t = ph.enter_context(nc.sbuf_tensor(self.name(nm), shape, dt))
                    r.append((t, Buf(nm + str(i))))
                return r if n > 1 else r[0]

            def ps(nm, shape, dt, n=1):
                r = []
                for i in range(n):
                    t = psum_view(ph, nc, self.name(nm), shape, dt)
                    r.append((t, Buf(nm + str(i))))
                return r if n > 1 else r[0]

            P.barrier('sync', reads=[SC['bx1']])
            idb, b_idb = sb("idb", [128, 128], BF16)
            self.dma('sync', idb[:], D['idb'].ap(), [], [b_idb], 'idb')
            wu, b_wu = sb("wu", [128, 8, 4096], BF16)
            wd, b_wd = sb("wd", [128, 32, 1024], BF16)
            gm, b_gm = sb("gm", [128, 8], F32)
            self.dma('sync', gm[:], D['norm_mlp'].ap()[L].rearrange("(c p) -> p c", p=128), [], [b_gm], 'gm',
                     allow_slow_non_contiguous=True)
            stg = sb("stg4b", [128, 1024], F32, 2)
            k = 0
            for c in range(8):
                for hf in range(4):
                    s = k % 2
                    self.dma('sync', stg[s][0][:], D['w_up'].ap()[L, c * 128:(c + 1) * 128, hf * 1024:(hf + 1) * 1024], [],
                             [stg[s][1]], 'stg4b%d' % s)
                    E('vector' if k % 2 == 0 else 'gpsimd', 'tensor_scalar', [stg[s][1], b_gm], [b_wu],
                      out=wu[:, c, hf * 1024:(hf + 1) * 1024], in0=stg[s][0][:], scalar1=gm[:, c:c + 1], scalar2=None,
                      op0=ALU.mult)
                    k += 1
            for c2 in range(32):
                s = k % 2
                self.dma('sync', stg[s][0][:], D['w_down'].ap()[L, c2 * 128:(c2 + 1) * 128, :], [],
                         [stg[s][1]], 'stg4b%d' % s)
                E('vector' if k % 2 == 0 else 'gpsimd', 'tensor_copy', [stg[s][1]], [b_wd],
                  out=wd[:, c2, :], in_=stg[s][0][:])
                k += 1
            nh1, b_nh1 = sb("nh1", [128, 1], F32)
            P.op('gpsimd', lambda e: e.memset(nh1[:], -0.5), [], [b_nh1])
            xc = sb("x4b", [128, 2, 1024], F32, 2)
            junk, b_junk = sb("junk4b", [128, 1024], BF16)
            ss = sb("ss4b", [128, 2], F32, 2)
            h2 = [sb("h2", [128, 2, 1024], BF16)] * 2
            h2T = [sb("h2T", [128, 8, 256], BF16)] * 2
            uT = [sb("uT", [128, 32, 256], BF16)] * 2
            rr = sb("rr", [128, 2, 256], F32, 3)
            TR = ps("TR4b", [128, 8, 128], BF16)
            U = ps("U4b", [128, 2, 256], F32, 2)
            Z = ps("Z4b", [128, 1024], F32, 2)
            def load4b(ch):
                s2 = ch % 2
                rows = slice(ch * 256, (ch + 1) * 256)
                self.dma('sync', xc[s2][0][:], x1_d[rows, :].rearrange("(t p) d -> p t d", p=128), [SC['bx1']],
                         [xc[s2][1]], 'x4b%d' % s2)
            load4b(0)
            for ch in range(16):
                s2 = ch % 2
                rows = slice(ch * 256, (ch + 1) * 256)
                x_t, x_b = xc[s2]
                if ch + 1 < 16:
                    load4b(ch + 1)
                for t in range(2):
                    A('activation', [x_b], [b_junk, ss[s2][1]], out=junk[:], in_=x_t[:, t, :], func=AF.Square,
                      accum_out=ss[s2][0][:, t:t + 1])
                G('tensor_scalar', [ss[s2][1]], [ss[s2][1]], out=ss[s2][0][:], in0=ss[s2][0][:], scalar1=1.0 / 1024,
                  scalar2=EPS, op0=ALU.mult, op1=ALU.add)
                G('tensor_tensor', [ss[s2][1], b_nh1], [ss[s2][1]], out=ss[s2][0][:], in0=ss[s2][0][:],
                  in1=bc(nh1[:], [128, 2]), op=ALU.pow)
                for t in range(2):
                    A('activation', [x_b, ss[s2][1]], [h2[s2][1]], out=h2[s2][0][:, t, :], in_=x_t[:, t, :], func=AF.Copy,
                      scale=ss[s2][0][:, t:t + 1])
                    for c in range(8):
                        T('transpose', [h2[s2][1], b_idb], [TR[1]], out=TR[0][:, c, :],
                          in_=h2[s2][0][:, t, c * 128:(c + 1) * 128], identity=idb[:])
                    V('tensor_copy', [TR[1]], [h2T[s2][1]], out=h2T[s2][0][:, :, t * 128:(t + 1) * 128], in_=TR[0][:])
                for f2 in range(16):
                    u_t, u_b = U[f2 % 2]
                    for ff in range(2):
                        fc = f2 * 2 + ff
                        for c in range(8):
                            T('matmul', [h2T[s2][1], b_wu], [u_b], out=u_t[:, ff, :], lhsT=wu[:, c, fc * 128:(fc + 1) * 128],
                              rhs=h2T[s2][0][:, c, :], start=(c == 0), stop=(c == 7))
                    r_t, r_b = rr[f2 % 3]
                    A('activation', [u_b], [r_b], out=r_t[:], in_=u_t[:], func=AF.Relu)
                    E('vector' if f2 % 2 == 0 else 'gpsimd', 'tensor_tensor', [r_b], [uT[s2][1]],
                      out=uT[s2][0][:, 2 * f2:2 * f2 + 2, :], in0=r_t[:], in1=r_t[:], op=ALU.mult)
                for t in range(2):
                    z_t, z_b = Z[t]
                    for hf in range(2):
                        for fc in range(32):
                            T('matmul', [uT[s2][1], b_wd], [z_b], out=z_t[:, hf * 512:(hf + 1) * 512],
                              lhsT=uT[s2][0][:, fc, t * 128:(t + 1) * 128], rhs=wd[:, fc, hf * 512:(hf + 1) * 512],
                              start=(fc == 0), stop=(fc == 31))
                    V('tensor_tensor', [z_b, x_b], [x_b], out=x_t[:, t, :], in0=z_t[:], in1=x_t[:, t, :], op=ALU.add)
                self.dma('sync', out_d[rows, :].rearrange("(t p) d -> p t d", p=128), x_t[:], [x_b], [b_out],
                         'x4bo%d' % s2)
        self.P.phase_barrier()

    def layer(self, L, x_own, x_oth, x_halo, x1_d, out_d, b_out, with_bias_setup=True, phases=None, pos_key='pos',
              valid_key='valid', halo_rows=None, reuse_ckv=False):
        ph = phases or ['bias', 'p1', 'C', 'B', 'A', 'A2', '4a', '4b']
        if with_bias_setup and 'bias' in ph:
            self.bias_setup()
        if 'p1' in ph:
            self.phase1(L, x_own, x_oth, x_halo, pos_key=pos_key, valid_key=valid_key, halo_rows=halo_rows,
                        skip_oth=reuse_ckv, skip_ck=reuse_ckv)
        if 'C' in ph:
            self.phaseC()
        if 'B' in ph:
            self.phaseB(L)
        if 'A' in ph:
            self.phaseA(cfgs=self.debug.get('cfgs', (0, 1, 2)) if self.debug else (0, 1, 2))
        if 'A2' in ph:
            self.phaseA2()
        if '4a' in ph:
            self.phase4a(L, x_own, x1_d)
        if '4b' in ph:
            self.phase4b(L, x1_d, out_d, b_out)


NLAYER = 2


def t5_bucket_np(rel):
    half, exact = 16, 8
    n = np.abs(rel)
    far = exact + (np.log(np.maximum(n, 1).astype(np.float32) / np.float32(exact))
                   / np.float32(math.log(1024 / exact)) * np.float32(half - exact)).astype(np.int32)
    far = np.minimum(far, half - 1)
    return np.where(rel > 0, half, 0) + np.where(n < exact, n, far)


def make_oh():
    oh = np.zeros((33, 1660), np.float32)
    d = np.arange(-255, 256)
    bk = t5_bucket_np(d)
    for i, dd in enumerate(d):
        if abs(dd) <= 128:
            oh[bk[i], i] = 1
        else:
            oh[32, i] = 1
    for ci, r in enumerate((1, 4, 16)):
        d = np.arange(-191, 192)
        bk = t5_bucket_np(d * r)
        for i, dd in enumerate(d):
            if abs(dd) <= 64:
                oh[bk[i], 511 + 383 * ci + i] = 1
            else:
                oh[32, 511 + 383 * ci + i] = 1
    return oh


WNAMES = ['norm_mix', 'w_in', 'qk_gain_a', 'qk_gain_b', 'sink_b', 'q_lat_gain', 'kv_lat_gain', 'w_uq', 'w_ukv',
          'qk_gain_c', 'out_norm', 'w_out', 'norm_mlp', 'w_up', 'w_down']


def declare(nc, NL):
    D = {}

    def inp(n, shape, dt=F32):
        D[n] = nc.dram_tensor(n, shape, dt, kind="ExternalInput")
    inp('x_own', [4096, 1024]); inp('x_oth', [4096, 1024]); inp('x_halo', [2048, 1024]); inp('x_halo2', [2048, 1024])
    inp('valid', [128, 16]); inp('valid2', [128, 16]); inp('pos', [128, 64], I32); inp('pos2', [128, 64], I32)
    inp('invf', [1, 16]); inp('idb', [128, 128], BF16); inp('idf', [128, 128])
    inp('norm_mix', [NL, 1024]); inp('w_in', [NL, 1024, 2080]); inp('qk_gain_a', [NL, 2, 64])
    inp('qk_gain_b', [NL, 2, 64]); inp('sink_b', [NL, 4]); inp('q_lat_gain', [NL, 256]); inp('kv_lat_gain', [NL, 128])
    inp('w_uq', [NL, 256, 576]); inp('w_ukv', [NL, 128, 768]); inp('qk_gain_c', [NL, 2, 96]); inp('out_norm', [NL, 1024])
    inp('w_out', [NL, 1024, 1024]); inp('norm_mlp', [NL, 1024]); inp('w_up', [NL, 1024, 4096]); inp('w_down', [NL, 4096, 1024])
    inp('rel_bias_table', [32, 10]); inp('oh', [33, 1660])
    SC = {}

    def scr(n, shape, dt=BF16):
        SC[n] = nc.dram_tensor(n, shape, dt, kind="Internal")
    scr('QA', [4096, 384]); scr('KAx', [6144, 384]); scr('VAx', [6144, 390]); scr('QB', [4096, 256])
    scr('KBx', [6144, 128]); scr('VBx', [6144, 130]); scr('QCT', [6, 96, 4096]); scr('KCT', [6, 96, 8192])
    scr('VC', [8192, 390]); scr('MIX', [4096, 1024]); scr('OA', [3, 4096, 390], F32); scr('FV', [10, 1660], F32)
    scr('X1', [4096, 1024], F32); scr('XA', [4096, 1024], F32); scr('XB', [4096, 1024], F32)
    SC['b'] = Buf('scratch', multi=True)
    SC['bmix'] = Buf('mix', multi=True)
    SC['boa'] = Buf('oa', multi=True)
    SC['bfv'] = Buf('fv', multi=True)
    SC['bx1'] = Buf('x1', multi=True)
    return D, SC


def make_inputs(inputs, core):
    b, half = core // 2, core % 2
    xs = np.asarray(inputs['x'], dtype=np.float32)[b]
    own = xs[half * 4096:(half + 1) * 4096]
    oth = xs[(1 - half) * 4096:(2 - half) * 4096]
    halo = np.zeros((2048, 1024), np.float32)
    valid = np.zeros((2048,), np.float32)
    halo2 = np.zeros((2048, 1024), np.float32)
    valid2 = np.zeros((2048,), np.float32)
    if half == 1:
        halo[0:1024] = oth[3072:4096]
        valid[0:1024] = 1
        halo2[1024:2048] = own[0:1024]
        valid2[1024:2048] = 1
    else:
        halo[1024:2048] = oth[0:1024]
        valid[1024:2048] = 1
        halo2[0:1024] = own[3072:4096]
        valid2[0:1024] = 1
    pos = np.asarray(inputs['positions'][b])
    p_own = pos[half * 4096:(half + 1) * 4096]
    p_oth = pos[(1 - half) * 4096:(2 - half) * 4096]
    pos_l = np.concatenate([p_own, p_oth])
    pos_l2 = np.concatenate([p_oth, p_own])
    m = {
        'x_own': np.ascontiguousarray(own), 'x_oth': np.ascontiguousarray(oth), 'x_halo': halo, 'x_halo2': halo2,
        'valid': np.ascontiguousarray(valid.reshape(16, 128).T),
        'valid2': np.ascontiguousarray(valid2.reshape(16, 128).T),
        'pos': np.ascontiguousarray(pos_l.reshape(64, 128).T.astype(np.int32)),
        'pos2': np.ascontiguousarray(pos_l2.reshape(64, 128).T.astype(np.int32)),
        'invf': (10000.0 ** (-np.arange(16, dtype=np.float32) / 16)).astype(np.float32).reshape(1, 16),
        'idb': np.eye(128).astype(ml_dtypes.bfloat16), 'idf': np.eye(128).astype(np.float32),
        'rel_bias_table': np.ascontiguousarray(inputs['rel_bias_table'], dtype=np.float32), 'oh': make_oh(),
    }
    for k in WNAMES:
        m[k] = np.ascontiguousarray(np.asarray(inputs[k], dtype=np.float32))
    return m


def build_program():
    nc = bass.Bass("TRN2", target_bir_lowering=False)
    D, SC = declare(nc, NLAYER)
    out_d = nc.dram_tensor('out', [4096, 1024], F32, kind="ExternalOutput")
    b_xa = Buf('xa', multi=True)
    b_xb = Buf('xb', multi=True)
    b_out = Buf('out', multi=True)
    with ExitStack() as es:
        P = Prog(nc, es)
        LB = LayerBuilder(nc, P, D, debug={})
        LB.SC = SC
        XA, XB = SC['XA'].ap(), SC['XB'].ap()
        LB.layer(0, D['x_own'].ap(), D['x_oth'].ap(), D['x_halo'].ap(), SC['X1'].ap(), XA, b_xa)
        LB.layer(0, D['x_oth'].ap(), D['x_own'].ap(), D['x_halo2'].ap(), SC['X1'].ap(), XB, b_xb,
                 with_bias_setup=False, pos_key='pos2', valid_key='valid2', reuse_ckv=True)

        def halo_rows(t):
            r0 = 3072 + t * 128 if t < 8 else (t - 8) * 128
            return XB[r0:r0 + 128, :]
        P.barrier('sync', reads=[b_xa, b_xb])
        LB.layer(1, XA, XB, None, SC['X1'].ap(), out_d.ap(), b_out, with_bias_setup=False, halo_rows=halo_rows)
        P.barrier('sync', reads=[b_out])
        P.emit()
    return nc


def kernel(**inputs):
    nc = build_program()
    in_maps = [make_inputs(inputs, c) for c in range(8)]
    res = run_bass_kernel_spmd(nc, in_maps, core_ids=list(range(8)))
    x = np.asarray(inputs['x'])
    out = np.empty(x.shape, np.float32)
    for c in range(8):
        b, half = c // 2, c % 2
        out[b, half * 4096:(half + 1) * 4096] = np.asarray(res.results[c]['out'], dtype=np.float32)
    return out
```

```python
import math
import ml_dtypes
from concourse.bass_utils import run_bass_kernel_spmd
import numpy as np
import concourse.bass as bass
import concourse.mybir as mybir
from contextlib import ExitStack

F32 = mybir.dt.float32
BF16 = mybir.dt.bfloat16
I32 = mybir.dt.int32
ALU = mybir.AluOpType
AF = mybir.ActivationFunctionType
AX = mybir.AxisListType

ENGS = ['sync', 'scalar', 'vector', 'gpsimd', 'tensor']
SEM_ROT = 24000


class Buf:
    __slots__ = ('name', 'writer', 'readers', 'dreaders', 'multi', 'mw')

    def __init__(self, name, multi=False):
        self.name = name
        self.writer = None
        self.readers = {}
        self.dreaders = []
        self.multi = multi
        self.mw = []


class Op:
    __slots__ = ('eng', 'fn', 'deps', 'idx', 'signal', 'is_dma', 'lane', 'ev', 'raw', 'barrier')


class Prog:
    def __init__(self, nc, es):
        self.nc = nc
        self.es = es
        self.ops = {e: [] for e in ENGS}
        self.order = []
        self.lanes = {}
        self.nsem = 0
        self.fence = []
        self.fence_pending = set()
        self.phase_lanes = {}

    def phase_barrier(self):
        fence = []
        for e in ENGS:
            for o in reversed(self.ops[e]):
                if not o.is_dma and not o.barrier:
                    fence.append(o)
                    break
        last = {}
        for o in self.order:
            if o.is_dma:
                last[o.lane] = o
        fence += list(last.values())
        self.fence = fence
        self.fence_pending = set(ENGS)
        self.phase_lanes = {}

    def new_sem(self, name):
        self.nsem += 1
        return self.es.enter_context(self.nc.semaphore(name))

    def sb(self, name, shape, dt):
        return self.es.enter_context(self.nc.sbuf_tensor(name, shape, dt))

    def ps(self, name, shape, dt):
        return self.es.enter_context(self.nc.psum_tensor(name, shape, dt))

    def barrier(self, eng, reads=(), writes=()):
        o = self.op(eng, lambda e: None, reads, writes)
        o.barrier = True
        return o

    def op(self, eng, fn, reads=(), writes=(), lane=None):
        o = Op()
        o.eng = eng
        o.fn = fn
        o.barrier = False
        o.is_dma = lane is not None
        if lane is not None:
            if lane not in self.phase_lanes:
                self.phase_lanes[lane] = 'L%d' % len(self.phase_lanes)
            lane = self.phase_lanes[lane]
        o.lane = lane
        o.signal = False
        o.ev = None
        deps = {}
        raw = set()
        for b in reads:
            if b.writer is not None:
                deps[id(b.writer)] = b.writer
                raw.add(id(b.writer))
            for w in b.mw:
                deps[id(w)] = w
                raw.add(id(w))
        for b in writes:
            if b.writer is not None and not b.multi:
                deps[id(b.writer)] = b.writer
                raw.add(id(b.writer))
            for r in b.readers.values():
                deps[id(r)] = r
            for r in b.dreaders:
                deps[id(r)] = r
        if eng in self.fence_pending:
            self.fence_pending.discard(eng)
            for w in self.fence:
                deps[id(w)] = w
        deps.pop(id(o), None)
        o.deps = list(deps.values())
        o.raw = raw
        for b in writes:
            if b.multi:
                b.mw.append(o)
            else:
                b.writer = o
            b.readers = {}
            b.dreaders = []
        for b in reads:
            if b.multi:
                continue
            if o.is_dma:
                b.dreaders.append(o)
            else:
                b.readers[eng] = o
        o.idx = len(self.ops[eng])
        self.ops[eng].append(o)
        self.order.append(o)
        return o

    def dma(self, q, out, in_, reads=(), writes=(), lane=None, **kw):
        assert lane is not None
        return self.op(q, lambda e: e.dma_start(out=out, in_=in_, **kw), reads, writes, lane=lane)

    def emit(self):
        nc = self.nc
        for o in self.order:
            for d in o.deps:
                if d.is_dma:
                    continue
                if d.barrier:
                    assert d.eng == o.eng, 'barrier dep across engines'
                    continue
                if d.eng == o.eng and not o.is_dma:
                    if o.eng == 'tensor':
                        continue
                    if id(d) not in o.raw:
                        continue
                d.signal = True
        esems = {}
        for e in ENGS:
            cnt = 0
            cur = None
            for o in self.ops[e]:
                if o.is_dma:
                    ln = self.lanes.get(o.lane)
                    if ln is None:
                        ln = [self.new_sem('l_%s' % o.lane), 0]
                        self.lanes[o.lane] = ln
                    ln[1] += 16
                    o.ev = (ln[0], ln[1])
                elif o.signal:
                    if cur is None or cnt >= SEM_ROT:
                        cur = self.new_sem('e_%s_%d' % (e, len(esems)))
                        esems[(e, len(esems))] = cur
                        cnt = 0
                    cnt += 1
                    o.ev = (cur, cnt)
        blk = self.es.enter_context(nc.Block())
        prog = self

        def run(e, eng):
            waited = {}
            for o in prog.ops[e]:
                need = {}
                for d in o.deps:
                    if not d.is_dma:
                        if d.barrier:
                            continue
                        if d.eng == o.eng and not o.is_dma:
                            if o.eng == 'tensor' or id(d) not in o.raw:
                                continue
                    sem, val = d.ev
                    k = id(sem)
                    if k not in need or need[k][1] < val:
                        need[k] = (sem, val)
                for k, (sem, val) in need.items():
                    if waited.get(k, 0) >= val:
                        continue
                    eng.wait_ge(sem, val)
                    waited[k] = val
                ins = o.fn(eng)
                if ins is None:
                    continue
                if o.is_dma:
                    ins.then_inc(o.ev[0], 16)
                elif o.signal:
                    ins.then_inc(o.ev[0], 1)

        @blk.sync
        def _(eng):
            run('sync', eng)

        @blk.scalar
        def _(eng):
            run('scalar', eng)

        @blk.vector
        def _(eng):
            run('vector', eng)

        @blk.gpsimd
        def _(eng):
            run('gpsimd', eng)

        @blk.tensor
        def _(eng):
            run('tensor', eng)

import numpy as np
import math

EPS = 1e-6
NEGB = -30000.0
TWO_PI_S = 6.2831845


def bc(ap, shape):
    return ap.to_broadcast(list(shape))


def psum_view(ph, nc, name, shape, dt):
    esz = 4 if dt == F32 else 2
    n = 1
    for d in shape[1:]:
        n *= d
    per_bank = 2048 // esz
    tot = ((n + per_bank - 1) // per_bank) * per_bank
    t = ph.enter_context(nc.psum_tensor(name, [128, tot], dt))
    v = t[0:shape[0], 0:n]
    if len(shape) == 3:
        v = v.rearrange("p (a b) -> p a b", a=shape[1])
    elif len(shape) == 4:
        v = v.rearrange("p (a b c) -> p a b c", a=shape[1], b=shape[2])
    return v


class LayerBuilder:
    def __init__(self, nc, P, D, debug=False):
        self.nc = nc
        self.P = P
        self.D = D
        self.debug = debug
        self.uid = 0

    def name(self, s):
        self.uid += 1
        return "%s_%d" % (s, self.uid)

    def V(self, fn, reads, writes, **kw):
        return self.P.op('vector', lambda e: getattr(e, fn)(**kw), reads, writes)

    def G(self, fn, reads, writes, **kw):
        return self.P.op('gpsimd', lambda e: getattr(e, fn)(**kw), reads, writes)

    def A(self, fn, reads, writes, **kw):
        return self.P.op('scalar', lambda e: getattr(e, fn)(**kw), reads, writes)

    def T(self, fn, reads, writes, **kw):
        return self.P.op('tensor', lambda e: getattr(e, fn)(**kw), reads, writes)

    def E(self, eng, fn, reads, writes, **kw):
        return self.P.op(eng, lambda e: getattr(e, fn)(**kw), reads, writes)

    def dma(self, q, out, in_, reads, writes, lane, **kw):
        return self.P.dma(q, out, in_, reads=reads, writes=writes, lane=lane, **kw)

    def phase1(self, L, x_own, x_oth, x_halo, first=True, pos_key='pos', valid_key='valid', halo_rows=None,
               skip_oth=False, skip_ck=False):
        nc, P, D = self.nc, self.P, self.D
        V, G, A, T, E = self.V, self.G, self.A, self.T, self.E
        NS = 4
        with ExitStack() as ph:
            def sb(nm, shape, dt, n=1):
                r = []
                for i in range(n):
                    t = ph.enter_context(nc.sbuf_tensor(self.name(nm), shape, dt))
                    r.append((t, Buf(nm + str(i))))
                return r if n > 1 else r[0]

            def ps(nm, shape, dt):
                t = psum_view(ph, nc, self.name(nm), shape, dt)
                return (t, Buf(nm))

            setup = ExitStack()

            def sbs(nm, shape, dt):
                t = setup.enter_context(nc.sbuf_tensor(self.name(nm), shape, dt))
                return (t, Buf(nm))

            idb, b_idb = sb("idb", [128, 128], BF16)
            wib, b_wib = sb("wib", [128, 8, 2080], BF16)
            wuq, b_wuq = sb("wuq", [128, 2, 576], BF16)
            wukv, b_wukv = sb("wukv", [128, 768], BF16)
            g8, b_g8 = sb("g8", [128, 8], F32)
            gq2, b_gq2 = sb("gq2", [128, 2], F32)
            gkv1, b_gkv1 = sb("gkv1", [128, 1], F32)
            ga, b_ga = sb("ga", [128, 2, 64], F32)
            gb, b_gb = sb("gb", [128, 2, 64], F32)
            gc, b_gc = sb("gc", [128, 2, 96], F32)
            GAq, b_GAq = sb("GAq", [128, 64], F32)
            GBq, b_GBq = sb("GBq", [128, 64], F32)
            GCq, b_GCq = sb("GCq", [128, 96], F32)
            invf, b_invf = sb("invf", [128, 16], F32)
            sin_t, b_sin = sb("sin_t", [128, 64, 16], F32)
            cos_t, b_cos = sb("cos_t", [128, 64, 16], F32)
            invd, b_invd = sb("invd", [128, 24], F32)
            nh24, b_nh24 = sb("nh24", [128, 24], F32)
            valid, b_valid = sb("valid", [128, 16], F32)
            posi, b_posi = sbs("posi", [128, 64], I32)
            posf, b_posf = sbs("posf", [128, 64], F32)
            ang, b_ang = sbs("ang", [128, 64, 16], F32)
            angk, b_angk = sbs("angk", [128, 64, 16], I32)
            angf, b_angf = sbs("angf", [128, 64, 16], F32)
            stage = [sbs("stage", [128, 2080], F32)] * 2
            stq, b_stq = sbs("stq", [128, 2, 576], F32)
            stkv, b_stkv = sbs("stkv", [128, 768], F32)

            self.dma('sync', idb[:], D['idb'].ap(), [], [b_idb], 'idb')
            self.dma('sync', g8[:], D['norm_mix'].ap()[L].rearrange("(c p) -> p c", p=128), [], [b_g8], 'g8',
                     allow_slow_non_contiguous=True)
            self.dma('sync', gq2[:], D['q_lat_gain'].ap()[L].rearrange("(c p) -> p c", p=128), [], [b_gq2], 'gq2',
                     allow_slow_non_contiguous=True)
            self.dma('sync', gkv1[:], D['kv_lat_gain'].ap()[L].rearrange("(c p) -> p c", p=128), [], [b_gkv1],
                     'gkv1', allow_slow_non_contiguous=True)
            self.dma('sync', ga[:], D['qk_gain_a'].ap()[L].rearrange("a d -> (a d)").partition_broadcast(128),
                     [], [b_ga], 'ga')
            self.dma('sync', gb[:], D['qk_gain_b'].ap()[L].rearrange("a d -> (a d)").partition_broadcast(128),
                     [], [b_gb], 'gb')
            self.dma('sync', gc[:], D['qk_gain_c'].ap()[L].rearrange("a d -> (a d)").partition_broadcast(128),
                     [], [b_gc], 'gc')
            self.dma('sync', invf[:], D['invf'].ap().rearrange("a d -> (a d)").partition_broadcast(128),
                     [], [b_invf], 'invf')
            self.dma('sync', posi[:], D[pos_key].ap(), [], [b_posi], 'posi')
            self.dma('sync', valid[:], D[valid_key].ap(), [], [b_valid], 'valid')
            V('scalar_tensor_tensor', [b_ga], [b_GAq], out=GAq[:], in0=ga[:, 0, :], scalar=0.125, in1=ga[:, 1, :],
              op0=ALU.mult, op1=ALU.mult)
            V('scalar_tensor_tensor', [b_gb], [b_GBq], out=GBq[:], in0=gb[:, 0, :], scalar=0.125, in1=gb[:, 1, :],
              op0=ALU.mult, op1=ALU.mult)
            V('tensor_scalar', [b_gc], [b_GCq], out=GCq[:], in0=gc[:, 0, :], scalar1=96.0 ** -0.5, scalar2=None,
              op0=ALU.mult)
            GCk = gc[:, 1, :]
            b_GCk = b_gc
            self.P.op('gpsimd', lambda e: e.memset(invd[:], 1.0 / 64), [], [b_invd])
            self.P.op('gpsimd', lambda e: e.memset(invd[:, 18:19], 1.0 / 256), [], [b_invd])
            self.P.op('gpsimd', lambda e: e.memset(invd[:, 19:20], 1.0 / 128), [], [b_invd])
            self.P.op('gpsimd', lambda e: e.memset(invd[:, 20:24], 1.0), [], [b_invd])
            self.P.op('gpsimd', lambda e: e.memset(nh24[:], -0.5), [], [b_nh24])

            V('tensor_copy', [b_posi], [b_posf], out=posf[:], in_=posi[:])
            V('tensor_tensor', [b_posf, b_invf], [b_ang], out=ang[:],
              in0=bc(posf[:].unsqueeze(2), [128, 64, 16]), in1=bc(invf[:].unsqueeze(1), [128, 64, 16]), op=ALU.mult)
            for (tab, b_tab, off) in ((sin_t, b_sin, 0.0), (cos_t, b_cos, 0.25)):
                V('tensor_scalar', [b_ang], [b_angf], out=angf[:], in0=ang[:], scalar1=1.0 / (2 * math.pi),
                  scalar2=off, op0=ALU.mult, op1=ALU.add)
                V('tensor_copy', [b_angf], [b_angk], out=angk[:], in_=angf[:])
                V('tensor_copy', [b_angk], [b_tab], out=tab[:], in_=angk[:])
                V('tensor_tensor', [b_angf, b_tab], [b_angf], out=angf[:], in0=angf[:], in1=tab[:], op=ALU.subtract)
                A('activation', [b_angf], [b_tab], out=tab[:], in_=angf[:], func=AF.Sin, scale=TWO_PI_S)

            blocks = [(0, 384, 0), (384, 768, 512), (768, 1152, 1024), (1152, 1408, 1536), (1408, 1536, 896),
                      (1536, 1664, 1408), (1664, 1920, 1792), (1920, 2048, 384), (2048, 2080, 2048)]
            k = 0
            for c in range(8):
                st_t, st_b = stage[c % 2]
                self.dma('sync', st_t[:], D['w_in'].ap()[L, c * 128:(c + 1) * 128, :], [], [st_b], 'stage0')
                for (o0, o1, n0) in blocks:
                    eng = 'vector' if k % 2 == 0 else 'gpsimd'
                    k += 1
                    E(eng, 'tensor_scalar', [st_b, b_g8], [b_wib], out=wib[:, c, n0:n0 + (o1 - o0)],
                      in0=st_t[:, o0:o1], scalar1=g8[:, c:c + 1], scalar2=None, op0=ALU.mult)
            self.dma('sync', stq[:], D['w_uq'].ap()[L].rearrange("(c p) n -> p c n", p=128), [], [b_stq], 'stq')
            self.dma('sync', stkv[:], D['w_ukv'].ap()[L], [], [b_stkv], 'stkv')
            for c in range(2):
                V('tensor_scalar', [b_stq, b_gq2], [b_wuq], out=wuq[:, c, :], in0=stq[:, c, :],
                  scalar1=gq2[:, c:c + 1], scalar2=None, op0=ALU.mult)
            V('tensor_scalar', [b_stkv, b_gkv1], [b_wukv], out=wukv[:], in0=stkv[:], scalar1=gkv1[:, 0:1],
              scalar2=None, op0=ALU.mult)

            self.P.phase_barrier()
            setup.close()
            xt = sb("xt", [128, 1024], F32, NS)
            junk, b_junk = sb("junk", [128, 1024], BF16)
            ssx = sb("ssx", [128, 1], F32, NS)
            rsx = sb("rsx", [128, 1], F32, NS)
            hb = sb("hb", [128, 1024], BF16, NS)
            hT = sb("hT", [128, 8, 128], BF16, NS)
            pj = sb("pj", [128, 2080], F32, NS)
            sq, b_sq = sb("sq", [128, 2080], F32)
            st = sb("st", [128, 24], F32, NS)
            rstd = sb("rstd", [128, 24], F32, NS)
            QAo = sb("QAo", [128, 384], BF16, NS)
            QAt = sb("QAt", [128, 384], F32, 1)
            KABo = sb("KABo", [128, 512], BF16, NS)
            QBo = sb("QBo", [128, 256], BF16, NS)
            QBt = sb("QBt", [128, 256], F32, 1)
            VABo = sb("VABo", [128, 8, 65], BF16, NS)
            LAT = sb("LAT", [128, 384], BF16, NS)
            latT = sb("latT", [128, 3, 128], BF16, NS)
            qcs = sb("qcs", [128, 576], F32, NS)
            kvcs = sb("kvcs", [128, 768], F32, NS)
            st2 = sb("st2", [128, 12], F32, NS)
            rstd2 = sb("rstd2", [128, 12], F32, NS)
            tmp1, b_tmp1 = sb("tmp1", [128, 6, 96], F32)
            trq, b_trq = sb("trq", [128, 6, 32], F32)
            tmpk, b_tmpk = sb("tmpk", [128, 6, 64], F32)
            krg, b_krg = sb("krg", [128, 1, 32], F32)
            krr, b_krr = sb("krr", [128, 1, 32], F32)
            rm = [sb("rm%d" % i, [128, 6, 16], F32) for i in range(4)]
            QCo = sb("QCo", [128, 6, 96], BF16, NS)
            KCo = sb("KCo", [128, 6, 96], BF16, NS)
            VCo = sb("VCo", [128, 6, 65], BF16, NS)
            QTs = sb("QTs", [96, 6, 128], BF16, NS)
            KTs = sb("KTs", [96, 6, 128], BF16, NS)
            TR, b_TR = ps("TR", [128, 8, 128], BF16)
            PJ = [ps("PJ%d" % i, [128, 512], F32) for i in range(5)]
            S = [ps("S%d" % i, [128, 512], F32) for i in range(2)]

            for (t_, b_) in VABo:
                self.P.op('gpsimd', lambda e, t_=t_: e.memset(t_[:], 1.0), [], [b_])
            for (t_, b_) in VCo:
                self.P.op('gpsimd', lambda e, t_=t_: e.memset(t_[:], 1.0), [], [b_])

            def rope(src, b_src, dst, b_dst, H, ti):
                cb = bc(cos_t[:, ti, :].unsqueeze(1), [128, H, 16])
                sbb = bc(sin_t[:, ti, :].unsqueeze(1), [128, H, 16])
                (m1, b1), (m2, b2), (m3, b3), (m4, b4) = rm
                V('tensor_tensor', [b_src, b_cos], [b1], out=m1[:, 0:H, :], in0=src[:, :, 0:16], in1=cb, op=ALU.mult)
                V('tensor_tensor', [b_src, b_sin], [b2], out=m2[:, 0:H, :], in0=src[:, :, 16:32], in1=sbb, op=ALU.mult)
                V('tensor_tensor', [b1, b2], [b_dst], out=dst[:, :, 0:16], in0=m1[:, 0:H, :], in1=m2[:, 0:H, :],
                  op=ALU.subtract)
                V('tensor_tensor', [b_src, b_cos], [b3], out=m3[:, 0:H, :], in0=src[:, :, 16:32], in1=cb, op=ALU.mult)
                V('tensor_tensor', [b_src, b_sin], [b4], out=m4[:, 0:H, :], in0=src[:, :, 0:16], in1=sbb, op=ALU.mult)
                V('tensor_tensor', [b3, b4], [b_dst], out=dst[:, :, 16:32], in0=m3[:, 0:H, :], in1=m4[:, 0:H, :],
                  op=ALU.add)

            SC = self.SC
            it = 0
            jobs = [('own', t) for t in range(32)] + [('oth', t) for t in range(32)] + [('halo', t) for t in range(16)]
            if skip_oth:
                jobs = [j for j in jobs if j[0] != 'oth']
            if self.debug and self.debug.get('p1_tiles'):
                jobs = self.debug['p1_tiles']
            def tile_gen(it, kind, t):
                s2 = it % NS
                s3 = it % NS
                src = {'own': x_own, 'oth': x_oth, 'halo': x_halo}[kind]
                x_t, b_x = xt[s3]
                ss_t, b_ss = ssx[s2]
                rs_t, b_rs = rsx[s2]
                hb_t, b_hb = hb[s2]
                hT_t, b_hT = hT[s2]
                pj_t, b_pj = pj[s3]
                st_t, b_st = st[s3]
                rstd_t, b_rstd = rstd[s3]
                if kind == 'halo':
                    G('tensor_scalar', [b_x, b_valid], [b_x], out=x_t[:], in0=x_t[:], scalar1=valid[:, t:t + 1],
                      scalar2=None, op0=ALU.mult)
                    yield
                A('activation', [b_x], [b_junk, b_ss], out=junk[:], in_=x_t[:], func=AF.Square, accum_out=ss_t[:])
                yield
                G('tensor_scalar', [b_ss], [b_ss], out=ss_t[:], in0=ss_t[:], scalar1=1.0 / 1024, scalar2=EPS,
                  op0=ALU.mult, op1=ALU.add)
                yield
                G('tensor_tensor', [b_ss, b_nh24], [b_rs], out=rs_t[:], in0=ss_t[:], in1=nh24[:, 0:1], op=ALU.pow)
                yield
                A('activation', [b_x, b_rs], [b_hb], out=hb_t[:], in_=x_t[:], func=AF.Copy, scale=rs_t[:, 0:1])
                yield
                for c in range(8):
                    T('transpose', [b_hb, b_idb], [b_TR], out=TR[:, c, :], in_=hb_t[:, c * 128:(c + 1) * 128],
                      identity=idb[:])
                V('tensor_copy', [b_TR], [b_hT], out=hT_t[:], in_=TR[:])
                yield
                if kind == 'own':
                    groups = [(0, 0, 512, 0), (1, 512, 1024, 0), (2, 1024, 1536, 0), (3, 1536, 2048, 0),
                              (4, 2048, 2080, 0)]
                elif kind == 'oth':
                    groups = [(0, 384, 512, 384), (4, 2048, 2080, 0)]
                else:
                    groups = [(1, 512, 1024, 0), (2, 1024, 1536, 0)]
                for (bk, c0, c1, po) in groups:
                    pt, pb = PJ[bk]
                    for c in range(8):
                        T('matmul', [b_hT, b_wib], [pb], out=pt[:, po:po + (c1 - c0)], lhsT=hT_t[:, c, :],
                          rhs=wib[:, c, c0:c1], start=(c == 0), stop=(c == 7))
                    A('activation', [pb], [b_pj], out=pj_t[:, c0:c1], in_=pt[:, po:po + (c1 - c0)], func=AF.Copy)
                    yield
                if kind == 'own':
                    V('tensor_tensor', [b_pj], [b_sq], out=sq[:, 0:1024], in0=pj_t[:, 0:1024], in1=pj_t[:, 0:1024],
                      op=ALU.mult)
                    V('tensor_tensor', [b_pj], [b_sq], out=sq[:, 1536:2080], in0=pj_t[:, 1536:2080],
                      in1=pj_t[:, 1536:2080], op=ALU.mult)
                    red = [(0, 6, 0, 384, 64), (6, 14, 512, 1024, 64), (14, 18, 1536, 1792, 64),
                           (18, 19, 1792, 2048, 256), (19, 20, 384, 512, 128), (20, 21, 2048, 2080, 32)]
                elif kind == 'oth':
                    V('tensor_tensor', [b_pj], [b_sq], out=sq[:, 384:512], in0=pj_t[:, 384:512], in1=pj_t[:, 384:512],
                      op=ALU.mult)
                    V('tensor_tensor', [b_pj], [b_sq], out=sq[:, 2048:2080], in0=pj_t[:, 2048:2080],
                      in1=pj_t[:, 2048:2080], op=ALU.mult)
                    red = [(19, 20, 384, 512, 128), (20, 21, 2048, 2080, 32)]
                else:
                    V('tensor_tensor', [b_pj], [b_sq], out=sq[:, 512:1024], in0=pj_t[:, 512:1024],
                      in1=pj_t[:, 512:1024], op=ALU.mult)
                    red = [(6, 14, 512, 1024, 64)]
                for (a0, a1, c0, c1, dd) in red:
                    V('tensor_reduce', [b_sq], [b_st], out=st_t[:, a0:a1],
                      in_=sq[:, c0:c1].rearrange("p (h d) -> p h d", d=dd), axis=AX.X, op=ALU.add)
                V('tensor_tensor', [b_st, b_invd], [b_rstd], out=rstd_t[:, 0:20], in0=st_t[:, 0:20], in1=invd[:, 0:20],
                  op=ALU.mult)
                yield
                V('tensor_scalar', [b_rstd], [b_rstd], out=rstd_t[:, 0:20], in0=rstd_t[:, 0:20], scalar1=EPS,
                  scalar2=None, op0=ALU.add)
                yield
                G('tensor_tensor', [b_rstd, b_nh24], [b_rstd], out=rstd_t[:, 0:20], in0=rstd_t[:, 0:20],
                  in1=nh24[:, 0:20], op=ALU.pow)
                yield
                if kind in ('own', 'halo'):
                    et = (8 + t) if kind == 'own' else (t if t < 8 else 40 + (t - 8))
                    kab_t, b_kab = KABo[s2]
                    vab_t, b_vab = VABo[s2]
                    V('tensor_tensor', [b_pj, b_rstd], [b_kab], out=kab_t[:].rearrange("p (h d) -> p h d", d=64),
                      in0=pj_t[:, 512:1024].rearrange("p (h d) -> p h d", d=64),
                      in1=bc(rstd_t[:, 6:14].unsqueeze(2), [128, 8, 64]), op=ALU.mult)
                    yield
                    V('tensor_copy', [b_pj], [b_vab], out=vab_t[:, :, 0:64],
                      in_=pj_t[:, 1024:1536].rearrange("p (h d) -> p h d", d=64))
                    yield
                    if kind == 'halo':
                        V('tensor_copy', [b_valid], [b_vab], out=vab_t[:, :, 64:65],
                          in_=bc(valid[:, t:t + 1].unsqueeze(1), [128, 8, 1]))
                        yield
                    else:
                        self.P.op('vector', lambda e, vab_t=vab_t: e.memset(vab_t[:, :, 64:65], 1.0), [], [b_vab])
                        yield
                    rows = slice(et * 128, (et + 1) * 128)
                    self.dma('sync', SC['KAx'].ap()[rows, :], kab_t[:, 0:384], [b_kab], [SC['b']], 'kabo%d' % s2)
                    yield
                    self.dma('sync', SC['KBx'].ap()[rows, :], kab_t[:, 384:512], [b_kab], [SC['b']], 'kabo%d' % s2)
                    yield
                    self.dma('sync', SC['VAx'].ap()[rows, :].rearrange("p (h d) -> p h d", d=65), vab_t[:, 0:6, :],
                             [b_vab], [SC['b']], 'vabo%d' % s2)
                    yield
                    self.dma('sync', SC['VBx'].ap()[rows, :].rearrange("p (h d) -> p h d", d=65), vab_t[:, 6:8, :],
                             [b_vab], [SC['b']], 'vabo%d' % s2)
                    yield
                if kind == 'own':
                    rows = slice(t * 128, (t + 1) * 128)
                    qa_t, b_qa = QAo[s2]
                    qat, b_qat = QAt
                    V('tensor_tensor', [b_pj, b_rstd], [b_qat], out=qat[:].rearrange("p (h d) -> p h d", d=64),
                      in0=pj_t[:, 0:384].rearrange("p (h d) -> p h d", d=64),
                      in1=bc(rstd_t[:, 0:6].unsqueeze(2), [128, 6, 64]), op=ALU.mult)
                    V('tensor_tensor', [b_qat, b_GAq], [b_qa], out=qa_t[:].rearrange("p (h d) -> p h d", d=64),
                      in0=qat[:].rearrange("p (h d) -> p h d", d=64),
                      in1=bc(GAq[:].unsqueeze(1), [128, 6, 64]), op=ALU.mult)
                    self.dma('sync', SC['QA'].ap()[rows, :], qa_t[:], [b_qa], [SC['b']], 'qao%d' % s2)
                    yield
                    qb_t, b_qb = QBo[s2]
                    qbt, b_qbt = QBt
                    V('tensor_tensor', [b_pj, b_rstd], [b_qbt], out=qbt[:].rearrange("p (h d) -> p h d", d=64),
                      in0=pj_t[:, 1536:1792].rearrange("p (h d) -> p h d", d=64),
                      in1=bc(rstd_t[:, 14:18].unsqueeze(2), [128, 4, 64]), op=ALU.mult)
                    V('tensor_tensor', [b_qbt, b_GBq], [b_qb],
                      out=qb_t[:].rearrange("p (b a d) -> p a b d", b=2, a=2, d=64),
                      in0=qbt[:].rearrange("p (a b d) -> p a b d", a=2, b=2, d=64),
                      in1=bc(GBq[:].unsqueeze(1).unsqueeze(1), [128, 2, 2, 64]), op=ALU.mult)
                    self.dma('sync', SC['QB'].ap()[rows, :], qb_t[:], [b_qb], [SC['b']], 'qbo%d' % s2)
                    yield
                if kind in ('own', 'oth'):
                    ti = t if kind == 'own' else 32 + t
                    lat_t, b_lat = LAT[s2]
                    latT_t, b_latT = latT[s2]
                    qcs_t, b_qcs = qcs[s2]
                    kvcs_t, b_kvcs = kvcs[s2]
                    st2_t, b_st2 = st2[s2]
                    rstd2_t, b_rstd2 = rstd2[s2]
                    if kind == 'own':
                        V('tensor_scalar', [b_pj, b_rstd], [b_lat], out=lat_t[:, 0:256], in0=pj_t[:, 1792:2048],
                          scalar1=rstd_t[:, 18:19], scalar2=None, op0=ALU.mult)
                        yield
                    if not skip_ck:
                        V('tensor_scalar', [b_pj, b_rstd], [b_lat], out=lat_t[:, 256:384], in0=pj_t[:, 384:512],
                          scalar1=rstd_t[:, 19:20], scalar2=None, op0=ALU.mult)
                        yield
                    jl = ([0, 1] if skip_ck else [0, 1, 2]) if kind == 'own' else [2]
                    for j in jl:
                        T('transpose', [b_lat, b_idb], [b_TR], out=TR[:, j, :], in_=lat_t[:, j * 128:(j + 1) * 128],
                          identity=idb[:])
                    V('tensor_copy', [b_TR], [b_latT], out=latT_t[:, jl[0]:jl[-1] + 1, :], in_=TR[:, jl[0]:jl[-1] + 1, :])
                    if kind == 'own':
                        for hf in range(2):
                            for c in range(2):
                                T('matmul', [b_latT, b_wuq], [S[hf][1]], out=S[hf][0][:, 0:288], lhsT=latT_t[:, c, :],
                                  rhs=wuq[:, c, hf * 288:(hf + 1) * 288], start=(c == 0), stop=(c == 1))
                            A('activation', [S[hf][1]], [b_qcs], out=qcs_t[:, hf * 288:(hf + 1) * 288],
                              in_=S[hf][0][:, 0:288], func=AF.Copy)
                    for hf in (range(2) if not skip_ck else []):
                        T('matmul', [b_latT, b_wukv], [S[hf][1]], out=S[hf][0][:, 0:384], lhsT=latT_t[:, 2, :],
                          rhs=wukv[:, hf * 384:(hf + 1) * 384], start=True, stop=True)
                        A('activation', [S[hf][1]], [b_kvcs], out=kvcs_t[:, hf * 384:(hf + 1) * 384],
                          in_=S[hf][0][:, 0:384], func=AF.Copy)
                    kv3 = kvcs_t[:].rearrange("p (h d) -> p h d", d=128)
                    if kind == 'own':
                        V('tensor_tensor', [b_qcs], [b_sq], out=sq[:, 0:576], in0=qcs_t[:], in1=qcs_t[:], op=ALU.mult)
                        V('tensor_reduce', [b_sq], [b_st2], out=st2_t[:, 0:6],
                          in_=sq[:, 0:576].rearrange("p (h d) -> p h d", d=96), axis=AX.X, op=ALU.add)
                    if not skip_ck:
                        V('tensor_tensor', [b_kvcs], [b_sq], out=sq[:, 1024:1408].rearrange("p (h d) -> p h d", d=64),
                          in0=kv3[:, :, 0:64], in1=kv3[:, :, 0:64], op=ALU.mult)
                        V('tensor_reduce', [b_sq], [b_st2], out=st2_t[:, 6:12],
                          in_=sq[:, 1024:1408].rearrange("p (h d) -> p h d", d=64), axis=AX.X, op=ALU.add)
                        V('tensor_scalar', [b_st2, b_st], [b_st2], out=st2_t[:, 6:12], in0=st2_t[:, 6:12],
                          scalar1=st_t[:, 20:21], scalar2=None, op0=ALU.add)
                    lo = 0 if kind == 'own' else 6
                    hi_ = 6 if skip_ck else 12
                    V('tensor_scalar', [b_st2], [b_rstd2], out=rstd2_t[:, lo:hi_], in0=st2_t[:, lo:hi_],
                      scalar1=1.0 / 96, scalar2=EPS, op0=ALU.mult, op1=ALU.add)
                    yield
                    G('tensor_tensor', [b_rstd2, b_nh24], [b_rstd2], out=rstd2_t[:, lo:hi_], in0=rstd2_t[:, lo:hi_],
                      in1=nh24[:, lo:hi_], op=ALU.pow)
                    yield
                    kc_t, b_kc = KCo[s2]
                    vc_t, b_vc = VCo[s2]
                    if kind == 'own':
                        qc_t, b_qc = QCo[s2]
                        V('tensor_tensor', [b_qcs, b_rstd2], [b_tmp1], out=tmp1[:],
                          in0=qcs_t[:].rearrange("p (h d) -> p h d", d=96),
                          in1=bc(rstd2_t[:, 0:6].unsqueeze(2), [128, 6, 96]), op=ALU.mult)
                        V('tensor_tensor', [b_tmp1, b_GCq], [b_qc], out=qc_t[:, :, 0:64], in0=tmp1[:, :, 0:64],
                          in1=bc(GCq[:, 0:64].unsqueeze(1), [128, 6, 64]), op=ALU.mult)
                        V('tensor_tensor', [b_tmp1, b_GCq], [b_trq], out=trq[:], in0=tmp1[:, :, 64:96],
                          in1=bc(GCq[:, 64:96].unsqueeze(1), [128, 6, 32]), op=ALU.mult)
                        rope(trq, b_trq, qc_t[:, :, 64:96], b_qc, 6, ti)
                    if kind == 'own':
                        qT_t, b_qT = QTs[s2]
                        for h in range(6):
                            T('transpose', [b_qc, b_idb], [b_TR], out=TR[0:96, h, :], in_=qc_t[:, h, :],
                              identity=idb[:])
                        V('tensor_copy', [b_TR], [b_qT], out=qT_t[:], in_=TR[0:96, 0:6, :])
                        self.dma('sync', SC['QCT'].ap()[:, :, t * 128:(t + 1) * 128].rearrange("h d n -> d h n"),
                                 qT_t[:], [b_qT], [SC['b']], 'qto%d' % s2)
                        yield
                    if skip_ck:
                        return
                    V('tensor_tensor', [b_kvcs, b_rstd2], [b_tmpk], out=tmpk[:], in0=kv3[:, :, 0:64],
                      in1=bc(rstd2_t[:, 6:12].unsqueeze(2), [128, 6, 64]), op=ALU.mult)
                    V('tensor_tensor', [b_tmpk, b_GCk], [b_kc], out=kc_t[:, :, 0:64], in0=tmpk[:],
                      in1=bc(GCk[:, 0:64].unsqueeze(1), [128, 6, 64]), op=ALU.mult)
                    V('tensor_tensor', [b_pj, b_GCk], [b_krg], out=krg[:, 0, :], in0=pj_t[:, 2048:2080],
                      in1=GCk[:, 64:96], op=ALU.mult)
                    rope(krg, b_krg, krr[:], b_krr, 1, ti)
                    V('tensor_tensor', [b_krr, b_rstd2], [b_kc], out=kc_t[:, :, 64:96],
                      in0=bc(krr[:], [128, 6, 32]), in1=bc(rstd2_t[:, 6:12].unsqueeze(2), [128, 6, 32]), op=ALU.mult)
                    V('tensor_copy', [b_kvcs], [b_vc], out=vc_t[:, :, 0:64], in_=kv3[:, :, 64:128])
                    yield
                    kT_t, b_kT = KTs[s2]
                    for h in range(6):
                        T('transpose', [b_kc, b_idb], [b_TR], out=TR[0:96, h, :], in_=kc_t[:, h, :], identity=idb[:])
                    V('tensor_copy', [b_TR], [b_kT], out=kT_t[:], in_=TR[0:96, 0:6, :])
                    self.dma('sync', SC['KCT'].ap()[:, :, ti * 128:(ti + 1) * 128].rearrange("h d n -> d h n"),
                             kT_t[:], [b_kT], [SC['b']], 'kto%d' % s2)
                    yield
                    self.dma('sync', SC['VC'].ap()[ti * 128:(ti + 1) * 128, :].rearrange("p (h d) -> p h d", d=65),
                             vc_t[:], [b_vc], [SC['b']], 'vco%d' % s2)
                    yield

            def issue_load(j):
                kind, t = jobs[j]
                x_t, b_x = xt[j % NS]
                if kind == 'halo' and halo_rows is not None:
                    self.dma('sync', x_t[:], halo_rows(t), [], [b_x], 'xt%d' % (j % NS))
                else:
                    src = {'own': x_own, 'oth': x_oth, 'halo': x_halo}[kind]
                    self.dma('sync', x_t[:], src[t * 128:(t + 1) * 128, :], [], [b_x], 'xt%d' % (j % NS))

            for j in range(min(2, len(jobs))):
                issue_load(j)
            active = []
            nxt = 0
            since = 10 ** 9
            STAG = 9
            while active or nxt < len(jobs):
                if nxt < len(jobs) and len(active) < NS and (since >= STAG or not active):
                    if nxt + 2 < len(jobs):
                        issue_load(nxt + 2)
                    active.append(tile_gen(nxt, jobs[nxt][0], jobs[nxt][1]))
                    nxt += 1
                    since = 0
                for g in list(active):
                    try:
                        next(g)
                    except StopIteration:
                        active.remove(g)
                since += 1
        self.P.phase_barrier()

    def phaseC(self, heads=range(6), nqt=8):
        nc, P, D, SC = self.nc, self.P, self.D, self.SC
        V, G, A, T, E = self.V, self.G, self.A, self.T, self.E
        with ExitStack() as ph:
            def sb(nm, shape, dt, n=1):
                r = []
                for i in range(n):
                    t = ph.enter_context(nc.sbuf_tensor(self.name(nm), shape, dt))
                    r.append((t, Buf(nm + str(i))))
                return r if n > 1 else r[0]

            def ps(nm, shape, dt, n=1):
                r = []
                for i in range(n):
                    t = psum_view(ph, nc, self.name(nm), shape, dt)
                    r.append((t, Buf(nm + str(i))))
                return r if n > 1 else r[0]

            P.barrier('sync', reads=[SC['b']])
            idf, b_idf = sb("idf", [128, 128], F32)
            self.dma('sync', idf[:], D['idf'].ap(), [], [b_idf], 'idf')
            KT = sb("cKT", [96, 8192], BF16, 2)
            VV = sb("cV", [128, 64, 65], BF16, 2)
            QT = sb("cQT", [96, 4096], BF16, 2)
            PT = sb("cPT", [128, 512], BF16, 4)
            OTs = sb("cOTs", [65, 512], F32, 2)
            rc = sb("crc", [128, 4], F32, 2)
            oc = sb("coc", [128, 4, 64], BF16, 2)
            ST = ps("cST", [128, 512], F32, 3)
            OT = ps("cOT", [65, 512], F32, 2)
            TO, b_TO = ps("cTO", [128, 4, 65], F32)

            heads = list(heads)

            def load_head(hi):
                h = heads[hi]
                s = hi % 2
                self.dma('sync', KT[s][0][:], SC['KCT'].ap()[h], [SC['b']], [KT[s][1]], 'cKT%d' % s)
                self.dma('sync', QT[s][0][:], SC['QCT'].ap()[h], [SC['b']], [QT[s][1]], 'cQT%d' % s)
                self.dma('sync', VV[s][0][:],
                         SC['VC'].ap().rearrange("(t p) c -> p t c", p=128)[:, :, h * 65:(h + 1) * 65],
                         [SC['b']], [VV[s][1]], 'cV%d' % s)

            steps = [(hi, qt, kc) for hi in range(len(heads)) for qt in range(nqt) for kc in range(64)]
            n = len(steps)
            LA = 2
            deferred = []
            load_head(0)
            nq = 0
            for i in range(n + LA):
                if i < n:
                    hi, qt, kc = steps[i]
                    s = hi % 2
                    st_t, st_b = ST[i % 3]
                    pt_t, pt_b = PT[i % 4]
                    T('matmul', [KT[s][1], QT[s][1]], [st_b], out=st_t[:], lhsT=KT[s][0][:, kc * 128:(kc + 1) * 128],
                      rhs=QT[s][0][:, qt * 512:(qt + 1) * 512], start=True, stop=True)
                    A('activation', [st_b], [pt_b], out=pt_t[:], in_=st_t[:], func=AF.Exp)
                j = i - LA
                if j >= 0:
                    hi, qt, kc = steps[j]
                    if qt == 0 and kc == 0 and hi + 1 < len(heads):
                        load_head(hi + 1)
                    s = hi % 2
                    qi = (hi * nqt + qt)
                    ot_t, ot_b = OT[qi % 2]
                    pt_t, pt_b = PT[j % 4]
                    T('matmul', [VV[s][1], pt_b], [ot_b], out=ot_t[:], lhsT=VV[s][0][:, kc, :], rhs=pt_t[:],
                      start=(kc == 0), stop=(kc == 63))
                    if kc == 63:
                        h = heads[hi]
                        os_t, os_b = OTs[qi % 2]
                        V('tensor_copy', [ot_b], [os_b], out=os_t[:], in_=ot_t[:])

                        def fin(os_t=os_t, os_b=os_b, qi=qi, qt=qt, h=h):
                            for jj in range(4):
                                T('transpose', [os_b, b_idf], [b_TO], out=TO[:, jj, :],
                                  in_=os_t[:, jj * 128:(jj + 1) * 128], identity=idf[0:65, 0:65])
                            rc_t, rc_b = rc[qi % 2]
                            oc_t, oc_b = oc[qi % 2]
                            V('reciprocal', [b_TO], [rc_b], out=rc_t[:].unsqueeze(2), in_=TO[:, :, 64:65])
                            V('tensor_tensor', [b_TO, rc_b], [oc_b], out=oc_t[:], in0=TO[:, :, 0:64],
                              in1=bc(rc_t[:].unsqueeze(2), [128, 4, 64]), op=ALU.mult)
                            self.dma('sync',
                                     SC['MIX'].ap()[qt * 512:(qt + 1) * 512, 640 + h * 64:640 + (h + 1) * 64]
                                     .rearrange("(t p) d -> p t d", p=128),
                                     oc_t[:], [oc_b], [SC['bmix']], 'coc%d' % (qi % 2))
                        deferred.append((i + 6, fin))
                while deferred and deferred[0][0] <= i:
                    deferred.pop(0)[1]()
            for (_, fn) in deferred:
                fn()
        self.P.phase_barrier()

    def bias_setup(self):
        nc, P, D, SC = self.nc, self.P, self.D, self.SC
        V, G, A, T, E = self.V, self.G, self.A, self.T, self.E
        with ExitStack() as ph:
            tabN = ph.enter_context(nc.sbuf_tensor(self.name("tabN"), [33, 10], F32)); b_tab = Buf("tabN")
            oh = ph.enter_context(nc.sbuf_tensor(self.name("oh"), [33, 1660], F32)); b_oh = Buf("oh")
            fv = ph.enter_context(nc.sbuf_tensor(self.name("fv"), [10, 1660], F32)); b_fv = Buf("fv")
            pf = [(psum_view(ph, nc, self.name("pf"), [10, 512], F32), Buf("pf%d" % i)) for i in range(4)]
            P.op('vector', lambda e: e.memset(tabN[:], NEGB), [], [b_tab])
            self.dma('sync', tabN[0:32, :], D['rel_bias_table'].ap(), [], [b_tab], 'tabN')
            self.dma('sync', oh[:], D['oh'].ap(), [], [b_oh], 'oh')
            segs = [(0, 511), (511, 894), (894, 1277), (1277, 1660)]
            for i, (a, b) in enumerate(segs):
                T('matmul', [b_tab, b_oh], [pf[i][1]], out=pf[i][0][:, 0:b - a], lhsT=tabN[:], rhs=oh[:, a:b],
                  start=True, stop=True)
                V('tensor_copy', [pf[i][1]], [b_fv], out=fv[:, a:b], in_=pf[i][0][:, 0:b - a])
            self.dma('sync', SC['FV'].ap(), fv[:], [b_fv], [SC['bfv']], 'fvo')
        self.P.phase_barrier()

    def phaseB(self, L):
        nc, P, D, SC = self.nc, self.P, self.D, self.SC
        V, G, A, T, E = self.V, self.G, self.A, self.T, self.E
        with ExitStack() as ph:
            def sb(nm, shape, dt, n=1):
                r = []
                for i in range(n):
                    t = ph.enter_context(nc.sbuf_tensor(self.name(nm), shape, dt))
                    r.append((t, Buf(nm + str(i))))
                return r if n > 1 else r[0]

            def ps(nm, shape, dt, n=1):
                r = []
                for i in range(n):
                    t = psum_view(ph, nc, self.name(nm), shape, dt)
                    r.append((t, Buf(nm + str(i))))
                return r if n > 1 else r[0]

            P.barrier('sync', reads=[SC['b'], SC['bfv']])
            idf, b_idf = sb("idf", [128, 128], F32)
            idb, b_idb = sb("idb", [128, 128], BF16)
            self.dma('sync', idf[:], D['idf'].ap(), [], [b_idf], 'idf')
            self.dma('sync', idb[:], D['idb'].ap(), [], [b_idb], 'idb')
            biasB = sb("biasB", [128, 4, 3, 128], F32)
            hk = sb("hkB", [128, 128], F32, 4)
            for h in range(4):
                for o in range(3):
                    g_t, g_b = hk[(h * 3 + o) % 4]
                    self.dma('sync', g_t[:], bass.AP(SC['FV'], (6 + h) * 1660 + 128 * o, [[1, 128], [1, 128]]),
                             [SC['bfv']], [g_b], 'hkB%d' % ((h * 3 + o) % 4))
                    V('tensor_copy', [g_b], [biasB[1]], out=biasB[0][:, h, o, :],
                      in_=bass.AP(g_t, 127, [[128, 128], [-1, 128]]))
            sk, b_sk = sb("sink", [128, 4], F32)
            esk, b_esk = sb("esink", [128, 4], F32)
            self.dma('sync', sk[:], D['sink_b'].ap()[L].partition_broadcast(128), [], [b_sk], 'sink')
            A('activation', [b_sk], [b_esk], out=esk[:], in_=sk[:], func=AF.Exp)
            Qc = sb("bQc", [128, 4, 256], BF16, 2)
            Kc = sb("bKc", [128, 6, 128], BF16, 2)
            Vc = sb("bVc", [128, 6, 130], BF16, 2)
            QTb = sb("bQT", [128, 2, 4, 128], BF16, 2)
            KTb = sb("bKT", [128, 6, 128], BF16, 2)
            Sb = sb("bS", [128, 3, 128], F32, 2)
            PTb = sb("bPT", [128, 3, 128], BF16, 2)
            OTs = sb("bOTs", [65, 4, 128], F32, 2)
            den = sb("bden", [128, 4], F32, 2)
            ob = sb("bo", [128, 4, 64], BF16, 2)
            TRq = ps("bTRq", [128, 2, 4, 128], BF16)
            TRk = ps("bTRk", [128, 6, 128], BF16)
            SP = ps("bSP", [128, 3, 128], F32, 2)
            OTp = ps("bOT", [65, 4, 128], F32, 2)
            TOp = ps("bTO", [128, 4, 65], F32)

            def stage1(J):
                s = J % 2
                self.dma('sync', Qc[s][0][:], SC['QB'].ap()[J * 512:(J + 1) * 512, :].rearrange("(t p) c -> p t c", p=128),
                         [SC['b']], [Qc[s][1]], 'bQc%d' % s)
                r0 = (7 + 4 * J) * 128
                self.dma('sync', Kc[s][0][:], SC['KBx'].ap()[r0:r0 + 768, :].rearrange("(t p) c -> p t c", p=128),
                         [SC['b']], [Kc[s][1]], 'bKc%d' % s)
                self.dma('sync', Vc[s][0][:], SC['VBx'].ap()[r0:r0 + 768, :].rearrange("(t p) c -> p t c", p=128),
                         [SC['b']], [Vc[s][1]], 'bVc%d' % s)
                for t in range(4):
                    for pi in range(2):
                        T('transpose', [Qc[s][1], b_idb], [TRq[1]], out=TRq[0][:, pi, t, :],
                          in_=Qc[s][0][:, t, pi * 128:(pi + 1) * 128], identity=idb[:])
                V('tensor_copy', [TRq[1]], [QTb[s][1]], out=QTb[s][0][:], in_=TRq[0][:])
                for kt in range(6):
                    T('transpose', [Kc[s][1], b_idb], [TRk[1]], out=TRk[0][:, kt, :], in_=Kc[s][0][:, kt, :],
                      identity=idb[:])
                V('tensor_copy', [TRk[1]], [KTb[s][1]], out=KTb[s][0][:], in_=TRk[0][:])

            cnt = [0]

            def stage2(J):
                s = J % 2
                for t in range(4):
                    qi = J * 4 + t
                    ot_t, ot_b = OTp[qi % 2]
                    for h in range(4):
                        base = 64 * (h // 2)
                        pi = h % 2
                        kvh = h // 2
                        c = cnt[0]
                        cnt[0] += 1
                        sp_t, sp_b = SP[c % 2]
                        for o in range(3):
                            T('matmul', [KTb[s][1], QTb[s][1]], [sp_b], out=sp_t[:, o, :],
                              lhsT=KTb[s][0][base:base + 64, t + o, :], rhs=QTb[s][0][base:base + 64, pi, t, :],
                              start=True, stop=True)
                        s_t, s_b = Sb[c % 2]
                        p_t, p_b = PTb[c % 2]
                        V('tensor_tensor', [sp_b, biasB[1]], [s_b], out=s_t[:], in0=sp_t[:], in1=biasB[0][:, h, :, :],
                          op=ALU.add)
                        A('activation', [s_b], [p_b], out=p_t[:], in_=s_t[:], func=AF.Exp)
                        for o in range(3):
                            T('matmul', [Vc[s][1], p_b], [ot_b], out=ot_t[:, h, :],
                              lhsT=Vc[s][0][:, t + o, kvh * 65:(kvh + 1) * 65], rhs=p_t[:, o, :],
                              start=(o == 0), stop=(o == 2))
                    os_t, os_b = OTs[qi % 2]
                    V('tensor_copy', [ot_b], [os_b], out=os_t[:], in_=ot_t[:])
                    for h in range(4):
                        T('transpose', [os_b, b_idf], [TOp[1]], out=TOp[0][:, h, :], in_=os_t[:, h, :],
                          identity=idf[0:65, 0:65])
                    d_t, d_b = den[qi % 2]
                    o_t, o_b = ob[qi % 2]
                    V('tensor_tensor', [TOp[1], b_esk], [d_b], out=d_t[:].unsqueeze(2), in0=TOp[0][:, :, 64:65],
                      in1=esk[:].unsqueeze(2), op=ALU.add)
                    V('reciprocal', [d_b], [d_b], out=d_t[:], in_=d_t[:])
                    V('tensor_tensor', [TOp[1], d_b], [o_b], out=o_t[:], in0=TOp[0][:, :, 0:64],
                      in1=bc(d_t[:].unsqueeze(2), [128, 4, 64]), op=ALU.mult)
                    self.dma('sync', SC['MIX'].ap()[qi * 128:(qi + 1) * 128, 384:640], o_t[:], [o_b], [SC['bmix']],
                             'bo%d' % (qi % 2))

            stage1(0)
            for J in range(8):
                if J + 1 < 8:
                    stage1(J + 1)
                stage2(J)
        self.P.phase_barrier()

    def phaseA(self, cfgs=(0, 1, 2)):
        nc, P, D, SC = self.nc, self.P, self.D, self.SC
        V, G, A, T, E = self.V, self.G, self.A, self.T, self.E
        RS = (1, 4, 16)
        with ExitStack() as ph:
            def sb(nm, shape, dt, n=1):
                r = []
                for i in range(n):
                    t = ph.enter_context(nc.sbuf_tensor(self.name(nm), shape, dt))
                    r.append((t, Buf(nm + str(i))))
                return r if n > 1 else r[0]

            def ps(nm, shape, dt, n=1):
                r = []
                for i in range(n):
                    t = psum_view(ph, nc, self.name(nm), shape, dt)
                    r.append((t, Buf(nm + str(i))))
                return r if n > 1 else r[0]

            P.barrier('sync', reads=[SC['b'], SC['bfv']])
            idf, b_idf = sb("idf", [128, 128], F32)
            idb, b_idb = sb("idb", [128, 128], BF16)
            self.dma('sync', idf[:], D['idf'].ap(), [], [b_idf], 'idf')
            self.dma('sync', idb[:], D['idb'].ap(), [], [b_idb], 'idb')
            biasA = sb("biasA", [128, 3, 6, 2, 128], F32)
            hk = sb("hkA", [128, 128], F32, 4)
            for ci in range(3):
                for h in range(6):
                    for c in range(2):
                        kk = (ci * 6 + h) * 2 + c
                        g_t, g_b = hk[kk % 4]
                        self.dma('sync', g_t[:],
                                 bass.AP(SC['FV'], h * 1660 + 511 + 383 * ci + 128 * c, [[1, 128], [1, 128]]),
                                 [SC['bfv']], [g_b], 'hkA%d' % (kk % 4))
                        V('tensor_copy', [g_b], [biasA[1]], out=biasA[0][:, ci, h, c, :],
                          in_=bass.AP(g_t, 127, [[128, 128], [-1, 128]]))
            Qa = sb("aQ", [128, 384], BF16, 3)
            Ka = sb("aK", [128, 2, 384], BF16, 3)
            Va = sb("aV", [128, 2, 390], BF16, 3)
            QTa = sb("aQT", [128, 2, 3, 128], BF16, 2)
            for (t_, b_) in QTa:
                P.op('vector', lambda e, t_=t_: e.memset(t_[:], 0.0), [], [b_])
            KTa = sb("aKT", [128, 3, 2, 128], BF16, 2)
            Sa = sb("aS", [128, 6, 2, 128], F32, 2)
            PTa = sb("aPT", [128, 6, 2, 128], BF16, 2)
            OTs = sb("aOTs", [65, 6, 128], F32, 2)
            Oo = sb("aOo", [128, 6, 65], F32, 2)
            TRq = ps("aTRq", [128, 3, 128], BF16)
            TRk = ps("aTRk", [128, 3, 2, 128], BF16)
            SP = ps("aSP", [128, 2, 2, 128], F32, 3)
            OTp = ps("aOT", [65, 3, 128], F32, 2)
            TOp = ps("aTO", [128, 6, 65], F32)

            jobs = []
            for ci in cfgs:
                r = RS[ci]
                for rho in range(r):
                    for j in range(32 // r):
                        jobs.append((ci, r, rho, j))

            if self.debug and self.debug.get('a_jobs'):
                jobs = self.debug['a_jobs']

            def loadA(i):
                ci, r, rho, j = jobs[i]
                s3 = i % 3
                self.dma('sync', Qa[s3][0][:], bass.AP(SC['QA'], (r * 128 * j + rho) * 384, [[r * 384, 128], [1, 384]]),
                         [SC['b']], [Qa[s3][1]], 'aQ%d' % s3)
                e0 = r * (128 * j - 64) + rho + 1024
                self.dma('sync', Ka[s3][0][:],
                         bass.AP(SC['KAx'], e0 * 384, [[r * 384, 128], [r * 384 * 128, 2], [1, 384]]),
                         [SC['b']], [Ka[s3][1]], 'aK%d' % s3)
                self.dma('sync', Va[s3][0][:],
                         bass.AP(SC['VAx'], e0 * 390, [[r * 390, 128], [r * 390 * 128, 2], [1, 390]]),
                         [SC['b']], [Va[s3][1]], 'aV%d' % s3)

            def stage1(i):
                ci, r, rho, j = jobs[i]
                s3 = i % 3
                s2 = i % 2
                for pi in range(3):
                    T('transpose', [Qa[s3][1], b_idb], [TRq[1]], out=TRq[0][:, pi, :],
                      in_=Qa[s3][0][:, pi * 128:(pi + 1) * 128], identity=idb[:])
                V('tensor_copy', [TRq[1]], [QTa[s2][1]], out=QTa[s2][0][0:64, 0, :, :], in_=TRq[0][0:64, :, :])
                V('tensor_copy', [TRq[1]], [QTa[s2][1]], out=QTa[s2][0][64:128, 1, :, :], in_=TRq[0][64:128, :, :])
                for pi in range(3):
                    for c in range(2):
                        T('transpose', [Ka[s3][1], b_idb], [TRk[1]], out=TRk[0][:, pi, c, :],
                          in_=Ka[s3][0][:, c, pi * 128:(pi + 1) * 128], identity=idb[:])
                V('tensor_copy', [TRk[1]], [KTa[s2][1]], out=KTa[s2][0][:], in_=TRk[0][:])

            astop = self.debug.get('a_stop', 99) if self.debug else 99

            def stage2(i):
                ci, r, rho, j = jobs[i]
                s3 = i % 3
                s2 = i % 2
                s_t, s_b = Sa[s2]
                p_t, p_b = PTa[s2]
                if astop < 2:
                    return
                for pi in range(3):
                    sp_t, sp_b = SP[pi]
                    for hh in range(2):
                        for c in range(2):
                            T('matmul', [KTa[s2][1], QTa[s2][1]], [sp_b], out=sp_t[:, hh, c, :],
                              lhsT=KTa[s2][0][:, pi, c, :],
                              rhs=QTa[s2][0][:, hh, pi, :], start=True, stop=True)
                    if self.debug and self.debug.get('a_nobias'):
                        continue
                    V('tensor_tensor', [sp_b, biasA[1]], [s_b], out=s_t[:, 2 * pi:2 * pi + 2, :, :], in0=sp_t[:],
                      in1=biasA[0][:, ci, 2 * pi:2 * pi + 2, :, :], op=ALU.add)
                if astop < 3:
                    return
                A('activation', [s_b], [p_b], out=p_t[:], in_=s_t[:], func=AF.Exp)
                os_t, os_b = OTs[s2]
                if astop < 4:
                    return
                for g3 in range(2):
                    ot_t, ot_b = OTp[g3]
                    for hh in range(3):
                        h = g3 * 3 + hh
                        for c in range(2):
                            T('matmul', [Va[s3][1], p_b], [ot_b], out=ot_t[:, hh, :],
                              lhsT=Va[s3][0][:, c, h * 65:(h + 1) * 65], rhs=p_t[:, h, c, :],
                              start=(c == 0), stop=(c == 1))
                    V('tensor_copy', [ot_b], [os_b], out=os_t[:, g3 * 3:g3 * 3 + 3, :], in_=ot_t[:])
                if astop < 5:
                    return
                for h in range(6):
                    T('transpose', [os_b, b_idf], [TOp[1]], out=TOp[0][:, h, :], in_=os_t[:, h, :],
                      identity=idf[0:65, 0:65])
                o_t, o_b = Oo[s2]
                V('tensor_copy', [TOp[1]], [o_b], out=o_t[:], in_=TOp[0][:])
                self.dma('sync', bass.AP(SC['OA'], ci * 4096 * 390 + (r * 128 * j + rho) * 390, [[r * 390, 128], [1, 390]]),
                         o_t[:].rearrange("p h d -> p (h d)"), [o_b], [SC['boa']], 'aOo%d' % s2)

            n = len(jobs)
            for i in range(min(2, n)):
                loadA(i)
            stage1(0)
            for i in range(n):
                if i + 2 < n:
                    loadA(i + 2)
                if i + 1 < n:
                    stage1(i + 1)
                stage2(i)
        self.P.phase_barrier()

    def phaseA2(self):
        nc, P, D, SC = self.nc, self.P, self.D, self.SC
        V, G, A, T, E = self.V, self.G, self.A, self.T, self.E
        with ExitStack() as ph:
            def sb(nm, shape, dt, n=1):
                r = []
                for i in range(n):
                    t = ph.enter_context(nc.sbuf_tensor(self.name(nm), shape, dt))
                    r.append((t, Buf(nm + str(i))))
                return r if n > 1 else r[0]
            P.barrier('sync', reads=[SC['boa']])
            O3 = sb("a2O", [128, 3, 6, 65], F32, 3)
            acc = sb("a2acc", [128, 6, 65], F32, 2)
            rc = sb("a2rc", [128, 6], F32, 2)
            oo = sb("a2o", [128, 6, 64], BF16, 2)
            for t in range(32):
                s3, s2 = t % 3, t % 2
                self.dma('sync', O3[s3][0][:].rearrange("p c h d -> p c (h d)"),
                         SC['OA'].ap()[:, t * 128:(t + 1) * 128, :].rearrange("c p d -> p c d"),
                         [SC['boa']], [O3[s3][1]], 'a2O%d' % s3)
                o3 = O3[s3][0]
                G('tensor_tensor', [O3[s3][1]], [acc[s2][1]], out=acc[s2][0][:], in0=o3[:, 0], in1=o3[:, 1], op=ALU.add)
                G('tensor_tensor', [O3[s3][1], acc[s2][1]], [acc[s2][1]], out=acc[s2][0][:], in0=acc[s2][0][:],
                  in1=o3[:, 2], op=ALU.add)
                V('reciprocal', [acc[s2][1]], [rc[s2][1]], out=rc[s2][0][:].unsqueeze(2), in_=acc[s2][0][:, :, 64:65])
                V('tensor_tensor', [acc[s2][1], rc[s2][1]], [oo[s2][1]], out=oo[s2][0][:], in0=acc[s2][0][:, :, 0:64],
                  in1=bc(rc[s2][0][:].unsqueeze(2), [128, 6, 64]), op=ALU.mult)
                self.dma('sync', SC['MIX'].ap()[t * 128:(t + 1) * 128, 0:384], oo[s2][0][:].rearrange("p h d -> p (h d)"),
                         [oo[s2][1]], [SC['bmix']], 'a2o%d' % s2)
        self.P.phase_barrier()

    def phase4a(self, L, x_own, x1_d):
        nc, P, D, SC = self.nc, self.P, self.D, self.SC
        V, G, A, T, E = self.V, self.G, self.A, self.T, self.E
        with ExitStack() as ph:
            def sb(nm, shape, dt, n=1):
                r = []
                for i in range(n):
                    t = ph.enter_context(nc.sbuf_tensor(self.name(nm), shape, dt))
                    r.append((t, Buf(nm + str(i))))
                return r if n > 1 else r[0]

            def ps(nm, shape, dt, n=1):
                r = []
                for i in range(n):
                    t = psum_view(ph, nc, self.name(nm), shape, dt)
                    r.append((t, Buf(nm + str(i))))
                return r if n > 1 else r[0]

            P.barrier('sync', reads=[SC['bmix']])
            idb, b_idb = sb("idb", [128, 128], BF16)
            self.dma('sync', idb[:], D['idb'].ap(), [], [b_idb], 'idb')
            wo, b_wo = sb("wo", [128, 8, 1024], BF16)
            go, b_go = sb("go", [128, 8], F32)
            self.dma('sync', go[:], D['out_norm'].ap()[L].rearrange("(c p) -> p c", p=128), [], [b_go], 'go',
                     allow_slow_non_contiguous=True)
            stg = sb("stg4a", [128, 1024], F32, 2)
            for c in range(8):
                self.dma('sync', stg[c % 2][0][:], D['w_out'].ap()[L, c * 128:(c + 1) * 128, :], [], [stg[c % 2][1]],
                         'stg4a%d' % (c % 2))
                E('vector' if c % 2 == 0 else 'gpsimd', 'tensor_scalar', [stg[c % 2][1], b_go], [b_wo], out=wo[:, c, :],
                  in0=stg[c % 2][0][:], scalar1=go[:, c:c + 1], scalar2=None, op0=ALU.mult)
            invd3, b_invd3 = sb("invd3", [128, 3], F32)
            nh3, b_nh3 = sb("nh3", [128, 3], F32)
            P.op('gpsimd', lambda e: e.memset(invd3[:], 1.0 / 384), [], [b_invd3])
            P.op('gpsimd', lambda e: e.memset(invd3[:, 1:2], 1.0 / 256), [], [b_invd3])
            P.op('gpsimd', lambda e: e.memset(nh3[:], -0.5), [], [b_nh3])
            mx = sb("mx", [128, 1024], BF16, 3)
            xt = sb("x4a", [128, 1024], F32, 3)
            sq, b_sq = sb("sq4a", [128, 1024], F32)
            ss = sb("ss4a", [128, 3], F32, 2)
            rs = sb("rs4a", [128, 3], F32, 2)
            mn = sb("mn", [128, 1024], BF16, 2)
            mT = sb("mT", [128, 8, 128], BF16, 2)
            TR = ps("TR4a", [128, 8, 128], BF16, 2)
            Y = ps("Y4a", [128, 1024], F32, 2)
            grp = [(0, 384), (384, 640), (640, 1024)]
            def load4a(t):
                s3 = t % 3
                rows = slice(t * 128, (t + 1) * 128)
                self.dma('sync', mx[s3][0][:], SC['MIX'].ap()[rows, :], [SC['bmix']], [mx[s3][1]], 'mx%d' % s3)
                self.dma('sync', xt[s3][0][:], x_own[rows, :], [], [xt[s3][1]], 'x4a%d' % s3)
            load4a(0)

            def s1_4a(t):
                s3, s2 = t % 3, t % 2
                rows = slice(t * 128, (t + 1) * 128)
                if t + 1 < 32:
                    load4a(t + 1)
                m_t, m_b = mx[s3]
                V('tensor_tensor', [m_b], [b_sq], out=sq[:], in0=m_t[:], in1=m_t[:], op=ALU.mult)
                for gi, (a, b) in enumerate(grp):
                    V('tensor_reduce', [b_sq], [ss[s2][1]], out=ss[s2][0][:, gi:gi + 1], in_=sq[:, a:b], axis=AX.X,
                      op=ALU.add)
                G('tensor_tensor', [ss[s2][1], b_invd3], [rs[s2][1]], out=rs[s2][0][:], in0=ss[s2][0][:], in1=invd3[:],
                  op=ALU.mult)
                G('tensor_scalar', [rs[s2][1]], [rs[s2][1]], out=rs[s2][0][:], in0=rs[s2][0][:], scalar1=EPS, scalar2=None,
                  op0=ALU.add)
                G('tensor_tensor', [rs[s2][1], b_nh3], [rs[s2][1]], out=rs[s2][0][:], in0=rs[s2][0][:], in1=nh3[:],
                  op=ALU.pow)
                for gi, (a, b) in enumerate(grp):
                    E('vector' if gi != 1 else 'gpsimd', 'tensor_scalar', [m_b, rs[s2][1]], [mn[s2][1]],
                      out=mn[s2][0][:, a:b], in0=m_t[:, a:b], scalar1=rs[s2][0][:, gi:gi + 1], scalar2=None, op0=ALU.mult)
                for c in range(8):
                    T('transpose', [mn[s2][1], b_idb], [TR[s2][1]], out=TR[s2][0][:, c, :],
                      in_=mn[s2][0][:, c * 128:(c + 1) * 128], identity=idb[:])
                A('activation', [TR[s2][1]], [mT[s2][1]], out=mT[s2][0][:], in_=TR[s2][0][:], func=AF.Copy)

            def s2_4a(t):
                s3, s2 = t % 3, t % 2
                rows = slice(t * 128, (t + 1) * 128)
                for hf in range(2):
                    for c in range(8):
                        T('matmul', [mT[s2][1], b_wo], [Y[s2][1]], out=Y[s2][0][:, hf * 512:(hf + 1) * 512],
                          lhsT=mT[s2][0][:, c, :], rhs=wo[:, c, hf * 512:(hf + 1) * 512], start=(c == 0), stop=(c == 7))
                V('tensor_tensor', [Y[s2][1], xt[s3][1]], [xt[s3][1]], out=xt[s3][0][:], in0=Y[s2][0][:],
                  in1=xt[s3][0][:], op=ALU.add)
                self.dma('sync', x1_d[rows, :], xt[s3][0][:], [xt[s3][1]], [SC['bx1']], 'x4ao%d' % s3)

            s1_4a(0)
            for t in range(32):
                if t + 1 < 32:
                    s1_4a(t + 1)
                s2_4a(t)
        self.P.phase_barrier()

    def phase4b(self, L, x1_d, out_d, b_out):
        nc, P, D, SC = self.nc, self.P, self.D, self.SC
        V, G, A, T, E = self.V, self.G, self.A, self.T, self.E
        with ExitStack() as ph:
            def sb(nm, shape, dt, n=1):
                r = []
                for i in range(n):
                    t = ph.enter_context(nc.sbuf_tensor(self.name(nm), shape, dt))
                    r.append((t, Buf(nm + str(i))))
                return r if n > 1 else r[0]

            def ps(nm, shape, dt, n=1):
                r = []
                for i in range(n):
                    t = psum_view(ph, nc, self.name(nm), shape, dt)
                    r.append((t, Buf(nm + str(i))))
                return r if n > 1 else r[0]

            P.barrier('sync', reads=[SC['bx1']])
            idb, b_idb = sb("idb", [128, 128], BF16)
            self.dma('sync', idb[:], D['idb'].ap(), [], [b_idb], 'idb')
            wu, b_wu = sb("wu", [128, 8, 4096], BF16)
            wd, b_wd = sb("wd", [128, 32, 1024], BF16)
            gm, b_gm = sb("gm", [128, 8], F32)
            self.dma('sync', gm[:], D['norm_mlp'].ap()[L].rearrange("(c p) -> p c", p=128), [], [b_gm], 'gm',
                     allow_slow_non_contiguous=True)
            stg = sb("stg4b", [128, 1024], F32, 2)
            k = 0
            for c in range(8):
                for hf in range(4):
                    s = k % 2
                    self.dma('sync', stg[s][0][:], D['w_up'].ap()[L, c * 128:(c + 1) * 128, hf * 1024:(hf + 1) * 1024], [],
                             [stg[s][1]], 'stg4b%d' % s)
                    E('vector' if k % 2 == 0 else 'gpsimd', 'tensor_scalar', [stg[s][1], b_gm], [b_wu],
                      out=wu[:, c, hf * 1024:(hf + 1) * 1024], in0=stg[s][0][:], scalar1=gm[:, c:c + 1], scalar2=None,
                      op0=ALU.mult)
                    k += 1
            for c2 in range(32):
                s = k % 2
                self.dma('sync', stg[s][0][:], D['w_down'].ap()[L, c2 * 128:(c2 + 1) * 128, :], [],
                         [stg[s][1]], 'stg4b%d' % s)
                E('vector' if k % 2 == 0 else 'gpsimd', 'tensor_copy', [stg[s][1]], [b_wd],
                  out=wd[:, c2, :], in_=stg[s][0][:])
                k += 1
            nh1, b_nh1 = sb("nh1", [128, 1], F32)
            P.op('gpsimd', lambda e: e.memset(nh1[:], -0.5), [], [b_nh1])
            xc = sb("x4b", [128, 2, 1024], F32, 2)
            junk, b_junk = sb("junk4b", [128, 1024], BF16)
            ss = sb("ss4b", [128, 2], F32, 2)
            h2 = [sb("h2", [128, 2, 1024], BF16)] * 2
            h2T = [sb("h2T", [128, 8, 256], BF16)] * 2
            uT = [sb("uT", [128, 32, 256], BF16)] * 2
            rr = sb("rr", [128, 2, 256], F32, 3)
            TR = ps("TR4b", [128, 8, 128], BF16)
            U = ps("U4b", [128, 2, 256], F32, 2)
            Z = ps("Z4b", [128, 1024], F32, 2)
            def load4b(ch):
                s2 = ch % 2
                rows = slice(ch * 256, (ch + 1) * 256)
                self.dma('sync', xc[s2][0][:], x1_d[rows, :].rearrange("(t p) d -> p t d", p=128), [SC['bx1']],
                         [xc[s2][1]], 'x4b%d' % s2)
            load4b(0)
            for ch in range(16):
                s2 = ch % 2
                rows = slice(ch * 256, (ch + 1) * 256)
                x_t, x_b = xc[s2]
                if ch + 1 < 16:
                    load4b(ch + 1)
                for t in range(2):
                    A('activation', [x_b], [b_junk, ss[s2][1]], out=junk[:], in_=x_t[:, t, :], func=AF.Square,
                      accum_out=ss[s2][0][:, t:t + 1])
                G('tensor_scalar', [ss[s2][1]], [ss[s2][1]], out=ss[s2][0][:], in0=ss[s2][0][:], scalar1=1.0 / 1024,
                  scalar2=EPS, op0=ALU.mult, op1=ALU.add)
                G('tensor_tensor', [ss[s2][1], b_nh1], [ss[s2][1]], out=ss[s2][0][:], in0=ss[s2][0][:],
                  in1=bc(nh1[:], [128, 2]), op=ALU.pow)
                for t in range(2):
                    A('activation', [x_b, ss[s2][1]], [h2[s2][1]], out=h2[s2][0][:, t, :], in_=x_t[:, t, :], func=AF.Copy,
                      scale=ss[s2][0][:, t:t + 1])
                    for c in range(8):
                        T('transpose', [h2[s2][1], b_idb], [TR[1]], out=TR[0][:, c, :],
                          in_=h2[s2][0][:, t, c * 128:(c + 1) * 128], identity=idb[:])
                    V('tensor_copy', [TR[1]], [h2T[s2][1]], out=h2T[s2][0][:, :, t * 128:(t + 1) * 128], in_=TR[0][:])
                for f2 in range(16):
                    u_t, u_b = U[f2 % 2]
                    for ff in range(2):
                        fc = f2 * 2 + ff
                        for c in range(8):
                            T('matmul', [h2T[s2][1], b_wu], [u_b], out=u_t[:, ff, :], lhsT=wu[:, c, fc * 128:(fc + 1) * 128],
                              rhs=h2T[s2][0][:, c, :], start=(c == 0), stop=(c == 7))
                    r_t, r_b = rr[f2 % 3]
                    A('activation', [u_b], [r_b], out=r_t[:], in_=u_t[:], func=AF.Relu)
                    E('vector' if f2 % 2 == 0 else 'gpsimd', 'tensor_tensor', [r_b], [uT[s2][1]],
                      out=uT[s2][0][:, 2 * f2:2 * f2 + 2, :], in0=r_t[:], in1=r_t[:], op=ALU.mult)
                for t in range(2):
                    z_t, z_b = Z[t]
                    for hf in range(2):
                        for fc in range(32):
                            T('matmul', [uT[s2][1], b_wd], [z_b], out=z_t[:, hf * 512:(hf + 1) * 512],
                              lhsT=uT[s2][0][:, fc, t * 128:(t + 1) * 128], rhs=wd[:, fc, hf * 512:(hf + 1) * 512],
                              start=(fc == 0), stop=(fc == 31))
                    V('tensor_tensor', [z_b, x_b], [x_b], out=x_t[:, t, :], in0=z_t[:], in1=x_t[:, t, :], op=ALU.add)
                self.dma('sync', out_d[rows, :].rearrange("(t p) d -> p t d", p=128), x_t[:], [x_b], [b_out],
                         'x4bo%d' % s2)
        self.P.phase_barrier()

    def layer(self, L, x_own, x_oth, x_halo, x1_d, out_d, b_out, with_bias_setup=True, phases=None, pos_key='pos',
              valid_key='valid', halo_rows=None, reuse_ckv=False):
        ph = phases or ['bias', 'p1', 'C', 'B', 'A', 'A2', '4a', '4b']
        if with_bias_setup and 'bias' in ph:
            self.bias_setup()
        if 'p1' in ph:
            self.phase1(L, x_own, x_oth, x_halo, pos_key=pos_key, valid_key=valid_key, halo_rows=halo_rows,
                        skip_oth=reuse_ckv, skip_ck=reuse_ckv)
        if 'C' in ph:
            self.phaseC()
        if 'B' in ph:
            self.phaseB(L)
        if 'A' in ph:
            self.phaseA(cfgs=self.debug.get('cfgs', (0, 1, 2)) if self.debug else (0, 1, 2))
        if 'A2' in ph:
            self.phaseA2()
        if '4a' in ph:
            self.phase4a(L, x_own, x1_d)
        if '4b' in ph:
            self.phase4b(L, x1_d, out_d, b_out)


NLAYER = 2


def t5_bucket_np(rel):
    half, exact = 16, 8
    n = np.abs(rel)
    far = exact + (np.log(np.maximum(n, 1).astype(np.float32) / np.float32(exact))
                   / np.float32(math.log(1024 / exact)) * np.float32(half - exact)).astype(np.int32)
    far = np.minimum(far, half - 1)
    return np.where(rel > 0, half, 0) + np.where(n < exact, n, far)


def make_oh():
    oh = np.zeros((33, 1660), np.float32)
    d = np.arange(-255, 256)
    bk = t5_bucket_np(d)
    for i, dd in enumerate(d):
        if abs(dd) <= 128:
            oh[bk[i], i] = 1
        else:
            oh[32, i] = 1
    for ci, r in enumerate((1, 4, 16)):
        d = np.arange(-191, 192)
        bk = t5_bucket_np(d * r)
        for i, dd in enumerate(d):
            if abs(dd) <= 64:
                oh[bk[i], 511 + 383 * ci + i] = 1
            else:
                oh[32, 511 + 383 * ci + i] = 1
    return oh


WNAMES = ['norm_mix', 'w_in', 'qk_gain_a', 'qk_gain_b', 'sink_b', 'q_lat_gain', 'kv_lat_gain', 'w_uq', 'w_ukv',
          'qk_gain_c', 'out_norm', 'w_out', 'norm_mlp', 'w_up', 'w_down']


def declare(nc, NL):
    D = {}

    def inp(n, shape, dt=F32):
        D[n] = nc.dram_tensor(n, shape, dt, kind="ExternalInput")
    inp('x_own', [4096, 1024]); inp('x_oth', [4096, 1024]); inp('x_halo', [2048, 1024]); inp('x_halo2', [2048, 1024])
    inp('valid', [128, 16]); inp('valid2', [128, 16]); inp('pos', [128, 64], I32); inp('pos2', [128, 64], I32)
    inp('invf', [1, 16]); inp('idb', [128, 128], BF16); inp('idf', [128, 128])
    inp('norm_mix', [NL, 1024]); inp('w_in', [NL, 1024, 2080]); inp('qk_gain_a', [NL, 2, 64])
    inp('qk_gain_b', [NL, 2, 64]); inp('sink_b', [NL, 4]); inp('q_lat_gain', [NL, 256]); inp('kv_lat_gain', [NL, 128])
    inp('w_uq', [NL, 256, 576]); inp('w_ukv', [NL, 128, 768]); inp('qk_gain_c', [NL, 2, 96]); inp('out_norm', [NL, 1024])
    inp('w_out', [NL, 1024, 1024]); inp('norm_mlp', [NL, 1024]); inp('w_up', [NL, 1024, 4096]); inp('w_down', [NL, 4096, 1024])
    inp('rel_bias_table', [32, 10]); inp('oh', [33, 1660])
    SC = {}

    def scr(n, shape, dt=BF16):
        SC[n] = nc.dram_tensor(n, shape, dt, kind="Internal")
    scr('QA', [4096, 384]); scr('KAx', [6144, 384]); scr('VAx', [6144, 390]); scr('QB', [4096, 256])
    scr('KBx', [6144, 128]); scr('VBx', [6144, 130]); scr('QCT', [6, 96, 4096]); scr('KCT', [6, 96, 8192])
    scr('VC', [8192, 390]); scr('MIX', [4096, 1024]); scr('OA', [3, 4096, 390], F32); scr('FV', [10, 1660], F32)
    scr('X1', [4096, 1024], F32); scr('XA', [4096, 1024], F32); scr('XB', [4096, 1024], F32)
    SC['b'] = Buf('scratch', multi=True)
    SC['bmix'] = Buf('mix', multi=True)
    SC['boa'] = Buf('oa', multi=True)
    SC['bfv'] = Buf('fv', multi=True)
    SC['bx1'] = Buf('x1', multi=True)
    return D, SC


def make_inputs(inputs, core):
    b, half = core // 2, core % 2
    xs = np.asarray(inputs['x'], dtype=np.float32)[b]
    own = xs[half * 4096:(half + 1) * 4096]
    oth = xs[(1 - half) * 4096:(2 - half) * 4096]
    halo = np.zeros((2048, 1024), np.float32)
    valid = np.zeros((2048,), np.float32)
    halo2 = np.zeros((2048, 1024), np.float32)
    valid2 = np.zeros((2048,), np.float32)
    if half == 1:
        halo[0:1024] = oth[3072:4096]
        valid[0:1024] = 1
        halo2[1024:2048] = own[0:1024]
        valid2[1024:2048] = 1
    else:
        halo[1024:2048] = oth[0:1024]
        valid[1024:2048] = 1
        halo2[0:1024] = own[3072:4096]
        valid2[0:1024] = 1
    pos = np.asarray(inputs['positions'][b])
    p_own = pos[half * 4096:(half + 1) * 4096]
    p_oth = pos[(1 - half) * 4096:(2 - half) * 4096]
    pos_l = np.concatenate([p_own, p_oth])
    pos_l2 = np.concatenate([p_oth, p_own])
    m = {
        'x_own': np.ascontiguousarray(own), 'x_oth': np.ascontiguousarray(oth), 'x_halo': halo, 'x_halo2': halo2,
        'valid': np.ascontiguousarray(valid.reshape(16, 128).T),
        'valid2': np.ascontiguousarray(valid2.reshape(16, 128).T),
        'pos': np.ascontiguousarray(pos_l.reshape(64, 128).T.astype(np.int32)),
        'pos2': np.ascontiguousarray(pos_l2.reshape(64, 128).T.astype(np.int32)),
        'invf': (10000.0 ** (-np.arange(16, dtype=np.float32) / 16)).astype(np.float32).reshape(1, 16),
        'idb': np.eye(128).astype(ml_dtypes.bfloat16), 'idf': np.eye(128).astype(np.float32),
        'rel_bias_table': np.ascontiguousarray(inputs['rel_bias_table'], dtype=np.float32), 'oh': make_oh(),
    }
    for k in WNAMES:
        m[k] = np.ascontiguousarray(np.asarray(inputs[k], dtype=np.float32))
    return m


def build_program():
    nc = bass.Bass("TRN2", target_bir_lowering=False)
    D, SC = declare(nc, NLAYER)
    out_d = nc.dram_tensor('out', [4096, 1024], F32, kind="ExternalOutput")
    b_xa = Buf('xa', multi=True)
    b_xb = Buf('xb', multi=True)
    b_out = Buf('out', multi=True)
    with ExitStack() as es:
        P = Prog(nc, es)
        LB = LayerBuilder(nc, P, D, debug={})
        LB.SC = SC
        XA, XB = SC['XA'].ap(), SC['XB'].ap()
        LB.layer(0, D['x_own'].ap(), D['x_oth'].ap(), D['x_halo'].ap(), SC['X1'].ap(), XA, b_xa)
        LB.layer(0, D['x_oth'].ap(), D['x_own'].ap(), D['x_halo2'].ap(), SC['X1'].ap(), XB, b_xb,
                 with_bias_setup=False, pos_key='pos2', valid_key='valid2', reuse_ckv=True)

        def halo_rows(t):
            r0 = 3072 + t * 128 if t < 8 else (t - 8) * 128
            return XB[r0:r0 + 128, :]
        P.barrier('sync', reads=[b_xa, b_xb])
        LB.layer(1, XA, XB, None, SC['X1'].ap(), out_d.ap(), b_out, with_bias_setup=False, halo_rows=halo_rows)
        P.barrier('sync', reads=[b_out])
        P.emit()
    return nc


def kernel(**inputs):
    nc = build_program()
    in_maps = [make_inputs(inputs, c) for c in range(8)]
    res = run_bass_kernel_spmd(nc, in_maps, core_ids=list(range(8)))
    x = np.asarray(inputs['x'])
    out = np.empty(x.shape, np.float32)
    for c in range(8):
        b, half = c // 2, c % 2
        out[b, half * 4096:(half + 1) * 4096] = np.asarray(res.results[c]['out'], dtype=np.float32)
    return out
```

```python
import math
import ml_dtypes
from concourse.bass_utils import run_bass_kernel_spmd
import numpy as np
import concourse.bass as bass
import concourse.mybir as mybir
from contextlib import ExitStack

F32 = mybir.dt.float32
BF16 = mybir.dt.bfloat16
I32 = mybir.dt.int32
ALU = mybir.AluOpType
AF = mybir.ActivationFunctionType
AX = mybir.AxisListType

ENGS = ['sync', 'scalar', 'vector', 'gpsimd', 'tensor']
SEM_ROT = 24000


class Buf:
    __slots__ = ('name', 'writer', 'readers', 'dreaders', 'multi', 'mw')

    def __init__(self, name, multi=False):
        self.name = name
        self.writer = None
        self.readers = {}
        self.dreaders = []
        self.multi = multi
        self.mw = []


class Op:
    __slots__ = ('eng', 'fn', 'deps', 'idx', 'signal', 'is_dma', 'lane', 'ev', 'raw', 'barrier')


class Prog:
    def __init__(self, nc, es):
        self.nc = nc
        self.es = es
        self.ops = {e: [] for e in ENGS}
        self.order = []
        self.lanes = {}
        self.nsem = 0
        self.fence = []
        self.fence_pending = set()
        self.phase_lanes = {}

    def phase_barrier(self):
        fence = []
        for e in ENGS:
            for o in reversed(self.ops[e]):
                if not o.is_dma and not o.barrier:
                    fence.append(o)
                    break
        last = {}
        for o in self.order:
            if o.is_dma:
                last[o.lane] = o
        fence += list(last.values())
        self.fence = fence
        self.fence_pending = set(ENGS)
        self.phase_lanes = {}

    def new_sem(self, name):
        self.nsem += 1
        return self.es.enter_context(self.nc.semaphore(name))

    def sb(self, name, shape, dt):
        return self.es.enter_context(self.nc.sbuf_tensor(name, shape, dt))

    def ps(self, name, shape, dt):
        return self.es.enter_context(self.nc.psum_tensor(name, shape, dt))

    def barrier(self, eng, reads=(), writes=()):
        o = self.op(eng, lambda e: None, reads, writes)
        o.barrier = True
        return o

    def op(self, eng, fn, reads=(), writes=(), lane=None):
        o = Op()
        o.eng = eng
        o.fn = fn
        o.barrier = False
        o.is_dma = lane is not None
        if lane is not None:
            if lane not in self.phase_lanes:
                self.phase_lanes[lane] = 'L%d' % len(self.phase_lanes)
            lane = self.phase_lanes[lane]
        o.lane = lane
        o.signal = False
        o.ev = None
        deps = {}
        raw = set()
        for b in reads:
            if b.writer is not None:
                deps[id(b.writer)] = b.writer
                raw.add(id(b.writer))
            for w in b.mw:
                deps[id(w)] = w
                raw.add(id(w))
        for b in writes:
            if b.writer is not None and not b.multi:
                deps[id(b.writer)] = b.writer
                raw.add(id(b.writer))
            for r in b.readers.values():
                deps[id(r)] = r
            for r in b.dreaders:
                deps[id(r)] = r
        if eng in self.fence_pending:
            self.fence_pending.discard(eng)
            for w in self.fence:
                deps[id(w)] = w
        deps.pop(id(o), None)
        o.deps = list(deps.values())
        o.raw = raw
        for b in writes:
            if b.multi:
                b.mw.append(o)
            else:
                b.writer = o
            b.readers = {}
            b.dreaders = []
        for b in reads:
            if b.multi:
                continue
            if o.is_dma:
                b.dreaders.append(o)
            else:
                b.readers[eng] = o
        o.idx = len(self.ops[eng])
        self.ops[eng].append(o)
        self.order.append(o)
        return o

    def dma(self, q, out, in_, reads=(), writes=(), lane=None, **kw):
        assert lane is not None
        return self.op(q, lambda e: e.dma_start(out=out, in_=in_, **kw), reads, writes, lane=lane)

    def emit(self):
        nc = self.nc
        for o in self.order:
            for d in o.deps:
                if d.is_dma:
                    continue
                if d.barrier:
                    assert d.eng == o.eng, 'barrier dep across engines'
                    continue
                if d.eng == o.eng and not o.is_dma:
                    if o.eng == 'tensor':
                        continue
                    if id(d) not in o.raw:
                        continue
                d.signal = True
        esems = {}
        for e in ENGS:
            cnt = 0
            cur = None
            for o in self.ops[e]:
                if o.is_dma:
                    ln = self.lanes.get(o.lane)
                    if ln is None:
                        ln = [self.new_sem('l_%s' % o.lane), 0]
                        self.lanes[o.lane] = ln
                    ln[1] += 16
                    o.ev = (ln[0], ln[1])
                elif o.signal:
                    if cur is None or cnt >= SEM_ROT:
                        cur = self.new_sem('e_%s_%d' % (e, len(esems)))
                        esems[(e, len(esems))] = cur
                        cnt = 0
                    cnt += 1
                    o.ev = (cur, cnt)
        blk = self.es.enter_context(nc.Block())
        prog = self

        def run(e, eng):
            waited = {}
            for o in prog.ops[e]:
                need = {}
                for d in o.deps:
                    if not d.is_dma:
                        if d.barrier:
                            continue
                        if d.eng == o.eng and not o.is_dma:
                            if o.eng == 'tensor' or id(d) not in o.raw:
                                continue
                    sem, val = d.ev
                    k = id(sem)
                    if k not in need or need[k][1] < val:
                        need[k] = (sem, val)
                for k, (sem, val) in need.items():
                    if waited.get(k, 0) >= val:
                        continue
                    eng.wait_ge(sem, val)
                    waited[k] = val
                ins = o.fn(eng)
                if ins is None:
                    continue
                if o.is_dma:
                    ins.then_inc(o.ev[0], 16)
                elif o.signal:
                    ins.then_inc(o.ev[0], 1)

        @blk.sync
        def _(eng):
            run('sync', eng)

        @blk.scalar
        def _(eng):
            run('scalar', eng)

        @blk.vector
        def _(eng):
            run('vector', eng)

        @blk.gpsimd
        def _(eng):
            run('gpsimd', eng)

        @blk.tensor
        def _(eng):
            run('tensor', eng)

import numpy as np
import math

EPS = 1e-6
NEGB = -30000.0
TWO_PI_S = 6.2831845


def bc(ap, shape):
    return ap.to_broadcast(list(shape))


def psum_view(ph, nc, name, shape, dt):
    esz = 4 if dt == F32 else 2
    n = 1
    for d in shape[1:]:
        n *= d
    per_bank = 2048 // esz
    tot = ((n + per_bank - 1) // per_bank) * per_bank
    t = ph.enter_context(nc.psum_tensor(name, [128, tot], dt))
    v = t[0:shape[0], 0:n]
    if len(shape) == 3:
        v = v.rearrange("p (a b) -> p a b", a=shape[1])
    elif len(shape) == 4:
        v = v.rearrange("p (a b c) -> p a b c", a=shape[1], b=shape[2])
    return v


class LayerBuilder:
    def __init__(self, nc, P, D, debug=False):
        self.nc = nc
        self.P = P
        self.D = D
        self.debug = debug
        self.uid = 0

    def name(self, s):
        self.uid += 1
        return "%s_%d" % (s, self.uid)

    def V(self, fn, reads, writes, **kw):
        return self.P.op('vector', lambda e: getattr(e, fn)(**kw), reads, writes)

    def G(self, fn, reads, writes, **kw):
        return self.P.op('gpsimd', lambda e: getattr(e, fn)(**kw), reads, writes)

    def A(self, fn, reads, writes, **kw):
        return self.P.op('scalar', lambda e: getattr(e, fn)(**kw), reads, writes)

    def T(self, fn, reads, writes, **kw):
        return self.P.op('tensor', lambda e: getattr(e, fn)(**kw), reads, writes)

    def E(self, eng, fn, reads, writes, **kw):
        return self.P.op(eng, lambda e: getattr(e, fn)(**kw), reads, writes)

    def dma(self, q, out, in_, reads, writes, lane, **kw):
        return self.P.dma(q, out, in_, reads=reads, writes=writes, lane=lane, **kw)

    def phase1(self, L, x_own, x_oth, x_halo, first=True, pos_key='pos', valid_key='valid', halo_rows=None,
               skip_oth=False, skip_ck=False):
        nc, P, D = self.nc, self.P, self.D
        V, G, A, T, E = self.V, self.G, self.A, self.T, self.E
        NS = 4
        with ExitStack() as ph:
            def sb(nm, shape, dt, n=1):
                r = []
                for i in range(n):
                    t = ph.enter_context(nc.sbuf_tensor(self.name(nm), shape, dt))
                    r.append((t, Buf(nm + str(i))))
                return r if n > 1 else r[0]

            def ps(nm, shape, dt):
                t = psum_view(ph, nc, self.name(nm), shape, dt)
                return (t, Buf(nm))

            setup = ExitStack()

            def sbs(nm, shape, dt):
                t = setup.enter_context(nc.sbuf_tensor(self.name(nm), shape, dt))
                return (t, Buf(nm))

            idb, b_idb = sb("idb", [128, 128], BF16)
            wib, b_wib = sb("wib", [128, 8, 2080], BF16)
            wuq, b_wuq = sb("wuq", [128, 2, 576], BF16)
            wukv, b_wukv = sb("wukv", [128, 768], BF16)
            g8, b_g8 = sb("g8", [128, 8], F32)
            gq2, b_gq2 = sb("gq2", [128, 2], F32)
            gkv1, b_gkv1 = sb("gkv1", [128, 1], F32)
            ga, b_ga = sb("ga", [128, 2, 64], F32)
            gb, b_gb = sb("gb", [128, 2, 64], F32)
            gc, b_gc = sb("gc", [128, 2, 96], F32)
            GAq, b_GAq = sb("GAq", [128, 64], F32)
            GBq, b_GBq = sb("GBq", [128, 64], F32)
            GCq, b_GCq = sb("GCq", [128, 96], F32)
            invf, b_invf = sb("invf", [128, 16], F32)
            sin_t, b_sin = sb("sin_t", [128, 64, 16], F32)
            cos_t, b_cos = sb("cos_t", [128, 64, 16], F32)
            invd, b_invd = sb("invd", [128, 24], F32)
            nh24, b_nh24 = sb("nh24", [128, 24], F32)
            valid, b_valid = sb("valid", [128, 16], F32)
            posi, b_posi = sbs("posi", [128, 64], I32)
            posf, b_posf = sbs("posf", [128, 64], F32)
            ang, b_ang = sbs("ang", [128, 64, 16], F32)
            angk, b_angk = sbs("angk", [128, 64, 16], I32)
            angf, b_angf = sbs("angf", [128, 64, 16], F32)
            stage = [sbs("stage", [128, 2080], F32)] * 2
            stq, b_stq = sbs("stq", [128, 2, 576], F32)
            stkv, b_stkv = sbs("stkv", [128, 768], F32)

            self.dma('sync', idb[:], D['idb'].ap(), [], [b_idb], 'idb')
            self.dma('sync', g8[:], D['norm_mix'].ap()[L].rearrange("(c p) -> p c", p=128), [], [b_g8], 'g8',
                     allow_slow_non_contiguous=True)
            self.dma('sync', gq2[:], D['q_lat_gain'].ap()[L].rearrange("(c p) -> p c", p=128), [], [b_gq2], 'gq2',
                     allow_slow_non_contiguous=True)
            self.dma('sync', gkv1[:], D['kv_lat_gain'].ap()[L].rearrange("(c p) -> p c", p=128), [], [b_gkv1],
                     'gkv1', allow_slow_non_contiguous=True)
            self.dma('sync', ga[:], D['qk_gain_a'].ap()[L].rearrange("a d -> (a d)").partition_broadcast(128),
                     [], [b_ga], 'ga')
            self.dma('sync', gb[:], D['qk_gain_b'].ap()[L].rearrange("a d -> (a d)").partition_broadcast(128),
                     [], [b_gb], 'gb')
            self.dma('sync', gc[:], D['qk_gain_c'].ap()[L].rearrange("a d -> (a d)").partition_broadcast(128),
                     [], [b_gc], 'gc')
            self.dma('sync', invf[:], D['invf'].ap().rearrange("a d -> (a d)").partition_broadcast(128),
                     [], [b_invf], 'invf')
            self.dma('sync', posi[:], D[pos_key].ap(), [], [b_posi], 'posi')
            self.dma('sync', valid[:], D[valid_key].ap(), [], [b_valid], 'valid')
            V('scalar_tensor_tensor', [b_ga], [b_GAq], out=GAq[:], in0=ga[:, 0, :], scalar=0.125, in1=ga[:, 1, :],
              op0=ALU.mult, op1=ALU.mult)
            V('scalar_tensor_tensor', [b_gb], [b_GBq], out=GBq[:], in0=gb[:, 0, :], scalar=0.125, in1=gb[:, 1, :],
              op0=ALU.mult, op1=ALU.mult)
            V('tensor_scalar', [b_gc], [b_GCq], out=GCq[:], in0=gc[:, 0, :], scalar1=96.0 ** -0.5, scalar2=None,
              op0=ALU.mult)
            GCk = gc[:, 1, :]
            b_GCk = b_gc
            self.P.op('gpsimd', lambda e: e.memset(invd[:], 1.0 / 64), [], [b_invd])
            self.P.op('gpsimd', lambda e: e.memset(invd[:, 18:19], 1.0 / 256), [], [b_invd])
            self.P.op('gpsimd', lambda e: e.memset(invd[:, 19:20], 1.0 / 128), [], [b_invd])
            self.P.op('gpsimd', lambda e: e.memset(invd[:, 20:24], 1.0), [], [b_invd])
            self.P.op('gpsimd', lambda e: e.memset(nh24[:], -0.5), [], [b_nh24])

            V('tensor_copy', [b_posi], [b_posf], out=posf[:], in_=posi[:])
            V('tensor_tensor', [b_posf, b_invf], [b_ang], out=ang[:],
              in0=bc(posf[:].unsqueeze(2), [128, 64, 16]), in1=bc(invf[:].unsqueeze(1), [128, 64, 16]), op=ALU.mult)
            for (tab, b_tab, off) in ((sin_t, b_sin, 0.0), (cos_t, b_cos, 0.25)):
                V('tensor_scalar', [b_ang], [b_angf], out=angf[:], in0=ang[:], scalar1=1.0 / (2 * math.pi),
                  scalar2=off, op0=ALU.mult, op1=ALU.add)
                V('tensor_copy', [b_angf], [b_angk], out=angk[:], in_=angf[:])
                V('tensor_copy', [b_angk], [b_tab], out=tab[:], in_=angk[:])
                V('tensor_tensor', [b_angf, b_tab], [b_angf], out=angf[:], in0=angf[:], in1=tab[:], op=ALU.subtract)
                A('activation', [b_angf], [b_tab], out=tab[:], in_=angf[:], func=AF.Sin, scale=TWO_PI_S)

            blocks = [(0, 384, 0), (384, 768, 512), (768, 1152, 1024), (1152, 1408, 1536), (1408, 1536, 896),
                      (1536, 1664, 1408), (1664, 1920, 1792), (1920, 2048, 384), (2048, 2080, 2048)]
            k = 0
            for c in range(8):
                st_t, st_b = stage[c % 2]
                self.dma('sync', st_t[:], D['w_in'].ap()[L, c * 128:(c + 1) * 128, :], [], [st_b], 'stage0')
                for (o0, o1, n0) in blocks:
                    eng = 'vector' if k % 2 == 0 else 'gpsimd'
                    k += 1
                    E(eng, 'tensor_scalar', [st_b, b_g8], [b_wib], out=wib[:, c, n0:n0 + (o1 - o0)],
                      in0=st_t[:, o0:o1], scalar1=g8[:, c:c + 1], scalar2=None, op0=ALU.mult)
            self.dma('sync', stq[:], D['w_uq'].ap()[L].rearrange("(c p) n -> p c n", p=128), [], [b_stq], 'stq')
            self.dma('sync', stkv[:], D['w_ukv'].ap()[L], [], [b_stkv], 'stkv')
            for c in range(2):
                V('tensor_scalar', [b_stq, b_gq2], [b_wuq], out=wuq[:, c, :], in0=stq[:, c, :],
                  scalar1=gq2[:, c:c + 1], scalar2=None, op0=ALU.mult)
            V('tensor_scalar', [b_stkv, b_gkv1], [b_wukv], out=wukv[:], in0=stkv[:], scalar1=gkv1[:, 0:1],
              scalar2=None, op0=ALU.mult)

            self.P.phase_barrier()
            setup.close()
            xt = sb("xt", [128, 1024], F32, NS)
            junk, b_junk = sb("junk", [128, 1024], BF16)
            ssx = sb("ssx", [128, 1], F32, NS)
            rsx = sb("rsx", [128, 1], F32, NS)
            hb = sb("hb", [128, 1024], BF16, NS)
            hT = sb("hT", [128, 8, 128], BF16, NS)
            pj = sb("pj", [128, 2080], F32, NS)
            sq, b_sq = sb("sq", [128, 2080], F32)
            st = sb("st", [128, 24], F32, NS)
            rstd = sb("rstd", [128, 24], F32, NS)
            QAo = sb("QAo", [128, 384], BF16, NS)
            QAt = sb("QAt", [128, 384], F32, 1)
            KABo = sb("KABo", [128, 512], BF16, NS)
            QBo = sb("QBo", [128, 256], BF16, NS)
            QBt = sb("QBt", [128, 256], F32, 1)
            VABo = sb("VABo", [128, 8, 65], BF16, NS)
            LAT = sb("LAT", [128, 384], BF16, NS)
            latT = sb("latT", [128, 3, 128], BF16, NS)
            qcs = sb("qcs", [128, 576], F32, NS)
            kvcs = sb("kvcs", [128, 768], F32, NS)
            st2 = sb("st2", [128, 12], F32, NS)
            rstd2 = sb("rstd2", [128, 12], F32, NS)
            tmp1, b_tmp1 = sb("tmp1", [128, 6, 96], F32)
            trq, b_trq = sb("trq", [128, 6, 32], F32)
            tmpk, b_tmpk = sb("tmpk", [128, 6, 64], F32)
            krg, b_krg = sb("krg", [128, 1, 32], F32)
            krr, b_krr = sb("krr", [128, 1, 32], F32)
            rm = [sb("rm%d" % i, [128, 6, 16], F32) for i in range(4)]
            QCo = sb("QCo", [128, 6, 96], BF16, NS)
            KCo = sb("KCo", [128, 6, 96], BF16, NS)
            VCo = sb("VCo", [128, 6, 65], BF16, NS)
            QTs = sb("QTs", [96, 6, 128], BF16, NS)
            KTs = sb("KTs", [96, 6, 128], BF16, NS)
            TR, b_TR = ps("TR", [128, 8, 128], BF16)
            PJ = [ps("PJ%d" % i, [128, 512], F32) for i in range(5)]
            S = [ps("S%d" % i, [128, 512], F32) for i in range(2)]

            for (t_, b_) in VABo:
                self.P.op('gpsimd', lambda e, t_=t_: e.memset(t_[:], 1.0), [], [b_])
            for (t_, b_) in VCo:
                self.P.op('gpsimd', lambda e, t_=t_: e.memset(t_[:], 1.0), [], [b_])

            def rope(src, b_src, dst, b_dst, H, ti):
                cb = bc(cos_t[:, ti, :].unsqueeze(1), [128, H, 16])
                sbb = bc(sin_t[:, ti, :].unsqueeze(1), [128, H, 16])
                (m1, b1), (m2, b2), (m3, b3), (m4, b4) = rm
                V('tensor_tensor', [b_src, b_cos], [b1], out=m1[:, 0:H, :], in0=src[:, :, 0:16], in1=cb, op=ALU.mult)
                V('tensor_tensor', [b_src, b_sin], [b2], out=m2[:, 0:H, :], in0=src[:, :, 16:32], in1=sbb, op=ALU.mult)
                V('tensor_tensor', [b1, b2], [b_dst], out=dst[:, :, 0:16], in0=m1[:, 0:H, :], in1=m2[:, 0:H, :],
                  op=ALU.subtract)
                V('tensor_tensor', [b_src, b_cos], [b3], out=m3[:, 0:H, :], in0=src[:, :, 16:32], in1=cb, op=ALU.mult)
                V('tensor_tensor', [b_src, b_sin], [b4], out=m4[:, 0:H, :], in0=src[:, :, 0:16], in1=sbb, op=ALU.mult)
                V('tensor_tensor', [b3, b4], [b_dst], out=dst[:, :, 16:32], in0=m3[:, 0:H, :], in1=m4[:, 0:H, :],
                  op=ALU.add)

            SC = self.SC
            it = 0
            jobs = [('own', t) for t in range(32)] + [('oth', t) for t in range(32)] + [('halo', t) for t in range(16)]
            if skip_oth:
                jobs = [j for j in jobs if j[0] != 'oth']
            if self.debug and self.debug.get('p1_tiles'):
                jobs = self.debug['p1_tiles']
            def tile_gen(it, kind, t):
                s2 = it % NS
                s3 = it % NS
                src = {'own': x_own, 'oth': x_oth, 'halo': x_halo}[kind]
                x_t, b_x = xt[s3]
                ss_t, b_ss = ssx[s2]
                rs_t, b_rs = rsx[s2]
                hb_t, b_hb = hb[s2]
                hT_t, b_hT = hT[s2]
                pj_t, b_pj = pj[s3]
                st_t, b_st = st[s3]
                rstd_t, b_rstd = rstd[s3]
                if kind == 'halo':
                    G('tensor_scalar', [b_x, b_valid], [b_x], out=x_t[:], in0=x_t[:], scalar1=valid[:, t:t + 1],
                      scalar2=None, op0=ALU.mult)
                    yield
                A('activation', [b_x], [b_junk, b_ss], out=junk[:], in_=x_t[:], func=AF.Square, accum_out=ss_t[:])
                yield
                G('tensor_scalar', [b_ss], [b_ss], out=ss_t[:], in0=ss_t[:], scalar1=1.0 / 1024, scalar2=EPS,
                  op0=ALU.mult, op1=ALU.add)
                yield
                G('tensor_tensor', [b_ss, b_nh24], [b_rs], out=rs_t[:], in0=ss_t[:], in1=nh24[:, 0:1], op=ALU.pow)
                yield
                A('activation', [b_x, b_rs], [b_hb], out=hb_t[:], in_=x_t[:], func=AF.Copy, scale=rs_t[:, 0:1])
                yield
                for c in range(8):
                    T('transpose', [b_hb, b_idb], [b_TR], out=TR[:, c, :], in_=hb_t[:, c * 128:(c + 1) * 128],
                      identity=idb[:])
                V('tensor_copy', [b_TR], [b_hT], out=hT_t[:], in_=TR[:])
                yield
                if kind == 'own':
                    groups = [(0, 0, 512, 0), (1, 512, 1024, 0), (2, 1024, 1536, 0), (3, 1536, 2048, 0),
                              (4, 2048, 2080, 0)]
                elif kind == 'oth':
                    groups = [(0, 384, 512, 384), (4, 2048, 2080, 0)]
                else:
                    groups = [(1, 512, 1024, 0), (2, 1024, 1536, 0)]
                for (bk, c0, c1, po) in groups:
                    pt, pb = PJ[bk]
                    for c in range(8):
                        T('matmul', [b_hT, b_wib], [pb], out=pt[:, po:po + (c1 - c0)], lhsT=hT_t[:, c, :],
                          rhs=wib[:, c, c0:c1], start=(c == 0), stop=(c == 7))
                    A('activation', [pb], [b_pj], out=pj_t[:, c0:c1], in_=pt[:, po:po + (c1 - c0)], func=AF.Copy)
                    yield
                if kind == 'own':
                    V('tensor_tensor', [b_pj], [b_sq], out=sq[:, 0:1024], in0=pj_t[:, 0:1024], in1=pj_t[:, 0:1024],
                      op=ALU.mult)
                    V('tensor_tensor', [b_pj], [b_sq], out=sq[:, 1536:2080], in0=pj_t[:, 1536:2080],
                      in1=pj_t[:, 1536:2080], op=ALU.mult)
                    red = [(0, 6, 0, 384, 64), (6, 14, 512, 1024, 64), (14, 18, 1536, 1792, 64),
                           (18, 19, 1792, 2048, 256), (19, 20, 384, 512, 128), (20, 21, 2048, 2080, 32)]
                elif kind == 'oth':
                    V('tensor_tensor', [b_pj], [b_sq], out=sq[:, 384:512], in0=pj_t[:, 384:512], in1=pj_t[:, 384:512],
                      op=ALU.mult)
                    V('tensor_tensor', [b_pj], [b_sq], out=sq[:, 2048:2080], in0=pj_t[:, 2048:2080],
                      in1=pj_t[:, 2048:2080], op=ALU.mult)
                    red = [(19, 20, 384, 512, 128), (20, 21, 2048, 2080, 32)]
                else:
                    V('tensor_tensor', [b_pj], [b_sq], out=sq[:, 512:1024], in0=pj_t[:, 512:1024],
                      in1=pj_t[:, 512:1024], op=ALU.mult)
                    red = [(6, 14, 512, 1024, 64)]
                for (a0, a1, c0, c1, dd) in red:
                    V('tensor_reduce', [b_sq], [b_st], out=st_t[:, a0:a1],
                      in_=sq[:, c0:c1].rearrange("p (h d) -> p h d", d=dd), axis=AX.X, op=ALU.add)
                V('tensor_tensor', [b_st, b_invd], [b_rstd], out=rstd_t[:, 0:20], in0=st_t[:, 0:20], in1=invd[:, 0:20],
                  op=ALU.mult)
                yield
                V('tensor_scalar', [b_rstd], [b_rstd], out=rstd_t[:, 0:20], in0=rstd_t[:, 0:20], scalar1=EPS,
                  scalar2=None, op0=ALU.add)
                yield
                G('tensor_tensor', [b_rstd, b_nh24], [b_rstd], out=rstd_t[:, 0:20], in0=rstd_t[:, 0:20],
                  in1=nh24[:, 0:20], op=ALU.pow)
                yield
                if kind in ('own', 'halo'):
                    et = (8 + t) if kind == 'own' else (t if t < 8 else 40 + (t - 8))
                    kab_t, b_kab = KABo[s2]
                    vab_t, b_vab = VABo[s2]
                    V('tensor_tensor', [b_pj, b_rstd], [b_kab], out=kab_t[:].rearrange("p (h d) -> p h d", d=64),
                      in0=pj_t[:, 512:1024].rearrange("p (h d) -> p h d", d=64),
                      in1=bc(rstd_t[:, 6:14].unsqueeze(2), [128, 8, 64]), op=ALU.mult)
                    yield
                    V('tensor_copy', [b_pj], [b_vab], out=vab_t[:, :, 0:64],
                      in_=pj_t[:, 1024:1536].rearrange("p (h d) -> p h d", d=64))
                    yield
                    if kind == 'halo':
                        V('tensor_copy', [b_valid], [b_vab], out=vab_t[:, :, 64:65],
                          in_=bc(valid[:, t:t + 1].unsqueeze(1), [128, 8, 1]))
                        yield
                    else:
                        self.P.op('vector', lambda e, vab_t=vab_t: e.memset(vab_t[:, :, 64:65], 1.0), [], [b_vab])
                        yield
                    rows = slice(et * 128, (et + 1) * 128)
                    self.dma('sync', SC['KAx'].ap()[rows, :], kab_t[:, 0:384], [b_kab], [SC['b']], 'kabo%d' % s2)
                    yield
                    self.dma('sync', SC['KBx'].ap()[rows, :], kab_t[:, 384:512], [b_kab], [SC['b']], 'kabo%d' % s2)
                    yield
                    self.dma('sync', SC['VAx'].ap()[rows, :].rearrange("p (h d) -> p h d", d=65), vab_t[:, 0:6, :],
                             [b_vab], [SC['b']], 'vabo%d' % s2)
                    yield
                    self.dma('sync', SC['VBx'].ap()[rows, :].rearrange("p (h d) -> p h d", d=65), vab_t[:, 6:8, :],
                             [b_vab], [SC['b']], 'vabo%d' % s2)
                    yield
                if kind == 'own':
                    rows = slice(t * 128, (t + 1) * 128)
                    qa_t, b_qa = QAo[s2]
                    qat, b_qat = QAt
                    V('tensor_tensor', [b_pj, b_rstd], [b_qat], out=qat[:].rearrange("p (h d) -> p h d", d=64),
                      in0=pj_t[:, 0:384].rearrange("p (h d) -> p h d", d=64),
                      in1=bc(rstd_t[:, 0:6].unsqueeze(2), [128, 6, 64]), op=ALU.mult)
                    V('tensor_tensor', [b_qat, b_GAq], [b_qa], out=qa_t[:].rearrange("p (h d) -> p h d", d=64),
                      in0=qat[:].rearrange("p (h d) -> p h d", d=64),
                      in1=bc(GAq[:].unsqueeze(1), [128, 6, 64]), op=ALU.mult)
                    self.dma('sync', SC['QA'].ap()[rows, :], qa_t[:], [b_qa], [SC['b']], 'qao%d' % s2)
                    yield
                    qb_t, b_qb = QBo[s2]
                    qbt, b_qbt = QBt
                    V('tensor_tensor', [b_pj, b_rstd], [b_qbt], out=qbt[:].rearrange("p (h d) -> p h d", d=64),
                      in0=pj_t[:, 1536:1792].rearrange("p (h d) -> p h d", d=64),
                      in1=bc(rstd_t[:, 14:18].unsqueeze(2), [128, 4, 64]), op=ALU.mult)
                    V('tensor_tensor', [b_qbt, b_GBq], [b_qb],
                      out=qb_t[:].rearrange("p (b a d) -> p a b d", b=2, a=2, d=64),
                      in0=qbt[:].rearrange("p (a b d) -> p a b d", a=2, b=2, d=64),
                      in1=bc(GBq[:].unsqueeze(1).unsqueeze(1), [128, 2, 2, 64]), op=ALU.mult)
                    self.dma('sync', SC['QB'].ap()[rows, :], qb_t[:], [b_qb], [SC['b']], 'qbo%d' % s2)
                    yield
                if kind in ('own', 'oth'):
                    ti = t if kind == 'own' else 32 + t
                    lat_t, b_lat = LAT[s2]
                    latT_t, b_latT = latT[s2]
                    qcs_t, b_qcs = qcs[s2]
                    kvcs_t, b_kvcs = kvcs[s2]
                    st2_t, b_st2 = st2[s2]
                    rstd2_t, b_rstd2 = rstd2[s2]
                    if kind == 'own':
                        V('tensor_scalar', [b_pj, b_rstd], [b_lat], out=lat_t[:, 0:256], in0=pj_t[:, 1792:2048],
                          scalar1=rstd_t[:, 18:19], scalar2=None, op0=ALU.mult)
                        yield
                    if not skip_ck:
                        V('tensor_scalar', [b_pj, b_rstd], [b_lat], out=lat_t[:, 256:384], in0=pj_t[:, 384:512],
                          scalar1=rstd_t[:, 19:20], scalar2=None, op0=ALU.mult)
                        yield
                    jl = ([0, 1] if skip_ck else [0, 1, 2]) if kind == 'own' else [2]
                    for j in jl:
                        T('transpose', [b_lat, b_idb], [b_TR], out=TR[:, j, :], in_=lat_t[:, j * 128:(j + 1) * 128],
                          identity=idb[:])
                    V('tensor_copy', [b_TR], [b_latT], out=latT_t[:, jl[0]:jl[-1] + 1, :], in_=TR[:, jl[0]:jl[-1] + 1, :])
                    if kind == 'own':
                        for hf in range(2):
                            for c in range(2):
                                T('matmul', [b_latT, b_wuq], [S[hf][1]], out=S[hf][0][:, 0:288], lhsT=latT_t[:, c, :],
                                  rhs=wuq[:, c, hf * 288:(hf + 1) * 288], start=(c == 0), stop=(c == 1))
                            A('activation', [S[hf][1]], [b_qcs], out=qcs_t[:, hf * 288:(hf + 1) * 288],
                              in_=S[hf][0][:, 0:288], func=AF.Copy)
                    for hf in (range(2) if not skip_ck else []):
                        T('matmul', [b_latT, b_wukv], [S[hf][1]], out=S[hf][0][:, 0:384], lhsT=latT_t[:, 2, :],
                          rhs=wukv[:, hf * 384:(hf + 1) * 384], start=True, stop=True)
                        A('activation', [S[hf][1]], [b_kvcs], out=kvcs_t[:, hf * 384:(hf + 1) * 384],
                          in_=S[hf][0][:, 0:384], func=AF.Copy)
                    kv3 = kvcs_t[:].rearrange("p (h d) -> p h d", d=128)
                    if kind == 'own':
                        V('tensor_tensor', [b_qcs], [b_sq], out=sq[:, 0:576], in0=qcs_t[:], in1=qcs_t[:], op=ALU.mult)
                        V('tensor_reduce', [b_sq], [b_st2], out=st2_t[:, 0:6],
                          in_=sq[:, 0:576].rearrange("p (h d) -> p h d", d=96), axis=AX.X, op=ALU.add)
                    if not skip_ck:
                        V('tensor_tensor', [b_kvcs], [b_sq], out=sq[:, 1024:1408].rearrange("p (h d) -> p h d", d=64),
                          in0=kv3[:, :, 0:64], in1=kv3[:, :, 0:64], op=ALU.mult)
                        V('tensor_reduce', [b_sq], [b_st2], out=st2_t[:, 6:12],
                          in_=sq[:, 1024:1408].rearrange("p (h d) -> p h d", d=64), axis=AX.X, op=ALU.add)
                        V('tensor_scalar', [b_st2, b_st], [b_st2], out=st2_t[:, 6:12], in0=st2_t[:, 6:12],
                          scalar1=st_t[:, 20:21], scalar2=None, op0=ALU.add)
                    lo = 0 if kind == 'own' else 6
                    hi_ = 6 if skip_ck else 12
                    V('tensor_scalar', [b_st2], [b_rstd2], out=rstd2_t[:, lo:hi_], in0=st2_t[:, lo:hi_],
                      scalar1=1.0 / 96, scalar2=EPS, op0=ALU.mult, op1=ALU.add)
                    yield
                    G('tensor_tensor', [b_rstd2, b_nh24], [b_rstd2], out=rstd2_t[:, lo:hi_], in0=rstd2_t[:, lo:hi_],
                      in1=nh24[:, lo:hi_], op=ALU.pow)
                    yield
                    kc_t, b_kc = KCo[s2]
                    vc_t, b_vc = VCo[s2]
                    if kind == 'own':
                        qc_t, b_qc = QCo[s2]
                        V('tensor_tensor', [b_qcs, b_rstd2], [b_tmp1], out=tmp1[:],
                          in0=qcs_t[:].rearrange("p (h d) -> p h d", d=96),
                          in1=bc(rstd2_t[:, 0:6].unsqueeze(2), [128, 6, 96]), op=ALU.mult)
                        V('tensor_tensor', [b_tmp1, b_GCq], [b_qc], out=qc_t[:, :, 0:64], in0=tmp1[:, :, 0:64],
                          in1=bc(GCq[:, 0:64].unsqueeze(1), [128, 6, 64]), op=ALU.mult)
                        V('tensor_tensor', [b_tmp1, b_GCq], [b_trq], out=trq[:], in0=tmp1[:, :, 64:96],
                          in1=bc(GCq[:, 64:96].unsqueeze(1), [128, 6, 32]), op=ALU.mult)
                        rope(trq, b_trq, qc_t[:, :, 64:96], b_qc, 6, ti)
                    if kind == 'own':
                        qT_t, b_qT = QTs[s2]
                        for h in range(6):
                            T('transpose', [b_qc, b_idb], [b_TR], out=TR[0:96, h, :], in_=qc_t[:, h, :],
                              identity=idb[:])
                        V('tensor_copy', [b_TR], [b_qT], out=qT_t[:], in_=TR[0:96, 0:6, :])
                        self.dma('sync', SC['QCT'].ap()[:, :, t * 128:(t + 1) * 128].rearrange("h d n -> d h n"),
                                 qT_t[:], [b_qT], [SC['b']], 'qto%d' % s2)
                        yield
                    if skip_ck:
                        return
                    V('tensor_tensor', [b_kvcs, b_rstd2], [b_tmpk], out=tmpk[:], in0=kv3[:, :, 0:64],
                      in1=bc(rstd2_t[:, 6:12].unsqueeze(2), [128, 6, 64]), op=ALU.mult)
                    V('tensor_tensor', [b_tmpk, b_GCk], [b_kc], out=kc_t[:, :, 0:64], in0=tmpk[:],
                      in1=bc(GCk[:, 0:64].unsqueeze(1), [128, 6, 64]), op=ALU.mult)
                    V('tensor_tensor', [b_pj, b_GCk], [b_krg], out=krg[:, 0, :], in0=pj_t[:, 2048:2080],
                      in1=GCk[:, 64:96], op=ALU.mult)
                    rope(krg, b_krg, krr[:], b_krr, 1, ti)
                    V('tensor_tensor', [b_krr, b_rstd2], [b_kc], out=kc_t[:, :, 64:96],
                      in0=bc(krr[:], [128, 6, 32]), in1=bc(rstd2_t[:, 6:12].unsqueeze(2), [128, 6, 32]), op=ALU.mult)
                    V('tensor_copy', [b_kvcs], [b_vc], out=vc_t[:, :, 0:64], in_=kv3[:, :, 64:128])
                    yield
                    kT_t, b_kT = KTs[s2]
                    for h in range(6):
                        T('transpose', [b_kc, b_idb], [b_TR], out=TR[0:96, h, :], in_=kc_t[:, h, :], identity=idb[:])
                    V('tensor_copy', [b_TR], [b_kT], out=kT_t[:], in_=TR[0:96, 0:6, :])
                    self.dma('sync', SC['KCT'].ap()[:, :, ti * 128:(ti + 1) * 128].rearrange("h d n -> d h n"),
                             kT_t[:], [b_kT], [SC['b']], 'kto%d' % s2)
                    yield
                    self.dma('sync', SC['VC'].ap()[ti * 128:(ti + 1) * 128, :].rearrange("p (h d) -> p h d", d=65),
                             vc_t[:], [b_vc], [SC['b']], 'vco%d' % s2)
                    yield

            def issue_load(j):
                kind, t = jobs[j]
                x_t, b_x = xt[j % NS]
                if kind == 'halo' and halo_rows is not None:
                    self.dma('sync', x_t[:], halo_rows(t), [], [b_x], 'xt%d' % (j % NS))
                else:
                    src = {'own': x_own, 'oth': x_oth, 'halo': x_halo}[kind]
                    self.dma('sync', x_t[:], src[t * 128:(t + 1) * 128, :], [], [b_x], 'xt%d' % (j % NS))

            for j in range(min(2, len(jobs))):
                issue_load(j)
            active = []
            nxt = 0
            since = 10 ** 9
            STAG = 9
            while active or nxt < len(jobs):
                if nxt < len(jobs) and len(active) < NS and (since >= STAG or not active):
                    if nxt + 2 < len(jobs):
                        issue_load(nxt + 2)
                    active.append(tile_gen(nxt, jobs[nxt][0], jobs[nxt][1]))
                    nxt += 1
                    since = 0
                for g in list(active):
                    try:
                        next(g)
                    except StopIteration:
                        active.remove(g)
                since += 1
        self.P.phase_barrier()

    def phaseC(self, heads=range(6), nqt=8):
        nc, P, D, SC = self.nc, self.P, self.D, self.SC
        V, G, A, T, E = self.V, self.G, self.A, self.T, self.E
        with ExitStack() as ph:
            def sb(nm, shape, dt, n=1):
                r = []
                for i in range(n):
                    t = ph.enter_context(nc.sbuf_tensor(self.name(nm), shape, dt))
                    r.append((t, Buf(nm + str(i))))
                return r if n > 1 else r[0]

            def ps(nm, shape, dt, n=1):
                r = []
                for i in range(n):
                    t = psum_view(ph, nc, self.name(nm), shape, dt)
                    r.append((t, Buf(nm + str(i))))
                return r if n > 1 else r[0]

            P.barrier('sync', reads=[SC['b']])
            idf, b_idf = sb("idf", [128, 128], F32)
            self.dma('sync', idf[:], D['idf'].ap(), [], [b_idf], 'idf')
            KT = sb("cKT", [96, 8192], BF16, 2)
            VV = sb("cV", [128, 64, 65], BF16, 2)
            QT = sb("cQT", [96, 4096], BF16, 2)
            PT = sb("cPT", [128, 512], BF16, 4)
            OTs = sb("cOTs", [65, 512], F32, 2)
            rc = sb("crc", [128, 4], F32, 2)
            oc = sb("coc", [128, 4, 64], BF16, 2)
            ST = ps("cST", [128, 512], F32, 3)
            OT = ps("cOT", [65, 512], F32, 2)
            TO, b_TO = ps("cTO", [128, 4, 65], F32)

            heads = list(heads)

            def load_head(hi):
                h = heads[hi]
                s = hi % 2
                self.dma('sync', KT[s][0][:], SC['KCT'].ap()[h], [SC['b']], [KT[s][1]], 'cKT%d' % s)
                self.dma('sync', QT[s][0][:], SC['QCT'].ap()[h], [SC['b']], [QT[s][1]], 'cQT%d' % s)
                self.dma('sync', VV[s][0][:],
                         SC['VC'].ap().rearrange("(t p) c -> p t c", p=128)[:, :, h * 65:(h + 1) * 65],
                         [SC['b']], [VV[s][1]], 'cV%d' % s)

            steps = [(hi, qt, kc) for hi in range(len(heads)) for qt in range(nqt) for kc in range(64)]
            n = len(steps)
            LA = 2
            deferred = []
            load_head(0)
            nq = 0
            for i in range(n + LA):
                if i < n:
                    hi, qt, kc = steps[i]
                    s = hi % 2
                    st_t, st_b = ST[i % 3]
                    pt_t, pt_b = PT[i % 4]
                    T('matmul', [KT[s][1], QT[s][1]], [st_b], out=st_t[:], lhsT=KT[s][0][:, kc * 128:(kc + 1) * 128],
                      rhs=QT[s][0][:, qt * 512:(qt + 1) * 512], start=True, stop=True)
                    A('activation', [st_b], [pt_b], out=pt_t[:], in_=st_t[:], func=AF.Exp)
                j = i - LA
                if j >= 0:
                    hi, qt, kc = steps[j]
                    if qt == 0 and kc == 0 and hi + 1 < len(heads):
                        load_head(hi + 1)
                    s = hi % 2
                    qi = (hi * nqt + qt)
                    ot_t, ot_b = OT[qi % 2]
                    pt_t, pt_b = PT[j % 4]
                    T('matmul', [VV[s][1], pt_b], [ot_b], out=ot_t[:], lhsT=VV[s][0][:, kc, :], rhs=pt_t[:],
                      start=(kc == 0), stop=(kc == 63))
                    if kc == 63:
                        h = heads[hi]
                        os_t, os_b = OTs[qi % 2]
                        V('tensor_copy', [ot_b], [os_b], out=os_t[:], in_=ot_t[:])

                        def fin(os_t=os_t, os_b=os_b, qi=qi, qt=qt, h=h):
                            for jj in range(4):
                                T('transpose', [os_b, b_idf], [b_TO], out=TO[:, jj, :],
                                  in_=os_t[:, jj * 128:(jj + 1) * 128], identity=idf[0:65, 0:65])
                            rc_t, rc_b = rc[qi % 2]
                            oc_t, oc_b = oc[qi % 2]
                            V('reciprocal', [b_TO], [rc_b], out=rc_t[:].unsqueeze(2), in_=TO[:, :, 64:65])
                            V('tensor_tensor', [b_TO, rc_b], [oc_b], out=oc_t[:], in0=TO[:, :, 0:64],
                              in1=bc(rc_t[:].unsqueeze(2), [128, 4, 64]), op=ALU.mult)
                            self.dma('sync',
                                     SC['MIX'].ap()[qt * 512:(qt + 1) * 512, 640 + h * 64:640 + (h + 1) * 64]
                                     .rearrange("(t p) d -> p t d", p=128),
                                     oc_t[:], [oc_b], [SC['bmix']], 'coc%d' % (qi % 2))
                        deferred.append((i + 6, fin))
                while deferred and deferred[0][0] <= i:
                    deferred.pop(0)[1]()
            for (_, fn) in deferred:
                fn()
        self.P.phase_barrier()

    def bias_setup(self):
        nc, P, D, SC = self.nc, self.P, self.D, self.SC
        V, G, A, T, E = self.V, self.G, self.A, self.T, self.E
        with ExitStack() as ph:
            tabN = ph.enter_context(nc.sbuf_tensor(self.name("tabN"), [33, 10], F32)); b_tab = Buf("tabN")
            oh = ph.enter_context(nc.sbuf_tensor(self.name("oh"), [33, 1660], F32)); b_oh = Buf("oh")
            fv = ph.enter_context(nc.sbuf_tensor(self.name("fv"), [10, 1660], F32)); b_fv = Buf("fv")
            pf = [(psum_view(ph, nc, self.name("pf"), [10, 512], F32), Buf("pf%d" % i)) for i in range(4)]
            P.op('vector', lambda e: e.memset(tabN[:], NEGB), [], [b_tab])
            self.dma('sync', tabN[0:32, :], D['rel_bias_table'].ap(), [], [b_tab], 'tabN')
            self.dma('sync', oh[:], D['oh'].ap(), [], [b_oh], 'oh')
            segs = [(0, 511), (511, 894), (894, 1277), (1277, 1660)]
            for i, (a, b) in enumerate(segs):
                T('matmul', [b_tab, b_oh], [pf[i][1]], out=pf[i][0][:, 0:b - a], lhsT=tabN[:], rhs=oh[:, a:b],
                  start=True, stop=True)
                V('tensor_copy', [pf[i][1]], [b_fv], out=fv[:, a:b], in_=pf[i][0][:, 0:b - a])
            self.dma('sync', SC['FV'].ap(), fv[:], [b_fv], [SC['bfv']], 'fvo')
        self.P.phase_barrier()

    def phaseB(self, L):
        nc, P, D, SC = self.nc, self.P, self.D, self.SC
        V, G, A, T, E = self.V, self.G, self.A, self.T, self.E
        with ExitStack() as ph:
            def sb(nm, shape, dt, n=1):
                r = []
                for i in range(n):
                    t = ph.enter_context(nc.sbuf_tensor(self.name(nm), shape, dt))
                    r.append((t, Buf(nm + str(i))))
                return r if n > 1 else r[0]

            def ps(nm, shape, dt, n=1):
                r = []
                for i in range(n):
                    t = psum_view(ph, nc, self.name(nm), shape, dt)
                    r.append((t, Buf(nm + str(i))))
                return r if n > 1 else r[0]

            P.barrier('sync', reads=[SC['b'], SC['bfv']])
            idf, b_idf = sb("idf", [128, 128], F32)
            idb, b_idb = sb("idb", [128, 128], BF16)
            self.dma('sync', idf[:], D['idf'].ap(), [], [b_idf], 'idf')
            self.dma('sync', idb[:], D['idb'].ap(), [], [b_idb], 'idb')
            biasB = sb("biasB", [128, 4, 3, 128], F32)
            hk = sb("hkB", [128, 128], F32, 4)
            for h in range(4):
                for o in range(3):
                    g_t, g_b = hk[(h * 3 + o) % 4]
                    self.dma('sync', g_t[:], bass.AP(SC['FV'], (6 + h) * 1660 + 128 * o, [[1, 128], [1, 128]]),
                             [SC['bfv']], [g_b], 'hkB%d' % ((h * 3 + o) % 4))
                    V('tensor_copy', [g_b], [biasB[1]], out=biasB[0][:, h, o, :],
                      in_=bass.AP(g_t, 127, [[128, 128], [-1, 128]]))
            sk, b_sk = sb("sink", [128, 4], F32)
            esk, b_esk = sb("esink", [128, 4], F32)
            self.dma('sync', sk[:], D['sink_b'].ap()[L].partition_broadcast(128), [], [b_sk], 'sink')
            A('activation', [b_sk], [b_esk], out=esk[:], in_=sk[:], func=AF.Exp)
            Qc = sb("bQc", [128, 4, 256], BF16, 2)
            Kc = sb("bKc", [128, 6, 128], BF16, 2)
            Vc = sb("bVc", [128, 6, 130], BF16, 2)
            QTb = sb("bQT", [128, 2, 4, 128], BF16, 2)
            KTb = sb("bKT", [128, 6, 128], BF16, 2)
            Sb = sb("bS", [128, 3, 128], F32, 2)
            PTb = sb("bPT", [128, 3, 128], BF16, 2)
            OTs = sb("bOTs", [65, 4, 128], F32, 2)
            den = sb("bden", [128, 4], F32, 2)
            ob = sb("bo", [128, 4, 64], BF16, 2)
            TRq = ps("bTRq", [128, 2, 4, 128], BF16)
            TRk = ps("bTRk", [128, 6, 128], BF16)
            SP = ps("bSP", [128, 3, 128], F32, 2)
            OTp = ps("bOT", [65, 4, 128], F32, 2)
            TOp = ps("bTO", [128, 4, 65], F32)

            def stage1(J):
                s = J % 2
                self.dma('sync', Qc[s][0][:], SC['QB'].ap()[J * 512:(J + 1) * 512, :].rearrange("(t p) c -> p t c", p=128),
                         [SC['b']], [Qc[s][1]], 'bQc%d' % s)
                r0 = (7 + 4 * J) * 128
                self.dma('sync', Kc[s][0][:], SC['KBx'].ap()[r0:r0 + 768, :].rearrange("(t p) c -> p t c", p=128),
                         [SC['b']], [Kc[s][1]], 'bKc%d' % s)
                self.dma('sync', Vc[s][0][:], SC['VBx'].ap()[r0:r0 + 768, :].rearrange("(t p) c -> p t c", p=128),
                         [SC['b']], [Vc[s][1]], 'bVc%d' % s)
                for t in range(4):
                    for pi in range(2):
                        T('transpose', [Qc[s][1], b_idb], [TRq[1]], out=TRq[0][:, pi, t, :],
                          in_=Qc[s][0][:, t, pi * 128:(pi + 1) * 128], identity=idb[:])
                V('tensor_copy', [TRq[1]], [QTb[s][1]], out=QTb[s][0][:], in_=TRq[0][:])
                for kt in range(6):
                    T('transpose', [Kc[s][1], b_idb], [TRk[1]], out=TRk[0][:, kt, :], in_=Kc[s][0][:, kt, :],
                      identity=idb[:])
                V('tensor_copy', [TRk[1]], [KTb[s][1]], out=KTb[s][0][:], in_=TRk[0][:])

            cnt = [0]

            def stage2(J):
                s = J % 2
                for t in range(4):
                    qi = J * 4 + t
                    ot_t, ot_b = OTp[qi % 2]
                    for h in range(4):
                        base = 64 * (h // 2)
                        pi = h % 2
                        kvh = h // 2
                        c = cnt[0]
                        cnt[0] += 1
                        sp_t, sp_b = SP[c % 2]
                        for o in range(3):
                            T('matmul', [KTb[s][1], QTb[s][1]], [sp_b], out=sp_t[:, o, :],
                              lhsT=KTb[s][0][base:base + 64, t + o, :], rhs=QTb[s][0][base:base + 64, pi, t, :],
                              start=True, stop=True)
                        s_t, s_b = Sb[c % 2]
                        p_t, p_b = PTb[c % 2]
                        V('tensor_tensor', [sp_b, biasB[1]], [s_b], out=s_t[:], in0=sp_t[:], in1=biasB[0][:, h, :, :],
                          op=ALU.add)
                        A('activation', [s_b], [p_b], out=p_t[:], in_=s_t[:], func=AF.Exp)
                        for o in range(3):
                            T('matmul', [Vc[s][1], p_b], [ot_b], out=ot_t[:, h, :],
                              lhsT=Vc[s][0][:, t + o, kvh * 65:(kvh + 1) * 65], rhs=p_t[:, o, :],
                              start=(o == 0), stop=(o == 2))
                    os_t, os_b = OTs[qi % 2]
                    V('tensor_copy', [ot_b], [os_b], out=os_t[:], in_=ot_t[:])
                    for h in range(4):
                        T('transpose', [os_b, b_idf], [TOp[1]], out=TOp[0][:, h, :], in_=os_t[:, h, :],
                          identity=idf[0:65, 0:65])
                    d_t, d_b = den[qi % 2]
                    o_t, o_b = ob[qi % 2]
                    V('tensor_tensor', [TOp[1], b_esk], [d_b], out=d_t[:].unsqueeze(2), in0=TOp[0][:, :, 64:65],
                      in1=esk[:].unsqueeze(2), op=ALU.add)
                    V('reciprocal', [d_b], [d_b], out=d_t[:], in_=d_t[:])
                    V('tensor_tensor', [TOp[1], d_b], [o_b], out=o_t[:], in0=TOp[0][:, :, 0:64],
                      in1=bc(d_t[:].unsqueeze(2), [128, 4, 64]), op=ALU.mult)
                    self.dma('sync', SC['MIX'].ap()[qi * 128:(qi + 1) * 128, 384:640], o_t[:], [o_b], [SC['bmix']],
                             'bo%d' % (qi % 2))

            stage1(0)
            for J in range(8):
                if J + 1 < 8:
                    stage1(J + 1)
                stage2(J)
        self.P.phase_barrier()

    def phaseA(self, cfgs=(0, 1, 2)):
        nc, P, D, SC = self.nc, self.P, self.D, self.SC
        V, G, A, T, E = self.V, self.G, self.A, self.T, self.E
        RS = (1, 4, 16)
        with ExitStack() as ph:
            def sb(nm, shape, dt, n=1):
                r = []
                for i in range(n):
                    t = ph.enter_context(nc.sbuf_tensor(self.name(nm), shape, dt))
                    r.append((t, Buf(nm + str(i))))
                return r if n > 1 else r[0]

            def ps(nm, shape, dt, n=1):
                r = []
                for i in range(n):
                    t = psum_view(ph, nc, self.name(nm), shape, dt)
                    r.append((t, Buf(nm + str(i))))
                return r if n > 1 else r[0]

            P.barrier('sync', reads=[SC['b'], SC['bfv']])
            idf, b_idf = sb("idf", [128, 128], F32)
            idb, b_idb = sb("idb", [128, 128], BF16)
            self.dma('sync', idf[:], D['idf'].ap(), [], [b_idf], 'idf')
            self.dma('sync', idb[:], D['idb'].ap(), [], [b_idb], 'idb')
            biasA = sb("biasA", [128, 3, 6, 2, 128], F32)
            hk = sb("hkA", [128, 128], F32, 4)
            for ci in range(3):
                for h in range(6):
                    for c in range(2):
                        kk = (ci * 6 + h) * 2 + c
                        g_t, g_b = hk[kk % 4]
                        self.dma('sync', g_t[:],
                                 bass.AP(SC['FV'], h * 1660 + 511 + 383 * ci + 128 * c, [[1, 128], [1, 128]]),
                                 [SC['bfv']], [g_b], 'hkA%d' % (kk % 4))
                        V('tensor_copy', [g_b], [biasA[1]], out=biasA[0][:, ci, h, c, :],
                          in_=bass.AP(g_t, 127, [[128, 128], [-1, 128]]))
            Qa = sb("aQ", [128, 384], BF16, 3)
            Ka = sb("aK", [128, 2, 384], BF16, 3)
            Va = sb("aV", [128, 2, 390], BF16, 3)
            QTa = sb("aQT", [128, 2, 3, 128], BF16, 2)
            for (t_, b_) in QTa:
                P.op('vector', lambda e, t_=t_: e.memset(t_[:], 0.0), [], [b_])
            KTa = sb("aKT", [128, 3, 2, 128], BF16, 2)
            Sa = sb("aS", [128, 6, 2, 128], F32, 2)
            PTa = sb("aPT", [128, 6, 2, 128], BF16, 2)
            OTs = sb("aOTs", [65, 6, 128], F32, 2)
            Oo = sb("aOo", [128, 6, 65], F32, 2)
            TRq = ps("aTRq", [128, 3, 128], BF16)
            TRk = ps("aTRk", [128, 3, 2, 128], BF16)
            SP = ps("aSP", [128, 2, 2, 128], F32, 3)
            OTp = ps("aOT", [65, 3, 128], F32, 2)
            TOp = ps("aTO", [128, 6, 65], F32)

            jobs = []
            for ci in cfgs:
                r = RS[ci]
                for rho in range(r):
                    for j in range(32 // r):
                        jobs.append((ci, r, rho, j))

            if self.debug and self.debug.get('a_jobs'):
                jobs = self.debug['a_jobs']

            def loadA(i):
                ci, r, rho, j = jobs[i]
                s3 = i % 3
                self.dma('sync', Qa[s3][0][:], bass.AP(SC['QA'], (r * 128 * j + rho) * 384, [[r * 384, 128], [1, 384]]),
                         [SC['b']], [Qa[s3][1]], 'aQ%d' % s3)
                e0 = r * (128 * j - 64) + rho + 1024
                self.dma('sync', Ka[s3][0][:],
                         bass.AP(SC['KAx'], e0 * 384, [[r * 384, 128], [r * 384 * 128, 2], [1, 384]]),
                         [SC['b']], [Ka[s3][1]], 'aK%d' % s3)
                self.dma('sync', Va[s3][0][:],
                         bass.AP(SC['VAx'], e0 * 390, [[r * 390, 128], [r * 390 * 128, 2], [1, 390]]),
                         [SC['b']], [Va[s3][1]], 'aV%d' % s3)

            def stage1(i):
                ci, r, rho, j = jobs[i]
                s3 = i % 3
                s2 = i % 2
                for pi in range(3):
                    T('transpose', [Qa[s3][1], b_idb], [TRq[1]], out=TRq[0][:, pi, :],
                      in_=Qa[s3][0][:, pi * 128:(pi + 1) * 128], identity=idb[:])
                V('tensor_copy', [TRq[1]], [QTa[s2][1]], out=QTa[s2][0][0:64, 0, :, :], in_=TRq[0][0:64, :, :])
                V('tensor_copy', [TRq[1]], [QTa[s2][1]], out=QTa[s2][0][64:128, 1, :, :], in_=TRq[0][64:128, :, :])
                for pi in range(3):
                    for c in range(2):
                        T('transpose', [Ka[s3][1], b_idb], [TRk[1]], out=TRk[0][:, pi, c, :],
                          in_=Ka[s3][0][:, c, pi * 128:(pi + 1) * 128], identity=idb[:])
                V('tensor_copy', [TRk[1]], [KTa[s2][1]], out=KTa[s2][0][:], in_=TRk[0][:])

            astop = self.debug.get('a_stop', 99) if self.debug else 99

            def stage2(i):
                ci, r, rho, j = jobs[i]
                s3 = i % 3
                s2 = i % 2
                s_t, s_b = Sa[s2]
                p_t, p_b = PTa[s2]
                if astop < 2:
                    return
                for pi in range(3):
                    sp_t, sp_b = SP[pi]
                    for hh in range(2):
                        for c in range(2):
                            T('matmul', [KTa[s2][1], QTa[s2][1]], [sp_b], out=sp_t[:, hh, c, :],
                              lhsT=KTa[s2][0][:, pi, c, :],
                              rhs=QTa[s2][0][:, hh, pi, :], start=True, stop=True)
                    if self.debug and self.debug.get('a_nobias'):
                        continue
                    V('tensor_tensor', [sp_b, biasA[1]], [s_b], out=s_t[:, 2 * pi:2 * pi + 2, :, :], in0=sp_t[:],
                      in1=biasA[0][:, ci, 2 * pi:2 * pi + 2, :, :], op=ALU.add)
                if astop < 3:
                    return
                A('activation', [s_b], [p_b], out=p_t[:], in_=s_t[:], func=AF.Exp)
                os_t, os_b = OTs[s2]
                if astop < 4:
                    return
                for g3 in range(2):
                    ot_t, ot_b = OTp[g3]
                    for hh in range(3):
                        h = g3 * 3 + hh
                        for c in range(2):
                            T('matmul', [Va[s3][1], p_b], [ot_b], out=ot_t[:, hh, :],
                              lhsT=Va[s3][0][:, c, h * 65:(h + 1) * 65], rhs=p_t[:, h, c, :],
                              start=(c == 0), stop=(c == 1))
                    V('tensor_copy', [ot_b], [os_b], out=os_t[:, g3 * 3:g3 * 3 + 3, :], in_=ot_t[:])
                if astop < 5:
                    return
                for h in range(6):
                    T('transpose', [os_b, b_idf], [TOp[1]], out=TOp[0][:, h, :], in_=os_t[:, h, :],
                      identity=idf[0:65, 0:65])
                o_t, o_b = Oo[s2]
                V('tensor_copy', [TOp[1]], [o_b], out=o_t[:], in_=TOp[0][:])
                self.dma('sync', bass.AP(SC['OA'], ci * 4096 * 390 + (r * 128 * j + rho) * 390, [[r * 390, 128], [1, 390]]),
                         o_t[:].rearrange("p h d -> p (h d)"), [o_b], [SC['boa']], 'aOo%d' % s2)

            n = len(jobs)
            for i in range(min(2, n)):
                loadA(i)
            stage1(0)
            for i in range(n):
                if i + 2 < n:
                    loadA(i + 2)
                if i + 1 < n:
                    stage1(i + 1)
                stage2(i)
        self.P.phase_barrier()

    def phaseA2(self):
        nc, P, D, SC = self.nc, self.P, self.D, self.SC
        V, G, A, T, E = self.V, self.G, self.A, self.T, self.E
        with ExitStack() as ph:
            def sb(nm, shape, dt, n=1):
                r = []
                for i in range(n):
                    t = ph.enter_context(nc.sbuf_tensor(self.name(nm), shape, dt))
                    r.append((t, Buf(nm + str(i))))
                return r if n > 1 else r[0]
            P.barrier('sync', reads=[SC['boa']])
            O3 = sb("a2O", [128, 3, 6, 65], F32, 3)
            acc = sb("a2acc", [128, 6, 65], F32, 2)
            rc = sb("a2rc", [128, 6], F32, 2)
            oo = sb("a2o", [128, 6, 64], BF16, 2)
            for t in range(32):
                s3, s2 = t % 3, t % 2
                self.dma('sync', O3[s3][0][:].rearrange("p c h d -> p c (h d)"),
                         SC['OA'].ap()[:, t * 128:(t + 1) * 128, :].rearrange("c p d -> p c d"),
                         [SC['boa']], [O3[s3][1]], 'a2O%d' % s3)
                o3 = O3[s3][0]
                G('tensor_tensor', [O3[s3][1]], [acc[s2][1]], out=acc[s2][0][:], in0=o3[:, 0], in1=o3[:, 1], op=ALU.add)
                G('tensor_tensor', [O3[s3][1], acc[s2][1]], [acc[s2][1]], out=acc[s2][0][:], in0=acc[s2][0][:],
                  in1=o3[:, 2], op=ALU.add)
                V('reciprocal', [acc[s2][1]], [rc[s2][1]], out=rc[s2][0][:].unsqueeze(2), in_=acc[s2][0][:, :, 64:65])
                V('tensor_tensor', [acc[s2][1], rc[s2][1]], [oo[s2][1]], out=oo[s2][0][:], in0=acc[s2][0][:, :, 0:64],
                  in1=bc(rc[s2][0][:].unsqueeze(2), [128, 6, 64]), op=ALU.mult)
                self.dma('sync', SC['MIX'].ap()[t * 128:(t + 1) * 128, 0:384], oo[s2][0][:].rearrange("p h d -> p (h d)"),
                         [oo[s2][1]], [SC['bmix']], 'a2o%d' % s2)
        self.P.phase_barrier()

    def phase4a(self, L, x_own, x1_d):
        nc, P, D, SC = self.nc, self.P, self.D, self.SC
        V, G, A, T, E = self.V, self.G, self.A, self.T, self.E
        with ExitStack() as ph:
            def sb(nm, shape, dt, n=1):
                r = []
                for i in range(n):
                    t = ph.enter_context(nc.sbuf_tensor(self.name(nm), shape, dt))
                    r.append((t, Buf(nm + str(i))))
                return r if n > 1 else r[0]

            def ps(nm, shape, dt, n=1):
                r = []
                for i in range(n):
                    t = psum_view(ph, nc, self.name(nm), shape, dt)
                    r.append((t, Buf(nm + str(i))))
                return r if n > 1 else r[0]

            P.barrier('sync', reads=[SC['bmix']])
            idb, b_idb = sb("idb", [128, 128], BF16)
            self.dma('sync', idb[:], D['idb'].ap(), [], [b_idb], 'idb')
            wo, b_wo = sb("wo", [128, 8, 1024], BF16)
            go, b_go = sb("go", [128, 8], F32)
            self.dma('sync', go[:], D['out_norm'].ap()[L].rearrange("(c p) -> p c", p=128), [], [b_go], 'go',
                     allow_slow_non_contiguous=True)
            stg = sb("stg4a", [128, 1024], F32, 2)
            for c in range(8):
                self.dma('sync', stg[c % 2][0][:], D['w_out'].ap()[L, c * 128:(c + 1) * 128, :], [], [stg[c % 2][1]],
                         'stg4a%d' % (c % 2))
                if c % 2 == 0:
                    V('tensor_scalar', [stg[c % 2][1], b_go], [b_wo], out=wo[:, c, :], in0=stg[c % 2][0][:],
                      scalar1=go[:, c:c + 1], scalar2=None, op0=ALU.mult)
                else:
                    A('activation', [stg[c % 2][1], b_go], [b_wo], out=wo[:, c, :], in_=stg[c % 2][0][:], func=AF.Copy,
                      scale=go[:, c:c + 1])
            invd3, b_invd3 = sb("invd3", [128, 3], F32)
            nh3, b_nh3 = sb("nh3", [128, 3], F32)
            P.op('gpsimd', lambda e: e.memset(invd3[:], 1.0 / 384), [], [b_invd3])
            P.op('gpsimd', lambda e: e.memset(invd3[:, 1:2], 1.0 / 256), [], [b_invd3])
            P.op('gpsimd', lambda e: e.memset(nh3[:], -0.5), [], [b_nh3])
            mx = sb("mx", [128, 1024], BF16, 3)
            xt = sb("x4a", [128, 1024], F32, 3)
            sq, b_sq = sb("sq4a", [128, 1024], F32)
            ss = sb("ss4a", [128, 3], F32, 2)
            rs = sb("rs4a", [128, 3], F32, 2)
            mn = sb("mn", [128, 1024], BF16, 2)
            mT = sb("mT", [128, 8, 128], BF16, 2)
            TR = ps("TR4a", [128, 8, 128], BF16, 2)
            Y = ps("Y4a", [128, 1024], F32, 2)
            grp = [(0, 384), (384, 640), (640, 1024)]
            def load4a(t):
                s3 = t % 3
                rows = slice(t * 128, (t + 1) * 128)
                self.dma('sync', mx[s3][0][:], SC['MIX'].ap()[rows, :], [SC['bmix']], [mx[s3][1]], 'mx%d' % s3)
                self.dma('sync', xt[s3][0][:], x_own[rows, :], [], [xt[s3][1]], 'x4a%d' % s3)
            load4a(0)

            def s1_4a(t):
                s3, s2 = t % 3, t % 2
                rows = slice(t * 128, (t + 1) * 128)
                if t + 1 < 32:
                    load4a(t + 1)
                m_t, m_b = mx[s3]
                V('tensor_tensor', [m_b], [b_sq], out=sq[:], in0=m_t[:], in1=m_t[:], op=ALU.mult)
                for gi, (a, b) in enumerate(grp):
                    V('tensor_reduce', [b_sq], [ss[s2][1]], out=ss[s2][0][:, gi:gi + 1], in_=sq[:, a:b], axis=AX.X,
                      op=ALU.add)
                G('tensor_tensor', [ss[s2][1], b_invd3], [rs[s2][1]], out=rs[s2][0][:], in0=ss[s2][0][:], in1=invd3[:],
                  op=ALU.mult)
                G('tensor_scalar', [rs[s2][1]], [rs[s2][1]], out=rs[s2][0][:], in0=rs[s2][0][:], scalar1=EPS, scalar2=None,
                  op0=ALU.add)
                G('tensor_tensor', [rs[s2][1], b_nh3], [rs[s2][1]], out=rs[s2][0][:], in0=rs[s2][0][:], in1=nh3[:],
                  op=ALU.pow)
                for gi, (a, b) in enumerate(grp):
                    E('vector' if gi != 1 else 'gpsimd', 'tensor_scalar', [m_b, rs[s2][1]], [mn[s2][1]],
                      out=mn[s2][0][:, a:b], in0=m_t[:, a:b], scalar1=rs[s2][0][:, gi:gi + 1], scalar2=None, op0=ALU.mult)
                for c in range(8):
                    T('transpose', [mn[s2][1], b_idb], [TR[s2][1]], out=TR[s2][0][:, c, :],
                      in_=mn[s2][0][:, c * 128:(c + 1) * 128], identity=idb[:])
                A('activation', [TR[s2][1]], [mT[s2][1]], out=mT[s2][0][:], in_=TR[s2][0][:], func=AF.Copy)

            def s2_4a(t):
                s3, s2 = t % 3, t % 2
                rows = slice(t * 128, (t + 1) * 128)
                for hf in range(2):
                    for c in range(8):
                        T('matmul', [mT[s2][1], b_wo], [Y[s2][1]], out=Y[s2][0][:, hf * 512:(hf + 1) * 512],
                          lhsT=mT[s2][0][:, c, :], rhs=wo[:, c, hf * 512:(hf + 1) * 512], start=(c == 0), stop=(c == 7))
                V('tensor_tensor', [Y[s2][1], xt[s3][1]], [xt[s3][1]], out=xt[s3][0][:], in0=Y[s2][0][:],
                  in1=xt[s3][0][:], op=ALU.add)
                self.dma('sync', x1_d[rows, :], xt[s3][0][:], [xt[s3][1]], [SC['bx1']], 'x4ao%d' % s3)

            s1_4a(0)
            for t in range(32):
                if t + 1 < 32:
                    s1_4a(t + 1)
                s2_4a(t)
        self.P.phase_barrier()

    def phase4b(self, L, x1_d, out_d, b_out):
        nc, P, D, SC = self.nc, self.P, self.D, self.SC
        V, G, A, T, E = self.V, self.G, self.A, self.T, self.E
        with ExitStack() as ph:
            def sb(nm, shape, dt, n=1):
                r = []
                for i in range(n):
                    t = ph.enter_context(nc.sbuf_tensor(self.name(nm), shape, dt))
                    r.append((t, Buf(nm + str(i))))
                return r if n > 1 else r[0]

            def ps(nm, shape, dt, n=1):
                r = []
                for i in range(n):
                    t = psum_view(ph, nc, self.name(nm), shape, dt)
                    r.append((t, Buf(nm + str(i))))
                return r if n > 1 else r[0]

            P.barrier('sync', reads=[SC['bx1']])
            idb, b_idb = sb("idb", [128, 128], BF16)
            self.dma('sync', idb[:], D['idb'].ap(), [], [b_idb], 'idb')
            wu, b_wu = sb("wu", [128, 8, 4096], BF16)
            wd, b_wd = sb("wd", [128, 32, 1024], BF16)
            gm, b_gm = sb("gm", [128, 8], F32)
            self.dma('sync', gm[:], D['norm_mlp'].ap()[L].rearrange("(c p) -> p c", p=128), [], [b_gm], 'gm',
                     allow_slow_non_contiguous=True)
            stg = sb("stg4b", [128, 1024], F32, 2)
            k = 0
            for c in range(8):
                for hf in range(4):
                    s = k % 2
                    self.dma('sync', stg[s][0][:], D['w_up'].ap()[L, c * 128:(c + 1) * 128, hf * 1024:(hf + 1) * 1024], [],
                             [stg[s][1]], 'stg4b%d' % s)
                    if k % 2 == 0:
                        V('tensor_scalar', [stg[s][1], b_gm], [b_wu], out=wu[:, c, hf * 1024:(hf + 1) * 1024],
                          in0=stg[s][0][:], scalar1=gm[:, c:c + 1], scalar2=None, op0=ALU.mult)
                    else:
                        A('activation', [stg[s][1], b_gm], [b_wu], out=wu[:, c, hf * 1024:(hf + 1) * 1024],
                          in_=stg[s][0][:], func=AF.Copy, scale=gm[:, c:c + 1])
                    k += 1
            for c2 in range(32):
                s = k % 2
                self.dma('sync', stg[s][0][:], D['w_down'].ap()[L, c2 * 128:(c2 + 1) * 128, :], [],
                         [stg[s][1]], 'stg4b%d' % s)
                if k % 2 == 0:
                    V('tensor_copy', [stg[s][1]], [b_wd], out=wd[:, c2, :], in_=stg[s][0][:])
                else:
                    A('activation', [stg[s][1]], [b_wd], out=wd[:, c2, :], in_=stg[s][0][:], func=AF.Copy)
                k += 1
            nh1, b_nh1 = sb("nh1", [128, 1], F32)
            P.op('gpsimd', lambda e: e.memset(nh1[:], -0.5), [], [b_nh1])
            xc = sb("x4b", [128, 2, 1024], F32, 2)
            junk, b_junk = sb("junk4b", [128, 1024], BF16)
            ss = sb("ss4b", [128, 2], F32, 2)
            h2 = [sb("h2", [128, 2, 1024], BF16)] * 2
            h2T = [sb("h2T", [128, 8, 256], BF16)] * 2
            uT = [sb("uT", [128, 32, 256], BF16)] * 2
            rr = sb("rr", [128, 2, 256], F32, 3)
            TR = ps("TR4b", [128, 8, 128], BF16)
            U = ps("U4b", [128, 2, 256], F32, 2)
            Z = ps("Z4b", [128, 1024], F32, 2)
            def load4b(ch):
                s2 = ch % 2
                rows = slice(ch * 256, (ch + 1) * 256)
                self.dma('sync', xc[s2][0][:], x1_d[rows, :].rearrange("(t p) d -> p t d", p=128), [SC['bx1']],
                         [xc[s2][1]], 'x4b%d' % s2)
            load4b(0)
            for ch in range(16):
                s2 = ch % 2
                rows = slice(ch * 256, (ch + 1) * 256)
                x_t, x_b = xc[s2]
                if ch + 1 < 16:
                    load4b(ch + 1)
                for t in range(2):
                    A('activation', [x_b], [b_junk, ss[s2][1]], out=junk[:], in_=x_t[:, t, :], func=AF.Square,
                      accum_out=ss[s2][0][:, t:t + 1])
                G('tensor_scalar', [ss[s2][1]], [ss[s2][1]], out=ss[s2][0][:], in0=ss[s2][0][:], scalar1=1.0 / 1024,
                  scalar2=EPS, op0=ALU.mult, op1=ALU.add)
                G('tensor_tensor', [ss[s2][1], b_nh1], [ss[s2][1]], out=ss[s2][0][:], in0=ss[s2][0][:],
                  in1=bc(nh1[:], [128, 2]), op=ALU.pow)
                for t in range(2):
                    A('activation', [x_b, ss[s2][1]], [h2[s2][1]], out=h2[s2][0][:, t, :], in_=x_t[:, t, :], func=AF.Copy,
                      scale=ss[s2][0][:, t:t + 1])
                    for c in range(8):
                        T('transpose', [h2[s2][1], b_idb], [TR[1]], out=TR[0][:, c, :],
                          in_=h2[s2][0][:, t, c * 128:(c + 1) * 128], identity=idb[:])
                    V('tensor_copy', [TR[1]], [h2T[s2][1]], out=h2T[s2][0][:, :, t * 128:(t + 1) * 128], in_=TR[0][:])
                for f2 in range(16):
                    u_t, u_b = U[f2 % 2]
                    for ff in range(2):
                        fc = f2 * 2 + ff
                        for c in range(8):
                            T('matmul', [h2T[s2][1], b_wu], [u_b], out=u_t[:, ff, :], lhsT=wu[:, c, fc * 128:(fc + 1) * 128],
                              rhs=h2T[s2][0][:, c, :], start=(c == 0), stop=(c == 7))
                    r_t, r_b = rr[f2 % 3]
                    A('activation', [u_b], [r_b], out=r_t[:], in_=u_t[:], func=AF.Relu)
                    E('vector' if f2 % 2 == 0 else 'gpsimd', 'tensor_tensor', [r_b], [uT[s2][1]],
                      out=uT[s2][0][:, 2 * f2:2 * f2 + 2, :], in0=r_t[:], in1=r_t[:], op=ALU.mult)
                for t in range(2):
                    z_t, z_b = Z[t]
                    for hf in range(2):
                        for fc in range(32):
                            T('matmul', [uT[s2][1], b_wd], [z_b], out=z_t[:, hf * 512:(hf + 1) * 512],
                              lhsT=uT[s2][0][:, fc, t * 128:(t + 1) * 128], rhs=wd[:, fc, hf * 512:(hf + 1) * 512],
                              start=(fc == 0), stop=(fc == 31))
                    V('tensor_tensor', [z_b, x_b], [x_b], out=x_t[:, t, :], in0=z_t[:], in1=x_t[:, t, :], op=ALU.add)
                self.dma('sync', out_d[rows, :].rearrange("(t p) d -> p t d", p=128), x_t[:], [x_b], [b_out],
                         'x4bo%d' % s2)
        self.P.phase_barrier()

    def layer(self, L, x_own, x_oth, x_halo, x1_d, out_d, b_out, with_bias_setup=True, phases=None, pos_key='pos',
              valid_key='valid', halo_rows=None, reuse_ckv=False):
        ph = phases or ['bias', 'p1', 'C', 'B', 'A', 'A2', '4a', '4b']
        if with_bias_setup and 'bias' in ph:
            self.bias_setup()
        if 'p1' in ph:
            self.phase1(L, x_own, x_oth, x_halo, pos_key=pos_key, valid_key=valid_key, halo_rows=halo_rows,
                        skip_oth=reuse_ckv, skip_ck=reuse_ckv)
        if 'C' in ph:
            self.phaseC()
        if 'B' in ph:
            self.phaseB(L)
        if 'A' in ph:
            self.phaseA(cfgs=self.debug.get('cfgs', (0, 1, 2)) if self.debug else (0, 1, 2))
        if 'A2' in ph:
            self.phaseA2()
        if '4a' in ph:
            self.phase4a(L, x_own, x1_d)
        if '4b' in ph:
            self.phase4b(L, x1_d, out_d, b_out)


NLAYER = 2


def t5_bucket_np(rel):
    half, exact = 16, 8
    n = np.abs(rel)
    far = exact + (np.log(np.maximum(n, 1).astype(np.float32) / np.float32(exact))
                   / np.float32(math.log(1024 / exact)) * np.float32(half - exact)).astype(np.int32)
    far = np.minimum(far, half - 1)
    return np.where(rel > 0, half, 0) + np.where(n < exact, n, far)


def make_oh():
    oh = np.zeros((33, 1660), np.float32)
    d = np.arange(-255, 256)
    bk = t5_bucket_np(d)
    for i, dd in enumerate(d):
        if abs(dd) <= 128:
            oh[bk[i], i] = 1
        else:
            oh[32, i] = 1
    for ci, r in enumerate((1, 4, 16)):
        d = np.arange(-191, 192)
        bk = t5_bucket_np(d * r)
        for i, dd in enumerate(d):
            if abs(dd) <= 64:
                oh[bk[i], 511 + 383 * ci + i] = 1
            else:
                oh[32, 511 + 383 * ci + i] = 1
    return oh


WNAMES = ['norm_mix', 'w_in', 'qk_gain_a', 'qk_gain_b', 'sink_b', 'q_lat_gain', 'kv_lat_gain', 'w_uq', 'w_ukv',
          'qk_gain_c', 'out_norm', 'w_out', 'norm_mlp', 'w_up', 'w_down']


def declare(nc, NL):
    D = {}

    def inp(n, shape, dt=F32):
        D[n] = nc.dram_tensor(n, shape, dt, kind="ExternalInput")
    inp('x_own', [4096, 1024]); inp('x_oth', [4096, 1024]); inp('x_halo', [2048, 1024]); inp('x_halo2', [2048, 1024])
    inp('valid', [128, 16]); inp('valid2', [128, 16]); inp('pos', [128, 64], I32); inp('pos2', [128, 64], I32)
    inp('invf', [1, 16]); inp('idb', [128, 128], BF16); inp('idf', [128, 128])
    inp('norm_mix', [NL, 1024]); inp('w_in', [NL, 1024, 2080]); inp('qk_gain_a', [NL, 2, 64])
    inp('qk_gain_b', [NL, 2, 64]); inp('sink_b', [NL, 4]); inp('q_lat_gain', [NL, 256]); inp('kv_lat_gain', [NL, 128])
    inp('w_uq', [NL, 256, 576]); inp('w_ukv', [NL, 128, 768]); inp('qk_gain_c', [NL, 2, 96]); inp('out_norm', [NL, 1024])
    inp('w_out', [NL, 1024, 1024]); inp('norm_mlp', [NL, 1024]); inp('w_up', [NL, 1024, 4096]); inp('w_down', [NL, 4096, 1024])
    inp('rel_bias_table', [32, 10]); inp('oh', [33, 1660])
    SC = {}

    def scr(n, shape, dt=BF16):
        SC[n] = nc.dram_tensor(n, shape, dt, kind="Internal")
    scr('QA', [4096, 384]); scr('KAx', [6144, 384]); scr('VAx', [6144, 390]); scr('QB', [4096, 256])
    scr('KBx', [6144, 128]); scr('VBx', [6144, 130]); scr('QCT', [6, 96, 4096]); scr('KCT', [6, 96, 8192])
    scr('VC', [8192, 390]); scr('MIX', [4096, 1024]); scr('OA', [3, 4096, 390], F32); scr('FV', [10, 1660], F32)
    scr('X1', [4096, 1024], F32); scr('XA', [4096, 1024], F32); scr('XB', [4096, 1024], F32)
    SC['b'] = Buf('scratch', multi=True)
    SC['bmix'] = Buf('mix', multi=True)
    SC['boa'] = Buf('oa', multi=True)
    SC['bfv'] = Buf('fv', multi=True)
    SC['bx1'] = Buf('x1', multi=True)
    return D, SC


def make_inputs(inputs, core):
    b, half = core // 2, core % 2
    xs = np.asarray(inputs['x'], dtype=np.float32)[b]
    own = xs[half * 4096:(half + 1) * 4096]
    oth = xs[(1 - half) * 4096:(2 - half) * 4096]
    halo = np.zeros((2048, 1024), np.float32)
    valid = np.zeros((2048,), np.float32)
    halo2 = np.zeros((2048, 1024), np.float32)
    valid2 = np.zeros((2048,), np.float32)
    if half == 1:
        halo[0:1024] = oth[3072:4096]
        valid[0:1024] = 1
        halo2[1024:2048] = own[0:1024]
        valid2[1024:2048] = 1
    else:
        halo[1024:2048] = oth[0:1024]
        valid[1024:2048] = 1
        halo2[0:1024] = own[3072:4096]
        valid2[0:1024] = 1
    pos = np.asarray(inputs['positions'][b])
    p_own = pos[half * 4096:(half + 1) * 4096]
    p_oth = pos[(1 - half) * 4096:(2 - half) * 4096]
    pos_l = np.concatenate([p_own, p_oth])
    pos_l2 = np.concatenate([p_oth, p_own])
    m = {
        'x_own': np.ascontiguousarray(own), 'x_oth': np.ascontiguousarray(oth), 'x_halo': halo, 'x_halo2': halo2,
        'valid': np.ascontiguousarray(valid.reshape(16, 128).T),
        'valid2': np.ascontiguousarray(valid2.reshape(16, 128).T),
        'pos': np.ascontiguousarray(pos_l.reshape(64, 128).T.astype(np.int32)),
        'pos2': np.ascontiguousarray(pos_l2.reshape(64, 128).T.astype(np.int32)),
        'invf': (10000.0 ** (-np.arange(16, dtype=np.float32) / 16)).astype(np.float32).reshape(1, 16),
        'idb': np.eye(128).astype(ml_dtypes.bfloat16), 'idf': np.eye(128).astype(np.float32),
        'rel_bias_table': np.ascontiguousarray(inputs['rel_bias_table'], dtype=np.float32), 'oh': make_oh(),
    }
    for k in WNAMES:
        m[k] = np.ascontiguousarray(np.asarray(inputs[k], dtype=np.float32))
    return m


def build_program():
    nc = bass.Bass("TRN2", target_bir_lowering=False)
    D, SC = declare(nc, NLAYER)
    out_d = nc.dram_tensor('out', [4096, 1024], F32, kind="ExternalOutput")
    b_xa = Buf('xa', multi=True)
    b_xb = Buf('xb', multi=True)
    b_out = Buf('out', multi=True)
    with ExitStack() as es:
        P = Prog(nc, es)
        LB = LayerBuilder(nc, P, D, debug={})
        LB.SC = SC
        XA, XB = SC['XA'].ap(), SC['XB'].ap()
        LB.layer(0, D['x_own'].ap(), D['x_oth'].ap(), D['x_halo'].ap(), SC['X1'].ap(), XA, b_xa)
        LB.layer(0, D['x_oth'].ap(), D['x_own'].ap(), D['x_halo2'].ap(), SC['X1'].ap(), XB, b_xb,
                 with_bias_setup=False, pos_key='pos2', valid_key='valid2', reuse_ckv=True)

        def halo_rows(t):
            r0 = 3072 + t * 128 if t < 8 else (t - 8) * 128
            return XB[r0:r0 + 128, :]
        P.barrier('sync', reads=[b_xa, b_xb])
        LB.layer(1, XA, XB, None, SC['X1'].ap(), out_d.ap(), b_out, with_bias_setup=False, halo_rows=halo_rows)
        P.barrier('sync', reads=[b_out])
        P.emit()
    return nc


def kernel(**inputs):
    nc = build_program()
    in_maps = [make_inputs(inputs, c) for c in range(8)]
    res = run_bass_kernel_spmd(nc, in_maps, core_ids=list(range(8)))
    x = np.asarray(inputs['x'])
    out = np.empty(x.shape, np.float32)
    for c in range(8):
        b, half = c // 2, c % 2
        out[b, half * 4096:(half + 1) * 4096] = np.asarray(res.results[c]['out'], dtype=np.float32)
    return out
```

```python
import math
import ml_dtypes
from concourse.bass_utils import run_bass_kernel_spmd
import numpy as np
import concourse.bass as bass
import concourse.mybir as mybir
from contextlib import ExitStack

F32 = mybir.dt.float32
BF16 = mybir.dt.bfloat16
I32 = mybir.dt.int32
ALU = mybir.AluOpType
AF = mybir.ActivationFunctionType
AX = mybir.AxisListType

ENGS = ['sync', 'scalar', 'vector', 'gpsimd', 'tensor']
SEM_ROT = 24000


class Buf:
    __slots__ = ('name', 'writer', 'readers', 'dreaders', 'multi', 'mw')

    def __init__(self, name, multi=False):
        self.name = name
        self.writer = None
        self.readers = {}
        self.dreaders = []
        self.multi = multi
        self.mw = []


class Op:
    __slots__ = ('eng', 'fn', 'deps', 'idx', 'signal', 'is_dma', 'lane', 'ev', 'raw', 'barrier')


class Prog:
    def __init__(self, nc, es):
        self.nc = nc
        self.es = es
        self.ops = {e: [] for e in ENGS}
        self.order = []
        self.lanes = {}
        self.nsem = 0
        self.fence = []
        self.fence_pending = set()
        self.phase_lanes = {}

    def phase_barrier(self):
        fence = []
        for e in ENGS:
            for o in reversed(self.ops[e]):
                if not o.is_dma and not o.barrier:
                    fence.append(o)
                    break
        last = {}
        for o in self.order:
            if o.is_dma:
                last[o.lane] = o
        fence += list(last.values())
        self.fence = fence
        self.fence_pending = set(ENGS)
        self.phase_lanes = {}

    def new_sem(self, name):
        self.nsem += 1
        return self.es.enter_context(self.nc.semaphore(name))

    def sb(self, name, shape, dt):
        return self.es.enter_context(self.nc.sbuf_tensor(name, shape, dt))

    def ps(self, name, shape, dt):
        return self.es.enter_context(self.nc.psum_tensor(name, shape, dt))

    def barrier(self, eng, reads=(), writes=()):
        o = self.op(eng, lambda e: None, reads, writes)
        o.barrier = True
        return o

    def op(self, eng, fn, reads=(), writes=(), lane=None):
        o = Op()
        o.eng = eng
        o.fn = fn
        o.barrier = False
        o.is_dma = lane is not None
        if lane is not None:
            if lane not in self.phase_lanes:
                self.phase_lanes[lane] = 'L%d' % len(self.phase_lanes)
            lane = self.phase_lanes[lane]
        o.lane = lane
        o.signal = False
        o.ev = None
        deps = {}
        raw = set()
        for b in reads:
            if b.writer is not None:
                deps[id(b.writer)] = b.writer
                raw.add(id(b.writer))
            for w in b.mw:
                deps[id(w)] = w
                raw.add(id(w))
        for b in writes:
            if b.writer is not None and not b.multi:
                deps[id(b.writer)] = b.writer
                raw.add(id(b.writer))
            for r in b.readers.values():
                deps[id(r)] = r
            for r in b.dreaders:
                deps[id(r)] = r
        if eng in self.fence_pending:
            self.fence_pending.discard(eng)
            for w in self.fence:
                deps[id(w)] = w
        deps.pop(id(o), None)
        o.deps = list(deps.values())
        o.raw = raw
        for b in writes:
            if b.multi:
                b.mw.append(o)
            else:
                b.writer = o
            b.readers = {}
            b.dreaders = []
        for b in reads:
            if b.multi:
                continue
            if o.is_dma:
                b.dreaders.append(o)
            else:
                b.readers[eng] = o
        o.idx = len(self.ops[eng])
        self.ops[eng].append(o)
        self.order.append(o)
        return o

    def dma(self, q, out, in_, reads=(), writes=(), lane=None, **kw):
        assert lane is not None
        return self.op(q, lambda e: e.dma_start(out=out, in_=in_, **kw), reads, writes, lane=lane)

    def emit(self):
        nc = self.nc
        for o in self.order:
            for d in o.deps:
                if d.is_dma:
                    continue
                if d.barrier:
                    assert d.eng == o.eng, 'barrier dep across engines'
                    continue
                if d.eng == o.eng and not o.is_dma:
                    if o.eng == 'tensor':
                        continue
                    if id(d) not in o.raw:
                        continue
                d.signal = True
        esems = {}
        for e in ENGS:
            cnt = 0
            cur = None
            for o in self.ops[e]:
                if o.is_dma:
                    ln = self.lanes.get(o.lane)
                    if ln is None:
                        ln = [self.new_sem('l_%s' % o.lane), 0]
                        self.lanes[o.lane] = ln
                    ln[1] += 16
                    o.ev = (ln[0], ln[1])
                elif o.signal:
                    if cur is None or cnt >= SEM_ROT:
                        cur = self.new_sem('e_%s_%d' % (e, len(esems)))
                        esems[(e, len(esems))] = cur
                        cnt = 0
                    cnt += 1
                    o.ev = (cur, cnt)
        blk = self.es.enter_context(nc.Block())
        prog = self

        def run(e, eng):
            waited = {}
            for o in prog.ops[e]:
                need = {}
                for d in o.deps:
                    if not d.is_dma:
                        if d.barrier:
                            continue
                        if d.eng == o.eng and not o.is_dma:
                            if o.eng == 'tensor' or id(d) not in o.raw:
                                continue
                    sem, val = d.ev
                    k = id(sem)
                    if k not in need or need[k][1] < val:
                        need[k] = (sem, val)
                for k, (sem, val) in need.items():
                    if waited.get(k, 0) >= val:
                        continue
                    eng.wait_ge(sem, val)
                    waited[k] = val
                ins = o.fn(eng)
                if ins is None:
                    continue
                if o.is_dma:
                    ins.then_inc(o.ev[0], 16)
                elif o.signal:
                    ins.then_inc(o.ev[0], 1)

        @blk.sync
        def _(eng):
            run('sync', eng)

        @blk.scalar
        def _(eng):
            run('scalar', eng)

        @blk.vector
        def _(eng):
            run('vector', eng)

        @blk.gpsimd
        def _(eng):
            run('gpsimd', eng)

        @blk.tensor
        def _(eng):
            run('tensor', eng)

import numpy as np
import math

EPS = 1e-6
NEGB = -30000.0
TWO_PI_S = 6.2831845


def bc(ap, shape):
    return ap.to_broadcast(list(shape))


def psum_view(ph, nc, name, shape, dt):
    esz = 4 if dt == F32 else 2
    n = 1
    for d in shape[1:]:
        n *= d
    per_bank = 2048 // esz
    tot = ((n + per_bank - 1) // per_bank) * per_bank
    t = ph.enter_context(nc.psum_tensor(name, [128, tot], dt))
    v = t[0:shape[0], 0:n]
    if len(shape) == 3:
        v = v.rearrange("p (a b) -> p a b", a=shape[1])
    elif len(shape) == 4:
        v = v.rearrange("p (a b c) -> p a b c", a=shape[1], b=shape[2])
    return v


class LayerBuilder:
    def __init__(self, nc, P, D, debug=False):
        self.nc = nc
        self.P = P
        self.D = D
        self.debug = debug
        self.uid = 0

    def name(self, s):
        self.uid += 1
        return "%s_%d" % (s, self.uid)

    def V(self, fn, reads, writes, **kw):
        return self.P.op('vector', lambda e: getattr(e, fn)(**kw), reads, writes)

    def G(self, fn, reads, writes, **kw):
        return self.P.op('gpsimd', lambda e: getattr(e, fn)(**kw), reads, writes)

    def A(self, fn, reads, writes, **kw):
        return self.P.op('scalar', lambda e: getattr(e, fn)(**kw), reads, writes)

    def T(self, fn, reads, writes, **kw):
        return self.P.op('tensor', lambda e: getattr(e, fn)(**kw), reads, writes)

    def E(self, eng, fn, reads, writes, **kw):
        return self.P.op(eng, lambda e: getattr(e, fn)(**kw), reads, writes)

    def dma(self, q, out, in_, reads, writes, lane, **kw):
        return self.P.dma(q, out, in_, reads=reads, writes=writes, lane=lane, **kw)

    def phase1(self, L, x_own, x_oth, x_halo, first=True, pos_key='pos', valid_key='valid', halo_rows=None,
               skip_oth=False, skip_ck=False):
        nc, P, D = self.nc, self.P, self.D
        V, G, A, T, E = self.V, self.G, self.A, self.T, self.E
        NS = 4
        with ExitStack() as ph:
            def sb(nm, shape, dt, n=1):
                r = []
                for i in range(n):
                    t = ph.enter_context(nc.sbuf_tensor(self.name(nm), shape, dt))
                    r.append((t, Buf(nm + str(i))))
                return r if n > 1 else r[0]

            def ps(nm, shape, dt):
                t = psum_view(ph, nc, self.name(nm), shape, dt)
                return (t, Buf(nm))

            setup = ExitStack()

            def sbs(nm, shape, dt):
                t = setup.enter_context(nc.sbuf_tensor(self.name(nm), shape, dt))
                return (t, Buf(nm))

            idb, b_idb = sb("idb", [128, 128], BF16)
            wib, b_wib = sb("wib", [128, 8, 2080], BF16)
            wuq, b_wuq = sb("wuq", [128, 2, 576], BF16)
            wukv, b_wukv = sb("wukv", [128, 768], BF16)
            g8, b_g8 = sb("g8", [128, 8], F32)
            gq2, b_gq2 = sb("gq2", [128, 2], F32)
            gkv1, b_gkv1 = sb("gkv1", [128, 1], F32)
            ga, b_ga = sb("ga", [128, 2, 64], F32)
            gb, b_gb = sb("gb", [128, 2, 64], F32)
            gc, b_gc = sb("gc", [128, 2, 96], F32)
            GAq, b_GAq = sb("GAq", [128, 64], F32)
            GBq, b_GBq = sb("GBq", [128, 64], F32)
            GCq, b_GCq = sb("GCq", [128, 96], F32)
            invf, b_invf = sb("invf", [128, 16], F32)
            sin_t, b_sin = sb("sin_t", [128, 64, 16], F32)
            cos_t, b_cos = sb("cos_t", [128, 64, 16], F32)
            invd, b_invd = sb("invd", [128, 24], F32)
            nh24, b_nh24 = sb("nh24", [128, 24], F32)
            valid, b_valid = sb("valid", [128, 16], F32)
            posi, b_posi = sbs("posi", [128, 64], I32)
            posf, b_posf = sbs("posf", [128, 64], F32)
            ang, b_ang = sbs("ang", [128, 64, 16], F32)
            angk, b_angk = sbs("angk", [128, 64, 16], I32)
            angf, b_angf = sbs("angf", [128, 64, 16], F32)
            stage = [sbs("stage", [128, 2080], F32)] * 2
            stq, b_stq = sbs("stq", [128, 2, 576], F32)
            stkv, b_stkv = sbs("stkv", [128, 768], F32)

            self.dma('sync', idb[:], D['idb'].ap(), [], [b_idb], 'idb')
            self.dma('sync', g8[:], D['norm_mix'].ap()[L].rearrange("(c p) -> p c", p=128), [], [b_g8], 'g8',
                     allow_slow_non_contiguous=True)
            self.dma('sync', gq2[:], D['q_lat_gain'].ap()[L].rearrange("(c p) -> p c", p=128), [], [b_gq2], 'gq2',
                     allow_slow_non_contiguous=True)
            self.dma('sync', gkv1[:], D['kv_lat_gain'].ap()[L].rearrange("(c p) -> p c", p=128), [], [b_gkv1],
                     'gkv1', allow_slow_non_contiguous=True)
            self.dma('sync', ga[:], D['qk_gain_a'].ap()[L].rearrange("a d -> (a d)").partition_broadcast(128),
                     [], [b_ga], 'ga')
            self.dma('sync', gb[:], D['qk_gain_b'].ap()[L].rearrange("a d -> (a d)").partition_broadcast(128),
                     [], [b_gb], 'gb')
            self.dma('sync', gc[:], D['qk_gain_c'].ap()[L].rearrange("a d -> (a d)").partition_broadcast(128),
                     [], [b_gc], 'gc')
            self.dma('sync', invf[:], D['invf'].ap().rearrange("a d -> (a d)").partition_broadcast(128),
                     [], [b_invf], 'invf')
            self.dma('sync', posi[:], D[pos_key].ap(), [], [b_posi], 'posi')
            self.dma('sync', valid[:], D[valid_key].ap(), [], [b_valid], 'valid')
            V('scalar_tensor_tensor', [b_ga], [b_GAq], out=GAq[:], in0=ga[:, 0, :], scalar=0.125, in1=ga[:, 1, :],
              op0=ALU.mult, op1=ALU.mult)
            V('scalar_tensor_tensor', [b_gb], [b_GBq], out=GBq[:], in0=gb[:, 0, :], scalar=0.125, in1=gb[:, 1, :],
              op0=ALU.mult, op1=ALU.mult)
            V('tensor_scalar', [b_gc], [b_GCq], out=GCq[:], in0=gc[:, 0, :], scalar1=96.0 ** -0.5, scalar2=None,
              op0=ALU.mult)
            GCk = gc[:, 1, :]
            b_GCk = b_gc
            self.P.op('gpsimd', lambda e: e.memset(invd[:], 1.0 / 64), [], [b_invd])
            self.P.op('gpsimd', lambda e: e.memset(invd[:, 18:19], 1.0 / 256), [], [b_invd])
            self.P.op('gpsimd', lambda e: e.memset(invd[:, 19:20], 1.0 / 128), [], [b_invd])
            self.P.op('gpsimd', lambda e: e.memset(invd[:, 20:24], 1.0), [], [b_invd])
            self.P.op('gpsimd', lambda e: e.memset(nh24[:], -0.5), [], [b_nh24])

            V('tensor_copy', [b_posi], [b_posf], out=posf[:], in_=posi[:])
            V('tensor_tensor', [b_posf, b_invf], [b_ang], out=ang[:],
              in0=bc(posf[:].unsqueeze(2), [128, 64, 16]), in1=bc(invf[:].unsqueeze(1), [128, 64, 16]), op=ALU.mult)
            for (tab, b_tab, off) in ((sin_t, b_sin, 0.0), (cos_t, b_cos, 0.25)):
                V('tensor_scalar', [b_ang], [b_angf], out=angf[:], in0=ang[:], scalar1=1.0 / (2 * math.pi),
                  scalar2=off, op0=ALU.mult, op1=ALU.add)
                V('tensor_copy', [b_angf], [b_angk], out=angk[:], in_=angf[:])
                V('tensor_copy', [b_angk], [b_tab], out=tab[:], in_=angk[:])
                V('tensor_tensor', [b_angf, b_tab], [b_angf], out=angf[:], in0=angf[:], in1=tab[:], op=ALU.subtract)
                A('activation', [b_angf], [b_tab], out=tab[:], in_=angf[:], func=AF.Sin, scale=TWO_PI_S)

            blocks = [(0, 384, 0), (384, 768, 512), (768, 1152, 1024), (1152, 1408, 1536), (1408, 1536, 896),
                      (1536, 1664, 1408), (1664, 1920, 1792), (1920, 2048, 384), (2048, 2080, 2048)]
            k = 0
            for c in range(8):
                st_t, st_b = stage[c % 2]
                self.dma('sync', st_t[:], D['w_in'].ap()[L, c * 128:(c + 1) * 128, :], [], [st_b], 'stage0')
                for (o0, o1, n0) in blocks:
                    k += 1
                    if k % 2 == 0:
                        V('tensor_scalar', [st_b, b_g8], [b_wib], out=wib[:, c, n0:n0 + (o1 - o0)],
                          in0=st_t[:, o0:o1], scalar1=g8[:, c:c + 1], scalar2=None, op0=ALU.mult)
                    else:
                        A('activation', [st_b, b_g8], [b_wib], out=wib[:, c, n0:n0 + (o1 - o0)],
                          in_=st_t[:, o0:o1], func=AF.Copy, scale=g8[:, c:c + 1])
            self.dma('sync', stq[:], D['w_uq'].ap()[L].rearrange("(c p) n -> p c n", p=128), [], [b_stq], 'stq')
            self.dma('sync', stkv[:], D['w_ukv'].ap()[L], [], [b_stkv], 'stkv')
            for c in range(2):
                V('tensor_scalar', [b_stq, b_gq2], [b_wuq], out=wuq[:, c, :], in0=stq[:, c, :],
                  scalar1=gq2[:, c:c + 1], scalar2=None, op0=ALU.mult)
            V('tensor_scalar', [b_stkv, b_gkv1], [b_wukv], out=wukv[:], in0=stkv[:], scalar1=gkv1[:, 0:1],
              scalar2=None, op0=ALU.mult)

            self.P.phase_barrier()
            setup.close()
            xt = sb("xt", [128, 1024], F32, NS)
            junk, b_junk = sb("junk", [128, 1024], BF16)
            ssx = sb("ssx", [128, 1], F32, NS)
            rsx = sb("rsx", [128, 1], F32, NS)
            hb = sb("hb", [128, 1024], BF16, NS)
            hT = sb("hT", [128, 8, 128], BF16, NS)
            pj = sb("pj", [128, 2080], F32, NS)
            sq, b_sq = sb("sq", [128, 2080], F32)
            st = sb("st", [128, 24], F32, NS)
            rstd = sb("rstd", [128, 24], F32, NS)
            QAo = sb("QAo", [128, 384], BF16, NS)
            QAt = sb("QAt", [128, 384], F32, 1)
            KABo = sb("KABo", [128, 512], BF16, NS)
            QBo = sb("QBo", [128, 256], BF16, NS)
            QBt = sb("QBt", [128, 256], F32, 1)
            VABo = sb("VABo", [128, 8, 65], BF16, NS)
            LAT = sb("LAT", [128, 384], BF16, NS)
            latT = sb("latT", [128, 3, 128], BF16, NS)
            qcs = sb("qcs", [128, 576], F32, NS)
            kvcs = sb("kvcs", [128, 768], F32, NS)
            st2 = sb("st2", [128, 12], F32, NS)
            rstd2 = sb("rstd2", [128, 12], F32, NS)
            tmp1, b_tmp1 = sb("tmp1", [128, 6, 96], F32)
            trq, b_trq = sb("trq", [128, 6, 32], F32)
            tmpk, b_tmpk = sb("tmpk", [128, 6, 64], F32)
            krg, b_krg = sb("krg", [128, 1, 32], F32)
            krr, b_krr = sb("krr", [128, 1, 32], F32)
            rm = [sb("rm%d" % i, [128, 6, 16], F32) for i in range(4)]
            QCo = sb("QCo", [128, 6, 96], BF16, NS)
            KCo = sb("KCo", [128, 6, 96], BF16, NS)
            VCo = sb("VCo", [128, 6, 65], BF16, NS)
            QTs = sb("QTs", [96, 6, 128], BF16, NS)
            KTs = sb("KTs", [96, 6, 128], BF16, NS)
            TR, b_TR = ps("TR", [128, 8, 128], BF16)
            PJ = [ps("PJ%d" % i, [128, 512], F32) for i in range(5)]
            S = [ps("S%d" % i, [128, 512], F32) for i in range(2)]

            for (t_, b_) in VABo:
                self.P.op('gpsimd', lambda e, t_=t_: e.memset(t_[:], 1.0), [], [b_])
            for (t_, b_) in VCo:
                self.P.op('gpsimd', lambda e, t_=t_: e.memset(t_[:], 1.0), [], [b_])

            def rope(src, b_src, dst, b_dst, H, ti):
                cb = bc(cos_t[:, ti, :].unsqueeze(1), [128, H, 16])
                sbb = bc(sin_t[:, ti, :].unsqueeze(1), [128, H, 16])
                (m1, b1), (m2, b2), (m3, b3), (m4, b4) = rm
                V('tensor_tensor', [b_src, b_cos], [b1], out=m1[:, 0:H, :], in0=src[:, :, 0:16], in1=cb, op=ALU.mult)
                V('tensor_tensor', [b_src, b_sin], [b2], out=m2[:, 0:H, :], in0=src[:, :, 16:32], in1=sbb, op=ALU.mult)
                V('tensor_tensor', [b1, b2], [b_dst], out=dst[:, :, 0:16], in0=m1[:, 0:H, :], in1=m2[:, 0:H, :],
                  op=ALU.subtract)
                V('tensor_tensor', [b_src, b_cos], [b3], out=m3[:, 0:H, :], in0=src[:, :, 16:32], in1=cb, op=ALU.mult)
                V('tensor_tensor', [b_src, b_sin], [b4], out=m4[:, 0:H, :], in0=src[:, :, 0:16], in1=sbb, op=ALU.mult)
                V('tensor_tensor', [b3, b4], [b_dst], out=dst[:, :, 16:32], in0=m3[:, 0:H, :], in1=m4[:, 0:H, :],
                  op=ALU.add)

            SC = self.SC
            it = 0
            jobs = [('own', t) for t in range(32)] + [('oth', t) for t in range(32)] + [('halo', t) for t in range(16)]
            if skip_oth:
                jobs = [j for j in jobs if j[0] != 'oth']
            if self.debug and self.debug.get('p1_tiles'):
                jobs = self.debug['p1_tiles']
            def tile_gen(it, kind, t):
                s2 = it % NS
                s3 = it % NS
                src = {'own': x_own, 'oth': x_oth, 'halo': x_halo}[kind]
                x_t, b_x = xt[s3]
                ss_t, b_ss = ssx[s2]
                rs_t, b_rs = rsx[s2]
                hb_t, b_hb = hb[s2]
                hT_t, b_hT = hT[s2]
                pj_t, b_pj = pj[s3]
                st_t, b_st = st[s3]
                rstd_t, b_rstd = rstd[s3]
                if kind == 'halo':
                    G('tensor_scalar', [b_x, b_valid], [b_x], out=x_t[:], in0=x_t[:], scalar1=valid[:, t:t + 1],
                      scalar2=None, op0=ALU.mult)
                    yield
                A('activation', [b_x], [b_junk, b_ss], out=junk[:], in_=x_t[:], func=AF.Square, accum_out=ss_t[:])
                yield
                G('tensor_scalar', [b_ss], [b_ss], out=ss_t[:], in0=ss_t[:], scalar1=1.0 / 1024, scalar2=EPS,
                  op0=ALU.mult, op1=ALU.add)
                yield
                G('tensor_tensor', [b_ss, b_nh24], [b_rs], out=rs_t[:], in0=ss_t[:], in1=nh24[:, 0:1], op=ALU.pow)
                yield
                A('activation', [b_x, b_rs], [b_hb], out=hb_t[:], in_=x_t[:], func=AF.Copy, scale=rs_t[:, 0:1])
                yield
                for c in range(8):
                    T('transpose', [b_hb, b_idb], [b_TR], out=TR[:, c, :], in_=hb_t[:, c * 128:(c + 1) * 128],
                      identity=idb[:])
                V('tensor_copy', [b_TR], [b_hT], out=hT_t[:], in_=TR[:])
                yield
                if kind == 'own':
                    groups = [(0, 0, 512, 0), (1, 512, 1024, 0), (2, 1024, 1536, 0), (3, 1536, 2048, 0),
                              (4, 2048, 2080, 0)]
                elif kind == 'oth':
                    groups = [(0, 384, 512, 384), (4, 2048, 2080, 0)]
                else:
                    groups = [(1, 512, 1024, 0), (2, 1024, 1536, 0)]
                for (bk, c0, c1, po) in groups:
                    pt, pb = PJ[bk]
                    for c in range(8):
                        T('matmul', [b_hT, b_wib], [pb], out=pt[:, po:po + (c1 - c0)], lhsT=hT_t[:, c, :],
                          rhs=wib[:, c, c0:c1], start=(c == 0), stop=(c == 7))
                    A('activation', [pb], [b_pj], out=pj_t[:, c0:c1], in_=pt[:, po:po + (c1 - c0)], func=AF.Copy)
                    yield
                if kind == 'own':
                    V('tensor_tensor', [b_pj], [b_sq], out=sq[:, 0:1024], in0=pj_t[:, 0:1024], in1=pj_t[:, 0:1024],
                      op=ALU.mult)
                    V('tensor_tensor', [b_pj], [b_sq], out=sq[:, 1536:2080], in0=pj_t[:, 1536:2080],
                      in1=pj_t[:, 1536:2080], op=ALU.mult)
                    red = [(0, 6, 0, 384, 64), (6, 14, 512, 1024, 64), (14, 18, 1536, 1792, 64),
                           (18, 19, 1792, 2048, 256), (19, 20, 384, 512, 128), (20, 21, 2048, 2080, 32)]
                elif kind == 'oth':
                    V('tensor_tensor', [b_pj], [b_sq], out=sq[:, 384:512], in0=pj_t[:, 384:512], in1=pj_t[:, 384:512],
                      op=ALU.mult)
                    V('tensor_tensor', [b_pj], [b_sq], out=sq[:, 2048:2080], in0=pj_t[:, 2048:2080],
                      in1=pj_t[:, 2048:2080], op=ALU.mult)
                    red = [(19, 20, 384, 512, 128), (20, 21, 2048, 2080, 32)]
                else:
                    V('tensor_tensor', [b_pj], [b_sq], out=sq[:, 512:1024], in0=pj_t[:, 512:1024],
                      in1=pj_t[:, 512:1024], op=ALU.mult)
                    red = [(6, 14, 512, 1024, 64)]
                for (a0, a1, c0, c1, dd) in red:
                    V('tensor_reduce', [b_sq], [b_st], out=st_t[:, a0:a1],
                      in_=sq[:, c0:c1].rearrange("p (h d) -> p h d", d=dd), axis=AX.X, op=ALU.add)
                V('tensor_tensor', [b_st, b_invd], [b_rstd], out=rstd_t[:, 0:20], in0=st_t[:, 0:20], in1=invd[:, 0:20],
                  op=ALU.mult)
                yield
                V('tensor_scalar', [b_rstd], [b_rstd], out=rstd_t[:, 0:20], in0=rstd_t[:, 0:20], scalar1=EPS,
                  scalar2=None, op0=ALU.add)
                yield
                G('tensor_tensor', [b_rstd, b_nh24], [b_rstd], out=rstd_t[:, 0:20], in0=rstd_t[:, 0:20],
                  in1=nh24[:, 0:20], op=ALU.pow)
                yield
                if kind in ('own', 'halo'):
                    et = (8 + t) if kind == 'own' else (t if t < 8 else 40 + (t - 8))
                    kab_t, b_kab = KABo[s2]
                    vab_t, b_vab = VABo[s2]
                    V('tensor_tensor', [b_pj, b_rstd], [b_kab], out=kab_t[:].rearrange("p (h d) -> p h d", d=64),
                      in0=pj_t[:, 512:1024].rearrange("p (h d) -> p h d", d=64),
                      in1=bc(rstd_t[:, 6:14].unsqueeze(2), [128, 8, 64]), op=ALU.mult)
                    yield
                    V('tensor_copy', [b_pj], [b_vab], out=vab_t[:, :, 0:64],
                      in_=pj_t[:, 1024:1536].rearrange("p (h d) -> p h d", d=64))
                    yield
                    if kind == 'halo':
                        V('tensor_copy', [b_valid], [b_vab], out=vab_t[:, :, 64:65],
                          in_=bc(valid[:, t:t + 1].unsqueeze(1), [128, 8, 1]))
                        yield
                    else:
                        self.P.op('vector', lambda e, vab_t=vab_t: e.memset(vab_t[:, :, 64:65], 1.0), [], [b_vab])
                        yield
                    rows = slice(et * 128, (et + 1) * 128)
                    self.dma('sync', SC['KAx'].ap()[rows, :], kab_t[:, 0:384], [b_kab], [SC['b']], 'kabo%d' % s2)
                    yield
                    self.dma('sync', SC['KBx'].ap()[rows, :], kab_t[:, 384:512], [b_kab], [SC['b']], 'kabo%d' % s2)
                    yield
                    self.dma('sync', SC['VAx'].ap()[rows, :].rearrange("p (h d) -> p h d", d=65), vab_t[:, 0:6, :],
                             [b_vab], [SC['b']], 'vabo%d' % s2)
                    yield
                    self.dma('sync', SC['VBx'].ap()[rows, :].rearrange("p (h d) -> p h d", d=65), vab_t[:, 6:8, :],
                             [b_vab], [SC['b']], 'vabo%d' % s2)
                    yield
                if kind == 'own':
                    rows = slice(t * 128, (t + 1) * 128)
                    qa_t, b_qa = QAo[s2]
                    qat, b_qat = QAt
                    V('tensor_tensor', [b_pj, b_rstd], [b_qat], out=qat[:].rearrange("p (h d) -> p h d", d=64),
                      in0=pj_t[:, 0:384].rearrange("p (h d) -> p h d", d=64),
                      in1=bc(rstd_t[:, 0:6].unsqueeze(2), [128, 6, 64]), op=ALU.mult)
                    V('tensor_tensor', [b_qat, b_GAq], [b_qa], out=qa_t[:].rearrange("p (h d) -> p h d", d=64),
                      in0=qat[:].rearrange("p (h d) -> p h d", d=64),
                      in1=bc(GAq[:].unsqueeze(1), [128, 6, 64]), op=ALU.mult)
                    self.dma('sync', SC['QA'].ap()[rows, :], qa_t[:], [b_qa], [SC['b']], 'qao%d' % s2)
                    yield
                    qb_t, b_qb = QBo[s2]
                    qbt, b_qbt = QBt
                    V('tensor_tensor', [b_pj, b_rstd], [b_qbt], out=qbt[:].rearrange("p (h d) -> p h d", d=64),
                      in0=pj_t[:, 1536:1792].rearrange("p (h d) -> p h d", d=64),
                      in1=bc(rstd_t[:, 14:18].unsqueeze(2), [128, 4, 64]), op=ALU.mult)
                    V('tensor_tensor', [b_qbt, b_GBq], [b_qb],
                      out=qb_t[:].rearrange("p (b a d) -> p a b d", b=2, a=2, d=64),
                      in0=qbt[:].rearrange("p (a b d) -> p a b d", a=2, b=2, d=64),
                      in1=bc(GBq[:].unsqueeze(1).unsqueeze(1), [128, 2, 2, 64]), op=ALU.mult)
                    self.dma('sync', SC['QB'].ap()[rows, :], qb_t[:], [b_qb], [SC['b']], 'qbo%d' % s2)
                    yield
                if kind in ('own', 'oth'):
                    ti = t if kind == 'own' else 32 + t
                    lat_t, b_lat = LAT[s2]
                    latT_t, b_latT = latT[s2]
                    qcs_t, b_qcs = qcs[s2]
                    kvcs_t, b_kvcs = kvcs[s2]
                    st2_t, b_st2 = st2[s2]
                    rstd2_t, b_rstd2 = rstd2[s2]
                    if kind == 'own':
                        V('tensor_scalar', [b_pj, b_rstd], [b_lat], out=lat_t[:, 0:256], in0=pj_t[:, 1792:2048],
                          scalar1=rstd_t[:, 18:19], scalar2=None, op0=ALU.mult)
                        yield
                    if not skip_ck:
                        V('tensor_scalar', [b_pj, b_rstd], [b_lat], out=lat_t[:, 256:384], in0=pj_t[:, 384:512],
                          scalar1=rstd_t[:, 19:20], scalar2=None, op0=ALU.mult)
                        yield
                    jl = ([0, 1] if skip_ck else [0, 1, 2]) if kind == 'own' else [2]
                    for j in jl:
                        T('transpose', [b_lat, b_idb], [b_TR], out=TR[:, j, :], in_=lat_t[:, j * 128:(j + 1) * 128],
                          identity=idb[:])
                    V('tensor_copy', [b_TR], [b_latT], out=latT_t[:, jl[0]:jl[-1] + 1, :], in_=TR[:, jl[0]:jl[-1] + 1, :])
                    if kind == 'own':
                        for hf in range(2):
                            for c in range(2):
                                T('matmul', [b_latT, b_wuq], [S[hf][1]], out=S[hf][0][:, 0:288], lhsT=latT_t[:, c, :],
                                  rhs=wuq[:, c, hf * 288:(hf + 1) * 288], start=(c == 0), stop=(c == 1))
                            A('activation', [S[hf][1]], [b_qcs], out=qcs_t[:, hf * 288:(hf + 1) * 288],
                              in_=S[hf][0][:, 0:288], func=AF.Copy)
                    for hf in (range(2) if not skip_ck else []):
                        T('matmul', [b_latT, b_wukv], [S[hf][1]], out=S[hf][0][:, 0:384], lhsT=latT_t[:, 2, :],
                          rhs=wukv[:, hf * 384:(hf + 1) * 384], start=True, stop=True)
                        A('activation', [S[hf][1]], [b_kvcs], out=kvcs_t[:, hf * 384:(hf + 1) * 384],
                          in_=S[hf][0][:, 0:384], func=AF.Copy)
                    kv3 = kvcs_t[:].rearrange("p (h d) -> p h d", d=128)
                    if kind == 'own':
                        V('tensor_tensor', [b_qcs], [b_sq], out=sq[:, 0:576], in0=qcs_t[:], in1=qcs_t[:], op=ALU.mult)
                        V('tensor_reduce', [b_sq], [b_st2], out=st2_t[:, 0:6],
                          in_=sq[:, 0:576].rearrange("p (h d) -> p h d", d=96), axis=AX.X, op=ALU.add)
                    if not skip_ck:
                        V('tensor_tensor', [b_kvcs], [b_sq], out=sq[:, 1024:1408].rearrange("p (h d) -> p h d", d=64),
                          in0=kv3[:, :, 0:64], in1=kv3[:, :, 0:64], op=ALU.mult)
                        V('tensor_reduce', [b_sq], [b_st2], out=st2_t[:, 6:12],
                          in_=sq[:, 1024:1408].rearrange("p (h d) -> p h d", d=64), axis=AX.X, op=ALU.add)
                        V('tensor_scalar', [b_st2, b_st], [b_st2], out=st2_t[:, 6:12], in0=st2_t[:, 6:12],
                          scalar1=st_t[:, 20:21], scalar2=None, op0=ALU.add)
                    lo = 0 if kind == 'own' else 6
                    hi_ = 6 if skip_ck else 12
                    V('tensor_scalar', [b_st2], [b_rstd2], out=rstd2_t[:, lo:hi_], in0=st2_t[:, lo:hi_],
                      scalar1=1.0 / 96, scalar2=EPS, op0=ALU.mult, op1=ALU.add)
                    yield
                    G('tensor_tensor', [b_rstd2, b_nh24], [b_rstd2], out=rstd2_t[:, lo:hi_], in0=rstd2_t[:, lo:hi_],
                      in1=nh24[:, lo:hi_], op=ALU.pow)
                    yield
                    kc_t, b_kc = KCo[s2]
                    vc_t, b_vc = VCo[s2]
                    if kind == 'own':
                        qc_t, b_qc = QCo[s2]
                        V('tensor_tensor', [b_qcs, b_rstd2], [b_tmp1], out=tmp1[:],
                          in0=qcs_t[:].rearrange("p (h d) -> p h d", d=96),
                          in1=bc(rstd2_t[:, 0:6].unsqueeze(2), [128, 6, 96]), op=ALU.mult)
                        V('tensor_tensor', [b_tmp1, b_GCq], [b_qc], out=qc_t[:, :, 0:64], in0=tmp1[:, :, 0:64],
                          in1=bc(GCq[:, 0:64].unsqueeze(1), [128, 6, 64]), op=ALU.mult)
                        V('tensor_tensor', [b_tmp1, b_GCq], [b_trq], out=trq[:], in0=tmp1[:, :, 64:96],
                          in1=bc(GCq[:, 64:96].unsqueeze(1), [128, 6, 32]), op=ALU.mult)
                        rope(trq, b_trq, qc_t[:, :, 64:96], b_qc, 6, ti)
                    if kind == 'own':
                        qT_t, b_qT = QTs[s2]
                        for h in range(6):
                            T('transpose', [b_qc, b_idb], [b_TR], out=TR[0:96, h, :], in_=qc_t[:, h, :],
                              identity=idb[:])
                        V('tensor_copy', [b_TR], [b_qT], out=qT_t[:], in_=TR[0:96, 0:6, :])
                        self.dma('sync', SC['QCT'].ap()[:, :, t * 128:(t + 1) * 128].rearrange("h d n -> d h n"),
                                 qT_t[:], [b_qT], [SC['b']], 'qto%d' % s2)
                        yield
                    if skip_ck:
                        return
                    V('tensor_tensor', [b_kvcs, b_rstd2], [b_tmpk], out=tmpk[:], in0=kv3[:, :, 0:64],
                      in1=bc(rstd2_t[:, 6:12].unsqueeze(2), [128, 6, 64]), op=ALU.mult)
                    V('tensor_tensor', [b_tmpk, b_GCk], [b_kc], out=kc_t[:, :, 0:64], in0=tmpk[:],
                      in1=bc(GCk[:, 0:64].unsqueeze(1), [128, 6, 64]), op=ALU.mult)
                    V('tensor_tensor', [b_pj, b_GCk], [b_krg], out=krg[:, 0, :], in0=pj_t[:, 2048:2080],
                      in1=GCk[:, 64:96], op=ALU.mult)
                    rope(krg, b_krg, krr[:], b_krr, 1, ti)
                    V('tensor_tensor', [b_krr, b_rstd2], [b_kc], out=kc_t[:, :, 64:96],
                      in0=bc(krr[:], [128, 6, 32]), in1=bc(rstd2_t[:, 6:12].unsqueeze(2), [128, 6, 32]), op=ALU.mult)
                    V('tensor_copy', [b_kvcs], [b_vc], out=vc_t[:, :, 0:64], in_=kv3[:, :, 64:128])
                    yield
                    kT_t, b_kT = KTs[s2]
                    for h in range(6):
                        T('transpose', [b_kc, b_idb], [b_TR], out=TR[0:96, h, :], in_=kc_t[:, h, :], identity=idb[:])
                    V('tensor_copy', [b_TR], [b_kT], out=kT_t[:], in_=TR[0:96, 0:6, :])
                    self.dma('sync', SC['KCT'].ap()[:, :, ti * 128:(ti + 1) * 128].rearrange("h d n -> d h n"),
                             kT_t[:], [b_kT], [SC['b']], 'kto%d' % s2)
                    yield
                    self.dma('sync', SC['VC'].ap()[ti * 128:(ti + 1) * 128, :].rearrange("p (h d) -> p h d", d=65),
                             vc_t[:], [b_vc], [SC['b']], 'vco%d' % s2)
                    yield

            def issue_load(j):
                kind, t = jobs[j]
                x_t, b_x = xt[j % NS]
                if kind == 'halo' and halo_rows is not None:
                    self.dma('sync', x_t[:], halo_rows(t), [], [b_x], 'xt%d' % (j % NS))
                else:
                    src = {'own': x_own, 'oth': x_oth, 'halo': x_halo}[kind]
                    self.dma('sync', x_t[:], src[t * 128:(t + 1) * 128, :], [], [b_x], 'xt%d' % (j % NS))

            for j in range(min(2, len(jobs))):
                issue_load(j)
            active = []
            nxt = 0
            since = 10 ** 9
            STAG = 9
            while active or nxt < len(jobs):
                if nxt < len(jobs) and len(active) < NS and (since >= STAG or not active):
                    if nxt + 2 < len(jobs):
                        issue_load(nxt + 2)
                    active.append(tile_gen(nxt, jobs[nxt][0], jobs[nxt][1]))
                    nxt += 1
                    since = 0
                for g in list(active):
                    try:
                        next(g)
                    except StopIteration:
                        active.remove(g)
                since += 1
        self.P.phase_barrier()

    def phaseC(self, heads=range(6), nqt=8):
        nc, P, D, SC = self.nc, self.P, self.D, self.SC
        V, G, A, T, E = self.V, self.G, self.A, self.T, self.E
        with ExitStack() as ph:
            def sb(nm, shape, dt, n=1):
                r = []
                for i in range(n):
                    t = ph.enter_context(nc.sbuf_tensor(self.name(nm), shape, dt))
                    r.append((t, Buf(nm + str(i))))
                return r if n > 1 else r[0]

            def ps(nm, shape, dt, n=1):
                r = []
                for i in range(n):
                    t = psum_view(ph, nc, self.name(nm), shape, dt)
                    r.append((t, Buf(nm + str(i))))
                return r if n > 1 else r[0]

            P.barrier('sync', reads=[SC['b']])
            idf, b_idf = sb("idf", [128, 128], F32)
            self.dma('sync', idf[:], D['idf'].ap(), [], [b_idf], 'idf')
            KT = sb("cKT", [96, 8192], BF16, 2)
            VV = sb("cV", [128, 64, 65], BF16, 2)
            QT = sb("cQT", [96, 4096], BF16, 2)
            PT = sb("cPT", [128, 512], BF16, 4)
            OTs = sb("cOTs", [65, 512], F32, 2)
            rc = sb("crc", [128, 4], F32, 2)
            oc = sb("coc", [128, 4, 64], BF16, 2)
            ST = ps("cST", [128, 512], F32, 3)
            OT = ps("cOT", [65, 512], F32, 2)
            TO, b_TO = ps("cTO", [128, 4, 65], F32)

            heads = list(heads)

            def load_head(hi):
                h = heads[hi]
                s = hi % 2
                self.dma('sync', KT[s][0][:], SC['KCT'].ap()[h], [SC['b']], [KT[s][1]], 'cKT%d' % s)
                self.dma('sync', QT[s][0][:], SC['QCT'].ap()[h], [SC['b']], [QT[s][1]], 'cQT%d' % s)
                self.dma('sync', VV[s][0][:],
                         SC['VC'].ap().rearrange("(t p) c -> p t c", p=128)[:, :, h * 65:(h + 1) * 65],
                         [SC['b']], [VV[s][1]], 'cV%d' % s)

            steps = [(hi, qt, kc) for hi in range(len(heads)) for qt in range(nqt) for kc in range(64)]
            n = len(steps)
            LA = 2
            deferred = []
            load_head(0)
            nq = 0
            for i in range(n + LA):
                if i < n:
                    hi, qt, kc = steps[i]
                    s = hi % 2
                    st_t, st_b = ST[i % 3]
                    pt_t, pt_b = PT[i % 4]
                    T('matmul', [KT[s][1], QT[s][1]], [st_b], out=st_t[:], lhsT=KT[s][0][:, kc * 128:(kc + 1) * 128],
                      rhs=QT[s][0][:, qt * 512:(qt + 1) * 512], start=True, stop=True)
                    A('activation', [st_b], [pt_b], out=pt_t[:], in_=st_t[:], func=AF.Exp)
                j = i - LA
                if j >= 0:
                    hi, qt, kc = steps[j]
                    if qt == 0 and kc == 0 and hi + 1 < len(heads):
                        load_head(hi + 1)
                    s = hi % 2
                    qi = (hi * nqt + qt)
                    ot_t, ot_b = OT[qi % 2]
                    pt_t, pt_b = PT[j % 4]
                    T('matmul', [VV[s][1], pt_b], [ot_b], out=ot_t[:], lhsT=VV[s][0][:, kc, :], rhs=pt_t[:],
                      start=(kc == 0), stop=(kc == 63))
                    if kc == 63:
                        h = heads[hi]
                        os_t, os_b = OTs[qi % 2]
                        V('tensor_copy', [ot_b], [os_b], out=os_t[:], in_=ot_t[:])

                        def fin(os_t=os_t, os_b=os_b, qi=qi, qt=qt, h=h):
                            for jj in range(4):
                                T('transpose', [os_b, b_idf], [b_TO], out=TO[:, jj, :],
                                  in_=os_t[:, jj * 128:(jj + 1) * 128], identity=idf[0:65, 0:65])
                            rc_t, rc_b = rc[qi % 2]
                            oc_t, oc_b = oc[qi % 2]
                            V('reciprocal', [b_TO], [rc_b], out=rc_t[:].unsqueeze(2), in_=TO[:, :, 64:65])
                            V('tensor_tensor', [b_TO, rc_b], [oc_b], out=oc_t[:], in0=TO[:, :, 0:64],
                              in1=bc(rc_t[:].unsqueeze(2), [128, 4, 64]), op=ALU.mult)
                            self.dma('sync',
                                     SC['MIX'].ap()[qt * 512:(qt + 1) * 512, 640 + h * 64:640 + (h + 1) * 64]
                                     .rearrange("(t p) d -> p t d", p=128),
                                     oc_t[:], [oc_b], [SC['bmix']], 'coc%d' % (qi % 2))
                        deferred.append((i + 6, fin))
                while deferred and deferred[0][0] <= i:
                    deferred.pop(0)[1]()
            for (_, fn) in deferred:
                fn()
        self.P.phase_barrier()

    def bias_setup(self):
        nc, P, D, SC = self.nc, self.P, self.D, self.SC
        V, G, A, T, E = self.V, self.G, self.A, self.T, self.E
        with ExitStack() as ph:
            tabN = ph.enter_context(nc.sbuf_tensor(self.name("tabN"), [33, 10], F32)); b_tab = Buf("tabN")
            oh = ph.enter_context(nc.sbuf_tensor(self.name("oh"), [33, 1660], F32)); b_oh = Buf("oh")
            fv = ph.enter_context(nc.sbuf_tensor(self.name("fv"), [10, 1660], F32)); b_fv = Buf("fv")
            pf = [(psum_view(ph, nc, self.name("pf"), [10, 512], F32), Buf("pf%d" % i)) for i in range(4)]
            P.op('vector', lambda e: e.memset(tabN[:], NEGB), [], [b_tab])
            self.dma('sync', tabN[0:32, :], D['rel_bias_table'].ap(), [], [b_tab], 'tabN')
            self.dma('sync', oh[:], D['oh'].ap(), [], [b_oh], 'oh')
            segs = [(0, 511), (511, 894), (894, 1277), (1277, 1660)]
            for i, (a, b) in enumerate(segs):
                T('matmul', [b_tab, b_oh], [pf[i][1]], out=pf[i][0][:, 0:b - a], lhsT=tabN[:], rhs=oh[:, a:b],
                  start=True, stop=True)
                V('tensor_copy', [pf[i][1]], [b_fv], out=fv[:, a:b], in_=pf[i][0][:, 0:b - a])
            self.dma('sync', SC['FV'].ap(), fv[:], [b_fv], [SC['bfv']], 'fvo')
        self.P.phase_barrier()

    def phaseB(self, L):
        nc, P, D, SC = self.nc, self.P, self.D, self.SC
        V, G, A, T, E = self.V, self.G, self.A, self.T, self.E
        with ExitStack() as ph:
            def sb(nm, shape, dt, n=1):
                r = []
                for i in range(n):
                    t = ph.enter_context(nc.sbuf_tensor(self.name(nm), shape, dt))
                    r.append((t, Buf(nm + str(i))))
                return r if n > 1 else r[0]

            def ps(nm, shape, dt, n=1):
                r = []
                for i in range(n):
                    t = psum_view(ph, nc, self.name(nm), shape, dt)
                    r.append((t, Buf(nm + str(i))))
                return r if n > 1 else r[0]

            P.barrier('sync', reads=[SC['b'], SC['bfv']])
            idf, b_idf = sb("idf", [128, 128], F32)
            idb, b_idb = sb("idb", [128, 128], BF16)
            self.dma('sync', idf[:], D['idf'].ap(), [], [b_idf], 'idf')
            self.dma('sync', idb[:], D['idb'].ap(), [], [b_idb], 'idb')
            biasB = sb("biasB", [128, 4, 3, 128], F32)
            hk = sb("hkB", [128, 128], F32, 4)
            for h in range(4):
                for o in range(3):
                    g_t, g_b = hk[(h * 3 + o) % 4]
                    self.dma('sync', g_t[:], bass.AP(SC['FV'], (6 + h) * 1660 + 128 * o, [[1, 128], [1, 128]]),
                             [SC['bfv']], [g_b], 'hkB%d' % ((h * 3 + o) % 4))
                    V('tensor_copy', [g_b], [biasB[1]], out=biasB[0][:, h, o, :],
                      in_=bass.AP(g_t, 127, [[128, 128], [-1, 128]]))
            sk, b_sk = sb("sink", [128, 4], F32)
            esk, b_esk = sb("esink", [128, 4], F32)
            self.dma('sync', sk[:], D['sink_b'].ap()[L].partition_broadcast(128), [], [b_sk], 'sink')
            A('activation', [b_sk], [b_esk], out=esk[:], in_=sk[:], func=AF.Exp)
            Qc = sb("bQc", [128, 4, 256], BF16, 2)
            Kc = sb("bKc", [128, 6, 128], BF16, 2)
            Vc = sb("bVc", [128, 6, 130], BF16, 2)
            QTb = sb("bQT", [128, 2, 4, 128], BF16, 2)
            KTb = sb("bKT", [128, 6, 128], BF16, 2)
            Sb = sb("bS", [128, 3, 128], F32, 2)
            PTb = sb("bPT", [128, 3, 128], BF16, 2)
            OTs = sb("bOTs", [65, 4, 128], F32, 2)
            den = sb("bden", [128, 4], F32, 2)
            ob = sb("bo", [128, 4, 64], BF16, 2)
            TRq = ps("bTRq", [128, 2, 4, 128], BF16)
            TRk = ps("bTRk", [128, 6, 128], BF16)
            SP = ps("bSP", [128, 3, 128], F32, 2)
            OTp = ps("bOT", [65, 4, 128], F32, 2)
            TOp = ps("bTO", [128, 4, 65], F32)

            def stage1(J):
                s = J % 2
                self.dma('sync', Qc[s][0][:], SC['QB'].ap()[J * 512:(J + 1) * 512, :].rearrange("(t p) c -> p t c", p=128),
                         [SC['b']], [Qc[s][1]], 'bQc%d' % s)
                r0 = (7 + 4 * J) * 128
                self.dma('sync', Kc[s][0][:], SC['KBx'].ap()[r0:r0 + 768, :].rearrange("(t p) c -> p t c", p=128),
                         [SC['b']], [Kc[s][1]], 'bKc%d' % s)
                self.dma('sync', Vc[s][0][:], SC['VBx'].ap()[r0:r0 + 768, :].rearrange("(t p) c -> p t c", p=128),
                         [SC['b']], [Vc[s][1]], 'bVc%d' % s)
                for t in range(4):
                    for pi in range(2):
                        T('transpose', [Qc[s][1], b_idb], [TRq[1]], out=TRq[0][:, pi, t, :],
                          in_=Qc[s][0][:, t, pi * 128:(pi + 1) * 128], identity=idb[:])
                V('tensor_copy', [TRq[1]], [QTb[s][1]], out=QTb[s][0][:], in_=TRq[0][:])
                for kt in range(6):
                    T('transpose', [Kc[s][1], b_idb], [TRk[1]], out=TRk[0][:, kt, :], in_=Kc[s][0][:, kt, :],
                      identity=idb[:])
                V('tensor_copy', [TRk[1]], [KTb[s][1]], out=KTb[s][0][:], in_=TRk[0][:])

            cnt = [0]

            def stage2(J):
                s = J % 2
                for t in range(4):
                    qi = J * 4 + t
                    ot_t, ot_b = OTp[qi % 2]
                    for h in range(4):
                        base = 64 * (h // 2)
                        pi = h % 2
                        kvh = h // 2
                        c = cnt[0]
                        cnt[0] += 1
                        sp_t, sp_b = SP[c % 2]
                        for o in range(3):
                            T('matmul', [KTb[s][1], QTb[s][1]], [sp_b], out=sp_t[:, o, :],
                              lhsT=KTb[s][0][base:base + 64, t + o, :], rhs=QTb[s][0][base:base + 64, pi, t, :],
                              start=True, stop=True)
                        s_t, s_b = Sb[c % 2]
                        p_t, p_b = PTb[c % 2]
                        V('tensor_tensor', [sp_b, biasB[1]], [s_b], out=s_t[:], in0=sp_t[:], in1=biasB[0][:, h, :, :],
                          op=ALU.add)
                        A('activation', [s_b], [p_b], out=p_t[:], in_=s_t[:], func=AF.Exp)
                        for o in range(3):
                            T('matmul', [Vc[s][1], p_b], [ot_b], out=ot_t[:, h, :],
                              lhsT=Vc[s][0][:, t + o, kvh * 65:(kvh + 1) * 65], rhs=p_t[:, o, :],
                              start=(o == 0), stop=(o == 2))
                    os_t, os_b = OTs[qi % 2]
                    V('tensor_copy', [ot_b], [os_b], out=os_t[:], in_=ot_t[:])
                    for h in range(4):
                        T('transpose', [os_b, b_idf], [TOp[1]], out=TOp[0][:, h, :], in_=os_t[:, h, :],
                          identity=idf[0:65, 0:65])
                    d_t, d_b = den[qi % 2]
                    o_t, o_b = ob[qi % 2]
                    V('tensor_tensor', [TOp[1], b_esk], [d_b], out=d_t[:].unsqueeze(2), in0=TOp[0][:, :, 64:65],
                      in1=esk[:].unsqueeze(2), op=ALU.add)
                    V('reciprocal', [d_b], [d_b], out=d_t[:], in_=d_t[:])
                    V('tensor_tensor', [TOp[1], d_b], [o_b], out=o_t[:], in0=TOp[0][:, :, 0:64],
                      in1=bc(d_t[:].unsqueeze(2), [128, 4, 64]), op=ALU.mult)
                    self.dma('sync', SC['MIX'].ap()[qi * 128:(qi + 1) * 128, 384:640], o_t[:], [o_b], [SC['bmix']],
                             'bo%d' % (qi % 2))

            stage1(0)
            for J in range(8):
                if J + 1 < 8:
                    stage1(J + 1)
                stage2(J)
        self.P.phase_barrier()

    def phaseA(self, cfgs=(0, 1, 2)):
        nc, P, D, SC = self.nc, self.P, self.D, self.SC
        V, G, A, T, E = self.V, self.G, self.A, self.T, self.E
        RS = (1, 4, 16)
        with ExitStack() as ph:
            def sb(nm, shape, dt, n=1):
                r = []
                for i in range(n):
                    t = ph.enter_context(nc.sbuf_tensor(self.name(nm), shape, dt))
                    r.append((t, Buf(nm + str(i))))
                return r if n > 1 else r[0]

            def ps(nm, shape, dt, n=1):
                r = []
                for i in range(n):
                    t = psum_view(ph, nc, self.name(nm), shape, dt)
                    r.append((t, Buf(nm + str(i))))
                return r if n > 1 else r[0]

            P.barrier('sync', reads=[SC['b'], SC['bfv']])
            idf, b_idf = sb("idf", [128, 128], F32)
            idb, b_idb = sb("idb", [128, 128], BF16)
            self.dma('sync', idf[:], D['idf'].ap(), [], [b_idf], 'idf')
            self.dma('sync', idb[:], D['idb'].ap(), [], [b_idb], 'idb')
            biasA = sb("biasA", [128, 3, 6, 2, 128], F32)
            hk = sb("hkA", [128, 128], F32, 4)
            for ci in range(3):
                for h in range(6):
                    for c in range(2):
                        kk = (ci * 6 + h) * 2 + c
                        g_t, g_b = hk[kk % 4]
                        self.dma('sync', g_t[:],
                                 bass.AP(SC['FV'], h * 1660 + 511 + 383 * ci + 128 * c, [[1, 128], [1, 128]]),
                                 [SC['bfv']], [g_b], 'hkA%d' % (kk % 4))
                        V('tensor_copy', [g_b], [biasA[1]], out=biasA[0][:, ci, h, c, :],
                          in_=bass.AP(g_t, 127, [[128, 128], [-1, 128]]))
            Qa = sb("aQ", [128, 384], BF16, 3)
            Ka = sb("aK", [128, 2, 384], BF16, 3)
            Va = sb("aV", [128, 2, 390], BF16, 3)
            QTa = sb("aQT", [128, 2, 3, 128], BF16, 2)
            for (t_, b_) in QTa:
                P.op('vector', lambda e, t_=t_: e.memset(t_[:], 0.0), [], [b_])
            KTa = sb("aKT", [128, 3, 2, 128], BF16, 2)
            Sa = sb("aS", [128, 6, 2, 128], F32, 2)
            PTa = sb("aPT", [128, 6, 2, 128], BF16, 2)
            OTs = sb("aOTs", [65, 6, 128], F32, 2)
            Oo = sb("aOo", [128, 6, 65], F32, 2)
            TRq = ps("aTRq", [128, 3, 128], BF16)
            TRk = ps("aTRk", [128, 3, 2, 128], BF16)
            SP = ps("aSP", [128, 2, 2, 128], F32, 3)
            OTp = ps("aOT", [65, 3, 128], F32, 2)
            TOp = ps("aTO", [128, 6, 65], F32)

            jobs = []
            for ci in cfgs:
                r = RS[ci]
                for rho in range(r):
                    for j in range(32 // r):
                        jobs.append((ci, r, rho, j))

            if self.debug and self.debug.get('a_jobs'):
                jobs = self.debug['a_jobs']

            def loadA(i):
                ci, r, rho, j = jobs[i]
                s3 = i % 3
                self.dma('sync', Qa[s3][0][:], bass.AP(SC['QA'], (r * 128 * j + rho) * 384, [[r * 384, 128], [1, 384]]),
                         [SC['b']], [Qa[s3][1]], 'aQ%d' % s3)
                e0 = r * (128 * j - 64) + rho + 1024
                self.dma('sync', Ka[s3][0][:],
                         bass.AP(SC['KAx'], e0 * 384, [[r * 384, 128], [r * 384 * 128, 2], [1, 384]]),
                         [SC['b']], [Ka[s3][1]], 'aK%d' % s3)
                self.dma('sync', Va[s3][0][:],
                         bass.AP(SC['VAx'], e0 * 390, [[r * 390, 128], [r * 390 * 128, 2], [1, 390]]),
                         [SC['b']], [Va[s3][1]], 'aV%d' % s3)

            def stage1(i):
                ci, r, rho, j = jobs[i]
                s3 = i % 3
                s2 = i % 2
                for pi in range(3):
                    T('transpose', [Qa[s3][1], b_idb], [TRq[1]], out=TRq[0][:, pi, :],
                      in_=Qa[s3][0][:, pi * 128:(pi + 1) * 128], identity=idb[:])
                V('tensor_copy', [TRq[1]], [QTa[s2][1]], out=QTa[s2][0][0:64, 0, :, :], in_=TRq[0][0:64, :, :])
                V('tensor_copy', [TRq[1]], [QTa[s2][1]], out=QTa[s2][0][64:128, 1, :, :], in_=TRq[0][64:128, :, :])
                for pi in range(3):
                    for c in range(2):
                        T('transpose', [Ka[s3][1], b_idb], [TRk[1]], out=TRk[0][:, pi, c, :],
                          in_=Ka[s3][0][:, c, pi * 128:(pi + 1) * 128], identity=idb[:])
                V('tensor_copy', [TRk[1]], [KTa[s2][1]], out=KTa[s2][0][:], in_=TRk[0][:])

            astop = self.debug.get('a_stop', 99) if self.debug else 99

            def stage2(i):
                ci, r, rho, j = jobs[i]
                s3 = i % 3
                s2 = i % 2
                s_t, s_b = Sa[s2]
                p_t, p_b = PTa[s2]
                if astop < 2:
                    return
                for pi in range(3):
                    sp_t, sp_b = SP[pi]
                    for hh in range(2):
                        for c in range(2):
                            T('matmul', [KTa[s2][1], QTa[s2][1]], [sp_b], out=sp_t[:, hh, c, :],
                              lhsT=KTa[s2][0][:, pi, c, :],
                              rhs=QTa[s2][0][:, hh, pi, :], start=True, stop=True)
                    if self.debug and self.debug.get('a_nobias'):
                        continue
                    V('tensor_tensor', [sp_b, biasA[1]], [s_b], out=s_t[:, 2 * pi:2 * pi + 2, :, :], in0=sp_t[:],
                      in1=biasA[0][:, ci, 2 * pi:2 * pi + 2, :, :], op=ALU.add)
                if astop < 3:
                    return
                A('activation', [s_b], [p_b], out=p_t[:], in_=s_t[:], func=AF.Exp)
                os_t, os_b = OTs[s2]
                if astop < 4:
                    return
                for g3 in range(2):
                    ot_t, ot_b = OTp[g3]
                    for hh in range(3):
                        h = g3 * 3 + hh
                        for c in range(2):
                            T('matmul', [Va[s3][1], p_b], [ot_b], out=ot_t[:, hh, :],
                              lhsT=Va[s3][0][:, c, h * 65:(h + 1) * 65], rhs=p_t[:, h, c, :],
                              start=(c == 0), stop=(c == 1))
                    V('tensor_copy', [ot_b], [os_b], out=os_t[:, g3 * 3:g3 * 3 + 3, :], in_=ot_t[:])
                if astop < 5:
                    return
                for h in range(6):
                    T('transpose', [os_b, b_idf], [TOp[1]], out=TOp[0][:, h, :], in_=os_t[:, h, :],
                      identity=idf[0:65, 0:65])
                o_t, o_b = Oo[s2]
                V('tensor_copy', [TOp[1]], [o_b], out=o_t[:], in_=TOp[0][:])
                self.dma('sync', bass.AP(SC['OA'], ci * 4096 * 390 + (r * 128 * j + rho) * 390, [[r * 390, 128], [1, 390]]),
                         o_t[:].rearrange("p h d -> p (h d)"), [o_b], [SC['boa']], 'aOo%d' % s2)

            n = len(jobs)
            for i in range(min(2, n)):
                loadA(i)
            stage1(0)
            for i in range(n):
                if i + 2 < n:
                    loadA(i + 2)
                if i + 1 < n:
                    stage1(i + 1)
                stage2(i)
        self.P.phase_barrier()

    def phaseA2(self):
        nc, P, D, SC = self.nc, self.P, self.D, self.SC
        V, G, A, T, E = self.V, self.G, self.A, self.T, self.E
        with ExitStack() as ph:
            def sb(nm, shape, dt, n=1):
                r = []
                for i in range(n):
                    t = ph.enter_context(nc.sbuf_tensor(self.name(nm), shape, dt))
                    r.append((t, Buf(nm + str(i))))
                return r if n > 1 else r[0]
            P.barrier('sync', reads=[SC['boa']])
            O3 = sb("a2O", [128, 3, 6, 65], F32, 3)
            acc = sb("a2acc", [128, 6, 65], F32, 2)
            rc = sb("a2rc", [128, 6], F32, 2)
            oo = sb("a2o", [128, 6, 64], BF16, 2)
            for t in range(32):
                s3, s2 = t % 3, t % 2
                self.dma('sync', O3[s3][0][:].rearrange("p c h d -> p c (h d)"),
                         SC['OA'].ap()[:, t * 128:(t + 1) * 128, :].rearrange("c p d -> p c d"),
                         [SC['boa']], [O3[s3][1]], 'a2O%d' % s3)
                o3 = O3[s3][0]
                V('tensor_tensor', [O3[s3][1]], [acc[s2][1]], out=acc[s2][0][:], in0=o3[:, 0], in1=o3[:, 1], op=ALU.add)
                V('tensor_tensor', [O3[s3][1], acc[s2][1]], [acc[s2][1]], out=acc[s2][0][:], in0=acc[s2][0][:],
                  in1=o3[:, 2], op=ALU.add)
                V('reciprocal', [acc[s2][1]], [rc[s2][1]], out=rc[s2][0][:].unsqueeze(2), in_=acc[s2][0][:, :, 64:65])
                V('tensor_tensor', [acc[s2][1], rc[s2][1]], [oo[s2][1]], out=oo[s2][0][:], in0=acc[s2][0][:, :, 0:64],
                  in1=bc(rc[s2][0][:].unsqueeze(2), [128, 6, 64]), op=ALU.mult)
                self.dma('sync', SC['MIX'].ap()[t * 128:(t + 1) * 128, 0:384], oo[s2][0][:].rearrange("p h d -> p (h d)"),
                         [oo[s2][1]], [SC['bmix']], 'a2o%d' % s2)
        self.P.phase_barrier()

    def phase4a(self, L, x_own, x1_d):
        nc, P, D, SC = self.nc, self.P, self.D, self.SC
        V, G, A, T, E = self.V, self.G, self.A, self.T, self.E
        with ExitStack() as ph:
            def sb(nm, shape, dt, n=1):
                r = []
                for i in range(n):
                    t = ph.enter_context(nc.sbuf_tensor(self.name(nm), shape, dt))
                    r.append((t, Buf(nm + str(i))))
                return r if n > 1 else r[0]

            def ps(nm, shape, dt, n=1):
                r = []
                for i in range(n):
                    t = psum_view(ph, nc, self.name(nm), shape, dt)
                    r.append((t, Buf(nm + str(i))))
                return r if n > 1 else r[0]

            P.barrier('sync', reads=[SC['bmix']])
            idb, b_idb = sb("idb", [128, 128], BF16)
            self.dma('sync', idb[:], D['idb'].ap(), [], [b_idb], 'idb')
            wo, b_wo = sb("wo", [128, 8, 1024], BF16)
            go, b_go = sb("go", [128, 8], F32)
            self.dma('sync', go[:], D['out_norm'].ap()[L].rearrange("(c p) -> p c", p=128), [], [b_go], 'go',
                     allow_slow_non_contiguous=True)
            stg = sb("stg4a", [128, 1024], F32, 2)
            for c in range(8):
                self.dma('sync', stg[c % 2][0][:], D['w_out'].ap()[L, c * 128:(c + 1) * 128, :], [], [stg[c % 2][1]],
                         'stg4a%d' % (c % 2))
                if c % 2 == 0:
                    V('tensor_scalar', [stg[c % 2][1], b_go], [b_wo], out=wo[:, c, :], in0=stg[c % 2][0][:],
                      scalar1=go[:, c:c + 1], scalar2=None, op0=ALU.mult)
                else:
                    A('activation', [stg[c % 2][1], b_go], [b_wo], out=wo[:, c, :], in_=stg[c % 2][0][:], func=AF.Copy,
                      scale=go[:, c:c + 1])
            invd3, b_invd3 = sb("invd3", [128, 3], F32)
            nh3, b_nh3 = sb("nh3", [128, 3], F32)
            P.op('gpsimd', lambda e: e.memset(invd3[:], 1.0 / 384), [], [b_invd3])
            P.op('gpsimd', lambda e: e.memset(invd3[:, 1:2], 1.0 / 256), [], [b_invd3])
            P.op('gpsimd', lambda e: e.memset(nh3[:], -0.5), [], [b_nh3])
            mx = sb("mx", [128, 1024], BF16, 3)
            xt = sb("x4a", [128, 1024], F32, 3)
            sq, b_sq = sb("sq4a", [128, 1024], F32)
            ss = sb("ss4a", [128, 3], F32, 2)
            rs = sb("rs4a", [128, 3], F32, 2)
            mn = sb("mn", [128, 1024], BF16, 2)
            mT = sb("mT", [128, 8, 128], BF16, 2)
            TR = ps("TR4a", [128, 8, 128], BF16, 2)
            Y = ps("Y4a", [128, 1024], F32, 2)
            grp = [(0, 384), (384, 640), (640, 1024)]
            def load4a(t):
                s3 = t % 3
                rows = slice(t * 128, (t + 1) * 128)
                self.dma('sync', mx[s3][0][:], SC['MIX'].ap()[rows, :], [SC['bmix']], [mx[s3][1]], 'mx%d' % s3)
                self.dma('sync', xt[s3][0][:], x_own[rows, :], [], [xt[s3][1]], 'x4a%d' % s3)
            load4a(0)

            def s1_4a(t):
                s3, s2 = t % 3, t % 2
                rows = slice(t * 128, (t + 1) * 128)
                if t + 1 < 32:
                    load4a(t + 1)
                m_t, m_b = mx[s3]
                V('tensor_tensor', [m_b], [b_sq], out=sq[:], in0=m_t[:], in1=m_t[:], op=ALU.mult)
                for gi, (a, b) in enumerate(grp):
                    V('tensor_reduce', [b_sq], [ss[s2][1]], out=ss[s2][0][:, gi:gi + 1], in_=sq[:, a:b], axis=AX.X,
                      op=ALU.add)
                V('tensor_tensor', [ss[s2][1], b_invd3], [rs[s2][1]], out=rs[s2][0][:], in0=ss[s2][0][:], in1=invd3[:],
                  op=ALU.mult)
                V('tensor_scalar', [rs[s2][1]], [rs[s2][1]], out=rs[s2][0][:], in0=rs[s2][0][:], scalar1=EPS, scalar2=None,
                  op0=ALU.add)
                G('tensor_tensor', [rs[s2][1], b_nh3], [rs[s2][1]], out=rs[s2][0][:], in0=rs[s2][0][:], in1=nh3[:],
                  op=ALU.pow)
                for gi, (a, b) in enumerate(grp):
                    E('vector', 'tensor_scalar', [m_b, rs[s2][1]], [mn[s2][1]],
                      out=mn[s2][0][:, a:b], in0=m_t[:, a:b], scalar1=rs[s2][0][:, gi:gi + 1], scalar2=None, op0=ALU.mult)
                for c in range(8):
                    T('transpose', [mn[s2][1], b_idb], [TR[s2][1]], out=TR[s2][0][:, c, :],
                      in_=mn[s2][0][:, c * 128:(c + 1) * 128], identity=idb[:])
                A('activation', [TR[s2][1]], [mT[s2][1]], out=mT[s2][0][:], in_=TR[s2][0][:], func=AF.Copy)

            def s2_4a(t):
                s3, s2 = t % 3, t % 2
                rows = slice(t * 128, (t + 1) * 128)
                for hf in range(2):
                    for c in range(8):
                        T('matmul', [mT[s2][1], b_wo], [Y[s2][1]], out=Y[s2][0][:, hf * 512:(hf + 1) * 512],
                          lhsT=mT[s2][0][:, c, :], rhs=wo[:, c, hf * 512:(hf + 1) * 512], start=(c == 0), stop=(c == 7))
                V('tensor_tensor', [Y[s2][1], xt[s3][1]], [xt[s3][1]], out=xt[s3][0][:], in0=Y[s2][0][:],
                  in1=xt[s3][0][:], op=ALU.add)
                self.dma('sync', x1_d[rows, :], xt[s3][0][:], [xt[s3][1]], [SC['bx1']], 'x4ao%d' % s3)

            s1_4a(0)
            for t in range(32):
                if t + 1 < 32:
                    s1_4a(t + 1)
                s2_4a(t)
        self.P.phase_barrier()

    def phase4b(self, L, x1_d, out_d, b_out):
        nc, P, D, SC = self.nc, self.P, self.D, self.SC
        V, G, A, T, E = self.V, self.G, self.A, self.T, self.E
        with ExitStack() as ph:
            def sb(nm, shape, dt, n=1):
                r = []
                for i in range(n):
                    t = ph.enter_context(nc.sbuf_tensor(self.name(nm), shape, dt))
                    r.append((t, Buf(nm + str(i))))
                return r if n > 1 else r[0]

            def ps(nm, shape, dt, n=1):
                r = []
                for i in range(n):
                    t = psum_view(ph, nc, self.name(nm), shape, dt)
                    r.append((t, Buf(nm + str(i))))
                return r if n > 1 else r[0]

            P.barrier('sync', reads=[SC['bx1']])
            idb, b_idb = sb("idb", [128, 128], BF16)
            self.dma('sync', idb[:], D['idb'].ap(), [], [b_idb], 'idb')
            wu, b_wu = sb("wu", [128, 8, 4096], BF16)
            wd, b_wd = sb("wd", [128, 32, 1024], BF16)
            gm, b_gm = sb("gm", [128, 8], F32)
            self.dma('sync', gm[:], D['norm_mlp'].ap()[L].rearrange("(c p) -> p c", p=128), [], [b_gm], 'gm',
                     allow_slow_non_contiguous=True)
            stg = sb("stg4b", [128, 1024], F32, 2)
            k = 0
            for c in range(8):
                for hf in range(4):
                    s = k % 2
                    self.dma('sync', stg[s][0][:], D['w_up'].ap()[L, c * 128:(c + 1) * 128, hf * 1024:(hf + 1) * 1024], [],
                             [stg[s][1]], 'stg4b%d' % s)
                    if k % 2 == 0:
                        V('tensor_scalar', [stg[s][1], b_gm], [b_wu], out=wu[:, c, hf * 1024:(hf + 1) * 1024],
                          in0=stg[s][0][:], scalar1=gm[:, c:c + 1], scalar2=None, op0=ALU.mult)
                    else:
                        A('activation', [stg[s][1], b_gm], [b_wu], out=wu[:, c, hf * 1024:(hf + 1) * 1024],
                          in_=stg[s][0][:], func=AF.Copy, scale=gm[:, c:c + 1])
                    k += 1
            for c2 in range(32):
                s = k % 2
                self.dma('sync', stg[s][0][:], D['w_down'].ap()[L, c2 * 128:(c2 + 1) * 128, :], [],
                         [stg[s][1]], 'stg4b%d' % s)
                if k % 2 == 0:
                    V('tensor_copy', [stg[s][1]], [b_wd], out=wd[:, c2, :], in_=stg[s][0][:])
                else:
                    A('activation', [stg[s][1]], [b_wd], out=wd[:, c2, :], in_=stg[s][0][:], func=AF.Copy)
                k += 1
            nh1, b_nh1 = sb("nh1", [128, 1], F32)
            P.op('gpsimd', lambda e: e.memset(nh1[:], -0.5), [], [b_nh1])
            xc = sb("x4b", [128, 2, 1024], F32, 2)
            junk, b_junk = sb("junk4b", [128, 1024], BF16)
            ss = sb("ss4b", [128, 2], F32, 2)
            h2 = [sb("h2", [128, 2, 1024], BF16)] * 2
            h2T = [sb("h2T", [128, 8, 256], BF16)] * 2
            uT = [sb("uT", [128, 32, 256], BF16)] * 2
            rr = sb("rr", [128, 2, 256], F32, 3)
            TR = ps("TR4b", [128, 8, 128], BF16)
            U = ps("U4b", [128, 2, 256], F32, 2)
            Z = ps("Z4b", [128, 1024], F32, 2)
            def load4b(ch):
                s2 = ch % 2
                rows = slice(ch * 256, (ch + 1) * 256)
                self.dma('sync', xc[s2][0][:], x1_d[rows, :].rearrange("(t p) d -> p t d", p=128), [SC['bx1']],
                         [xc[s2][1]], 'x4b%d' % s2)
            load4b(0)
            for ch in range(16):
                s2 = ch % 2
                rows = slice(ch * 256, (ch + 1) * 256)
                x_t, x_b = xc[s2]
                if ch + 1 < 16:
                    load4b(ch + 1)
                for t in range(2):
                    A('activation', [x_b], [b_junk, ss[s2][1]], out=junk[:], in_=x_t[:, t, :], func=AF.Square,
                      accum_out=ss[s2][0][:, t:t + 1])
                G('tensor_scalar', [ss[s2][1]], [ss[s2][1]], out=ss[s2][0][:], in0=ss[s2][0][:], scalar1=1.0 / 1024,
                  scalar2=EPS, op0=ALU.mult, op1=ALU.add)
                G('tensor_tensor', [ss[s2][1], b_nh1], [ss[s2][1]], out=ss[s2][0][:], in0=ss[s2][0][:],
                  in1=bc(nh1[:], [128, 2]), op=ALU.pow)
                for t in range(2):
                    A('activation', [x_b, ss[s2][1]], [h2[s2][1]], out=h2[s2][0][:, t, :], in_=x_t[:, t, :], func=AF.Copy,
                      scale=ss[s2][0][:, t:t + 1])
                    for c in range(8):
                        T('transpose', [h2[s2][1], b_idb], [TR[1]], out=TR[0][:, c, :],
                          in_=h2[s2][0][:, t, c * 128:(c + 1) * 128], identity=idb[:])
                    V('tensor_copy', [TR[1]], [h2T[s2][1]], out=h2T[s2][0][:, :, t * 128:(t + 1) * 128], in_=TR[0][:])
                for f2 in range(16):
                    u_t, u_b = U[f2 % 2]
                    for ff in range(2):
                        fc = f2 * 2 + ff
                        for c in range(8):
                            T('matmul', [h2T[s2][1], b_wu], [u_b], out=u_t[:, ff, :], lhsT=wu[:, c, fc * 128:(fc + 1) * 128],
                              rhs=h2T[s2][0][:, c, :], start=(c == 0), stop=(c == 7))
                    r_t, r_b = rr[f2 % 3]
                    A('activation', [u_b], [r_b], out=r_t[:], in_=u_t[:], func=AF.Relu)
                    E('vector' if f2 % 2 == 0 else 'gpsimd', 'tensor_tensor', [r_b], [uT[s2][1]],
                      out=uT[s2][0][:, 2 * f2:2 * f2 + 2, :], in0=r_t[:], in1=r_t[:], op=ALU.mult)
                for t in range(2):
                    z_t, z_b = Z[t]
                    for hf in range(2):
                        for fc in range(32):
                            T('matmul', [uT[s2][1], b_wd], [z_b], out=z_t[:, hf * 512:(hf + 1) * 512],
                              lhsT=uT[s2][0][:, fc, t * 128:(t + 1) * 128], rhs=wd[:, fc, hf * 512:(hf + 1) * 512],
                              start=(fc == 0), stop=(fc == 31))
                    V('tensor_tensor', [z_b, x_b], [x_b], out=x_t[:, t, :], in0=z_t[:], in1=x_t[:, t, :], op=ALU.add)
                self.dma('sync', out_d[rows, :].rearrange("(t p) d -> p t d", p=128), x_t[:], [x_b], [b_out],
                         'x4bo%d' % s2)
        self.P.phase_barrier()

    def layer(self, L, x_own, x_oth, x_halo, x1_d, out_d, b_out, with_bias_setup=True, phases=None, pos_key='pos',
              valid_key='valid', halo_rows=None, reuse_ckv=False):
        ph = phases or ['bias', 'p1', 'C', 'B', 'A', 'A2', '4a', '4b']
        if with_bias_setup and 'bias' in ph:
            self.bias_setup()
        if 'p1' in ph:
            self.phase1(L, x_own, x_oth, x_halo, pos_key=pos_key, valid_key=valid_key, halo_rows=halo_rows,
                        skip_oth=reuse_ckv, skip_ck=reuse_ckv)
        if 'C' in ph:
            self.phaseC()
        if 'B' in ph:
            self.phaseB(L)
        if 'A' in ph:
            self.phaseA(cfgs=self.debug.get('cfgs', (0, 1, 2)) if self.debug else (0, 1, 2))
        if 'A2' in ph:
            self.phaseA2()
        if '4a' in ph:
            self.phase4a(L, x_own, x1_d)
        if '4b' in ph:
            self.phase4b(L, x1_d, out_d, b_out)


NLAYER = 2


def t5_bucket_np(rel):
    half, exact = 16, 8
    n = np.abs(rel)
    far = exact + (np.log(np.maximum(n, 1).astype(np.float32) / np.float32(exact))
                   / np.float32(math.log(1024 / exact)) * np.float32(half - exact)).astype(np.int32)
    far = np.minimum(far, half - 1)
    return np.where(rel > 0, half, 0) + np.where(n < exact, n, far)


def make_oh():
    oh = np.zeros((33, 1660), np.float32)
    d = np.arange(-255, 256)
    bk = t5_bucket_np(d)
    for i, dd in enumerate(d):
        if abs(dd) <= 128:
            oh[bk[i], i] = 1
        else:
            oh[32, i] = 1
    for ci, r in enumerate((1, 4, 16)):
        d = np.arange(-191, 192)
        bk = t5_bucket_np(d * r)
        for i, dd in enumerate(d):
            if abs(dd) <= 64:
                oh[bk[i], 511 + 383 * ci + i] = 1
            else:
                oh[32, 511 + 383 * ci + i] = 1
    return oh


WNAMES = ['norm_mix', 'w_in', 'qk_gain_a', 'qk_gain_b', 'sink_b', 'q_lat_gain', 'kv_lat_gain', 'w_uq', 'w_ukv',
          'qk_gain_c', 'out_norm', 'w_out', 'norm_mlp', 'w_up', 'w_down']


def declare(nc, NL):
    D = {}

    def inp(n, shape, dt=F32):
        D[n] = nc.dram_tensor(n, shape, dt, kind="ExternalInput")
    inp('x_own', [4096, 1024]); inp('x_oth', [4096, 1024]); inp('x_halo', [2048, 1024]); inp('x_halo2', [2048, 1024])
    inp('valid', [128, 16]); inp('valid2', [128, 16]); inp('pos', [128, 64], I32); inp('pos2', [128, 64], I32)
    inp('invf', [1, 16]); inp('idb', [128, 128], BF16); inp('idf', [128, 128])
    inp('norm_mix', [NL, 1024]); inp('w_in', [NL, 1024, 2080]); inp('qk_gain_a', [NL, 2, 64])
    inp('qk_gain_b', [NL, 2, 64]); inp('sink_b', [NL, 4]); inp('q_lat_gain', [NL, 256]); inp('kv_lat_gain', [NL, 128])
    inp('w_uq', [NL, 256, 576]); inp('w_ukv', [NL, 128, 768]); inp('qk_gain_c', [NL, 2, 96]); inp('out_norm', [NL, 1024])
    inp('w_out', [NL, 1024, 1024]); inp('norm_mlp', [NL, 1024]); inp('w_up', [NL, 1024, 4096]); inp('w_down', [NL, 4096, 1024])
    inp('rel_bias_table', [32, 10]); inp('oh', [33, 1660])
    SC = {}

    def scr(n, shape, dt=BF16):
        SC[n] = nc.dram_tensor(n, shape, dt, kind="Internal")
    scr('QA', [4096, 384]); scr('KAx', [6144, 384]); scr('VAx', [6144, 390]); scr('QB', [4096, 256])
    scr('KBx', [6144, 128]); scr('VBx', [6144, 130]); scr('QCT', [6, 96, 4096]); scr('KCT', [6, 96, 8192])
    scr('VC', [8192, 390]); scr('MIX', [4096, 1024]); scr('OA', [3, 4096, 390], F32); scr('FV', [10, 1660], F32)
    scr('X1', [4096, 1024], F32); scr('XA', [4096, 1024], F32); scr('XB', [4096, 1024], F32)
    SC['b'] = Buf('scratch', multi=True)
    SC['bmix'] = Buf('mix', multi=True)
    SC['boa'] = Buf('oa', multi=True)
    SC['bfv'] = Buf('fv', multi=True)
    SC['bx1'] = Buf('x1', multi=True)
    return D, SC


def make_inputs(inputs, core):
    b, half = core // 2, core % 2
    xs = np.asarray(inputs['x'], dtype=np.float32)[b]
    own = xs[half * 4096:(half + 1) * 4096]
    oth = xs[(1 - half) * 4096:(2 - half) * 4096]
    halo = np.zeros((2048, 1024), np.float32)
    valid = np.zeros((2048,), np.float32)
    halo2 = np.zeros((2048, 1024), np.float32)
    valid2 = np.zeros((2048,), np.float32)
    if half == 1:
        halo[0:1024] = oth[3072:4096]
        valid[0:1024] = 1
        halo2[1024:2048] = own[0:1024]
        valid2[1024:2048] = 1
    else:
        halo[1024:2048] = oth[0:1024]
        valid[1024:2048] = 1
        halo2[0:1024] = own[3072:4096]
        valid2[0:1024] = 1
    pos = np.asarray(inputs['positions'][b])
    p_own = pos[half * 4096:(half + 1) * 4096]
    p_oth = pos[(1 - half) * 4096:(2 - half) * 4096]
    pos_l = np.concatenate([p_own, p_oth])
    pos_l2 = np.concatenate([p_oth, p_own])
    m = {
        'x_own': np.ascontiguousarray(own), 'x_oth': np.ascontiguousarray(oth), 'x_halo': halo, 'x_halo2': halo2,
        'valid': np.ascontiguousarray(valid.reshape(16, 128).T),
        'valid2': np.ascontiguousarray(valid2.reshape(16, 128).T),
        'pos': np.ascontiguousarray(pos_l.reshape(64, 128).T.astype(np.int32)),
        'pos2': np.ascontiguousarray(pos_l2.reshape(64, 128).T.astype(np.int32)),
        'invf': (10000.0 ** (-np.arange(16, dtype=np.float32) / 16)).astype(np.float32).reshape(1, 16),
        'idb': np.eye(128).astype(ml_dtypes.bfloat16), 'idf': np.eye(128).astype(np.float32),
        'rel_bias_table': np.ascontiguousarray(inputs['rel_bias_table'], dtype=np.float32), 'oh': make_oh(),
    }
    for k in WNAMES:
        m[k] = np.ascontiguousarray(np.asarray(inputs[k], dtype=np.float32))
    return m


def build_program():
    nc = bass.Bass("TRN2", target_bir_lowering=False)
    D, SC = declare(nc, NLAYER)
    out_d = nc.dram_tensor('out', [4096, 1024], F32, kind="ExternalOutput")
    b_xa = Buf('xa', multi=True)
    b_xb = Buf('xb', multi=True)
    b_out = Buf('out', multi=True)
    with ExitStack() as es:
        P = Prog(nc, es)
        LB = LayerBuilder(nc, P, D, debug={})
        LB.SC = SC
        XA, XB = SC['XA'].ap(), SC['XB'].ap()
        LB.layer(0, D['x_own'].ap(), D['x_oth'].ap(), D['x_halo'].ap(), SC['X1'].ap(), XA, b_xa)
        LB.layer(0, D['x_oth'].ap(), D['x_own'].ap(), D['x_halo2'].ap(), SC['X1'].ap(), XB, b_xb,
                 with_bias_setup=False, pos_key='pos2', valid_key='valid2', reuse_ckv=True)

        def halo_rows(t):
            r0 = 3072 + t * 128 if t < 8 else (t - 8) * 128
            return XB[r0:r0 + 128, :]
        P.barrier('sync', reads=[b_xa, b_xb])
        LB.layer(1, XA, XB, None, SC['X1'].ap(), out_d.ap(), b_out, with_bias_setup=False, halo_rows=halo_rows)
        P.barrier('sync', reads=[b_out])
        P.emit()
    return nc


def kernel(**inputs):
    nc = build_program()
    in_maps = [make_inputs(inputs, c) for c in range(8)]
    res = run_bass_kernel_spmd(nc, in_maps, core_ids=list(range(8)))
    x = np.asarray(inputs['x'])
    out = np.empty(x.shape, np.float32)
    for c in range(8):
        b, half = c // 2, c % 2
        out[b, half * 4096:(half + 1) * 4096] = np.asarray(res.results[c]['out'], dtype=np.float32)
    return out
```

```python
import math
import ml_dtypes
from concourse.bass_utils import run_bass_kernel_spmd
import numpy as np
import concourse.bass as bass
import concourse.mybir as mybir
from contextlib import ExitStack

F32 = mybir.dt.float32
BF16 = mybir.dt.bfloat16
I32 = mybir.dt.int32
ALU = mybir.AluOpType
AF = mybir.ActivationFunctionType
AX = mybir.AxisListType

ENGS = ['sync', 'scalar', 'vector', 'gpsimd', 'tensor']
SEM_ROT = 24000


class Buf:
    __slots__ = ('name', 'writer', 'readers', 'dreaders', 'multi', 'mw')

    def __init__(self, name, multi=False):
        self.name = name
        self.writer = None
        self.readers = {}
        self.dreaders = []
        self.multi = multi
        self.mw = []


class Op:
    __slots__ = ('eng', 'fn', 'deps', 'idx', 'signal', 'is_dma', 'lane', 'ev', 'raw', 'barrier')


class Prog:
    def __init__(self, nc, es):
        self.nc = nc
        self.es = es
        self.ops = {e: [] for e in ENGS}
        self.order = []
        self.lanes = {}
        self.nsem = 0
        self.fence = []
        self.fence_pending = set()
        self.phase_lanes = {}

    def phase_barrier(self):
        fence = []
        for e in ENGS:
            for o in reversed(self.ops[e]):
                if not o.is_dma and not o.barrier:
                    fence.append(o)
                    break
        last = {}
        for o in self.order:
            if o.is_dma:
                last[o.lane] = o
        fence += list(last.values())
        self.fence = fence
        self.fence_pending = set(ENGS)
        self.phase_lanes = {}

    def new_sem(self, name):
        self.nsem += 1
        return self.es.enter_context(self.nc.semaphore(name))

    def sb(self, name, shape, dt):
        return self.es.enter_context(self.nc.sbuf_tensor(name, shape, dt))

    def ps(self, name, shape, dt):
        return self.es.enter_context(self.nc.psum_tensor(name, shape, dt))

    def barrier(self, eng, reads=(), writes=()):
        o = self.op(eng, lambda e: None, reads, writes)
        o.barrier = True
        return o

    def op(self, eng, fn, reads=(), writes=(), lane=None):
        o = Op()
        o.eng = eng
        o.fn = fn
        o.barrier = False
        o.is_dma = lane is not None
        if lane is not None:
            if lane not in self.phase_lanes:
                self.phase_lanes[lane] = 'L%d' % len(self.phase_lanes)
            lane = self.phase_lanes[lane]
        o.lane = lane
        o.signal = False
        o.ev = None
        deps = {}
        raw = set()
        for b in reads:
            if b.writer is not None:
                deps[id(b.writer)] = b.writer
                raw.add(id(b.writer))
            for w in b.mw:
                deps[id(w)] = w
                raw.add(id(w))
        for b in writes:
            if b.writer is not None and not b.multi:
                deps[id(b.writer)] = b.writer
                raw.add(id(b.writer))
            for r in b.readers.values():
                deps[id(r)] = r
            for r in b.dreaders:
                deps[id(r)] = r
        if eng in self.fence_pending:
            self.fence_pending.discard(eng)
            for w in self.fence:
                deps[id(w)] = w
        deps.pop(id(o), None)
        o.deps = list(deps.values())
        o.raw = raw
        for b in writes:
            if b.multi:
                b.mw.append(o)
            else:
                b.writer = o
            b.readers = {}
            b.dreaders = []
        for b in reads:
            if b.multi:
                continue
            if o.is_dma:
                b.dreaders.append(o)
            else:
                b.readers[eng] = o
        o.idx = len(self.ops[eng])
        self.ops[eng].append(o)
        self.order.append(o)
        return o

    def dma(self, q, out, in_, reads=(), writes=(), lane=None, **kw):
        assert lane is not None
        return self.op(q, lambda e: e.dma_start(out=out, in_=in_, **kw), reads, writes, lane=lane)

    def emit(self):
        nc = self.nc
        for o in self.order:
            for d in o.deps:
                if d.is_dma:
                    continue
                if d.barrier:
                    assert d.eng == o.eng, 'barrier dep across engines'
                    continue
                if d.eng == o.eng and not o.is_dma:
                    if o.eng == 'tensor':
                        continue
                    if id(d) not in o.raw:
                        continue
                d.signal = True
        esems = {}
        for e in ENGS:
            cnt = 0
            cur = None
            for o in self.ops[e]:
                if o.is_dma:
                    ln = self.lanes.get(o.lane)
                    if ln is None:
                        ln = [self.new_sem('l_%s' % o.lane), 0]
                        self.lanes[o.lane] = ln
                    ln[1] += 16
                    o.ev = (ln[0], ln[1])
                elif o.signal:
                    if cur is None or cnt >= SEM_ROT:
                        cur = self.new_sem('e_%s_%d' % (e, len(esems)))
                        esems[(e, len(esems))] = cur
                        cnt = 0
                    cnt += 1
                    o.ev = (cur, cnt)
        blk = self.es.enter_context(nc.Block())
        prog = self

        def run(e, eng):
            waited = {}
            for o in prog.ops[e]:
                need = {}
                for d in o.deps:
                    if not d.is_dma:
                        if d.barrier:
                            continue
                        if d.eng == o.eng and not o.is_dma:
                            if o.eng == 'tensor' or id(d) not in o.raw:
                                continue
                    sem, val = d.ev
                    k = id(sem)
                    if k not in need or need[k][1] < val:
                        need[k] = (sem, val)
                for k, (sem, val) in need.items():
                    if waited.get(k, 0) >= val:
                        continue
                    eng.wait_ge(sem, val)
                    waited[k] = val
                ins = o.fn(eng)
                if ins is None:
                    continue
                if o.is_dma:
                    ins.then_inc(o.ev[0], 16)
                elif o.signal:
                    ins.then_inc(o.ev[0], 1)

        @blk.sync
        def _(eng):
            run('sync', eng)

        @blk.scalar
        def _(eng):
            run('scalar', eng)

        @blk.vector
        def _(eng):
            run('vector', eng)

        @blk.gpsimd
        def _(eng):
            run('gpsimd', eng)

        @blk.tensor
        def _(eng):
            run('tensor', eng)

import numpy as np
import math

EPS = 1e-6
NEGB = -30000.0
TWO_PI_S = 6.2831845


def bc(ap, shape):
    return ap.to_broadcast(list(shape))


def psum_view(ph, nc, name, shape, dt):
    esz = 4 if dt == F32 else 2
    n = 1
    for d in shape[1:]:
        n *= d
    per_bank = 2048 // esz
    tot = ((n + per_bank - 1) // per_bank) * per_bank
    t = ph.enter_context(nc.psum_tensor(name, [128, tot], dt))
    v = t[0:shape[0], 0:n]
    if len(shape) == 3:
        v = v.rearrange("p (a b) -> p a b", a=shape[1])
    elif len(shape) == 4:
        v = v.rearrange("p (a b c) -> p a b c", a=shape[1], b=shape[2])
    return v


class LayerBuilder:
    def __init__(self, nc, P, D, debug=False):
        self.nc = nc
        self.P = P
        self.D = D
        self.debug = debug
        self.uid = 0

    def name(self, s):
        self.uid += 1
        return "%s_%d" % (s, self.uid)

    def V(self, fn, reads, writes, **kw):
        return self.P.op('vector', lambda e: getattr(e, fn)(**kw), reads, writes)

    def G(self, fn, reads, writes, **kw):
        return self.P.op('gpsimd', lambda e: getattr(e, fn)(**kw), reads, writes)

    def A(self, fn, reads, writes, **kw):
        return self.P.op('scalar', lambda e: getattr(e, fn)(**kw), reads, writes)

    def T(self, fn, reads, writes, **kw):
        return self.P.op('tensor', lambda e: getattr(e, fn)(**kw), reads, writes)

    def E(self, eng, fn, reads, writes, **kw):
        return self.P.op(eng, lambda e: getattr(e, fn)(**kw), reads, writes)

    def dma(self, q, out, in_, reads, writes, lane, **kw):
        return self.P.dma(q, out, in_, reads=reads, writes=writes, lane=lane, **kw)

    def phase1(self, L, x_own, x_oth, x_halo, first=True, pos_key='pos', valid_key='valid', halo_rows=None,
               skip_oth=False, skip_ck=False):
        nc, P, D = self.nc, self.P, self.D
        V, G, A, T, E = self.V, self.G, self.A, self.T, self.E
        NS = 4
        with ExitStack() as ph:
            def sb(nm, shape, dt, n=1):
                r = []
                for i in range(n):
                    t = ph.enter_context(nc.sbuf_tensor(self.name(nm), shape, dt))
                    r.append((t, Buf(nm + str(i))))
                return r if n > 1 else r[0]

            def ps(nm, shape, dt):
                t = psum_view(ph, nc, self.name(nm), shape, dt)
                return (t, Buf(nm))

            setup = ExitStack()

            def sbs(nm, shape, dt):
                t = setup.enter_context(nc.sbuf_tensor(self.name(nm), shape, dt))
                return (t, Buf(nm))

            idb, b_idb = sb("idb", [128, 128], BF16)
            wib, b_wib = sb("wib", [128, 8, 2080], BF16)
            wuq, b_wuq = sb("wuq", [128, 2, 576], BF16)
            wukv, b_wukv = sb("wukv", [128, 768], BF16)
            g8, b_g8 = sb("g8", [128, 8], F32)
            gq2, b_gq2 = sb("gq2", [128, 2], F32)
            gkv1, b_gkv1 = sb("gkv1", [128, 1], F32)
            ga, b_ga = sb("ga", [128, 2, 64], F32)
            gb, b_gb = sb("gb", [128, 2, 64], F32)
            gc, b_gc = sb("gc", [128, 2, 96], F32)
            GAq, b_GAq = sb("GAq", [128, 64], F32)
            GBq, b_GBq = sb("GBq", [128, 64], F32)
            GCq, b_GCq = sb("GCq", [128, 96], F32)
            invf, b_invf = sb("invf", [128, 16], F32)
            sin_t, b_sin = sb("sin_t", [128, 64, 16], F32)
            cos_t, b_cos = sb("cos_t", [128, 64, 16], F32)
            invd, b_invd = sb("invd", [128, 24], F32)
            nh24, b_nh24 = sb("nh24", [128, 24], F32)
            valid, b_valid = sb("valid", [128, 16], F32)
            posi, b_posi = sbs("posi", [128, 64], I32)
            posf, b_posf = sbs("posf", [128, 64], F32)
            ang, b_ang = sbs("ang", [128, 64, 16], F32)
            angk, b_angk = sbs("angk", [128, 64, 16], I32)
            angf, b_angf = sbs("angf", [128, 64, 16], F32)
            stage = [sbs("stage", [128, 2080], F32)] * 2
            stq, b_stq = sbs("stq", [128, 2, 576], F32)
            stkv, b_stkv = sbs("stkv", [128, 768], F32)

            self.dma('sync', idb[:], D['idb'].ap(), [], [b_idb], 'idb')
            self.dma('sync', g8[:], D['norm_mix'].ap()[L].rearrange("(c p) -> p c", p=128), [], [b_g8], 'g8',
                     allow_slow_non_contiguous=True)
            self.dma('sync', gq2[:], D['q_lat_gain'].ap()[L].rearrange("(c p) -> p c", p=128), [], [b_gq2], 'gq2',
                     allow_slow_non_contiguous=True)
            self.dma('sync', gkv1[:], D['kv_lat_gain'].ap()[L].rearrange("(c p) -> p c", p=128), [], [b_gkv1],
                     'gkv1', allow_slow_non_contiguous=True)
            self.dma('sync', ga[:], D['qk_gain_a'].ap()[L].rearrange("a d -> (a d)").partition_broadcast(128),
                     [], [b_ga], 'ga')
            self.dma('sync', gb[:], D['qk_gain_b'].ap()[L].rearrange("a d -> (a d)").partition_broadcast(128),
                     [], [b_gb], 'gb')
            self.dma('sync', gc[:], D['qk_gain_c'].ap()[L].rearrange("a d -> (a d)").partition_broadcast(128),
                     [], [b_gc], 'gc')
            self.dma('sync', invf[:], D['invf'].ap().rearrange("a d -> (a d)").partition_broadcast(128),
                     [], [b_invf], 'invf')
            self.dma('sync', posi[:], D[pos_key].ap(), [], [b_posi], 'posi')
            self.dma('sync', valid[:], D[valid_key].ap(), [], [b_valid], 'valid')
            V('scalar_tensor_tensor', [b_ga], [b_GAq], out=GAq[:], in0=ga[:, 0, :], scalar=0.125, in1=ga[:, 1, :],
              op0=ALU.mult, op1=ALU.mult)
            V('scalar_tensor_tensor', [b_gb], [b_GBq], out=GBq[:], in0=gb[:, 0, :], scalar=0.125, in1=gb[:, 1, :],
              op0=ALU.mult, op1=ALU.mult)
            V('tensor_scalar', [b_gc], [b_GCq], out=GCq[:], in0=gc[:, 0, :], scalar1=96.0 ** -0.5, scalar2=None,
              op0=ALU.mult)
            GCk = gc[:, 1, :]
            b_GCk = b_gc
            self.P.op('gpsimd', lambda e: e.memset(invd[:], 1.0 / 64), [], [b_invd])
            self.P.op('gpsimd', lambda e: e.memset(invd[:, 18:19], 1.0 / 256), [], [b_invd])
            self.P.op('gpsimd', lambda e: e.memset(invd[:, 19:20], 1.0 / 128), [], [b_invd])
            self.P.op('gpsimd', lambda e: e.memset(invd[:, 20:24], 1.0), [], [b_invd])
            self.P.op('gpsimd', lambda e: e.memset(nh24[:], -0.5), [], [b_nh24])

            V('tensor_copy', [b_posi], [b_posf], out=posf[:], in_=posi[:])
            V('tensor_tensor', [b_posf, b_invf], [b_ang], out=ang[:],
              in0=bc(posf[:].unsqueeze(2), [128, 64, 16]), in1=bc(invf[:].unsqueeze(1), [128, 64, 16]), op=ALU.mult)
            for (tab, b_tab, off) in ((sin_t, b_sin, 0.0), (cos_t, b_cos, 0.25)):
                V('tensor_scalar', [b_ang], [b_angf], out=angf[:], in0=ang[:], scalar1=1.0 / (2 * math.pi),
                  scalar2=off, op0=ALU.mult, op1=ALU.add)
                V('tensor_copy', [b_angf], [b_angk], out=angk[:], in_=angf[:])
                V('tensor_copy', [b_angk], [b_tab], out=tab[:], in_=angk[:])
                V('tensor_tensor', [b_angf, b_tab], [b_angf], out=angf[:], in0=angf[:], in1=tab[:], op=ALU.subtract)
                A('activation', [b_angf], [b_tab], out=tab[:], in_=angf[:], func=AF.Sin, scale=TWO_PI_S)

            blocks = [(0, 384, 0), (384, 768, 512), (768, 1152, 1024), (1152, 1408, 1536), (1408, 1536, 896),
                      (1536, 1664, 1408), (1664, 1920, 1792), (1920, 2048, 384), (2048, 2080, 2048)]
            k = 0
            for c in range(8):
                st_t, st_b = stage[c % 2]
                self.dma('sync', st_t[:], D['w_in'].ap()[L, c * 128:(c + 1) * 128, :], [], [st_b], 'stage0')
                for (o0, o1, n0) in blocks:
                    k += 1
                    if k % 2 == 0:
                        V('tensor_scalar', [st_b, b_g8], [b_wib], out=wib[:, c, n0:n0 + (o1 - o0)],
                          in0=st_t[:, o0:o1], scalar1=g8[:, c:c + 1], scalar2=None, op0=ALU.mult)
                    else:
                        A('activation', [st_b, b_g8], [b_wib], out=wib[:, c, n0:n0 + (o1 - o0)],
                          in_=st_t[:, o0:o1], func=AF.Copy, scale=g8[:, c:c + 1])
            self.dma('sync', stq[:], D['w_uq'].ap()[L].rearrange("(c p) n -> p c n", p=128), [], [b_stq], 'stq')
            self.dma('sync', stkv[:], D['w_ukv'].ap()[L], [], [b_stkv], 'stkv')
            for c in range(2):
                V('tensor_scalar', [b_stq, b_gq2], [b_wuq], out=wuq[:, c, :], in0=stq[:, c, :],
                  scalar1=gq2[:, c:c + 1], scalar2=None, op0=ALU.mult)
            V('tensor_scalar', [b_stkv, b_gkv1], [b_wukv], out=wukv[:], in0=stkv[:], scalar1=gkv1[:, 0:1],
              scalar2=None, op0=ALU.mult)

            self.P.phase_barrier()
            setup.close()
            xt = sb("xt", [128, 1024], F32, NS)
            junk, b_junk = sb("junk", [128, 1024], BF16)
            ssx = sb("ssx", [128, 1], F32, NS)
            rsx = sb("rsx", [128, 1], F32, NS)
            hb = sb("hb", [128, 1024], BF16, NS)
            hT = sb("hT", [128, 8, 128], BF16, NS)
            pj = sb("pj", [128, 2080], F32, NS)
            sq, b_sq = sb("sq", [128, 2080], F32)
            st = sb("st", [128, 24], F32, NS)
            rstd = sb("rstd", [128, 24], F32, NS)
            QAo = sb("QAo", [128, 384], BF16, NS)
            QAt = sb("QAt", [128, 384], F32, 1)
            KABo = sb("KABo", [128, 512], BF16, NS)
            QBo = sb("QBo", [128, 256], BF16, NS)
            QBt = sb("QBt", [128, 256], F32, 1)
            VABo = sb("VABo", [128, 8, 65], BF16, NS)
            LAT = sb("LAT", [128, 384], BF16, NS)
            latT = sb("latT", [128, 3, 128], BF16, NS)
            qcs = sb("qcs", [128, 576], F32, NS)
            kvcs = sb("kvcs", [128, 768], F32, NS)
            st2 = sb("st2", [128, 12], F32, NS)
            rstd2 = sb("rstd2", [128, 12], F32, NS)
            tmp1, b_tmp1 = sb("tmp1", [128, 6, 96], F32)
            trq, b_trq = sb("trq", [128, 6, 32], F32)
            tmpk, b_tmpk = sb("tmpk", [128, 6, 64], F32)
            krg, b_krg = sb("krg", [128, 1, 32], F32)
            krr, b_krr = sb("krr", [128, 1, 32], F32)
            rm = [sb("rm%d" % i, [128, 6, 16], F32) for i in range(4)]
            QCo = sb("QCo", [128, 6, 96], BF16, NS)
            KCo = sb("KCo", [128, 6, 96], BF16, NS)
            VCo = sb("VCo", [128, 6, 65], BF16, NS)
            QTs = sb("QTs", [96, 6, 128], BF16, NS)
            KTs = sb("KTs", [96, 6, 128], BF16, NS)
            TR, b_TR = ps("TR", [128, 8, 128], BF16)
            PJ = [ps("PJ%d" % i, [128, 512], F32) for i in range(5)]
            S = [ps("S%d" % i, [128, 512], F32) for i in range(2)]

            for (t_, b_) in VABo:
                self.P.op('gpsimd', lambda e, t_=t_: e.memset(t_[:], 1.0), [], [b_])
            for (t_, b_) in VCo:
                self.P.op('gpsimd', lambda e, t_=t_: e.memset(t_[:], 1.0), [], [b_])

            def rope(src, b_src, dst, b_dst, H, ti):
                cb = bc(cos_t[:, ti, :].unsqueeze(1), [128, H, 16])
                sbb = bc(sin_t[:, ti, :].unsqueeze(1), [128, H, 16])
                (m1, b1), (m2, b2), (m3, b3), (m4, b4) = rm
                V('tensor_tensor', [b_src, b_cos], [b1], out=m1[:, 0:H, :], in0=src[:, :, 0:16], in1=cb, op=ALU.mult)
                V('tensor_tensor', [b_src, b_sin], [b2], out=m2[:, 0:H, :], in0=src[:, :, 16:32], in1=sbb, op=ALU.mult)
                V('tensor_tensor', [b1, b2], [b_dst], out=dst[:, :, 0:16], in0=m1[:, 0:H, :], in1=m2[:, 0:H, :],
                  op=ALU.subtract)
                V('tensor_tensor', [b_src, b_cos], [b3], out=m3[:, 0:H, :], in0=src[:, :, 16:32], in1=cb, op=ALU.mult)
                V('tensor_tensor', [b_src, b_sin], [b4], out=m4[:, 0:H, :], in0=src[:, :, 0:16], in1=sbb, op=ALU.mult)
                V('tensor_tensor', [b3, b4], [b_dst], out=dst[:, :, 16:32], in0=m3[:, 0:H, :], in1=m4[:, 0:H, :],
                  op=ALU.add)

            SC = self.SC
            it = 0
            jobs = [('own', t) for t in range(32)] + [('oth', t) for t in range(32)] + [('halo', t) for t in range(16)]
            if skip_oth:
                jobs = [j for j in jobs if j[0] != 'oth']
            if self.debug and self.debug.get('p1_tiles'):
                jobs = self.debug['p1_tiles']
            def tile_gen(it, kind, t):
                s2 = it % NS
                s3 = it % NS
                src = {'own': x_own, 'oth': x_oth, 'halo': x_halo}[kind]
                x_t, b_x = xt[s3]
                ss_t, b_ss = ssx[s2]
                rs_t, b_rs = rsx[s2]
                hb_t, b_hb = hb[s2]
                hT_t, b_hT = hT[s2]
                pj_t, b_pj = pj[s3]
                st_t, b_st = st[s3]
                rstd_t, b_rstd = rstd[s3]
                if kind == 'halo':
                    V('tensor_scalar', [b_x, b_valid], [b_x], out=x_t[:], in0=x_t[:], scalar1=valid[:, t:t + 1],
                      scalar2=None, op0=ALU.mult)
                    yield
                A('activation', [b_x], [b_junk, b_ss], out=junk[:], in_=x_t[:], func=AF.Square, accum_out=ss_t[:])
                yield
                G('tensor_scalar', [b_ss], [b_ss], out=ss_t[:], in0=ss_t[:], scalar1=1.0 / 1024, scalar2=EPS,
                  op0=ALU.mult, op1=ALU.add)
                yield
                G('tensor_tensor', [b_ss, b_nh24], [b_rs], out=rs_t[:], in0=ss_t[:], in1=nh24[:, 0:1], op=ALU.pow)
                yield
                A('activation', [b_x, b_rs], [b_hb], out=hb_t[:], in_=x_t[:], func=AF.Copy, scale=rs_t[:, 0:1])
                yield
                for c in range(8):
                    T('transpose', [b_hb, b_idb], [b_TR], out=TR[:, c, :], in_=hb_t[:, c * 128:(c + 1) * 128],
                      identity=idb[:])
                V('tensor_copy', [b_TR], [b_hT], out=hT_t[:], in_=TR[:])
                yield
                if kind == 'own':
                    groups = [(0, 0, 512, 0), (1, 512, 1024, 0), (2, 1024, 1536, 0), (3, 1536, 2048, 0),
                              (4, 2048, 2080, 0)]
                elif kind == 'oth':
                    groups = [(0, 384, 512, 384), (4, 2048, 2080, 0)]
                else:
                    groups = [(1, 512, 1024, 0), (2, 1024, 1536, 0)]
                for (bk, c0, c1, po) in groups:
                    pt, pb = PJ[bk]
                    for c in range(8):
                        T('matmul', [b_hT, b_wib], [pb], out=pt[:, po:po + (c1 - c0)], lhsT=hT_t[:, c, :],
                          rhs=wib[:, c, c0:c1], start=(c == 0), stop=(c == 7))
                    A('activation', [pb], [b_pj], out=pj_t[:, c0:c1], in_=pt[:, po:po + (c1 - c0)], func=AF.Copy)
                    yield
                if kind == 'own':
                    V('tensor_tensor', [b_pj], [b_sq], out=sq[:, 0:1024], in0=pj_t[:, 0:1024], in1=pj_t[:, 0:1024],
                      op=ALU.mult)
                    V('tensor_tensor', [b_pj], [b_sq], out=sq[:, 1536:2080], in0=pj_t[:, 1536:2080],
                      in1=pj_t[:, 1536:2080], op=ALU.mult)
                    red = [(0, 6, 0, 384, 64), (6, 14, 512, 1024, 64), (14, 18, 1536, 1792, 64),
                           (18, 19, 1792, 2048, 256), (19, 20, 384, 512, 128), (20, 21, 2048, 2080, 32)]
                elif kind == 'oth':
                    V('tensor_tensor', [b_pj], [b_sq], out=sq[:, 384:512], in0=pj_t[:, 384:512], in1=pj_t[:, 384:512],
                      op=ALU.mult)
                    V('tensor_tensor', [b_pj], [b_sq], out=sq[:, 2048:2080], in0=pj_t[:, 2048:2080],
                      in1=pj_t[:, 2048:2080], op=ALU.mult)
                    red = [(19, 20, 384, 512, 128), (20, 21, 2048, 2080, 32)]
                else:
                    V('tensor_tensor', [b_pj], [b_sq], out=sq[:, 512:1024], in0=pj_t[:, 512:1024],
                      in1=pj_t[:, 512:1024], op=ALU.mult)
                    red = [(6, 14, 512, 1024, 64)]
                for (a0, a1, c0, c1, dd) in red:
                    V('tensor_reduce', [b_sq], [b_st], out=st_t[:, a0:a1],
                      in_=sq[:, c0:c1].rearrange("p (h d) -> p h d", d=dd), axis=AX.X, op=ALU.add)
                V('tensor_tensor', [b_st, b_invd], [b_rstd], out=rstd_t[:, 0:20], in0=st_t[:, 0:20], in1=invd[:, 0:20],
                  op=ALU.mult)
                yield
                V('tensor_scalar', [b_rstd], [b_rstd], out=rstd_t[:, 0:20], in0=rstd_t[:, 0:20], scalar1=EPS,
                  scalar2=None, op0=ALU.add)
                yield
                G('tensor_tensor', [b_rstd, b_nh24], [b_rstd], out=rstd_t[:, 0:20], in0=rstd_t[:, 0:20],
                  in1=nh24[:, 0:20], op=ALU.pow)
                yield
                if kind in ('own', 'halo'):
                    et = (8 + t) if kind == 'own' else (t if t < 8 else 40 + (t - 8))
                    kab_t, b_kab = KABo[s2]
                    vab_t, b_vab = VABo[s2]
                    V('tensor_tensor', [b_pj, b_rstd], [b_kab], out=kab_t[:].rearrange("p (h d) -> p h d", d=64),
                      in0=pj_t[:, 512:1024].rearrange("p (h d) -> p h d", d=64),
                      in1=bc(rstd_t[:, 6:14].unsqueeze(2), [128, 8, 64]), op=ALU.mult)
                    yield
                    V('tensor_copy', [b_pj], [b_vab], out=vab_t[:, :, 0:64],
                      in_=pj_t[:, 1024:1536].rearrange("p (h d) -> p h d", d=64))
                    yield
                    if kind == 'halo':
                        V('tensor_copy', [b_valid], [b_vab], out=vab_t[:, :, 64:65],
                          in_=bc(valid[:, t:t + 1].unsqueeze(1), [128, 8, 1]))
                        yield
                    else:
                        self.P.op('vector', lambda e, vab_t=vab_t: e.memset(vab_t[:, :, 64:65], 1.0), [], [b_vab])
                        yield
                    rows = slice(et * 128, (et + 1) * 128)
                    self.dma('sync', SC['KAx'].ap()[rows, :], kab_t[:, 0:384], [b_kab], [SC['b']], 'kabo%d' % s2)
                    yield
                    self.dma('sync', SC['KBx'].ap()[rows, :], kab_t[:, 384:512], [b_kab], [SC['b']], 'kabo%d' % s2)
                    yield
                    self.dma('sync', SC['VAx'].ap()[rows, :].rearrange("p (h d) -> p h d", d=65), vab_t[:, 0:6, :],
                             [b_vab], [SC['b']], 'vabo%d' % s2)
                    yield
                    self.dma('sync', SC['VBx'].ap()[rows, :].rearrange("p (h d) -> p h d", d=65), vab_t[:, 6:8, :],
                             [b_vab], [SC['b']], 'vabo%d' % s2)
                    yield
                if kind == 'own':
                    rows = slice(t * 128, (t + 1) * 128)
                    qa_t, b_qa = QAo[s2]
                    qat, b_qat = QAt
                    V('tensor_tensor', [b_pj, b_rstd], [b_qat], out=qat[:].rearrange("p (h d) -> p h d", d=64),
                      in0=pj_t[:, 0:384].rearrange("p (h d) -> p h d", d=64),
                      in1=bc(rstd_t[:, 0:6].unsqueeze(2), [128, 6, 64]), op=ALU.mult)
                    V('tensor_tensor', [b_qat, b_GAq], [b_qa], out=qa_t[:].rearrange("p (h d) -> p h d", d=64),
                      in0=qat[:].rearrange("p (h d) -> p h d", d=64),
                      in1=bc(GAq[:].unsqueeze(1), [128, 6, 64]), op=ALU.mult)
                    self.dma('sync', SC['QA'].ap()[rows, :], qa_t[:], [b_qa], [SC['b']], 'qao%d' % s2)
                    yield
                    qb_t, b_qb = QBo[s2]
                    qbt, b_qbt = QBt
                    V('tensor_tensor', [b_pj, b_rstd], [b_qbt], out=qbt[:].rearrange("p (h d) -> p h d", d=64),
                      in0=pj_t[:, 1536:1792].rearrange("p (h d) -> p h d", d=64),
                      in1=bc(rstd_t[:, 14:18].unsqueeze(2), [128, 4, 64]), op=ALU.mult)
                    V('tensor_tensor', [b_qbt, b_GBq], [b_qb],
                      out=qb_t[:].rearrange("p (b a d) -> p a b d", b=2, a=2, d=64),
                      in0=qbt[:].rearrange("p (a b d) -> p a b d", a=2, b=2, d=64),
                      in1=bc(GBq[:].unsqueeze(1).unsqueeze(1), [128, 2, 2, 64]), op=ALU.mult)
                    self.dma('sync', SC['QB'].ap()[rows, :], qb_t[:], [b_qb], [SC['b']], 'qbo%d' % s2)
                    yield
                if kind in ('own', 'oth'):
                    ti = t if kind == 'own' else 32 + t
                    lat_t, b_lat = LAT[s2]
                    latT_t, b_latT = latT[s2]
                    qcs_t, b_qcs = qcs[s2]
                    kvcs_t, b_kvcs = kvcs[s2]
                    st2_t, b_st2 = st2[s2]
                    rstd2_t, b_rstd2 = rstd2[s2]
                    if kind == 'own':
                        V('tensor_scalar', [b_pj, b_rstd], [b_lat], out=lat_t[:, 0:256], in0=pj_t[:, 1792:2048],
                          scalar1=rstd_t[:, 18:19], scalar2=None, op0=ALU.mult)
                        yield
                    if not skip_ck:
                        V('tensor_scalar', [b_pj, b_rstd], [b_lat], out=lat_t[:, 256:384], in0=pj_t[:, 384:512],
                          scalar1=rstd_t[:, 19:20], scalar2=None, op0=ALU.mult)
                        yield
                    jl = ([0, 1] if skip_ck else [0, 1, 2]) if kind == 'own' else [2]
                    for j in jl:
                        T('transpose', [b_lat, b_idb], [b_TR], out=TR[:, j, :], in_=lat_t[:, j * 128:(j + 1) * 128],
                          identity=idb[:])
                    V('tensor_copy', [b_TR], [b_latT], out=latT_t[:, jl[0]:jl[-1] + 1, :], in_=TR[:, jl[0]:jl[-1] + 1, :])
                    if kind == 'own':
                        for hf in range(2):
                            for c in range(2):
                                T('matmul', [b_latT, b_wuq], [S[hf][1]], out=S[hf][0][:, 0:288], lhsT=latT_t[:, c, :],
                                  rhs=wuq[:, c, hf * 288:(hf + 1) * 288], start=(c == 0), stop=(c == 1))
                            A('activation', [S[hf][1]], [b_qcs], out=qcs_t[:, hf * 288:(hf + 1) * 288],
                              in_=S[hf][0][:, 0:288], func=AF.Copy)
                    for hf in (range(2) if not skip_ck else []):
                        T('matmul', [b_latT, b_wukv], [S[hf][1]], out=S[hf][0][:, 0:384], lhsT=latT_t[:, 2, :],
                          rhs=wukv[:, hf * 384:(hf + 1) * 384], start=True, stop=True)
                        A('activation', [S[hf][1]], [b_kvcs], out=kvcs_t[:, hf * 384:(hf + 1) * 384],
                          in_=S[hf][0][:, 0:384], func=AF.Copy)
                    kv3 = kvcs_t[:].rearrange("p (h d) -> p h d", d=128)
                    if kind == 'own':
                        V('tensor_tensor', [b_qcs], [b_sq], out=sq[:, 0:576], in0=qcs_t[:], in1=qcs_t[:], op=ALU.mult)
                        V('tensor_reduce', [b_sq], [b_st2], out=st2_t[:, 0:6],
                          in_=sq[:, 0:576].rearrange("p (h d) -> p h d", d=96), axis=AX.X, op=ALU.add)
                    if not skip_ck:
                        V('tensor_tensor', [b_kvcs], [b_sq], out=sq[:, 1024:1408].rearrange("p (h d) -> p h d", d=64),
                          in0=kv3[:, :, 0:64], in1=kv3[:, :, 0:64], op=ALU.mult)
                        V('tensor_reduce', [b_sq], [b_st2], out=st2_t[:, 6:12],
                          in_=sq[:, 1024:1408].rearrange("p (h d) -> p h d", d=64), axis=AX.X, op=ALU.add)
                        V('tensor_scalar', [b_st2, b_st], [b_st2], out=st2_t[:, 6:12], in0=st2_t[:, 6:12],
                          scalar1=st_t[:, 20:21], scalar2=None, op0=ALU.add)
                    lo = 0 if kind == 'own' else 6
                    hi_ = 6 if skip_ck else 12
                    V('tensor_scalar', [b_st2], [b_rstd2], out=rstd2_t[:, lo:hi_], in0=st2_t[:, lo:hi_],
                      scalar1=1.0 / 96, scalar2=EPS, op0=ALU.mult, op1=ALU.add)
                    yield
                    G('tensor_tensor', [b_rstd2, b_nh24], [b_rstd2], out=rstd2_t[:, lo:hi_], in0=rstd2_t[:, lo:hi_],
                      in1=nh24[:, lo:hi_], op=ALU.pow)
                    yield
                    kc_t, b_kc = KCo[s2]
                    vc_t, b_vc = VCo[s2]
                    if kind == 'own':
                        qc_t, b_qc = QCo[s2]
                        V('tensor_tensor', [b_qcs, b_rstd2], [b_tmp1], out=tmp1[:],
                          in0=qcs_t[:].rearrange("p (h d) -> p h d", d=96),
                          in1=bc(rstd2_t[:, 0:6].unsqueeze(2), [128, 6, 96]), op=ALU.mult)
                        V('tensor_tensor', [b_tmp1, b_GCq], [b_qc], out=qc_t[:, :, 0:64], in0=tmp1[:, :, 0:64],
                          in1=bc(GCq[:, 0:64].unsqueeze(1), [128, 6, 64]), op=ALU.mult)
                        V('tensor_tensor', [b_tmp1, b_GCq], [b_trq], out=trq[:], in0=tmp1[:, :, 64:96],
                          in1=bc(GCq[:, 64:96].unsqueeze(1), [128, 6, 32]), op=ALU.mult)
                        rope(trq, b_trq, qc_t[:, :, 64:96], b_qc, 6, ti)
                    if kind == 'own':
                        qT_t, b_qT = QTs[s2]
                        for h in range(6):
                            T('transpose', [b_qc, b_idb], [b_TR], out=TR[0:96, h, :], in_=qc_t[:, h, :],
                              identity=idb[:])
                        V('tensor_copy', [b_TR], [b_qT], out=qT_t[:], in_=TR[0:96, 0:6, :])
                        self.dma('sync', SC['QCT'].ap()[:, :, t * 128:(t + 1) * 128].rearrange("h d n -> d h n"),
                                 qT_t[:], [b_qT], [SC['b']], 'qto%d' % s2)
                        yield
                    if skip_ck:
                        return
                    V('tensor_tensor', [b_kvcs, b_rstd2], [b_tmpk], out=tmpk[:], in0=kv3[:, :, 0:64],
                      in1=bc(rstd2_t[:, 6:12].unsqueeze(2), [128, 6, 64]), op=ALU.mult)
                    V('tensor_tensor', [b_tmpk, b_GCk], [b_kc], out=kc_t[:, :, 0:64], in0=tmpk[:],
                      in1=bc(GCk[:, 0:64].unsqueeze(1), [128, 6, 64]), op=ALU.mult)
                    V('tensor_tensor', [b_pj, b_GCk], [b_krg], out=krg[:, 0, :], in0=pj_t[:, 2048:2080],
                      in1=GCk[:, 64:96], op=ALU.mult)
                    rope(krg, b_krg, krr[:], b_krr, 1, ti)
                    V('tensor_tensor', [b_krr, b_rstd2], [b_kc], out=kc_t[:, :, 64:96],
                      in0=bc(krr[:], [128, 6, 32]), in1=bc(rstd2_t[:, 6:12].unsqueeze(2), [128, 6, 32]), op=ALU.mult)
                    V('tensor_copy', [b_kvcs], [b_vc], out=vc_t[:, :, 0:64], in_=kv3[:, :, 64:128])
                    yield
                    kT_t, b_kT = KTs[s2]
                    for h in range(6):
                        T('transpose', [b_kc, b_idb], [b_TR], out=TR[0:96, h, :], in_=kc_t[:, h, :], identity=idb[:])
                    V('tensor_copy', [b_TR], [b_kT], out=kT_t[:], in_=TR[0:96, 0:6, :])
                    self.dma('sync', SC['KCT'].ap()[:, :, ti * 128:(ti + 1) * 128].rearrange("h d n -> d h n"),
                             kT_t[:], [b_kT], [SC['b']], 'kto%d' % s2)
                    yield
                    self.dma('sync', SC['VC'].ap()[ti * 128:(ti + 1) * 128, :].rearrange("p (h d) -> p h d", d=65),
                             vc_t[:], [b_vc], [SC['b']], 'vco%d' % s2)
                    yield

            def issue_load(j):
                kind, t = jobs[j]
                x_t, b_x = xt[j % NS]
                if kind == 'halo' and halo_rows is not None:
                    self.dma('sync', x_t[:], halo_rows(t), [], [b_x], 'xt%d' % (j % NS))
                else:
                    src = {'own': x_own, 'oth': x_oth, 'halo': x_halo}[kind]
                    self.dma('sync', x_t[:], src[t * 128:(t + 1) * 128, :], [], [b_x], 'xt%d' % (j % NS))

            for j in range(min(2, len(jobs))):
                issue_load(j)
            active = []
            nxt = 0
            since = 10 ** 9
            STAG = 6
            while active or nxt < len(jobs):
                if nxt < len(jobs) and len(active) < NS and (since >= STAG or not active):
                    if nxt + 2 < len(jobs):
                        issue_load(nxt + 2)
                    active.append(tile_gen(nxt, jobs[nxt][0], jobs[nxt][1]))
                    nxt += 1
                    since = 0
                for g in list(active):
                    try:
                        next(g)
                    except StopIteration:
                        active.remove(g)
                since += 1
        self.P.phase_barrier()

    def phaseC(self, heads=range(6), nqt=8):
        nc, P, D, SC = self.nc, self.P, self.D, self.SC
        V, G, A, T, E = self.V, self.G, self.A, self.T, self.E
        with ExitStack() as ph:
            def sb(nm, shape, dt, n=1):
                r = []
                for i in range(n):
                    t = ph.enter_context(nc.sbuf_tensor(self.name(nm), shape, dt))
                    r.append((t, Buf(nm + str(i))))
                return r if n > 1 else r[0]

            def ps(nm, shape, dt, n=1):
                r = []
                for i in range(n):
                    t = psum_view(ph, nc, self.name(nm), shape, dt)
                    r.append((t, Buf(nm + str(i))))
                return r if n > 1 else r[0]

            P.barrier('sync', reads=[SC['b']])
            idf, b_idf = sb("idf", [128, 128], F32)
            self.dma('sync', idf[:], D['idf'].ap(), [], [b_idf], 'idf')
            KT = sb("cKT", [96, 8192], BF16, 2)
            VV = sb("cV", [128, 64, 65], BF16, 2)
            QT = sb("cQT", [96, 4096], BF16, 2)
            PT = sb("cPT", [128, 512], BF16, 4)
            OTs = sb("cOTs", [65, 512], F32, 2)
            rc = sb("crc", [128, 4], F32, 2)
            oc = sb("coc", [128, 4, 64], BF16, 2)
            ST = ps("cST", [128, 512], F32, 3)
            OT = ps("cOT", [65, 512], F32, 2)
            TO, b_TO = ps("cTO", [128, 4, 65], F32)

            heads = list(heads)

            def load_head(hi):
                h = heads[hi]
                s = hi % 2
                self.dma('sync', KT[s][0][:], SC['KCT'].ap()[h], [SC['b']], [KT[s][1]], 'cKT%d' % s)
                self.dma('sync', QT[s][0][:], SC['QCT'].ap()[h], [SC['b']], [QT[s][1]], 'cQT%d' % s)
                self.dma('sync', VV[s][0][:],
                         SC['VC'].ap().rearrange("(t p) c -> p t c", p=128)[:, :, h * 65:(h + 1) * 65],
                         [SC['b']], [VV[s][1]], 'cV%d' % s)

            steps = [(hi, qt, kc) for hi in range(len(heads)) for qt in range(nqt) for kc in range(64)]
            n = len(steps)
            LA = 2
            deferred = []
            load_head(0)
            nq = 0
            for i in range(n + LA):
                if i < n:
                    hi, qt, kc = steps[i]
                    s = hi % 2
                    st_t, st_b = ST[i % 3]
                    pt_t, pt_b = PT[i % 4]
                    T('matmul', [KT[s][1], QT[s][1]], [st_b], out=st_t[:], lhsT=KT[s][0][:, kc * 128:(kc + 1) * 128],
                      rhs=QT[s][0][:, qt * 512:(qt + 1) * 512], start=True, stop=True)
                    A('activation', [st_b], [pt_b], out=pt_t[:], in_=st_t[:], func=AF.Exp)
                j = i - LA
                if j >= 0:
                    hi, qt, kc = steps[j]
                    if qt == 0 and kc == 0 and hi + 1 < len(heads):
                        load_head(hi + 1)
                    s = hi % 2
                    qi = (hi * nqt + qt)
                    ot_t, ot_b = OT[qi % 2]
                    pt_t, pt_b = PT[j % 4]
                    T('matmul', [VV[s][1], pt_b], [ot_b], out=ot_t[:], lhsT=VV[s][0][:, kc, :], rhs=pt_t[:],
                      start=(kc == 0), stop=(kc == 63))
                    if kc == 63:
                        h = heads[hi]
                        os_t, os_b = OTs[qi % 2]
                        V('tensor_copy', [ot_b], [os_b], out=os_t[:], in_=ot_t[:])

                        def fin(os_t=os_t, os_b=os_b, qi=qi, qt=qt, h=h):
                            for jj in range(4):
                                T('transpose', [os_b, b_idf], [b_TO], out=TO[:, jj, :],
                                  in_=os_t[:, jj * 128:(jj + 1) * 128], identity=idf[0:65, 0:65])
                            rc_t, rc_b = rc[qi % 2]
                            oc_t, oc_b = oc[qi % 2]
                            V('reciprocal', [b_TO], [rc_b], out=rc_t[:].unsqueeze(2), in_=TO[:, :, 64:65])
                            V('tensor_tensor', [b_TO, rc_b], [oc_b], out=oc_t[:], in0=TO[:, :, 0:64],
                              in1=bc(rc_t[:].unsqueeze(2), [128, 4, 64]), op=ALU.mult)
                            self.dma('sync',
                                     SC['MIX'].ap()[qt * 512:(qt + 1) * 512, 640 + h * 64:640 + (h + 1) * 64]
                                     .rearrange("(t p) d -> p t d", p=128),
                                     oc_t[:], [oc_b], [SC['bmix']], 'coc%d' % (qi % 2))
                        deferred.append((i + 6, fin))
                while deferred and deferred[0][0] <= i:
                    deferred.pop(0)[1]()
            for (_, fn) in deferred:
                fn()
        self.P.phase_barrier()

    def bias_setup(self):
        nc, P, D, SC = self.nc, self.P, self.D, self.SC
        V, G, A, T, E = self.V, self.G, self.A, self.T, self.E
        with ExitStack() as ph:
            tabN = ph.enter_context(nc.sbuf_tensor(self.name("tabN"), [33, 10], F32)); b_tab = Buf("tabN")
            oh = ph.enter_context(nc.sbuf_tensor(self.name("oh"), [33, 1660], F32)); b_oh = Buf("oh")
            fv = ph.enter_context(nc.sbuf_tensor(self.name("fv"), [10, 1660], F32)); b_fv = Buf("fv")
            pf = [(psum_view(ph, nc, self.name("pf"), [10, 512], F32), Buf("pf%d" % i)) for i in range(4)]
            P.op('vector', lambda e: e.memset(tabN[:], NEGB), [], [b_tab])
            self.dma('sync', tabN[0:32, :], D['rel_bias_table'].ap(), [], [b_tab], 'tabN')
            self.dma('sync', oh[:], D['oh'].ap(), [], [b_oh], 'oh')
            segs = [(0, 511), (511, 894), (894, 1277), (1277, 1660)]
            for i, (a, b) in enumerate(segs):
                T('matmul', [b_tab, b_oh], [pf[i][1]], out=pf[i][0][:, 0:b - a], lhsT=tabN[:], rhs=oh[:, a:b],
                  start=True, stop=True)
                V('tensor_copy', [pf[i][1]], [b_fv], out=fv[:, a:b], in_=pf[i][0][:, 0:b - a])
            self.dma('sync', SC['FV'].ap(), fv[:], [b_fv], [SC['bfv']], 'fvo')
        self.P.phase_barrier()

    def phaseB(self, L):
        nc, P, D, SC = self.nc, self.P, self.D, self.SC
        V, G, A, T, E = self.V, self.G, self.A, self.T, self.E
        with ExitStack() as ph:
            def sb(nm, shape, dt, n=1):
                r = []
                for i in range(n):
                    t = ph.enter_context(nc.sbuf_tensor(self.name(nm), shape, dt))
                    r.append((t, Buf(nm + str(i))))
                return r if n > 1 else r[0]

            def ps(nm, shape, dt, n=1):
                r = []
                for i in range(n):
                    t = psum_view(ph, nc, self.name(nm), shape, dt)
                    r.append((t, Buf(nm + str(i))))
                return r if n > 1 else r[0]

            P.barrier('sync', reads=[SC['b'], SC['bfv']])
            idf, b_idf = sb("idf", [128, 128], F32)
            idb, b_idb = sb("idb", [128, 128], BF16)
            self.dma('sync', idf[:], D['idf'].ap(), [], [b_idf], 'idf')
            self.dma('sync', idb[:], D['idb'].ap(), [], [b_idb], 'idb')
            biasB = sb("biasB", [128, 4, 3, 128], F32)
            hk = sb("hkB", [128, 128], F32, 4)
            for h in range(4):
                for o in range(3):
                    g_t, g_b = hk[(h * 3 + o) % 4]
                    self.dma('sync', g_t[:], bass.AP(SC['FV'], (6 + h) * 1660 + 128 * o, [[1, 128], [1, 128]]),
                             [SC['bfv']], [g_b], 'hkB%d' % ((h * 3 + o) % 4))
                    V('tensor_copy', [g_b], [biasB[1]], out=biasB[0][:, h, o, :],
                      in_=bass.AP(g_t, 127, [[128, 128], [-1, 128]]))
            sk, b_sk = sb("sink", [128, 4], F32)
            esk, b_esk = sb("esink", [128, 4], F32)
            self.dma('sync', sk[:], D['sink_b'].ap()[L].partition_broadcast(128), [], [b_sk], 'sink')
            A('activation', [b_sk], [b_esk], out=esk[:], in_=sk[:], func=AF.Exp)
            Qc = sb("bQc", [128, 4, 256], BF16, 2)
            Kc = sb("bKc", [128, 6, 128], BF16, 2)
            Vc = sb("bVc", [128, 6, 130], BF16, 2)
            QTb = sb("bQT", [128, 2, 4, 128], BF16, 2)
            KTb = sb("bKT", [128, 6, 128], BF16, 2)
            Sb = sb("bS", [128, 3, 128], F32, 2)
            PTb = sb("bPT", [128, 3, 128], BF16, 2)
            OTs = sb("bOTs", [65, 4, 128], F32, 2)
            den = sb("bden", [128, 4], F32, 2)
            ob = sb("bo", [128, 4, 64], BF16, 2)
            TRq = ps("bTRq", [128, 2, 4, 128], BF16)
            TRk = ps("bTRk", [128, 6, 128], BF16)
            SP = ps("bSP", [128, 3, 128], F32, 2)
            OTp = ps("bOT", [65, 4, 128], F32, 2)
            TOp = ps("bTO", [128, 4, 65], F32)

            def stage1(J):
                s = J % 2
                self.dma('sync', Qc[s][0][:], SC['QB'].ap()[J * 512:(J + 1) * 512, :].rearrange("(t p) c -> p t c", p=128),
                         [SC['b']], [Qc[s][1]], 'bQc%d' % s)
                r0 = (7 + 4 * J) * 128
                self.dma('sync', Kc[s][0][:], SC['KBx'].ap()[r0:r0 + 768, :].rearrange("(t p) c -> p t c", p=128),
                         [SC['b']], [Kc[s][1]], 'bKc%d' % s)
                self.dma('sync', Vc[s][0][:], SC['VBx'].ap()[r0:r0 + 768, :].rearrange("(t p) c -> p t c", p=128),
                         [SC['b']], [Vc[s][1]], 'bVc%d' % s)
                for t in range(4):
                    for pi in range(2):
                        T('transpose', [Qc[s][1], b_idb], [TRq[1]], out=TRq[0][:, pi, t, :],
                          in_=Qc[s][0][:, t, pi * 128:(pi + 1) * 128], identity=idb[:])
                V('tensor_copy', [TRq[1]], [QTb[s][1]], out=QTb[s][0][:], in_=TRq[0][:])
                for kt in range(6):
                    T('transpose', [Kc[s][1], b_idb], [TRk[1]], out=TRk[0][:, kt, :], in_=Kc[s][0][:, kt, :],
                      identity=idb[:])
                V('tensor_copy', [TRk[1]], [KTb[s][1]], out=KTb[s][0][:], in_=TRk[0][:])

            cnt = [0]

            def stage2(J):
                s = J % 2
                for t in range(4):
                    qi = J * 4 + t
                    ot_t, ot_b = OTp[qi % 2]
                    for h in range(4):
                        base = 64 * (h // 2)
                        pi = h % 2
                        kvh = h // 2
                        c = cnt[0]
                        cnt[0] += 1
                        sp_t, sp_b = SP[c % 2]
                        for o in range(3):
                            T('matmul', [KTb[s][1], QTb[s][1]], [sp_b], out=sp_t[:, o, :],
                              lhsT=KTb[s][0][base:base + 64, t + o, :], rhs=QTb[s][0][base:base + 64, pi, t, :],
                              start=True, stop=True)
                        s_t, s_b = Sb[c % 2]
                        p_t, p_b = PTb[c % 2]
                        V('tensor_tensor', [sp_b, biasB[1]], [s_b], out=s_t[:], in0=sp_t[:], in1=biasB[0][:, h, :, :],
                          op=ALU.add)
                        A('activation', [s_b], [p_b], out=p_t[:], in_=s_t[:], func=AF.Exp)
                        for o in range(3):
                            T('matmul', [Vc[s][1], p_b], [ot_b], out=ot_t[:, h, :],
                              lhsT=Vc[s][0][:, t + o, kvh * 65:(kvh + 1) * 65], rhs=p_t[:, o, :],
                              start=(o == 0), stop=(o == 2))
                    os_t, os_b = OTs[qi % 2]
                    V('tensor_copy', [ot_b], [os_b], out=os_t[:], in_=ot_t[:])
                    for h in range(4):
                        T('transpose', [os_b, b_idf], [TOp[1]], out=TOp[0][:, h, :], in_=os_t[:, h, :],
                          identity=idf[0:65, 0:65])
                    d_t, d_b = den[qi % 2]
                    o_t, o_b = ob[qi % 2]
                    V('tensor_tensor', [TOp[1], b_esk], [d_b], out=d_t[:].unsqueeze(2), in0=TOp[0][:, :, 64:65],
                      in1=esk[:].unsqueeze(2), op=ALU.add)
                    V('reciprocal', [d_b], [d_b], out=d_t[:], in_=d_t[:])
                    V('tensor_tensor', [TOp[1], d_b], [o_b], out=o_t[:], in0=TOp[0][:, :, 0:64],
                      in1=bc(d_t[:].unsqueeze(2), [128, 4, 64]), op=ALU.mult)
                    self.dma('sync', SC['MIX'].ap()[qi * 128:(qi + 1) * 128, 384:640], o_t[:], [o_b], [SC['bmix']],
                             'bo%d' % (qi % 2))

            stage1(0)
            for J in range(8):
                if J + 1 < 8:
                    stage1(J + 1)
                stage2(J)
        self.P.phase_barrier()

    def phaseA(self, cfgs=(0, 1, 2)):
        nc, P, D, SC = self.nc, self.P, self.D, self.SC
        V, G, A, T, E = self.V, self.G, self.A, self.T, self.E
        RS = (1, 4, 16)
        with ExitStack() as ph:
            def sb(nm, shape, dt, n=1):
                r = []
                for i in range(n):
                    t = ph.enter_context(nc.sbuf_tensor(self.name(nm), shape, dt))
                    r.append((t, Buf(nm + str(i))))
                return r if n > 1 else r[0]

            def ps(nm, shape, dt, n=1):
                r = []
                for i in range(n):
                    t = psum_view(ph, nc, self.name(nm), shape, dt)
                    r.append((t, Buf(nm + str(i))))
                return r if n > 1 else r[0]

            P.barrier('sync', reads=[SC['b'], SC['bfv']])
            idf, b_idf = sb("idf", [128, 128], F32)
            idb, b_idb = sb("idb", [128, 128], BF16)
            self.dma('sync', idf[:], D['idf'].ap(), [], [b_idf], 'idf')
            self.dma('sync', idb[:], D['idb'].ap(), [], [b_idb], 'idb')
            biasA = sb("biasA", [128, 3, 6, 2, 128], F32)
            hk = sb("hkA", [128, 128], F32, 4)
            for ci in range(3):
                for h in range(6):
                    for c in range(2):
                        kk = (ci * 6 + h) * 2 + c
                        g_t, g_b = hk[kk % 4]
                        self.dma('sync', g_t[:],
                                 bass.AP(SC['FV'], h * 1660 + 511 + 383 * ci + 128 * c, [[1, 128], [1, 128]]),
                                 [SC['bfv']], [g_b], 'hkA%d' % (kk % 4))
                        V('tensor_copy', [g_b], [biasA[1]], out=biasA[0][:, ci, h, c, :],
                          in_=bass.AP(g_t, 127, [[128, 128], [-1, 128]]))
            Qa = sb("aQ", [128, 384], BF16, 3)
            Ka = sb("aK", [128, 2, 384], BF16, 3)
            Va = sb("aV", [128, 2, 390], BF16, 3)
            QTa = sb("aQT", [128, 2, 3, 128], BF16, 2)
            for (t_, b_) in QTa:
                P.op('vector', lambda e, t_=t_: e.memset(t_[:], 0.0), [], [b_])
            KTa = sb("aKT", [128, 3, 2, 128], BF16, 2)
            Sa = sb("aS", [128, 6, 2, 128], F32, 2)
            PTa = sb("aPT", [128, 6, 2, 128], BF16, 2)
            OTs = sb("aOTs", [65, 6, 128], F32, 2)
            Oo = sb("aOo", [128, 6, 65], F32, 2)
            TRq = ps("aTRq", [128, 3, 128], BF16)
            TRk = ps("aTRk", [128, 3, 2, 128], BF16)
            SP = ps("aSP", [128, 2, 2, 128], F32, 3)
            OTp = ps("aOT", [65, 3, 128], F32, 2)
            TOp = ps("aTO", [128, 6, 65], F32)

            jobs = []
            for ci in cfgs:
                r = RS[ci]
                for rho in range(r):
                    for j in range(32 // r):
                        jobs.append((ci, r, rho, j))

            if self.debug and self.debug.get('a_jobs'):
                jobs = self.debug['a_jobs']

            def loadA(i):
                ci, r, rho, j = jobs[i]
                s3 = i % 3
                self.dma('sync', Qa[s3][0][:], bass.AP(SC['QA'], (r * 128 * j + rho) * 384, [[r * 384, 128], [1, 384]]),
                         [SC['b']], [Qa[s3][1]], 'aQ%d' % s3)
                e0 = r * (128 * j - 64) + rho + 1024
                self.dma('sync', Ka[s3][0][:],
                         bass.AP(SC['KAx'], e0 * 384, [[r * 384, 128], [r * 384 * 128, 2], [1, 384]]),
                         [SC['b']], [Ka[s3][1]], 'aK%d' % s3)
                self.dma('sync', Va[s3][0][:],
                         bass.AP(SC['VAx'], e0 * 390, [[r * 390, 128], [r * 390 * 128, 2], [1, 390]]),
                         [SC['b']], [Va[s3][1]], 'aV%d' % s3)

            def stage1(i):
                ci, r, rho, j = jobs[i]
                s3 = i % 3
                s2 = i % 2
                for pi in range(3):
                    T('transpose', [Qa[s3][1], b_idb], [TRq[1]], out=TRq[0][:, pi, :],
                      in_=Qa[s3][0][:, pi * 128:(pi + 1) * 128], identity=idb[:])
                V('tensor_copy', [TRq[1]], [QTa[s2][1]], out=QTa[s2][0][0:64, 0, :, :], in_=TRq[0][0:64, :, :])
                V('tensor_copy', [TRq[1]], [QTa[s2][1]], out=QTa[s2][0][64:128, 1, :, :], in_=TRq[0][64:128, :, :])
                for pi in range(3):
                    for c in range(2):
                        T('transpose', [Ka[s3][1], b_idb], [TRk[1]], out=TRk[0][:, pi, c, :],
                          in_=Ka[s3][0][:, c, pi * 128:(pi + 1) * 128], identity=idb[:])
                V('tensor_copy', [TRk[1]], [KTa[s2][1]], out=KTa[s2][0][:], in_=TRk[0][:])

            astop = self.debug.get('a_stop', 99) if self.debug else 99

            def stage2(i):
                ci, r, rho, j = jobs[i]
                s3 = i % 3
                s2 = i % 2
                s_t, s_b = Sa[s2]
                p_t, p_b = PTa[s2]
                if astop < 2:
                    return
                for pi in range(3):
                    sp_t, sp_b = SP[pi]
                    for hh in range(2):
                        for c in range(2):
                            T('matmul', [KTa[s2][1], QTa[s2][1]], [sp_b], out=sp_t[:, hh, c, :],
                              lhsT=KTa[s2][0][:, pi, c, :],
                              rhs=QTa[s2][0][:, hh, pi, :], start=True, stop=True)
                    if self.debug and self.debug.get('a_nobias'):
                        continue
                    V('tensor_tensor', [sp_b, biasA[1]], [s_b], out=s_t[:, 2 * pi:2 * pi + 2, :, :], in0=sp_t[:],
                      in1=biasA[0][:, ci, 2 * pi:2 * pi + 2, :, :], op=ALU.add)
                if astop < 3:
                    return
                A('activation', [s_b], [p_b], out=p_t[:], in_=s_t[:], func=AF.Exp)
                os_t, os_b = OTs[s2]
                if astop < 4:
                    return
                for g3 in range(2):
                    ot_t, ot_b = OTp[g3]
                    for hh in range(3):
                        h = g3 * 3 + hh
                        for c in range(2):
                            T('matmul', [Va[s3][1], p_b], [ot_b], out=ot_t[:, hh, :],
                              lhsT=Va[s3][0][:, c, h * 65:(h + 1) * 65], rhs=p_t[:, h, c, :],
                              start=(c == 0), stop=(c == 1))
                    V('tensor_copy', [ot_b], [os_b], out=os_t[:, g3 * 3:g3 * 3 + 3, :], in_=ot_t[:])
                if astop < 5:
                    return
                for h in range(6):
                    T('transpose', [os_b, b_idf], [TOp[1]], out=TOp[0][:, h, :], in_=os_t[:, h, :],
                      identity=idf[0:65, 0:65])
                o_t, o_b = Oo[s2]
                V('tensor_copy', [TOp[1]], [o_b], out=o_t[:], in_=TOp[0][:])
                self.dma('sync', bass.AP(SC['OA'], ci * 4096 * 390 + (r * 128 * j + rho) * 390, [[r * 390, 128], [1, 390]]),
                         o_t[:].rearrange("p h d -> p (h d)"), [o_b], [SC['boa']], 'aOo%d' % s2)

            n = len(jobs)
            for i in range(min(2, n)):
                loadA(i)
            stage1(0)
            for i in range(n):
                if i + 2 < n:
                    loadA(i + 2)
                if i + 1 < n:
                    stage1(i + 1)
                stage2(i)
        self.P.phase_barrier()

    def phaseA2(self):
        nc, P, D, SC = self.nc, self.P, self.D, self.SC
        V, G, A, T, E = self.V, self.G, self.A, self.T, self.E
        with ExitStack() as ph:
            def sb(nm, shape, dt, n=1):
                r = []
                for i in range(n):
                    t = ph.enter_context(nc.sbuf_tensor(self.name(nm), shape, dt))
                    r.append((t, Buf(nm + str(i))))
                return r if n > 1 else r[0]
            P.barrier('sync', reads=[SC['boa']])
            O3 = sb("a2O", [128, 3, 6, 65], F32, 3)
            acc = sb("a2acc", [128, 6, 65], F32, 2)
            rc = sb("a2rc", [128, 6], F32, 2)
            oo = sb("a2o", [128, 6, 64], BF16, 2)
            for t in range(32):
                s3, s2 = t % 3, t % 2
                self.dma('sync', O3[s3][0][:].rearrange("p c h d -> p c (h d)"),
                         SC['OA'].ap()[:, t * 128:(t + 1) * 128, :].rearrange("c p d -> p c d"),
                         [SC['boa']], [O3[s3][1]], 'a2O%d' % s3)
                o3 = O3[s3][0]
                V('tensor_tensor', [O3[s3][1]], [acc[s2][1]], out=acc[s2][0][:], in0=o3[:, 0], in1=o3[:, 1], op=ALU.add)
                V('tensor_tensor', [O3[s3][1], acc[s2][1]], [acc[s2][1]], out=acc[s2][0][:], in0=acc[s2][0][:],
                  in1=o3[:, 2], op=ALU.add)
                V('reciprocal', [acc[s2][1]], [rc[s2][1]], out=rc[s2][0][:].unsqueeze(2), in_=acc[s2][0][:, :, 64:65])
                V('tensor_tensor', [acc[s2][1], rc[s2][1]], [oo[s2][1]], out=oo[s2][0][:], in0=acc[s2][0][:, :, 0:64],
                  in1=bc(rc[s2][0][:].unsqueeze(2), [128, 6, 64]), op=ALU.mult)
                self.dma('sync', SC['MIX'].ap()[t * 128:(t + 1) * 128, 0:384], oo[s2][0][:].rearrange("p h d -> p (h d)"),
                         [oo[s2][1]], [SC['bmix']], 'a2o%d' % s2)
        self.P.phase_barrier()

    def phase4a(self, L, x_own, x1_d):
        nc, P, D, SC = self.nc, self.P, self.D, self.SC
        V, G, A, T, E = self.V, self.G, self.A, self.T, self.E
        with ExitStack() as ph:
            def sb(nm, shape, dt, n=1):
                r = []
                for i in range(n):
                    t = ph.enter_context(nc.sbuf_tensor(self.name(nm), shape, dt))
                    r.append((t, Buf(nm + str(i))))
                return r if n > 1 else r[0]

            def ps(nm, shape, dt, n=1):
                r = []
                for i in range(n):
                    t = psum_view(ph, nc, self.name(nm), shape, dt)
                    r.append((t, Buf(nm + str(i))))
                return r if n > 1 else r[0]

            P.barrier('sync', reads=[SC['bmix']])
            idb, b_idb = sb("idb", [128, 128], BF16)
            self.dma('sync', idb[:], D['idb'].ap(), [], [b_idb], 'idb')
            wo, b_wo = sb("wo", [128, 8, 1024], BF16)
            go, b_go = sb("go", [128, 8], F32)
            self.dma('sync', go[:], D['out_norm'].ap()[L].rearrange("(c p) -> p c", p=128), [], [b_go], 'go',
                     allow_slow_non_contiguous=True)
            stg = sb("stg4a", [128, 1024], F32, 2)
            for c in range(8):
                self.dma('sync', stg[c % 2][0][:], D['w_out'].ap()[L, c * 128:(c + 1) * 128, :], [], [stg[c % 2][1]],
                         'stg4a%d' % (c % 2))
                if c % 2 == 0:
                    V('tensor_scalar', [stg[c % 2][1], b_go], [b_wo], out=wo[:, c, :], in0=stg[c % 2][0][:],
                      scalar1=go[:, c:c + 1], scalar2=None, op0=ALU.mult)
                else:
                    A('activation', [stg[c % 2][1], b_go], [b_wo], out=wo[:, c, :], in_=stg[c % 2][0][:], func=AF.Copy,
                      scale=go[:, c:c + 1])
            invd3, b_invd3 = sb("invd3", [128, 3], F32)
            nh3, b_nh3 = sb("nh3", [128, 3], F32)
            P.op('gpsimd', lambda e: e.memset(invd3[:], 1.0 / 384), [], [b_invd3])
            P.op('gpsimd', lambda e: e.memset(invd3[:, 1:2], 1.0 / 256), [], [b_invd3])
            P.op('gpsimd', lambda e: e.memset(nh3[:], -0.5), [], [b_nh3])
            mx = sb("mx", [128, 1024], BF16, 3)
            xt = sb("x4a", [128, 1024], F32, 3)
            sq, b_sq = sb("sq4a", [128, 1024], F32)
            ss = sb("ss4a", [128, 3], F32, 2)
            rs = sb("rs4a", [128, 3], F32, 2)
            mn = sb("mn", [128, 1024], BF16, 2)
            mT = sb("mT", [128, 8, 128], BF16, 2)
            TR = ps("TR4a", [128, 8, 128], BF16, 2)
            Y = ps("Y4a", [128, 1024], F32, 2)
            grp = [(0, 384), (384, 640), (640, 1024)]
            def load4a(t):
                s3 = t % 3
                rows = slice(t * 128, (t + 1) * 128)
                self.dma('sync', mx[s3][0][:], SC['MIX'].ap()[rows, :], [SC['bmix']], [mx[s3][1]], 'mx%d' % s3)
                self.dma('sync', xt[s3][0][:], x_own[rows, :], [], [xt[s3][1]], 'x4a%d' % s3)
            load4a(0)

            def s1_4a(t):
                s3, s2 = t % 3, t % 2
                rows = slice(t * 128, (t + 1) * 128)
                if t + 1 < 32:
                    load4a(t + 1)
                m_t, m_b = mx[s3]
                V('tensor_tensor', [m_b], [b_sq], out=sq[:], in0=m_t[:], in1=m_t[:], op=ALU.mult)
                for gi, (a, b) in enumerate(grp):
                    V('tensor_reduce', [b_sq], [ss[s2][1]], out=ss[s2][0][:, gi:gi + 1], in_=sq[:, a:b], axis=AX.X,
                      op=ALU.add)
                V('tensor_tensor', [ss[s2][1], b_invd3], [rs[s2][1]], out=rs[s2][0][:], in0=ss[s2][0][:], in1=invd3[:],
                  op=ALU.mult)
                V('tensor_scalar', [rs[s2][1]], [rs[s2][1]], out=rs[s2][0][:], in0=rs[s2][0][:], scalar1=EPS, scalar2=None,
                  op0=ALU.add)
                G('tensor_tensor', [rs[s2][1], b_nh3], [rs[s2][1]], out=rs[s2][0][:], in0=rs[s2][0][:], in1=nh3[:],
                  op=ALU.pow)
                for gi, (a, b) in enumerate(grp):
                    E('vector', 'tensor_scalar', [m_b, rs[s2][1]], [mn[s2][1]],
                      out=mn[s2][0][:, a:b], in0=m_t[:, a:b], scalar1=rs[s2][0][:, gi:gi + 1], scalar2=None, op0=ALU.mult)
                for c in range(8):
                    T('transpose', [mn[s2][1], b_idb], [TR[s2][1]], out=TR[s2][0][:, c, :],
                      in_=mn[s2][0][:, c * 128:(c + 1) * 128], identity=idb[:])
                A('activation', [TR[s2][1]], [mT[s2][1]], out=mT[s2][0][:], in_=TR[s2][0][:], func=AF.Copy)

            def s2_4a(t):
                s3, s2 = t % 3, t % 2
                rows = slice(t * 128, (t + 1) * 128)
                for hf in range(2):
                    for c in range(8):
                        T('matmul', [mT[s2][1], b_wo], [Y[s2][1]], out=Y[s2][0][:, hf * 512:(hf + 1) * 512],
                          lhsT=mT[s2][0][:, c, :], rhs=wo[:, c, hf * 512:(hf + 1) * 512], start=(c == 0), stop=(c == 7))
                V('tensor_tensor', [Y[s2][1], xt[s3][1]], [xt[s3][1]], out=xt[s3][0][:], in0=Y[s2][0][:],
                  in1=xt[s3][0][:], op=ALU.add)
                self.dma('sync', x1_d[rows, :], xt[s3][0][:], [xt[s3][1]], [SC['bx1']], 'x4ao%d' % s3)

            s1_4a(0)
            for t in range(32):
                if t + 1 < 32:
                    s1_4a(t + 1)
                s2_4a(t)
        self.P.phase_barrier()

    def phase4b(self, L, x1_d, out_d, b_out):
        nc, P, D, SC = self.nc, self.P, self.D, self.SC
        V, G, A, T, E = self.V, self.G, self.A, self.T, self.E
        with ExitStack() as ph:
            def sb(nm, shape, dt, n=1):
                r = []
                for i in range(n):
                    t = ph.enter_context(nc.sbuf_tensor(self.name(nm), shape, dt))
                    r.append((t, Buf(nm + str(i))))
                return r if n > 1 else r[0]

            def ps(nm, shape, dt, n=1):
                r = []
                for i in range(n):
                    t = psum_view(ph, nc, self.name(nm), shape, dt)
                    r.append((t, Buf(nm + str(i))))
                return r if n > 1 else r[0]

            P.barrier('sync', reads=[SC['bx1']])
            idb, b_idb = sb("idb", [128, 128], BF16)
            self.dma('sync', idb[:], D['idb'].ap(), [], [b_idb], 'idb')
            wu, b_wu = sb("wu", [128, 8, 4096], BF16)
            wd, b_wd = sb("wd", [128, 32, 1024], BF16)
            gm, b_gm = sb("gm", [128, 8], F32)
            self.dma('sync', gm[:], D['norm_mlp'].ap()[L].rearrange("(c p) -> p c", p=128), [], [b_gm], 'gm',
                     allow_slow_non_contiguous=True)
            stg = sb("stg4b", [128, 1024], F32, 2)
            k = 0
            for c in range(8):
                for hf in range(4):
                    s = k % 2
                    self.dma('sync', stg[s][0][:], D['w_up'].ap()[L, c * 128:(c + 1) * 128, hf * 1024:(hf + 1) * 1024], [],
                             [stg[s][1]], 'stg4b%d' % s)
                    if k % 2 == 0:
                        V('tensor_scalar', [stg[s][1], b_gm], [b_wu], out=wu[:, c, hf * 1024:(hf + 1) * 1024],
                          in0=stg[s][0][:], scalar1=gm[:, c:c + 1], scalar2=None, op0=ALU.mult)
                    else:
                        A('activation', [stg[s][1], b_gm], [b_wu], out=wu[:, c, hf * 1024:(hf + 1) * 1024],
                          in_=stg[s][0][:], func=AF.Copy, scale=gm[:, c:c + 1])
                    k += 1
            for c2 in range(32):
                s = k % 2
                self.dma('sync', stg[s][0][:], D['w_down'].ap()[L, c2 * 128:(c2 + 1) * 128, :], [],
                         [stg[s][1]], 'stg4b%d' % s)
                if k % 2 == 0:
                    V('tensor_copy', [stg[s][1]], [b_wd], out=wd[:, c2, :], in_=stg[s][0][:])
                else:
                    A('activation', [stg[s][1]], [b_wd], out=wd[:, c2, :], in_=stg[s][0][:], func=AF.Copy)
                k += 1
            nh1, b_nh1 = sb("nh1", [128, 1], F32)
            P.op('gpsimd', lambda e: e.memset(nh1[:], -0.5), [], [b_nh1])
            xc = sb("x4b", [128, 2, 1024], F32, 2)
            junk, b_junk = sb("junk4b", [128, 1024], BF16)
            ss = sb("ss4b", [128, 2], F32, 2)
            h2 = [sb("h2", [128, 2, 1024], BF16)] * 2
            h2T = [sb("h2T", [128, 8, 256], BF16)] * 2
            uT = [sb("uT", [128, 32, 256], BF16)] * 2
            rr = sb("rr", [128, 2, 256], F32, 3)
            TR = ps("TR4b", [128, 8, 128], BF16)
            U = ps("U4b", [128, 2, 256], F32, 2)
            Z = ps("Z4b", [128, 1024], F32, 2)
            def load4b(ch):
                s2 = ch % 2
                rows = slice(ch * 256, (ch + 1) * 256)
                self.dma('sync', xc[s2][0][:], x1_d[rows, :].rearrange("(t p) d -> p t d", p=128), [SC['bx1']],
                         [xc[s2][1]], 'x4b%d' % s2)
            load4b(0)
            for ch in range(16):
                s2 = ch % 2
                rows = slice(ch * 256, (ch + 1) * 256)
                x_t, x_b = xc[s2]
                if ch + 1 < 16:
                    load4b(ch + 1)
                for t in range(2):
                    A('activation', [x_b], [b_junk, ss[s2][1]], out=junk[:], in_=x_t[:, t, :], func=AF.Square,
                      accum_out=ss[s2][0][:, t:t + 1])
                G('tensor_scalar', [ss[s2][1]], [ss[s2][1]], out=ss[s2][0][:], in0=ss[s2][0][:], scalar1=1.0 / 1024,
                  scalar2=EPS, op0=ALU.mult, op1=ALU.add)
                G('tensor_tensor', [ss[s2][1], b_nh1], [ss[s2][1]], out=ss[s2][0][:], in0=ss[s2][0][:],
                  in1=bc(nh1[:], [128, 2]), op=ALU.pow)
                for t in range(2):
                    A('activation', [x_b, ss[s2][1]], [h2[s2][1]], out=h2[s2][0][:, t, :], in_=x_t[:, t, :], func=AF.Copy,
                      scale=ss[s2][0][:, t:t + 1])
                    for c in range(8):
                        T('transpose', [h2[s2][1], b_idb], [TR[1]], out=TR[0][:, c, :],
                          in_=h2[s2][0][:, t, c * 128:(c + 1) * 128], identity=idb[:])
                    V('tensor_copy', [TR[1]], [h2T[s2][1]], out=h2T[s2][0][:, :, t * 128:(t + 1) * 128], in_=TR[0][:])
                for f2 in range(16):
                    u_t, u_b = U[f2 % 2]
                    for ff in range(2):
                        fc = f2 * 2 + ff
                        for c in range(8):
                            T('matmul', [h2T[s2][1], b_wu], [u_b], out=u_t[:, ff, :], lhsT=wu[:, c, fc * 128:(fc + 1) * 128],
                              rhs=h2T[s2][0][:, c, :], start=(c == 0), stop=(c == 7))
                    r_t, r_b = rr[f2 % 3]
                    A('activation', [u_b], [r_b], out=r_t[:], in_=u_t[:], func=AF.Relu)
                    E('vector' if f2 % 2 == 0 else 'gpsimd', 'tensor_tensor', [r_b], [uT[s2][1]],
                      out=uT[s2][0][:, 2 * f2:2 * f2 + 2, :], in0=r_t[:], in1=r_t[:], op=ALU.mult)
                for t in range(2):
                    z_t, z_b = Z[t]
                    for hf in range(2):
                        for fc in range(32):
                            T('matmul', [uT[s2][1], b_wd], [z_b], out=z_t[:, hf * 512:(hf + 1) * 512],
                              lhsT=uT[s2][0][:, fc, t * 128:(t + 1) * 128], rhs=wd[:, fc, hf * 512:(hf + 1) * 512],
                              start=(fc == 0), stop=(fc == 31))
                    V('tensor_tensor', [z_b, x_b], [x_b], out=x_t[:, t, :], in0=z_t[:], in1=x_t[:, t, :], op=ALU.add)
                self.dma('sync', out_d[rows, :].rearrange("(t p) d -> p t d", p=128), x_t[:], [x_b], [b_out],
                         'x4bo%d' % s2)
        self.P.phase_barrier()

    def layer(self, L, x_own, x_oth, x_halo, x1_d, out_d, b_out, with_bias_setup=True, phases=None, pos_key='pos',
              valid_key='valid', halo_rows=None, reuse_ckv=False):
        ph = phases or ['bias', 'p1', 'C', 'B', 'A', 'A2', '4a', '4b']
        if with_bias_setup and 'bias' in ph:
            self.bias_setup()
        if 'p1' in ph:
            self.phase1(L, x_own, x_oth, x_halo, pos_key=pos_key, valid_key=valid_key, halo_rows=halo_rows,
                        skip_oth=reuse_ckv, skip_ck=reuse_ckv)
        if 'C' in ph:
            self.phaseC()
        if 'B' in ph:
            self.phaseB(L)
        if 'A' in ph:
            self.phaseA(cfgs=self.debug.get('cfgs', (0, 1, 2)) if self.debug else (0, 1, 2))
        if 'A2' in ph:
            self.phaseA2()
        if '4a' in ph:
            self.phase4a(L, x_own, x1_d)
        if '4b' in ph:
            self.phase4b(L, x1_d, out_d, b_out)


NLAYER = 2


def t5_bucket_np(rel):
    half, exact = 16, 8
    n = np.abs(rel)
    far = exact + (np.log(np.maximum(n, 1).astype(np.float32) / np.float32(exact))
                   / np.float32(math.log(1024 / exact)) * np.float32(half - exact)).astype(np.int32)
    far = np.minimum(far, half - 1)
    return np.where(rel > 0, half, 0) + np.where(n < exact, n, far)


def make_oh():
    oh = np.zeros((33, 1660), np.float32)
    d = np.arange(-255, 256)
    bk = t5_bucket_np(d)
    for i, dd in enumerate(d):
        if abs(dd) <= 128:
            oh[bk[i], i] = 1
        else:
            oh[32, i] = 1
    for ci, r in enumerate((1, 4, 16)):
        d = np.arange(-191, 192)
        bk = t5_bucket_np(d * r)
        for i, dd in enumerate(d):
            if abs(dd) <= 64:
                oh[bk[i], 511 + 383 * ci + i] = 1
            else:
                oh[32, 511 + 383 * ci + i] = 1
    return oh


WNAMES = ['norm_mix', 'w_in', 'qk_gain_a', 'qk_gain_b', 'sink_b', 'q_lat_gain', 'kv_lat_gain', 'w_uq', 'w_ukv',
          'qk_gain_c', 'out_norm', 'w_out', 'norm_mlp', 'w_up', 'w_down']


def declare(nc, NL):
    D = {}

    def inp(n, shape, dt=F32):
        D[n] = nc.dram_tensor(n, shape, dt, kind="ExternalInput")
    inp('x_own', [4096, 1024]); inp('x_oth', [4096, 1024]); inp('x_halo', [2048, 1024]); inp('x_halo2', [2048, 1024])
    inp('valid', [128, 16]); inp('valid2', [128, 16]); inp('pos', [128, 64], I32); inp('pos2', [128, 64], I32)
    inp('invf', [1, 16]); inp('idb', [128, 128], BF16); inp('idf', [128, 128])
    inp('norm_mix', [NL, 1024]); inp('w_in', [NL, 1024, 2080]); inp('qk_gain_a', [NL, 2, 64])
    inp('qk_gain_b', [NL, 2, 64]); inp('sink_b', [NL, 4]); inp('q_lat_gain', [NL, 256]); inp('kv_lat_gain', [NL, 128])
    inp('w_uq', [NL, 256, 576]); inp('w_ukv', [NL, 128, 768]); inp('qk_gain_c', [NL, 2, 96]); inp('out_norm', [NL, 1024])
    inp('w_out', [NL, 1024, 1024]); inp('norm_mlp', [NL, 1024]); inp('w_up', [NL, 1024, 4096]); inp('w_down', [NL, 4096, 1024])
    inp('rel_bias_table', [32, 10]); inp('oh', [33, 1660])
    SC = {}

    def scr(n, shape, dt=BF16):
        SC[n] = nc.dram_tensor(n, shape, dt, kind="Internal")
    scr('QA', [4096, 384]); scr('KAx', [6144, 384]); scr('VAx', [6144, 390]); scr('QB', [4096, 256])
    scr('KBx', [6144, 128]); scr('VBx', [6144, 130]); scr('QCT', [6, 96, 4096]); scr('KCT', [6, 96, 8192])
    scr('VC', [8192, 390]); scr('MIX', [4096, 1024]); scr('OA', [3, 4096, 390], F32); scr('FV', [10, 1660], F32)
    scr('X1', [4096, 1024], F32); scr('XA', [4096, 1024], F32); scr('XB', [4096, 1024], F32)
    SC['b'] = Buf('scratch', multi=True)
    SC['bmix'] = Buf('mix', multi=True)
    SC['boa'] = Buf('oa', multi=True)
    SC['bfv'] = Buf('fv', multi=True)
    SC['bx1'] = Buf('x1', multi=True)
    return D, SC


def make_inputs(inputs, core):
    b, half = core // 2, core % 2
    xs = np.asarray(inputs['x'], dtype=np.float32)[b]
    own = xs[half * 4096:(half + 1) * 4096]
    oth = xs[(1 - half) * 4096:(2 - half) * 4096]
    halo = np.zeros((2048, 1024), np.float32)
    valid = np.zeros((2048,), np.float32)
    halo2 = np.zeros((2048, 1024), np.float32)
    valid2 = np.zeros((2048,), np.float32)
    if half == 1:
        halo[0:1024] = oth[3072:4096]
        valid[0:1024] = 1
        halo2[1024:2048] = own[0:1024]
        valid2[1024:2048] = 1
    else:
        halo[1024:2048] = oth[0:1024]
        valid[1024:2048] = 1
        halo2[0:1024] = own[3072:4096]
        valid2[0:1024] = 1
    pos = np.asarray(inputs['positions'][b])
    p_own = pos[half * 4096:(half + 1) * 4096]
    p_oth = pos[(1 - half) * 4096:(2 - half) * 4096]
    pos_l = np.concatenate([p_own, p_oth])
    pos_l2 = np.concatenate([p_oth, p_own])
    m = {
        'x_own': np.ascontiguousarray(own), 'x_oth': np.ascontiguousarray(oth), 'x_halo': halo, 'x_halo2': halo2,
        'valid': np.ascontiguousarray(valid.reshape(16, 128).T),
        'valid2': np.ascontiguousarray(valid2.reshape(16, 128).T),
        'pos': np.ascontiguousarray(pos_l.reshape(64, 128).T.astype(np.int32)),
        'pos2': np.ascontiguousarray(pos_l2.reshape(64, 128).T.astype(np.int32)),
        'invf': (10000.0 ** (-np.arange(16, dtype=np.float32) / 16)).astype(np.float32).reshape(1, 16),
        'idb': np.eye(128).astype(ml_dtypes.bfloat16), 'idf': np.eye(128).astype(np.float32),
        'rel_bias_table': np.ascontiguousarray(inputs['rel_bias_table'], dtype=np.float32), 'oh': make_oh(),
    }
    for k in WNAMES:
        m[k] = np.ascontiguousarray(np.asarray(inputs[k], dtype=np.float32))
    return m


def build_program():
    nc = bass.Bass("TRN2", target_bir_lowering=False)
    D, SC = declare(nc, NLAYER)
    out_d = nc.dram_tensor('out', [4096, 1024], F32, kind="ExternalOutput")
    b_xa = Buf('xa', multi=True)
    b_xb = Buf('xb', multi=True)
    b_out = Buf('out', multi=True)
    with ExitStack() as es:
        P = Prog(nc, es)
        LB = LayerBuilder(nc, P, D, debug={})
        LB.SC = SC
        XA, XB = SC['XA'].ap(), SC['XB'].ap()
        LB.layer(0, D['x_own'].ap(), D['x_oth'].ap(), D['x_halo'].ap(), SC['X1'].ap(), XA, b_xa)
        LB.layer(0, D['x_oth'].ap(), D['x_own'].ap(), D['x_halo2'].ap(), SC['X1'].ap(), XB, b_xb,
                 with_bias_setup=False, pos_key='pos2', valid_key='valid2', reuse_ckv=True)

        def halo_rows(t):
            r0 = 3072 + t * 128 if t < 8 else (t - 8) * 128
            return XB[r0:r0 + 128, :]
        P.barrier('sync', reads=[b_xa, b_xb])
        LB.layer(1, XA, XB, None, SC['X1'].ap(), out_d.ap(), b_out, with_bias_setup=False, halo_rows=halo_rows)
        P.barrier('sync', reads=[b_out])
        P.emit()
    return nc


def kernel(**inputs):
    nc = build_program()
    in_maps = [make_inputs(inputs, c) for c in range(8)]
    res = run_bass_kernel_spmd(nc, in_maps, core_ids=list(range(8)))
    x = np.asarray(inputs['x'])
    out = np.empty(x.shape, np.float32)
    for c in range(8):
        b, half = c // 2, c % 2
        out[b, half * 4096:(half + 1) * 4096] = np.asarray(res.results[c]['out'], dtype=np.float32)
    return out
```

```python
import math
import ml_dtypes
from concourse.bass_utils import run_bass_kernel_spmd
import numpy as np
import concourse.bass as bass
import concourse.mybir as mybir
from contextlib import ExitStack

F32 = mybir.dt.float32
BF16 = mybir.dt.bfloat16
I32 = mybir.dt.int32
ALU = mybir.AluOpType
AF = mybir.ActivationFunctionType
AX = mybir.AxisListType

ENGS = ['sync', 'scalar', 'vector', 'gpsimd', 'tensor']
SEM_ROT = 24000


class Buf:
    __slots__ = ('name', 'writer', 'readers', 'dreaders', 'multi', 'mw')

    def __init__(self, name, multi=False):
        self.name = name
        self.writer = None
        self.readers = {}
        self.dreaders = []
        self.multi = multi
        self.mw = []


class Op:
    __slots__ = ('eng', 'fn', 'deps', 'idx', 'signal', 'is_dma', 'lane', 'ev', 'raw', 'barrier')


class Prog:
    def __init__(self, nc, es):
        self.nc = nc
        self.es = es
        self.ops = {e: [] for e in ENGS}
        self.order = []
        self.lanes = {}
        self.nsem = 0
        self.fence = []
        self.fence_pending = set()
        self.phase_lanes = {}

    def phase_barrier(self):
        fence = []
        for e in ENGS:
            for o in reversed(self.ops[e]):
                if not o.is_dma and not o.barrier:
                    fence.append(o)
                    break
        last = {}
        for o in self.order:
            if o.is_dma:
                last[o.lane] = o
        fence += list(last.values())
        self.fence = fence
        self.fence_pending = set(ENGS)
        self.phase_lanes = {}

    def new_sem(self, name):
        self.nsem += 1
        return self.es.enter_context(self.nc.semaphore(name))

    def sb(self, name, shape, dt):
        return self.es.enter_context(self.nc.sbuf_tensor(name, shape, dt))

    def ps(self, name, shape, dt):
        return self.es.enter_context(self.nc.psum_tensor(name, shape, dt))

    def barrier(self, eng, reads=(), writes=()):
        o = self.op(eng, lambda e: None, reads, writes)
        o.barrier = True
        return o

    def op(self, eng, fn, reads=(), writes=(), lane=None):
        o = Op()
        o.eng = eng
        o.fn = fn
        o.barrier = False
        o.is_dma = lane is not None
        if lane is not None:
            if lane not in self.phase_lanes:
                self.phase_lanes[lane] = 'L%d' % len(self.phase_lanes)
            lane = self.phase_lanes[lane]
        o.lane = lane
        o.signal = False
        o.ev = None
        deps = {}
        raw = set()
        for b in reads:
            if b.writer is not None:
                deps[id(b.writer)] = b.writer
                raw.add(id(b.writer))
            for w in b.mw:
                deps[id(w)] = w
                raw.add(id(w))
        for b in writes:
            if b.writer is not None and not b.multi:
                deps[id(b.writer)] = b.writer
                raw.add(id(b.writer))
            for r in b.readers.values():
                deps[id(r)] = r
            for r in b.dreaders:
                deps[id(r)] = r
        if eng in self.fence_pending:
            self.fence_pending.discard(eng)
            for w in self.fence:
                deps[id(w)] = w
        deps.pop(id(o), None)
        o.deps = list(deps.values())
        o.raw = raw
        for b in writes:
            if b.multi:
                b.mw.append(o)
            else:
                b.writer = o
            b.readers = {}
            b.dreaders = []
        for b in reads:
            if b.multi:
                continue
            if o.is_dma:
                b.dreaders.append(o)
            else:
                b.readers[eng] = o
        o.idx = len(self.ops[eng])
        self.ops[eng].append(o)
        self.order.append(o)
        return o

    def dma(self, q, out, in_, reads=(), writes=(), lane=None, **kw):
        assert lane is not None
        return self.op(q, lambda e: e.dma_start(out=out, in_=in_, **kw), reads, writes, lane=lane)

    def emit(self):
        nc = self.nc
        for o in self.order:
            for d in o.deps:
                if d.is_dma:
                    continue
                if d.barrier:
                    assert d.eng == o.eng, 'barrier dep across engines'
                    continue
                if d.eng == o.eng and not o.is_dma:
                    if o.eng == 'tensor':
                        continue
                    if id(d) not in o.raw:
                        continue
                d.signal = True
        esems = {}
        for e in ENGS:
            cnt = 0
            cur = None
            for o in self.ops[e]:
                if o.is_dma:
                    ln = self.lanes.get(o.lane)
                    if ln is None:
                        ln = [self.new_sem('l_%s' % o.lane), 0]
                        self.lanes[o.lane] = ln
                    ln[1] += 16
                    o.ev = (ln[0], ln[1])
                elif o.signal:
                    if cur is None or cnt >= SEM_ROT:
                        cur = self.new_sem('e_%s_%d' % (e, len(esems)))
                        esems[(e, len(esems))] = cur
                        cnt = 0
                    cnt += 1
                    o.ev = (cur, cnt)
        blk = self.es.enter_context(nc.Block())
        prog = self

        def run(e, eng):
            waited = {}
            for o in prog.ops[e]:
                need = {}
                for d in o.deps:
                    if not d.is_dma:
                        if d.barrier:
                            continue
                        if d.eng == o.eng and not o.is_dma:
                            if o.eng == 'tensor' or id(d) not in o.raw:
                                continue
                    sem, val = d.ev
                    k = id(sem)
                    if k not in need or need[k][1] < val:
                        need[k] = (sem, val)
                for k, (sem, val) in need.items():
                    if waited.get(k, 0) >= val:
                        continue
                    eng.wait_ge(sem, val)
                    waited[k] = val
                ins = o.fn(eng)
                if ins is None:
                    continue
                if o.is_dma:
                    ins.then_inc(o.ev[0], 16)
                elif o.signal:
                    ins.then_inc(o.ev[0], 1)

        @blk.sync
        def _(eng):
            run('sync', eng)

        @blk.scalar
        def _(eng):
            run('scalar', eng)

        @blk.vector
        def _(eng):
            run('vector', eng)

        @blk.gpsimd
        def _(eng):
            run('gpsimd', eng)

        @blk.tensor
        def _(eng):
            run('tensor', eng)

import numpy as np
import math

EPS = 1e-6
NEGB = -30000.0
TWO_PI_S = 6.2831845


def bc(ap, shape):
    return ap.to_broadcast(list(shape))


def psum_view(ph, nc, name, shape, dt):
    esz = 4 if dt == F32 else 2
    n = 1
    for d in shape[1:]:
        n *= d
    per_bank = 2048 // esz
    tot = ((n + per_bank - 1) // per_bank) * per_bank
    t = ph.enter_context(nc.psum_tensor(name, [128, tot], dt))
    v = t[0:shape[0], 0:n]
    if len(shape) == 3:
        v = v.rearrange("p (a b) -> p a b", a=shape[1])
    elif len(shape) == 4:
        v = v.rearrange("p (a b c) -> p a b c", a=shape[1], b=shape[2])
    return v


class LayerBuilder:
    def __init__(self, nc, P, D, debug=False):
        self.nc = nc
        self.P = P
        self.D = D
        self.debug = debug
        self.uid = 0

    def name(self, s):
        self.uid += 1
        return "%s_%d" % (s, self.uid)

    def V(self, fn, reads, writes, **kw):
        return self.P.op('vector', lambda e: getattr(e, fn)(**kw), reads, writes)

    def G(self, fn, reads, writes, **kw):
        return self.P.op('gpsimd', lambda e: getattr(e, fn)(**kw), reads, writes)

    def A(self, fn, reads, writes, **kw):
        return self.P.op('scalar', lambda e: getattr(e, fn)(**kw), reads, writes)

    def T(self, fn, reads, writes, **kw):
        return self.P.op('tensor', lambda e: getattr(e, fn)(**kw), reads, writes)

    def E(self, eng, fn, reads, writes, **kw):
        return self.P.op(eng, lambda e: getattr(e, fn)(**kw), reads, writes)

    def dma(self, q, out, in_, reads, writes, lane, **kw):
        return self.P.dma(q, out, in_, reads=reads, writes=writes, lane=lane, **kw)

    def phase1(self, L, x_own, x_oth, x_halo, first=True, pos_key='pos', valid_key='valid', halo_rows=None,
               skip_oth=False, skip_ck=False):
        nc, P, D = self.nc, self.P, self.D
        V, G, A, T, E = self.V, self.G, self.A, self.T, self.E
        NS = 4
        with ExitStack() as ph:
            def sb(nm, shape, dt, n=1):
                r = []
                for i in range(n):
                    t = ph.enter_context(nc.sbuf_tensor(self.name(nm), shape, dt))
                    r.append((t, Buf(nm + str(i))))
                return r if n > 1 else r[0]

            def ps(nm, shape, dt):
                t = psum_view(ph, nc, self.name(nm), shape, dt)
                return (t, Buf(nm))

            setup = ExitStack()

            def sbs(nm, shape, dt):
                t = setup.enter_context(nc.sbuf_tensor(self.name(nm), shape, dt))
                return (t, Buf(nm))

            idb, b_idb = sb("idb", [128, 128], BF16)
            wib, b_wib = sb("wib", [128, 8, 2080], BF16)
            wuq, b_wuq = sb("wuq", [128, 2, 576], BF16)
            wukv, b_wukv = sb("wukv", [128, 768], BF16)
            g8, b_g8 = sb("g8", [128, 8], F32)
            gq2, b_gq2 = sb("gq2", [128, 2], F32)
            gkv1, b_gkv1 = sb("gkv1", [128, 1], F32)
            ga, b_ga = sb("ga", [128, 2, 64], F32)
            gb, b_gb = sb("gb", [128, 2, 64], F32)
            gc, b_gc = sb("gc", [128, 2, 96], F32)
            GAq, b_GAq = sb("GAq", [128, 64], F32)
            GBq, b_GBq = sb("GBq", [128, 64], F32)
            GCq, b_GCq = sb("GCq", [128, 96], F32)
            invf, b_invf = sb("invf", [128, 16], F32)
            sin_t, b_sin = sb("sin_t", [128, 64, 16], F32)
            cos_t, b_cos = sb("cos_t", [128, 64, 16], F32)
            invd, b_invd = sb("invd", [128, 24], F32)
            nh24, b_nh24 = sb("nh24", [128, 24], F32)
            valid, b_valid = sb("valid", [128, 16], F32)
            posi, b_posi = sbs("posi", [128, 64], I32)
            posf, b_posf = sbs("posf", [128, 64], F32)
            ang, b_ang = sbs("ang", [128, 64, 16], F32)
            angk, b_angk = sbs("angk", [128, 64, 16], I32)
            angf, b_angf = sbs("angf", [128, 64, 16], F32)
            stage = [sbs("stage", [128, 2080], F32)] * 2
            stq, b_stq = sbs("stq", [128, 2, 576], F32)
            stkv, b_stkv = sbs("stkv", [128, 768], F32)

            self.dma('sync', idb[:], D['idb'].ap(), [], [b_idb], 'idb')
            self.dma('sync', g8[:], D['norm_mix'].ap()[L].rearrange("(c p) -> p c", p=128), [], [b_g8], 'g8',
                     allow_slow_non_contiguous=True)
            self.dma('sync', gq2[:], D['q_lat_gain'].ap()[L].rearrange("(c p) -> p c", p=128), [], [b_gq2], 'gq2',
                     allow_slow_non_contiguous=True)
            self.dma('sync', gkv1[:], D['kv_lat_gain'].ap()[L].rearrange("(c p) -> p c", p=128), [], [b_gkv1],
                     'gkv1', allow_slow_non_contiguous=True)
            self.dma('sync', ga[:], D['qk_gain_a'].ap()[L].rearrange("a d -> (a d)").partition_broadcast(128),
                     [], [b_ga], 'ga')
            self.dma('sync', gb[:], D['qk_gain_b'].ap()[L].rearrange("a d -> (a d)").partition_broadcast(128),
                     [], [b_gb], 'gb')
            self.dma('sync', gc[:], D['qk_gain_c'].ap()[L].rearrange("a d -> (a d)").partition_broadcast(128),
                     [], [b_gc], 'gc')
            self.dma('sync', invf[:], D['invf'].ap().rearrange("a d -> (a d)").partition_broadcast(128),
                     [], [b_invf], 'invf')
            self.dma('sync', posi[:], D[pos_key].ap(), [], [b_posi], 'posi')
            self.dma('sync', valid[:], D[valid_key].ap(), [], [b_valid], 'valid')
            V('scalar_tensor_tensor', [b_ga], [b_GAq], out=GAq[:], in0=ga[:, 0, :], scalar=0.125, in1=ga[:, 1, :],
              op0=ALU.mult, op1=ALU.mult)
            V('scalar_tensor_tensor', [b_gb], [b_GBq], out=GBq[:], in0=gb[:, 0, :], scalar=0.125, in1=gb[:, 1, :],
              op0=ALU.mult, op1=ALU.mult)
            V('tensor_scalar', [b_gc], [b_GCq], out=GCq[:], in0=gc[:, 0, :], scalar1=96.0 ** -0.5, scalar2=None,
              op0=ALU.mult)
            GCk = gc[:, 1, :]
            b_GCk = b_gc
            self.P.op('gpsimd', lambda e: e.memset(invd[:], 1.0 / 64), [], [b_invd])
            self.P.op('gpsimd', lambda e: e.memset(invd[:, 18:19], 1.0 / 256), [], [b_invd])
            self.P.op('gpsimd', lambda e: e.memset(invd[:, 19:20], 1.0 / 128), [], [b_invd])
            self.P.op('gpsimd', lambda e: e.memset(invd[:, 20:24], 1.0), [], [b_invd])
            self.P.op('gpsimd', lambda e: e.memset(nh24[:], -0.5), [], [b_nh24])

            V('tensor_copy', [b_posi], [b_posf], out=posf[:], in_=posi[:])
            V('tensor_tensor', [b_posf, b_invf], [b_ang], out=ang[:],
              in0=bc(posf[:].unsqueeze(2), [128, 64, 16]), in1=bc(invf[:].unsqueeze(1), [128, 64, 16]), op=ALU.mult)
            for (tab, b_tab, off) in ((sin_t, b_sin, 0.0), (cos_t, b_cos, 0.25)):
                V('tensor_scalar', [b_ang], [b_angf], out=angf[:], in0=ang[:], scalar1=1.0 / (2 * math.pi),
                  scalar2=off, op0=ALU.mult, op1=ALU.add)
                V('tensor_copy', [b_angf], [b_angk], out=angk[:], in_=angf[:])
                V('tensor_copy', [b_angk], [b_tab], out=tab[:], in_=angk[:])
                V('tensor_tensor', [b_angf, b_tab], [b_angf], out=angf[:], in0=angf[:], in1=tab[:], op=ALU.subtract)
                A('activation', [b_angf], [b_tab], out=tab[:], in_=angf[:], func=AF.Sin, scale=TWO_PI_S)

            blocks = [(0, 384, 0), (384, 768, 512), (768, 1152, 1024), (1152, 1408, 1536), (1408, 1536, 896),
                      (1536, 1664, 1408), (1664, 1920, 1792), (1920, 2048, 384), (2048, 2080, 2048)]
            k = 0
            for c in range(8):
                st_t, st_b = stage[c % 2]
                self.dma('sync', st_t[:], D['w_in'].ap()[L, c * 128:(c + 1) * 128, :], [], [st_b], 'stage0')
                for (o0, o1, n0) in blocks:
                    k += 1
                    if k % 2 == 0:
                        V('tensor_scalar', [st_b, b_g8], [b_wib], out=wib[:, c, n0:n0 + (o1 - o0)],
                          in0=st_t[:, o0:o1], scalar1=g8[:, c:c + 1], scalar2=None, op0=ALU.mult)
                    else:
                        A('activation', [st_b, b_g8], [b_wib], out=wib[:, c, n0:n0 + (o1 - o0)],
                          in_=st_t[:, o0:o1], func=AF.Copy, scale=g8[:, c:c + 1])
            self.dma('sync', stq[:], D['w_uq'].ap()[L].rearrange("(c p) n -> p c n", p=128), [], [b_stq], 'stq')
            self.dma('sync', stkv[:], D['w_ukv'].ap()[L], [], [b_stkv], 'stkv')
            for c in range(2):
                V('tensor_scalar', [b_stq, b_gq2], [b_wuq], out=wuq[:, c, :], in0=stq[:, c, :],
                  scalar1=gq2[:, c:c + 1], scalar2=None, op0=ALU.mult)
            V('tensor_scalar', [b_stkv, b_gkv1], [b_wukv], out=wukv[:], in0=stkv[:], scalar1=gkv1[:, 0:1],
              scalar2=None, op0=ALU.mult)

            self.P.phase_barrier()
            setup.close()
            xt = sb("xt", [128, 1024], F32, NS)
            junk, b_junk = sb("junk", [128, 1024], BF16)
            ssx = sb("ssx", [128, 1], F32, NS)
            rsx = sb("rsx", [128, 1], F32, NS)
            hb = sb("hb", [128, 1024], BF16, NS)
            hT = sb("hT", [128, 8, 128], BF16, NS)
            pj = sb("pj", [128, 2080], F32, NS)
            sq, b_sq = sb("sq", [128, 2080], F32)
            st = sb("st", [128, 24], F32, NS)
            rstd = sb("rstd", [128, 24], F32, NS)
            QAo = sb("QAo", [128, 384], BF16, NS)
            QAt = sb("QAt", [128, 384], F32, 1)
            KABo = sb("KABo", [128, 512], BF16, NS)
            QBo = sb("QBo", [128, 256], BF16, NS)
            QBt = sb("QBt", [128, 256], F32, 1)
            VABo = sb("VABo", [128, 8, 65], BF16, NS)
            LAT = sb("LAT", [128, 384], BF16, NS)
            latT = sb("latT", [128, 3, 128], BF16, NS)
            qcs = sb("qcs", [128, 576], F32, NS)
            kvcs = sb("kvcs", [128, 768], F32, NS)
            st2 = sb("st2", [128, 12], F32, NS)
            rstd2 = sb("rstd2", [128, 12], F32, NS)
            tmp1, b_tmp1 = sb("tmp1", [128, 6, 96], F32)
            trq, b_trq = sb("trq", [128, 6, 32], F32)
            tmpk, b_tmpk = sb("tmpk", [128, 6, 64], F32)
            krg, b_krg = sb("krg", [128, 1, 32], F32)
            krr, b_krr = sb("krr", [128, 1, 32], F32)
            rm = [sb("rm%d" % i, [128, 6, 16], F32) for i in range(4)]
            QCo = sb("QCo", [128, 6, 96], BF16, NS)
            KCo = sb("KCo", [128, 6, 96], BF16, NS)
            VCo = sb("VCo", [128, 6, 65], BF16, NS)
            QTs = sb("QTs", [96, 6, 128], BF16, NS)
            KTs = sb("KTs", [96, 6, 128], BF16, NS)
            TR, b_TR = ps("TR", [128, 8, 128], BF16)
            PJ = [ps("PJ%d" % i, [128, 512], F32) for i in range(5)]
            S = [ps("S%d" % i, [128, 512], F32) for i in range(2)]

            for (t_, b_) in VABo:
                self.P.op('gpsimd', lambda e, t_=t_: e.memset(t_[:], 1.0), [], [b_])
            for (t_, b_) in VCo:
                self.P.op('gpsimd', lambda e, t_=t_: e.memset(t_[:], 1.0), [], [b_])

            def rope(src, b_src, dst, b_dst, H, ti):
                cb = bc(cos_t[:, ti, :].unsqueeze(1), [128, H, 16])
                sbb = bc(sin_t[:, ti, :].unsqueeze(1), [128, H, 16])
                (m1, b1), (m2, b2), (m3, b3), (m4, b4) = rm
                V('tensor_tensor', [b_src, b_cos], [b1], out=m1[:, 0:H, :], in0=src[:, :, 0:16], in1=cb, op=ALU.mult)
                V('tensor_tensor', [b_src, b_sin], [b2], out=m2[:, 0:H, :], in0=src[:, :, 16:32], in1=sbb, op=ALU.mult)
                V('tensor_tensor', [b1, b2], [b_dst], out=dst[:, :, 0:16], in0=m1[:, 0:H, :], in1=m2[:, 0:H, :],
                  op=ALU.subtract)
                V('tensor_tensor', [b_src, b_cos], [b3], out=m3[:, 0:H, :], in0=src[:, :, 16:32], in1=cb, op=ALU.mult)
                V('tensor_tensor', [b_src, b_sin], [b4], out=m4[:, 0:H, :], in0=src[:, :, 0:16], in1=sbb, op=ALU.mult)
                V('tensor_tensor', [b3, b4], [b_dst], out=dst[:, :, 16:32], in0=m3[:, 0:H, :], in1=m4[:, 0:H, :],
                  op=ALU.add)

            SC = self.SC
            it = 0
            jobs = [('own', t) for t in range(32)] + [('oth', t) for t in range(32)] + [('halo', t) for t in range(16)]
            if skip_oth:
                jobs = [j for j in jobs if j[0] != 'oth']
            if self.debug and self.debug.get('p1_tiles'):
                jobs = self.debug['p1_tiles']
            def tile_gen(it, kind, t):
                s2 = it % NS
                s3 = it % NS
                src = {'own': x_own, 'oth': x_oth, 'halo': x_halo}[kind]
                x_t, b_x = xt[s3]
                ss_t, b_ss = ssx[s2]
                rs_t, b_rs = rsx[s2]
                hb_t, b_hb = hb[s2]
                hT_t, b_hT = hT[s2]
                pj_t, b_pj = pj[s3]
                st_t, b_st = st[s3]
                rstd_t, b_rstd = rstd[s3]
                if kind == 'halo':
                    V('tensor_scalar', [b_x, b_valid], [b_x], out=x_t[:], in0=x_t[:], scalar1=valid[:, t:t + 1],
                      scalar2=None, op0=ALU.mult)
                    yield
                A('activation', [b_x], [b_junk, b_ss], out=junk[:], in_=x_t[:], func=AF.Square, accum_out=ss_t[:])
                yield
                G('tensor_scalar', [b_ss], [b_ss], out=ss_t[:], in0=ss_t[:], scalar1=1.0 / 1024, scalar2=EPS,
                  op0=ALU.mult, op1=ALU.add)
                yield
                G('tensor_tensor', [b_ss, b_nh24], [b_rs], out=rs_t[:], in0=ss_t[:], in1=nh24[:, 0:1], op=ALU.pow)
                yield
                A('activation', [b_x, b_rs], [b_hb], out=hb_t[:], in_=x_t[:], func=AF.Copy, scale=rs_t[:, 0:1])
                yield
                for c in range(8):
                    T('transpose', [b_hb, b_idb], [b_TR], out=TR[:, c, :], in_=hb_t[:, c * 128:(c + 1) * 128],
                      identity=idb[:])
                V('tensor_copy', [b_TR], [b_hT], out=hT_t[:], in_=TR[:])
                yield
                if kind == 'own':
                    groups = [(0, 0, 512, 0), (1, 512, 1024, 0), (2, 1024, 1536, 0), (3, 1536, 2048, 0),
                              (4, 2048, 2080, 0)]
                elif kind == 'oth':
                    groups = [(0, 384, 512, 384), (4, 2048, 2080, 0)]
                else:
                    groups = [(1, 512, 1024, 0), (2, 1024, 1536, 0)]
                for (bk, c0, c1, po) in groups:
                    pt, pb = PJ[bk]
                    for c in range(8):
                        T('matmul', [b_hT, b_wib], [pb], out=pt[:, po:po + (c1 - c0)], lhsT=hT_t[:, c, :],
                          rhs=wib[:, c, c0:c1], start=(c == 0), stop=(c == 7))
                    A('activation', [pb], [b_pj], out=pj_t[:, c0:c1], in_=pt[:, po:po + (c1 - c0)], func=AF.Copy)
                    yield
                if kind == 'own':
                    V('tensor_tensor', [b_pj], [b_sq], out=sq[:, 0:1024], in0=pj_t[:, 0:1024], in1=pj_t[:, 0:1024],
                      op=ALU.mult)
                    V('tensor_tensor', [b_pj], [b_sq], out=sq[:, 1536:2080], in0=pj_t[:, 1536:2080],
                      in1=pj_t[:, 1536:2080], op=ALU.mult)
                    red = [(0, 6, 0, 384, 64), (6, 14, 512, 1024, 64), (14, 18, 1536, 1792, 64),
                           (18, 19, 1792, 2048, 256), (19, 20, 384, 512, 128), (20, 21, 2048, 2080, 32)]
                elif kind == 'oth':
                    V('tensor_tensor', [b_pj], [b_sq], out=sq[:, 384:512], in0=pj_t[:, 384:512], in1=pj_t[:, 384:512],
                      op=ALU.mult)
                    V('tensor_tensor', [b_pj], [b_sq], out=sq[:, 2048:2080], in0=pj_t[:, 2048:2080],
                      in1=pj_t[:, 2048:2080], op=ALU.mult)
                    red = [(19, 20, 384, 512, 128), (20, 21, 2048, 2080, 32)]
                else:
                    V('tensor_tensor', [b_pj], [b_sq], out=sq[:, 512:1024], in0=pj_t[:, 512:1024],
                      in1=pj_t[:, 512:1024], op=ALU.mult)
                    red = [(6, 14, 512, 1024, 64)]
                for (a0, a1, c0, c1, dd) in red:
                    V('tensor_reduce', [b_sq], [b_st], out=st_t[:, a0:a1],
                      in_=sq[:, c0:c1].rearrange("p (h d) -> p h d", d=dd), axis=AX.X, op=ALU.add)
                V('tensor_tensor', [b_st, b_invd], [b_rstd], out=rstd_t[:, 0:20], in0=st_t[:, 0:20], in1=invd[:, 0:20],
                  op=ALU.mult)
                yield
                V('tensor_scalar', [b_rstd], [b_rstd], out=rstd_t[:, 0:20], in0=rstd_t[:, 0:20], scalar1=EPS,
                  scalar2=None, op0=ALU.add)
                yield
                G('tensor_tensor', [b_rstd, b_nh24], [b_rstd], out=rstd_t[:, 0:20], in0=rstd_t[:, 0:20],
                  in1=nh24[:, 0:20], op=ALU.pow)
                yield
                if kind in ('own', 'halo'):
                    et = (8 + t) if kind == 'own' else (t if t < 8 else 40 + (t - 8))
                    kab_t, b_kab = KABo[s2]
                    vab_t, b_vab = VABo[s2]
                    V('tensor_tensor', [b_pj, b_rstd], [b_kab], out=kab_t[:].rearrange("p (h d) -> p h d", d=64),
                      in0=pj_t[:, 512:1024].rearrange("p (h d) -> p h d", d=64),
                      in1=bc(rstd_t[:, 6:14].unsqueeze(2), [128, 8, 64]), op=ALU.mult)
                    yield
                    V('tensor_copy', [b_pj], [b_vab], out=vab_t[:, :, 0:64],
                      in_=pj_t[:, 1024:1536].rearrange("p (h d) -> p h d", d=64))
                    yield
                    if kind == 'halo':
                        V('tensor_copy', [b_valid], [b_vab], out=vab_t[:, :, 64:65],
                          in_=bc(valid[:, t:t + 1].unsqueeze(1), [128, 8, 1]))
                        yield
                    else:
                        self.P.op('vector', lambda e, vab_t=vab_t: e.memset(vab_t[:, :, 64:65], 1.0), [], [b_vab])
                        yield
                    rows = slice(et * 128, (et + 1) * 128)
                    self.dma('sync', SC['KAx'].ap()[rows, :], kab_t[:, 0:384], [b_kab], [SC['b']], 'kabo%d' % s2)
                    yield
                    self.dma('sync', SC['KBx'].ap()[rows, :], kab_t[:, 384:512], [b_kab], [SC['b']], 'kabo%d' % s2)
                    yield
                    self.dma('sync', SC['VAx'].ap()[rows, :].rearrange("p (h d) -> p h d", d=65), vab_t[:, 0:6, :],
                             [b_vab], [SC['b']], 'vabo%d' % s2)
                    yield
                    self.dma('sync', SC['VBx'].ap()[rows, :].rearrange("p (h d) -> p h d", d=65), vab_t[:, 6:8, :],
                             [b_vab], [SC['b']], 'vabo%d' % s2)
                    yield
                if kind == 'own':
                    rows = slice(t * 128, (t + 1) * 128)
                    qa_t, b_qa = QAo[s2]
                    qat, b_qat = QAt
                    V('tensor_tensor', [b_pj, b_rstd], [b_qat], out=qat[:].rearrange("p (h d) -> p h d", d=64),
                      in0=pj_t[:, 0:384].rearrange("p (h d) -> p h d", d=64),
                      in1=bc(rstd_t[:, 0:6].unsqueeze(2), [128, 6, 64]), op=ALU.mult)
                    V('tensor_tensor', [b_qat, b_GAq], [b_qa], out=qa_t[:].rearrange("p (h d) -> p h d", d=64),
                      in0=qat[:].rearrange("p (h d) -> p h d", d=64),
                      in1=bc(GAq[:].unsqueeze(1), [128, 6, 64]), op=ALU.mult)
                    self.dma('sync', SC['QA'].ap()[rows, :], qa_t[:], [b_qa], [SC['b']], 'qao%d' % s2)
                    yield
                    qb_t, b_qb = QBo[s2]
                    qbt, b_qbt = QBt
                    V('tensor_tensor', [b_pj, b_rstd], [b_qbt], out=qbt[:].rearrange("p (h d) -> p h d", d=64),
                      in0=pj_t[:, 1536:1792].rearrange("p (h d) -> p h d", d=64),
                      in1=bc(rstd_t[:, 14:18].unsqueeze(2), [128, 4, 64]), op=ALU.mult)
                    V('tensor_tensor', [b_qbt, b_GBq], [b_qb],
                      out=qb_t[:].rearrange("p (b a d) -> p a b d", b=2, a=2, d=64),
                      in0=qbt[:].rearrange("p (a b d) -> p a b d", a=2, b=2, d=64),
                      in1=bc(GBq[:].unsqueeze(1).unsqueeze(1), [128, 2, 2, 64]), op=ALU.mult)
                    self.dma('sync', SC['QB'].ap()[rows, :], qb_t[:], [b_qb], [SC['b']], 'qbo%d' % s2)
                    yield
                if kind in ('own', 'oth'):
                    ti = t if kind == 'own' else 32 + t
                    lat_t, b_lat = LAT[s2]
                    latT_t, b_latT = latT[s2]
                    qcs_t, b_qcs = qcs[s2]
                    kvcs_t, b_kvcs = kvcs[s2]
                    st2_t, b_st2 = st2[s2]
                    rstd2_t, b_rstd2 = rstd2[s2]
                    if kind == 'own':
                        V('tensor_scalar', [b_pj, b_rstd], [b_lat], out=lat_t[:, 0:256], in0=pj_t[:, 1792:2048],
                          scalar1=rstd_t[:, 18:19], scalar2=None, op0=ALU.mult)
                        yield
                    if not skip_ck:
                        V('tensor_scalar', [b_pj, b_rstd], [b_lat], out=lat_t[:, 256:384], in0=pj_t[:, 384:512],
                          scalar1=rstd_t[:, 19:20], scalar2=None, op0=ALU.mult)
                        yield
                    jl = ([0, 1] if skip_ck else [0, 1, 2]) if kind == 'own' else [2]
                    for j in jl:
                        T('transpose', [b_lat, b_idb], [b_TR], out=TR[:, j, :], in_=lat_t[:, j * 128:(j + 1) * 128],
                          identity=idb[:])
                    V('tensor_copy', [b_TR], [b_latT], out=latT_t[:, jl[0]:jl[-1] + 1, :], in_=TR[:, jl[0]:jl[-1] + 1, :])
                    if kind == 'own':
                        for hf in range(2):
                            for c in range(2):
                                T('matmul', [b_latT, b_wuq], [S[hf][1]], out=S[hf][0][:, 0:288], lhsT=latT_t[:, c, :],
                                  rhs=wuq[:, c, hf * 288:(hf + 1) * 288], start=(c == 0), stop=(c == 1))
                            A('activation', [S[hf][1]], [b_qcs], out=qcs_t[:, hf * 288:(hf + 1) * 288],
                              in_=S[hf][0][:, 0:288], func=AF.Copy)
                    for hf in (range(2) if not skip_ck else []):
                        T('matmul', [b_latT, b_wukv], [S[hf][1]], out=S[hf][0][:, 0:384], lhsT=latT_t[:, 2, :],
                          rhs=wukv[:, hf * 384:(hf + 1) * 384], start=True, stop=True)
                        A('activation', [S[hf][1]], [b_kvcs], out=kvcs_t[:, hf * 384:(hf + 1) * 384],
                          in_=S[hf][0][:, 0:384], func=AF.Copy)
                    kv3 = kvcs_t[:].rearrange("p (h d) -> p h d", d=128)
                    if kind == 'own':
                        V('tensor_tensor', [b_qcs], [b_sq], out=sq[:, 0:576], in0=qcs_t[:], in1=qcs_t[:], op=ALU.mult)
                        V('tensor_reduce', [b_sq], [b_st2], out=st2_t[:, 0:6],
                          in_=sq[:, 0:576].rearrange("p (h d) -> p h d", d=96), axis=AX.X, op=ALU.add)
                    if not skip_ck:
                        V('tensor_tensor', [b_kvcs], [b_sq], out=sq[:, 1024:1408].rearrange("p (h d) -> p h d", d=64),
                          in0=kv3[:, :, 0:64], in1=kv3[:, :, 0:64], op=ALU.mult)
                        V('tensor_reduce', [b_sq], [b_st2], out=st2_t[:, 6:12],
                          in_=sq[:, 1024:1408].rearrange("p (h d) -> p h d", d=64), axis=AX.X, op=ALU.add)
                        V('tensor_scalar', [b_st2, b_st], [b_st2], out=st2_t[:, 6:12], in0=st2_t[:, 6:12],
                          scalar1=st_t[:, 20:21], scalar2=None, op0=ALU.add)
                    lo = 0 if kind == 'own' else 6
                    hi_ = 6 if skip_ck else 12
                    V('tensor_scalar', [b_st2], [b_rstd2], out=rstd2_t[:, lo:hi_], in0=st2_t[:, lo:hi_],
                      scalar1=1.0 / 96, scalar2=EPS, op0=ALU.mult, op1=ALU.add)
                    yield
                    G('tensor_tensor', [b_rstd2, b_nh24], [b_rstd2], out=rstd2_t[:, lo:hi_], in0=rstd2_t[:, lo:hi_],
                      in1=nh24[:, lo:hi_], op=ALU.pow)
                    yield
                    kc_t, b_kc = KCo[s2]
                    vc_t, b_vc = VCo[s2]
                    if kind == 'own':
                        qc_t, b_qc = QCo[s2]
                        V('tensor_tensor', [b_qcs, b_rstd2], [b_tmp1], out=tmp1[:],
                          in0=qcs_t[:].rearrange("p (h d) -> p h d", d=96),
                          in1=bc(rstd2_t[:, 0:6].unsqueeze(2), [128, 6, 96]), op=ALU.mult)
                        V('tensor_tensor', [b_tmp1, b_GCq], [b_qc], out=qc_t[:, :, 0:64], in0=tmp1[:, :, 0:64],
                          in1=bc(GCq[:, 0:64].unsqueeze(1), [128, 6, 64]), op=ALU.mult)
                        V('tensor_tensor', [b_tmp1, b_GCq], [b_trq], out=trq[:], in0=tmp1[:, :, 64:96],
                          in1=bc(GCq[:, 64:96].unsqueeze(1), [128, 6, 32]), op=ALU.mult)
                        rope(trq, b_trq, qc_t[:, :, 64:96], b_qc, 6, ti)
                    if kind == 'own':
                        qT_t, b_qT = QTs[s2]
                        for h in range(6):
                            T('transpose', [b_qc, b_idb], [b_TR], out=TR[0:96, h, :], in_=qc_t[:, h, :],
                              identity=idb[:])
                        V('tensor_copy', [b_TR], [b_qT], out=qT_t[:], in_=TR[0:96, 0:6, :])
                        self.dma('sync', SC['QCT'].ap()[:, :, t * 128:(t + 1) * 128].rearrange("h d n -> d h n"),
                                 qT_t[:], [b_qT], [SC['b']], 'qto%d' % s2)
                        yield
                    if skip_ck:
                        return
                    V('tensor_tensor', [b_kvcs, b_rstd2], [b_tmpk], out=tmpk[:], in0=kv3[:, :, 0:64],
                      in1=bc(rstd2_t[:, 6:12].unsqueeze(2), [128, 6, 64]), op=ALU.mult)
                    V('tensor_tensor', [b_tmpk, b_GCk], [b_kc], out=kc_t[:, :, 0:64], in0=tmpk[:],
                      in1=bc(GCk[:, 0:64].unsqueeze(1), [128, 6, 64]), op=ALU.mult)
                    V('tensor_tensor', [b_pj, b_GCk], [b_krg], out=krg[:, 0, :], in0=pj_t[:, 2048:2080],
                      in1=GCk[:, 64:96], op=ALU.mult)
                    rope(krg, b_krg, krr[:], b_krr, 1, ti)
                    V('tensor_tensor', [b_krr, b_rstd2], [b_kc], out=kc_t[:, :, 64:96],
                      in0=bc(krr[:], [128, 6, 32]), in1=bc(rstd2_t[:, 6:12].unsqueeze(2), [128, 6, 32]), op=ALU.mult)
                    V('tensor_copy', [b_kvcs], [b_vc], out=vc_t[:, :, 0:64], in_=kv3[:, :, 64:128])
                    yield
                    kT_t, b_kT = KTs[s2]
                    for h in range(6):
                        T('transpose', [b_kc, b_idb], [b_TR], out=TR[0:96, h, :], in_=kc_t[:, h, :], identity=idb[:])
                    V('tensor_copy', [b_TR], [b_kT], out=kT_t[:], in_=TR[0:96, 0:6, :])
                    self.dma('sync', SC['KCT'].ap()[:, :, ti * 128:(ti + 1) * 128].rearrange("h d n -> d h n"),
                             kT_t[:], [b_kT], [SC['b']], 'kto%d' % s2)
                    yield
                    self.dma('sync', SC['VC'].ap()[ti * 128:(ti + 1) * 128, :].rearrange("p (h d) -> p h d", d=65),
                             vc_t[:], [b_vc], [SC['b']], 'vco%d' % s2)
                    yield

            def issue_load(j):
                kind, t = jobs[j]
                x_t, b_x = xt[j % NS]
                if kind == 'halo' and halo_rows is not None:
                    self.dma('sync', x_t[:], halo_rows(t), [], [b_x], 'xt%d' % (j % NS))
                else:
                    src = {'own': x_own, 'oth': x_oth, 'halo': x_halo}[kind]
                    self.dma('sync', x_t[:], src[t * 128:(t + 1) * 128, :], [], [b_x], 'xt%d' % (j % NS))

            for j in range(min(2, len(jobs))):
                issue_load(j)
            active = []
            nxt = 0
            since = 10 ** 9
            STAG = 3
            while active or nxt < len(jobs):
                if nxt < len(jobs) and len(active) < NS and (since >= STAG or not active):
                    if nxt + 2 < len(jobs):
                        issue_load(nxt + 2)
                    active.append(tile_gen(nxt, jobs[nxt][0], jobs[nxt][1]))
                    nxt += 1
                    since = 0
                for g in list(active):
                    try:
                        next(g)
                    except StopIteration:
                        active.remove(g)
                since += 1
        self.P.phase_barrier()

    def phaseC(self, heads=range(6), nqt=8):
        nc, P, D, SC = self.nc, self.P, self.D, self.SC
        V, G, A, T, E = self.V, self.G, self.A, self.T, self.E
        with ExitStack() as ph:
            def sb(nm, shape, dt, n=1):
                r = []
                for i in range(n):
                    t = ph.enter_context(nc.sbuf_tensor(self.name(nm), shape, dt))
                    r.append((t, Buf(nm + str(i))))
                return r if n > 1 else r[0]

            def ps(nm, shape, dt, n=1):
                r = []
                for i in range(n):
                    t = psum_view(ph, nc, self.name(nm), shape, dt)
                    r.append((t, Buf(nm + str(i))))
                return r if n > 1 else r[0]

            P.barrier('sync', reads=[SC['b']])
            idf, b_idf = sb("idf", [128, 128], F32)
            self.dma('sync', idf[:], D['idf'].ap(), [], [b_idf], 'idf')
            KT = sb("cKT", [96, 8192], BF16, 2)
            VV = sb("cV", [128, 64, 65], BF16, 2)
            QT = sb("cQT", [96, 4096], BF16, 2)
            PT = sb("cPT", [128, 512], BF16, 4)
            OTs = sb("cOTs", [65, 512], F32, 2)
            rc = sb("crc", [128, 4], F32, 2)
            oc = sb("coc", [128, 4, 64], BF16, 2)
            ST = ps("cST", [128, 512], F32, 3)
            OT = ps("cOT", [65, 512], F32, 2)
            TO, b_TO = ps("cTO", [128, 4, 65], F32)

            heads = list(heads)

            def load_head(hi):
                h = heads[hi]
                s = hi % 2
                self.dma('sync', KT[s][0][:], SC['KCT'].ap()[h], [SC['b']], [KT[s][1]], 'cKT%d' % s)
                self.dma('sync', QT[s][0][:], SC['QCT'].ap()[h], [SC['b']], [QT[s][1]], 'cQT%d' % s)
                self.dma('sync', VV[s][0][:],
                         SC['VC'].ap().rearrange("(t p) c -> p t c", p=128)[:, :, h * 65:(h + 1) * 65],
                         [SC['b']], [VV[s][1]], 'cV%d' % s)

            steps = [(hi, qt, kc) for hi in range(len(heads)) for qt in range(nqt) for kc in range(64)]
            n = len(steps)
            LA = 2
            deferred = []
            load_head(0)
            nq = 0
            for i in range(n + LA):
                if i < n:
                    hi, qt, kc = steps[i]
                    s = hi % 2
                    st_t, st_b = ST[i % 3]
                    pt_t, pt_b = PT[i % 4]
                    T('matmul', [KT[s][1], QT[s][1]], [st_b], out=st_t[:], lhsT=KT[s][0][:, kc * 128:(kc + 1) * 128],
                      rhs=QT[s][0][:, qt * 512:(qt + 1) * 512], start=True, stop=True)
                    A('activation', [st_b], [pt_b], out=pt_t[:], in_=st_t[:], func=AF.Exp)
                j = i - LA
                if j >= 0:
                    hi, qt, kc = steps[j]
                    if qt == 0 and kc == 0 and hi + 1 < len(heads):
                        load_head(hi + 1)
                    s = hi % 2
                    qi = (hi * nqt + qt)
                    ot_t, ot_b = OT[qi % 2]
                    pt_t, pt_b = PT[j % 4]
                    T('matmul', [VV[s][1], pt_b], [ot_b], out=ot_t[:], lhsT=VV[s][0][:, kc, :], rhs=pt_t[:],
                      start=(kc == 0), stop=(kc == 63))
                    if kc == 63:
                        h = heads[hi]
                        os_t, os_b = OTs[qi % 2]
                        V('tensor_copy', [ot_b], [os_b], out=os_t[:], in_=ot_t[:])

                        def fin(os_t=os_t, os_b=os_b, qi=qi, qt=qt, h=h):
                            for jj in range(4):
                                T('transpose', [os_b, b_idf], [b_TO], out=TO[:, jj, :],
                                  in_=os_t[:, jj * 128:(jj + 1) * 128], identity=idf[0:65, 0:65])
                            rc_t, rc_b = rc[qi % 2]
                            oc_t, oc_b = oc[qi % 2]
                            V('reciprocal', [b_TO], [rc_b], out=rc_t[:].unsqueeze(2), in_=TO[:, :, 64:65])
                            V('tensor_tensor', [b_TO, rc_b], [oc_b], out=oc_t[:], in0=TO[:, :, 0:64],
                              in1=bc(rc_t[:].unsqueeze(2), [128, 4, 64]), op=ALU.mult)
                            self.dma('sync',
                                     SC['MIX'].ap()[qt * 512:(qt + 1) * 512, 640 + h * 64:640 + (h + 1) * 64]
                                     .rearrange("(t p) d -> p t d", p=128),
                                     oc_t[:], [oc_b], [SC['bmix']], 'coc%d' % (qi % 2))
                        deferred.append((i + 6, fin))
                while deferred and deferred[0][0] <= i:
                    deferred.pop(0)[1]()
            for (_, fn) in deferred:
                fn()
        self.P.phase_barrier()

    def bias_setup(self):
        nc, P, D, SC = self.nc, self.P, self.D, self.SC
        V, G, A, T, E = self.V, self.G, self.A, self.T, self.E
        with ExitStack() as ph:
            tabN = ph.enter_context(nc.sbuf_tensor(self.name("tabN"), [33, 10], F32)); b_tab = Buf("tabN")
            oh = ph.enter_context(nc.sbuf_tensor(self.name("oh"), [33, 1660], F32)); b_oh = Buf("oh")
            fv = ph.enter_context(nc.sbuf_tensor(self.name("fv"), [10, 1660], F32)); b_fv = Buf("fv")
            pf = [(psum_view(ph, nc, self.name("pf"), [10, 512], F32), Buf("pf%d" % i)) for i in range(4)]
            P.op('vector', lambda e: e.memset(tabN[:], NEGB), [], [b_tab])
            self.dma('sync', tabN[0:32, :], D['rel_bias_table'].ap(), [], [b_tab], 'tabN')
            self.dma('sync', oh[:], D['oh'].ap(), [], [b_oh], 'oh')
            segs = [(0, 511), (511, 894), (894, 1277), (1277, 1660)]
            for i, (a, b) in enumerate(segs):
                T('matmul', [b_tab, b_oh], [pf[i][1]], out=pf[i][0][:, 0:b - a], lhsT=tabN[:], rhs=oh[:, a:b],
                  start=True, stop=True)
                V('tensor_copy', [pf[i][1]], [b_fv], out=fv[:, a:b], in_=pf[i][0][:, 0:b - a])
            self.dma('sync', SC['FV'].ap(), fv[:], [b_fv], [SC['bfv']], 'fvo')
        self.P.phase_barrier()

    def phaseB(self, L):
        nc, P, D, SC = self.nc, self.P, self.D, self.SC
        V, G, A, T, E = self.V, self.G, self.A, self.T, self.E
        with ExitStack() as ph:
            def sb(nm, shape, dt, n=1):
                r = []
                for i in range(n):
                    t = ph.enter_context(nc.sbuf_tensor(self.name(nm), shape, dt))
                    r.append((t, Buf(nm + str(i))))
                return r if n > 1 else r[0]

            def ps(nm, shape, dt, n=1):
                r = []
                for i in range(n):
                    t = psum_view(ph, nc, self.name(nm), shape, dt)
                    r.append((t, Buf(nm + str(i))))
                return r if n > 1 else r[0]

            P.barrier('sync', reads=[SC['b'], SC['bfv']])
            idf, b_idf = sb("idf", [128, 128], F32)
            idb, b_idb = sb("idb", [128, 128], BF16)
            self.dma('sync', idf[:], D['idf'].ap(), [], [b_idf], 'idf')
            self.dma('sync', idb[:], D['idb'].ap(), [], [b_idb], 'idb')
            biasB = sb("biasB", [128, 4, 3, 128], F32)
            hk = sb("hkB", [128, 128], F32, 4)
            for h in range(4):
                for o in range(3):
                    g_t, g_b = hk[(h * 3 + o) % 4]
                    self.dma('sync', g_t[:], bass.AP(SC['FV'], (6 + h) * 1660 + 128 * o, [[1, 128], [1, 128]]),
                             [SC['bfv']], [g_b], 'hkB%d' % ((h * 3 + o) % 4))
                    V('tensor_copy', [g_b], [biasB[1]], out=biasB[0][:, h, o, :],
                      in_=bass.AP(g_t, 127, [[128, 128], [-1, 128]]))
            sk, b_sk = sb("sink", [128, 4], F32)
            esk, b_esk = sb("esink", [128, 4], F32)
            self.dma('sync', sk[:], D['sink_b'].ap()[L].partition_broadcast(128), [], [b_sk], 'sink')
            A('activation', [b_sk], [b_esk], out=esk[:], in_=sk[:], func=AF.Exp)
            Qc = sb("bQc", [128, 4, 256], BF16, 2)
            Kc = sb("bKc", [128, 6, 128], BF16, 2)
            Vc = sb("bVc", [128, 6, 130], BF16, 2)
            QTb = sb("bQT", [128, 2, 4, 128], BF16, 2)
            KTb = sb("bKT", [128, 6, 128], BF16, 2)
            Sb = sb("bS", [128, 3, 128], F32, 2)
            PTb = sb("bPT", [128, 3, 128], BF16, 2)
            OTs = sb("bOTs", [65, 4, 128], F32, 2)
            den = sb("bden", [128, 4], F32, 2)
            ob = sb("bo", [128, 4, 64], BF16, 2)
            TRq = ps("bTRq", [128, 2, 4, 128], BF16)
            TRk = ps("bTRk", [128, 6, 128], BF16)
            SP = ps("bSP", [128, 3, 128], F32, 2)
            OTp = ps("bOT", [65, 4, 128], F32, 2)
            TOp = ps("bTO", [128, 4, 65], F32)

            def stage1(J):
                s = J % 2
                self.dma('sync', Qc[s][0][:], SC['QB'].ap()[J * 512:(J + 1) * 512, :].rearrange("(t p) c -> p t c", p=128),
                         [SC['b']], [Qc[s][1]], 'bQc%d' % s)
                r0 = (7 + 4 * J) * 128
                self.dma('sync', Kc[s][0][:], SC['KBx'].ap()[r0:r0 + 768, :].rearrange("(t p) c -> p t c", p=128),
                         [SC['b']], [Kc[s][1]], 'bKc%d' % s)
                self.dma('sync', Vc[s][0][:], SC['VBx'].ap()[r0:r0 + 768, :].rearrange("(t p) c -> p t c", p=128),
                         [SC['b']], [Vc[s][1]], 'bVc%d' % s)
                for t in range(4):
                    for pi in range(2):
                        T('transpose', [Qc[s][1], b_idb], [TRq[1]], out=TRq[0][:, pi, t, :],
                          in_=Qc[s][0][:, t, pi * 128:(pi + 1) * 128], identity=idb[:])
                V('tensor_copy', [TRq[1]], [QTb[s][1]], out=QTb[s][0][:], in_=TRq[0][:])
                for kt in range(6):
                    T('transpose', [Kc[s][1], b_idb], [TRk[1]], out=TRk[0][:, kt, :], in_=Kc[s][0][:, kt, :],
                      identity=idb[:])
                V('tensor_copy', [TRk[1]], [KTb[s][1]], out=KTb[s][0][:], in_=TRk[0][:])

            cnt = [0]

            def stage2(J):
                s = J % 2
                for t in range(4):
                    qi = J * 4 + t
                    ot_t, ot_b = OTp[qi % 2]
                    for h in range(4):
                        base = 64 * (h // 2)
                        pi = h % 2
                        kvh = h // 2
                        c = cnt[0]
                        cnt[0] += 1
                        sp_t, sp_b = SP[c % 2]
                        for o in range(3):
                            T('matmul', [KTb[s][1], QTb[s][1]], [sp_b], out=sp_t[:, o, :],
                              lhsT=KTb[s][0][base:base + 64, t + o, :], rhs=QTb[s][0][base:base + 64, pi, t, :],
                              start=True, stop=True)
                        s_t, s_b = Sb[c % 2]
                        p_t, p_b = PTb[c % 2]
                        V('tensor_tensor', [sp_b, biasB[1]], [s_b], out=s_t[:], in0=sp_t[:], in1=biasB[0][:, h, :, :],
                          op=ALU.add)
                        A('activation', [s_b], [p_b], out=p_t[:], in_=s_t[:], func=AF.Exp)
                        for o in range(3):
                            T('matmul', [Vc[s][1], p_b], [ot_b], out=ot_t[:, h, :],
                              lhsT=Vc[s][0][:, t + o, kvh * 65:(kvh + 1) * 65], rhs=p_t[:, o, :],
                              start=(o == 0), stop=(o == 2))
                    os_t, os_b = OTs[qi % 2]
                    V('tensor_copy', [ot_b], [os_b], out=os_t[:], in_=ot_t[:])
                    for h in range(4):
                        T('transpose', [os_b, b_idf], [TOp[1]], out=TOp[0][:, h, :], in_=os_t[:, h, :],
                          identity=idf[0:65, 0:65])
                    d_t, d_b = den[qi % 2]
                    o_t, o_b = ob[qi % 2]
                    V('tensor_tensor', [TOp[1], b_esk], [d_b], out=d_t[:].unsqueeze(2), in0=TOp[0][:, :, 64:65],
                      in1=esk[:].unsqueeze(2), op=ALU.add)
                    V('reciprocal', [d_b], [d_b], out=d_t[:], in_=d_t[:])
                    V('tensor_tensor', [TOp[1], d_b], [o_b], out=o_t[:], in0=TOp[0][:, :, 0:64],
                      in1=bc(d_t[:].unsqueeze(2), [128, 4, 64]), op=ALU.mult)
                    self.dma('sync', SC['MIX'].ap()[qi * 128:(qi + 1) * 128, 384:640], o_t[:], [o_b], [SC['bmix']],
                             'bo%d' % (qi % 2))

            stage1(0)
            for J in range(8):
                if J + 1 < 8:
                    stage1(J + 1)
                stage2(J)
        self.P.phase_barrier()

    def phaseA(self, cfgs=(0, 1, 2)):
        nc, P, D, SC = self.nc, self.P, self.D, self.SC
        V, G, A, T, E = self.V, self.G, self.A, self.T, self.E
        RS = (1, 4, 16)
        with ExitStack() as ph:
            def sb(nm, shape, dt, n=1):
                r = []
                for i in range(n):
                    t = ph.enter_context(nc.sbuf_tensor(self.name(nm), shape, dt))
                    r.append((t, Buf(nm + str(i))))
                return r if n > 1 else r[0]

            def ps(nm, shape, dt, n=1):
                r = []
                for i in range(n):
                    t = psum_view(ph, nc, self.name(nm), shape, dt)
                    r.append((t, Buf(nm + str(i))))
                return r if n > 1 else r[0]

            P.barrier('sync', reads=[SC['b'], SC['bfv']])
            idf, b_idf = sb("idf", [128, 128], F32)
            idb, b_idb = sb("idb", [128, 128], BF16)
            self.dma('sync', idf[:], D['idf'].ap(), [], [b_idf], 'idf')
            self.dma('sync', idb[:], D['idb'].ap(), [], [b_idb], 'idb')
            biasA = sb("biasA", [128, 3, 6, 2, 128], F32)
            hk = sb("hkA", [128, 128], F32, 4)
            for ci in range(3):
                for h in range(6):
                    for c in range(2):
                        kk = (ci * 6 + h) * 2 + c
                        g_t, g_b = hk[kk % 4]
                        self.dma('sync', g_t[:],
                                 bass.AP(SC['FV'], h * 1660 + 511 + 383 * ci + 128 * c, [[1, 128], [1, 128]]),
                                 [SC['bfv']], [g_b], 'hkA%d' % (kk % 4))
                        V('tensor_copy', [g_b], [biasA[1]], out=biasA[0][:, ci, h, c, :],
                          in_=bass.AP(g_t, 127, [[128, 128], [-1, 128]]))
            Qa = sb("aQ", [128, 384], BF16, 3)
            Ka = sb("aK", [128, 2, 384], BF16, 3)
            Va = sb("aV", [128, 2, 390], BF16, 3)
            QTa = sb("aQT", [128, 2, 3, 128], BF16, 2)
            for (t_, b_) in QTa:
                P.op('vector', lambda e, t_=t_: e.memset(t_[:], 0.0), [], [b_])
            KTa = sb("aKT", [128, 3, 2, 128], BF16, 2)
            Sa = sb("aS", [128, 6, 2, 128], F32, 2)
            PTa = sb("aPT", [128, 6, 2, 128], BF16, 2)
            OTs = sb("aOTs", [65, 6, 128], F32, 2)
            Oo = sb("aOo", [128, 6, 65], F32, 2)
            TRq = ps("aTRq", [128, 3, 128], BF16)
            TRk = ps("aTRk", [128, 3, 2, 128], BF16)
            SP = ps("aSP", [128, 2, 2, 128], F32, 3)
            OTp = ps("aOT", [65, 3, 128], F32, 2)
            TOp = ps("aTO", [128, 6, 65], F32)

            jobs = []
            for ci in cfgs:
                r = RS[ci]
                for rho in range(r):
                    for j in range(32 // r):
                        jobs.append((ci, r, rho, j))

            if self.debug and self.debug.get('a_jobs'):
                jobs = self.debug['a_jobs']

            def loadA(i):
                ci, r, rho, j = jobs[i]
                s3 = i % 3
                self.dma('sync', Qa[s3][0][:], bass.AP(SC['QA'], (r * 128 * j + rho) * 384, [[r * 384, 128], [1, 384]]),
                         [SC['b']], [Qa[s3][1]], 'aQ%d' % s3)
                e0 = r * (128 * j - 64) + rho + 1024
                self.dma('sync', Ka[s3][0][:],
                         bass.AP(SC['KAx'], e0 * 384, [[r * 384, 128], [r * 384 * 128, 2], [1, 384]]),
                         [SC['b']], [Ka[s3][1]], 'aK%d' % s3)
                self.dma('sync', Va[s3][0][:],
                         bass.AP(SC['VAx'], e0 * 390, [[r * 390, 128], [r * 390 * 128, 2], [1, 390]]),
                         [SC['b']], [Va[s3][1]], 'aV%d' % s3)

            def stage1(i):
                ci, r, rho, j = jobs[i]
                s3 = i % 3
                s2 = i % 2
                for pi in range(3):
                    T('transpose', [Qa[s3][1], b_idb], [TRq[1]], out=TRq[0][:, pi, :],
                      in_=Qa[s3][0][:, pi * 128:(pi + 1) * 128], identity=idb[:])
                V('tensor_copy', [TRq[1]], [QTa[s2][1]], out=QTa[s2][0][0:64, 0, :, :], in_=TRq[0][0:64, :, :])
                V('tensor_copy', [TRq[1]], [QTa[s2][1]], out=QTa[s2][0][64:128, 1, :, :], in_=TRq[0][64:128, :, :])
                for pi in range(3):
                    for c in range(2):
                        T('transpose', [Ka[s3][1], b_idb], [TRk[1]], out=TRk[0][:, pi, c, :],
                          in_=Ka[s3][0][:, c, pi * 128:(pi + 1) * 128], identity=idb[:])
                V('tensor_copy', [TRk[1]], [KTa[s2][1]], out=KTa[s2][0][:], in_=TRk[0][:])

            astop = self.debug.get('a_stop', 99) if self.debug else 99

            def stage2(i):
                ci, r, rho, j = jobs[i]
                s3 = i % 3
                s2 = i % 2
                s_t, s_b = Sa[s2]
                p_t, p_b = PTa[s2]
                if astop < 2:
                    return
                for pi in range(3):
                    sp_t, sp_b = SP[pi]
                    for hh in range(2):
                        for c in range(2):
                            T('matmul', [KTa[s2][1], QTa[s2][1]], [sp_b], out=sp_t[:, hh, c, :],
                              lhsT=KTa[s2][0][:, pi, c, :],
                              rhs=QTa[s2][0][:, hh, pi, :], start=True, stop=True)
                    if self.debug and self.debug.get('a_nobias'):
                        continue
                    V('tensor_tensor', [sp_b, biasA[1]], [s_b], out=s_t[:, 2 * pi:2 * pi + 2, :, :], in0=sp_t[:],
                      in1=biasA[0][:, ci, 2 * pi:2 * pi + 2, :, :], op=ALU.add)
                if astop < 3:
                    return
                A('activation', [s_b], [p_b], out=p_t[:], in_=s_t[:], func=AF.Exp)
                os_t, os_b = OTs[s2]
                if astop < 4:
                    return
                for g3 in range(2):
                    ot_t, ot_b = OTp[g3]
                    for hh in range(3):
                        h = g3 * 3 + hh
                        for c in range(2):
                            T('matmul', [Va[s3][1], p_b], [ot_b], out=ot_t[:, hh, :],
                              lhsT=Va[s3][0][:, c, h * 65:(h + 1) * 65], rhs=p_t[:, h, c, :],
                              start=(c == 0), stop=(c == 1))
                    V('tensor_copy', [ot_b], [os_b], out=os_t[:, g3 * 3:g3 * 3 + 3, :], in_=ot_t[:])
                if astop < 5:
                    return
                for h in range(6):
                    T('transpose', [os_b, b_idf], [TOp[1]], out=TOp[0][:, h, :], in_=os_t[:, h, :],
                      identity=idf[0:65, 0:65])
                o_t, o_b = Oo[s2]
                V('tensor_copy', [TOp[1]], [o_b], out=o_t[:], in_=TOp[0][:])
                self.dma('sync', bass.AP(SC['OA'], ci * 4096 * 390 + (r * 128 * j + rho) * 390, [[r * 390, 128], [1, 390]]),
                         o_t[:].rearrange("p h d -> p (h d)"), [o_b], [SC['boa']], 'aOo%d' % s2)

            n = len(jobs)
            for i in range(min(2, n)):
                loadA(i)
            stage1(0)
            for i in range(n):
                if i + 2 < n:
                    loadA(i + 2)
                if i + 1 < n:
                    stage1(i + 1)
                stage2(i)
        self.P.phase_barrier()

    def phaseA2(self):
        nc, P, D, SC = self.nc, self.P, self.D, self.SC
        V, G, A, T, E = self.V, self.G, self.A, self.T, self.E
        with ExitStack() as ph:
            def sb(nm, shape, dt, n=1):
                r = []
                for i in range(n):
                    t = ph.enter_context(nc.sbuf_tensor(self.name(nm), shape, dt))
                    r.append((t, Buf(nm + str(i))))
                return r if n > 1 else r[0]
            P.barrier('sync', reads=[SC['boa']])
            O3 = sb("a2O", [128, 3, 6, 65], F32, 3)
            acc = sb("a2acc", [128, 6, 65], F32, 2)
            rc = sb("a2rc", [128, 6], F32, 2)
            oo = sb("a2o", [128, 6, 64], BF16, 2)
            for t in range(32):
                s3, s2 = t % 3, t % 2
                self.dma('sync', O3[s3][0][:].rearrange("p c h d -> p c (h d)"),
                         SC['OA'].ap()[:, t * 128:(t + 1) * 128, :].rearrange("c p d -> p c d"),
                         [SC['boa']], [O3[s3][1]], 'a2O%d' % s3)
                o3 = O3[s3][0]
                V('tensor_tensor', [O3[s3][1]], [acc[s2][1]], out=acc[s2][0][:], in0=o3[:, 0], in1=o3[:, 1], op=ALU.add)
                V('tensor_tensor', [O3[s3][1], acc[s2][1]], [acc[s2][1]], out=acc[s2][0][:], in0=acc[s2][0][:],
                  in1=o3[:, 2], op=ALU.add)
                V('reciprocal', [acc[s2][1]], [rc[s2][1]], out=rc[s2][0][:].unsqueeze(2), in_=acc[s2][0][:, :, 64:65])
                V('tensor_tensor', [acc[s2][1], rc[s2][1]], [oo[s2][1]], out=oo[s2][0][:], in0=acc[s2][0][:, :, 0:64],
                  in1=bc(rc[s2][0][:].unsqueeze(2), [128, 6, 64]), op=ALU.mult)
                self.dma('sync', SC['MIX'].ap()[t * 128:(t + 1) * 128, 0:384], oo[s2][0][:].rearrange("p h d -> p (h d)"),
                         [oo[s2][1]], [SC['bmix']], 'a2o%d' % s2)
        self.P.phase_barrier()

    def phase4a(self, L, x_own, x1_d):
        nc, P, D, SC = self.nc, self.P, self.D, self.SC
        V, G, A, T, E = self.V, self.G, self.A, self.T, self.E
        with ExitStack() as ph:
            def sb(nm, shape, dt, n=1):
                r = []
                for i in range(n):
                    t = ph.enter_context(nc.sbuf_tensor(self.name(nm), shape, dt))
                    r.append((t, Buf(nm + str(i))))
                return r if n > 1 else r[0]

            def ps(nm, shape, dt, n=1):
                r = []
                for i in range(n):
                    t = psum_view(ph, nc, self.name(nm), shape, dt)
                    r.append((t, Buf(nm + str(i))))
                return r if n > 1 else r[0]

            P.barrier('sync', reads=[SC['bmix']])
            idb, b_idb = sb("idb", [128, 128], BF16)
            self.dma('sync', idb[:], D['idb'].ap(), [], [b_idb], 'idb')
            wo, b_wo = sb("wo", [128, 8, 1024], BF16)
            go, b_go = sb("go", [128, 8], F32)
            self.dma('sync', go[:], D['out_norm'].ap()[L].rearrange("(c p) -> p c", p=128), [], [b_go], 'go',
                     allow_slow_non_contiguous=True)
            stg = sb("stg4a", [128, 1024], F32, 2)
            for c in range(8):
                self.dma('sync', stg[c % 2][0][:], D['w_out'].ap()[L, c * 128:(c + 1) * 128, :], [], [stg[c % 2][1]],
                         'stg4a%d' % (c % 2))
                if c % 2 == 0:
                    V('tensor_scalar', [stg[c % 2][1], b_go], [b_wo], out=wo[:, c, :], in0=stg[c % 2][0][:],
                      scalar1=go[:, c:c + 1], scalar2=None, op0=ALU.mult)
                else:
                    A('activation', [stg[c % 2][1], b_go], [b_wo], out=wo[:, c, :], in_=stg[c % 2][0][:], func=AF.Copy,
                      scale=go[:, c:c + 1])
            invd3, b_invd3 = sb("invd3", [128, 3], F32)
            nh3, b_nh3 = sb("nh3", [128, 3], F32)
            P.op('gpsimd', lambda e: e.memset(invd3[:], 1.0 / 384), [], [b_invd3])
            P.op('gpsimd', lambda e: e.memset(invd3[:, 1:2], 1.0 / 256), [], [b_invd3])
            P.op('gpsimd', lambda e: e.memset(nh3[:], -0.5), [], [b_nh3])
            mx = sb("mx", [128, 1024], BF16, 3)
            xt = sb("x4a", [128, 1024], F32, 3)
            sq, b_sq = sb("sq4a", [128, 1024], F32)
            ss = sb("ss4a", [128, 3], F32, 2)
            rs = sb("rs4a", [128, 3], F32, 2)
            mn = sb("mn", [128, 1024], BF16, 2)
            mT = sb("mT", [128, 8, 128], BF16, 2)
            TR = ps("TR4a", [128, 8, 128], BF16, 2)
            Y = ps("Y4a", [128, 1024], F32, 2)
            grp = [(0, 384), (384, 640), (640, 1024)]
            def load4a(t):
                s3 = t % 3
                rows = slice(t * 128, (t + 1) * 128)
                self.dma('sync', mx[s3][0][:], SC['MIX'].ap()[rows, :], [SC['bmix']], [mx[s3][1]], 'mx%d' % s3)
                self.dma('sync', xt[s3][0][:], x_own[rows, :], [], [xt[s3][1]], 'x4a%d' % s3)
            load4a(0)

            def s1_4a(t):
                s3, s2 = t % 3, t % 2
                rows = slice(t * 128, (t + 1) * 128)
                if t + 1 < 32:
                    load4a(t + 1)
                m_t, m_b = mx[s3]
                V('tensor_tensor', [m_b], [b_sq], out=sq[:], in0=m_t[:], in1=m_t[:], op=ALU.mult)
                for gi, (a, b) in enumerate(grp):
                    V('tensor_reduce', [b_sq], [ss[s2][1]], out=ss[s2][0][:, gi:gi + 1], in_=sq[:, a:b], axis=AX.X,
                      op=ALU.add)
                V('tensor_tensor', [ss[s2][1], b_invd3], [rs[s2][1]], out=rs[s2][0][:], in0=ss[s2][0][:], in1=invd3[:],
                  op=ALU.mult)
                V('tensor_scalar', [rs[s2][1]], [rs[s2][1]], out=rs[s2][0][:], in0=rs[s2][0][:], scalar1=EPS, scalar2=None,
                  op0=ALU.add)
                G('tensor_tensor', [rs[s2][1], b_nh3], [rs[s2][1]], out=rs[s2][0][:], in0=rs[s2][0][:], in1=nh3[:],
                  op=ALU.pow)
                for gi, (a, b) in enumerate(grp):
                    E('vector', 'tensor_scalar', [m_b, rs[s2][1]], [mn[s2][1]],
                      out=mn[s2][0][:, a:b], in0=m_t[:, a:b], scalar1=rs[s2][0][:, gi:gi + 1], scalar2=None, op0=ALU.mult)
                for c in range(8):
                    T('transpose', [mn[s2][1], b_idb], [TR[s2][1]], out=TR[s2][0][:, c, :],
                      in_=mn[s2][0][:, c * 128:(c + 1) * 128], identity=idb[:])
                A('activation', [TR[s2][1]], [mT[s2][1]], out=mT[s2][0][:], in_=TR[s2][0][:], func=AF.Copy)

            def s2_4a(t):
                s3, s2 = t % 3, t % 2
                rows = slice(t * 128, (t + 1) * 128)
                for hf in range(2):
                    for c in range(8):
                        T('matmul', [mT[s2][1], b_wo], [Y[s2][1]], out=Y[s2][0][:, hf * 512:(hf + 1) * 512],
                          lhsT=mT[s2][0][:, c, :], rhs=wo[:, c, hf * 512:(hf + 1) * 512], start=(c == 0), stop=(c == 7))
                V('tensor_tensor', [Y[s2][1], xt[s3][1]], [xt[s3][1]], out=xt[s3][0][:], in0=Y[s2][0][:],
                  in1=xt[s3][0][:], op=ALU.add)
                self.dma('sync', x1_d[rows, :], xt[s3][0][:], [xt[s3][1]], [SC['bx1']], 'x4ao%d' % s3)

            s1_4a(0)
            for t in range(32):
                if t + 1 < 32:
                    s1_4a(t + 1)
                s2_4a(t)
        self.P.phase_barrier()

    def phase4b(self, L, x1_d, out_d, b_out):
        nc, P, D, SC = self.nc, self.P, self.D, self.SC
        V, G, A, T, E = self.V, self.G, self.A, self.T, self.E
        with ExitStack() as ph:
            def sb(nm, shape, dt, n=1):
                r = []
                for i in range(n):
                    t = ph.enter_context(nc.sbuf_tensor(self.name(nm), shape, dt))
                    r.append((t, Buf(nm + str(i))))
                return r if n > 1 else r[0]

            def ps(nm, shape, dt, n=1):
                r = []
                for i in range(n):
                    t = psum_view(ph, nc, self.name(nm), shape, dt)
                    r.append((t, Buf(nm + str(i))))
                return r if n > 1 else r[0]

            P.barrier('sync', reads=[SC['bx1']])
            idb, b_idb = sb("idb", [128, 128], BF16)
            self.dma('sync', idb[:], D['idb'].ap(), [], [b_idb], 'idb')
            wu, b_wu = sb("wu", [128, 8, 4096], BF16)
            wd, b_wd = sb("wd", [128, 32, 1024], BF16)
            gm, b_gm = sb("gm", [128, 8], F32)
            self.dma('sync', gm[:], D['norm_mlp'].ap()[L].rearrange("(c p) -> p c", p=128), [], [b_gm], 'gm',
                     allow_slow_non_contiguous=True)
            stg = sb("stg4b", [128, 1024], F32, 2)
            k = 0
            for c in range(8):
                for hf in range(4):
                    s = k % 2
                    self.dma('sync', stg[s][0][:], D['w_up'].ap()[L, c * 128:(c + 1) * 128, hf * 1024:(hf + 1) * 1024], [],
                             [stg[s][1]], 'stg4b%d' % s)
                    if k % 2 == 0:
                        V('tensor_scalar', [stg[s][1], b_gm], [b_wu], out=wu[:, c, hf * 1024:(hf + 1) * 1024],
                          in0=stg[s][0][:], scalar1=gm[:, c:c + 1], scalar2=None, op0=ALU.mult)
                    else:
                        A('activation', [stg[s][1], b_gm], [b_wu], out=wu[:, c, hf * 1024:(hf + 1) * 1024],
                          in_=stg[s][0][:], func=AF.Copy, scale=gm[:, c:c + 1])
                    k += 1
            for c2 in range(32):
                s = k % 2
                self.dma('sync', stg[s][0][:], D['w_down'].ap()[L, c2 * 128:(c2 + 1) * 128, :], [],
                         [stg[s][1]], 'stg4b%d' % s)
                if k % 2 == 0:
                    V('tensor_copy', [stg[s][1]], [b_wd], out=wd[:, c2, :], in_=stg[s][0][:])
                else:
                    A('activation', [stg[s][1]], [b_wd], out=wd[:, c2, :], in_=stg[s][0][:], func=AF.Copy)
                k += 1
            nh1, b_nh1 = sb("nh1", [128, 1], F32)
            P.op('gpsimd', lambda e: e.memset(nh1[:], -0.5), [], [b_nh1])
            xc = sb("x4b", [128, 2, 1024], F32, 2)
            junk, b_junk = sb("junk4b", [128, 1024], BF16)
            ss = sb("ss4b", [128, 2], F32, 2)
            h2 = [sb("h2", [128, 2, 1024], BF16)] * 2
            h2T = [sb("h2T", [128, 8, 256], BF16)] * 2
            uT = [sb("uT", [128, 32, 256], BF16)] * 2
            rr = sb("rr", [128, 2, 256], F32, 3)
            TR = ps("TR4b", [128, 8, 128], BF16)
            U = ps("U4b", [128, 2, 256], F32, 2)
            Z = ps("Z4b", [128, 1024], F32, 2)
            def load4b(ch):
                s2 = ch % 2
                rows = slice(ch * 256, (ch + 1) * 256)
                self.dma('sync', xc[s2][0][:], x1_d[rows, :].rearrange("(t p) d -> p t d", p=128), [SC['bx1']],
                         [xc[s2][1]], 'x4b%d' % s2)
            load4b(0)
            for ch in range(16):
                s2 = ch % 2
                rows = slice(ch * 256, (ch + 1) * 256)
                x_t, x_b = xc[s2]
                if ch + 1 < 16:
                    load4b(ch + 1)
                for t in range(2):
                    A('activation', [x_b], [b_junk, ss[s2][1]], out=junk[:], in_=x_t[:, t, :], func=AF.Square,
                      accum_out=ss[s2][0][:, t:t + 1])
                G('tensor_scalar', [ss[s2][1]], [ss[s2][1]], out=ss[s2][0][:], in0=ss[s2][0][:], scalar1=1.0 / 1024,
                  scalar2=EPS, op0=ALU.mult, op1=ALU.add)
                G('tensor_tensor', [ss[s2][1], b_nh1], [ss[s2][1]], out=ss[s2][0][:], in0=ss[s2][0][:],
                  in1=bc(nh1[:], [128, 2]), op=ALU.pow)
                for t in range(2):
                    A('activation', [x_b, ss[s2][1]], [h2[s2][1]], out=h2[s2][0][:, t, :], in_=x_t[:, t, :], func=AF.Copy,
                      scale=ss[s2][0][:, t:t + 1])
                    for c in range(8):
                        T('transpose', [h2[s2][1], b_idb], [TR[1]], out=TR[0][:, c, :],
                          in_=h2[s2][0][:, t, c * 128:(c + 1) * 128], identity=idb[:])
                    V('tensor_copy', [TR[1]], [h2T[s2][1]], out=h2T[s2][0][:, :, t * 128:(t + 1) * 128], in_=TR[0][:])
                for f2 in range(16):
                    u_t, u_b = U[f2 % 2]
                    for ff in range(2):
                        fc = f2 * 2 + ff
                        for c in range(8):
                            T('matmul', [h2T[s2][1], b_wu], [u_b], out=u_t[:, ff, :], lhsT=wu[:, c, fc * 128:(fc + 1) * 128],
                              rhs=h2T[s2][0][:, c, :], start=(c == 0), stop=(c == 7))
                    r_t, r_b = rr[f2 % 3]
                    A('activation', [u_b], [r_b], out=r_t[:], in_=u_t[:], func=AF.Relu)
                    E('vector' if f2 % 2 == 0 else 'gpsimd', 'tensor_tensor', [r_b], [uT[s2][1]],
                      out=uT[s2][0][:, 2 * f2:2 * f2 + 2, :], in0=r_t[:], in1=r_t[:], op=ALU.mult)
                for t in range(2):
                    z_t, z_b = Z[t]
                    for hf in range(2):
                        for fc in range(32):
                            T('matmul', [uT[s2][1], b_wd], [z_b], out=z_t[:, hf * 512:(hf + 1) * 512],
                              lhsT=uT[s2][0][:, fc, t * 128:(t + 1) * 128], rhs=wd[:, fc, hf * 512:(hf + 1) * 512],
                              start=(fc == 0), stop=(fc == 31))
                    V('tensor_tensor', [z_b, x_b], [x_b], out=x_t[:, t, :], in0=z_t[:], in1=x_t[:, t, :], op=ALU.add)
                self.dma('sync', out_d[rows, :].rearrange("(t p) d -> p t d", p=128), x_t[:], [x_b], [b_out],
                         'x4bo%d' % s2)
        self.P.phase_barrier()

    def layer(self, L, x_own, x_oth, x_halo, x1_d, out_d, b_out, with_bias_setup=True, phases=None, pos_key='pos',
              valid_key='valid', halo_rows=None, reuse_ckv=False):
        ph = phases or ['bias', 'p1', 'C', 'B', 'A', 'A2', '4a', '4b']
        if with_bias_setup and 'bias' in ph:
            self.bias_setup()
        if 'p1' in ph:
            self.phase1(L, x_own, x_oth, x_halo, pos_key=pos_key, valid_key=valid_key, halo_rows=halo_rows,
                        skip_oth=reuse_ckv, skip_ck=reuse_ckv)
        if 'C' in ph:
            self.phaseC()
        if 'B' in ph:
            self.phaseB(L)
        if 'A' in ph:
            self.phaseA(cfgs=self.debug.get('cfgs', (0, 1, 2)) if self.debug else (0, 1, 2))
        if 'A2' in ph:
            self.phaseA2()
        if '4a' in ph:
            self.phase4a(L, x_own, x1_d)
        if '4b' in ph:
            self.phase4b(L, x1_d, out_d, b_out)


NLAYER = 2


def t5_bucket_np(rel):
    half, exact = 16, 8
    n = np.abs(rel)
    far = exact + (np.log(np.maximum(n, 1).astype(np.float32) / np.float32(exact))
                   / np.float32(math.log(1024 / exact)) * np.float32(half - exact)).astype(np.int32)
    far = np.minimum(far, half - 1)
    return np.where(rel > 0, half, 0) + np.where(n < exact, n, far)


def make_oh():
    oh = np.zeros((33, 1660), np.float32)
    d = np.arange(-255, 256)
    bk = t5_bucket_np(d)
    for i, dd in enumerate(d):
        if abs(dd) <= 128:
            oh[bk[i], i] = 1
        else:
            oh[32, i] = 1
    for ci, r in enumerate((1, 4, 16)):
        d = np.arange(-191, 192)
        bk = t5_bucket_np(d * r)
        for i, dd in enumerate(d):
            if abs(dd) <= 64:
                oh[bk[i], 511 + 383 * ci + i] = 1
            else:
                oh[32, 511 + 383 * ci + i] = 1
    return oh


WNAMES = ['norm_mix', 'w_in', 'qk_gain_a', 'qk_gain_b', 'sink_b', 'q_lat_gain', 'kv_lat_gain', 'w_uq', 'w_ukv',
          'qk_gain_c', 'out_norm', 'w_out', 'norm_mlp', 'w_up', 'w_down']


def declare(nc, NL):
    D = {}

    def inp(n, shape, dt=F32):
        D[n] = nc.dram_tensor(n, shape, dt, kind="ExternalInput")
    inp('x_own', [4096, 1024]); inp('x_oth', [4096, 1024]); inp('x_halo', [2048, 1024]); inp('x_halo2', [2048, 1024])
    inp('valid', [128, 16]); inp('valid2', [128, 16]); inp('pos', [128, 64], I32); inp('pos2', [128, 64], I32)
    inp('invf', [1, 16]); inp('idb', [128, 128], BF16); inp('idf', [128, 128])
    inp('norm_mix', [NL, 1024]); inp('w_in', [NL, 1024, 2080]); inp('qk_gain_a', [NL, 2, 64])
    inp('qk_gain_b', [NL, 2, 64]); inp('sink_b', [NL, 4]); inp('q_lat_gain', [NL, 256]); inp('kv_lat_gain', [NL, 128])
    inp('w_uq', [NL, 256, 576]); inp('w_ukv', [NL, 128, 768]); inp('qk_gain_c', [NL, 2, 96]); inp('out_norm', [NL, 1024])
    inp('w_out', [NL, 1024, 1024]); inp('norm_mlp', [NL, 1024]); inp('w_up', [NL, 1024, 4096]); inp('w_down', [NL, 4096, 1024])
    inp('rel_bias_table', [32, 10]); inp('oh', [33, 1660])
    SC = {}

    def scr(n, shape, dt=BF16):
        SC[n] = nc.dram_tensor(n, shape, dt, kind="Internal")
    scr('QA', [4096, 384]); scr('KAx', [6144, 384]); scr('VAx', [6144, 390]); scr('QB', [4096, 256])
    scr('KBx', [6144, 128]); scr('VBx', [6144, 130]); scr('QCT', [6, 96, 4096]); scr('KCT', [6, 96, 8192])
    scr('VC', [8192, 390]); scr('MIX', [4096, 1024]); scr('OA', [3, 4096, 390], F32); scr('FV', [10, 1660], F32)
    scr('X1', [4096, 1024], F32); scr('XA', [4096, 1024], F32); scr('XB', [4096, 1024], F32)
    SC['b'] = Buf('scratch', multi=True)
    SC['bmix'] = Buf('mix', multi=True)
    SC['boa'] = Buf('oa', multi=True)
    SC['bfv'] = Buf('fv', multi=True)
    SC['bx1'] = Buf('x1', multi=True)
    return D, SC


def make_inputs(inputs, core):
    b, half = core // 2, core % 2
    xs = np.asarray(inputs['x'], dtype=np.float32)[b]
    own = xs[half * 4096:(half + 1) * 4096]
    oth = xs[(1 - half) * 4096:(2 - half) * 4096]
    halo = np.zeros((2048, 1024), np.float32)
    valid = np.zeros((2048,), np.float32)
    halo2 = np.zeros((2048, 1024), np.float32)
    valid2 = np.zeros((2048,), np.float32)
    if half == 1:
        halo[0:1024] = oth[3072:4096]
        valid[0:1024] = 1
        halo2[1024:2048] = own[0:1024]
        valid2[1024:2048] = 1
    else:
        halo[1024:2048] = oth[0:1024]
        valid[1024:2048] = 1
        halo2[0:1024] = own[3072:4096]
        valid2[0:1024] = 1
    pos = np.asarray(inputs['positions'][b])
    p_own = pos[half * 4096:(half + 1) * 4096]
    p_oth = pos[(1 - half) * 4096:(2 - half) * 4096]
    pos_l = np.concatenate([p_own, p_oth])
    pos_l2 = np.concatenate([p_oth, p_own])
    m = {
        'x_own': np.ascontiguousarray(own), 'x_oth': np.ascontiguousarray(oth), 'x_halo': halo, 'x_halo2': halo2,
        'valid': np.ascontiguousarray(valid.reshape(16, 128).T),
        'valid2': np.ascontiguousarray(valid2.reshape(16, 128).T),
        'pos': np.ascontiguousarray(pos_l.reshape(64, 128).T.astype(np.int32)),
        'pos2': np.ascontiguousarray(pos_l2.reshape(64, 128).T.astype(np.int32)),
        'invf': (10000.0 ** (-np.arange(16, dtype=np.float32) / 16)).astype(np.float32).reshape(1, 16),
        'idb': np.eye(128).astype(ml_dtypes.bfloat16), 'idf': np.eye(128).astype(np.float32),
        'rel_bias_table': np.ascontiguousarray(inputs['rel_bias_table'], dtype=np.float32), 'oh': make_oh(),
    }
    for k in WNAMES:
        m[k] = np.ascontiguousarray(np.asarray(inputs[k], dtype=np.float32))
    return m


def build_program():
    nc = bass.Bass("TRN2", target_bir_lowering=False)
    D, SC = declare(nc, NLAYER)
    out_d = nc.dram_tensor('out', [4096, 1024], F32, kind="ExternalOutput")
    b_xa = Buf('xa', multi=True)
    b_xb = Buf('xb', multi=True)
    b_out = Buf('out', multi=True)
    with ExitStack() as es:
        P = Prog(nc, es)
        LB = LayerBuilder(nc, P, D, debug={})
        LB.SC = SC
        XA, XB = SC['XA'].ap(), SC['XB'].ap()
        LB.layer(0, D['x_own'].ap(), D['x_oth'].ap(), D['x_halo'].ap(), SC['X1'].ap(), XA, b_xa)
        LB.layer(0, D['x_oth'].ap(), D['x_own'].ap(), D['x_halo2'].ap(), SC['X1'].ap(), XB, b_xb,
                 with_bias_setup=False, pos_key='pos2', valid_key='valid2', reuse_ckv=True)

        def halo_rows(t):
            r0 = 3072 + t * 128 if t < 8 else (t - 8) * 128
            return XB[r0:r0 + 128, :]
        P.barrier('sync', reads=[b_xa, b_xb])
        LB.layer(1, XA, XB, None, SC['X1'].ap(), out_d.ap(), b_out, with_bias_setup=False, halo_rows=halo_rows)
        P.barrier('sync', reads=[b_out])
        P.emit()
    return nc


def kernel(**inputs):
    nc = build_program()
    in_maps = [make_inputs(inputs, c) for c in range(8)]
    res = run_bass_kernel_spmd(nc, in_maps, core_ids=list(range(8)))
    x = np.asarray(inputs['x'])
    out = np.empty(x.shape, np.float32)
    for c in range(8):
        b, half = c // 2, c % 2
        out[b, half * 4096:(half + 1) * 4096] = np.asarray(res.results[c]['out'], dtype=np.float32)
    return out
```
